# Optimizing a Trainium2 kernel written in Bass

```python
import math
import jax, jax.numpy as jnp
from jax import lax
import numpy as np

D_MODEL = 1024
BATCH = 8
SEQ = 2048
DEPTH = 2
DEC_BATCH = 128
DEC_SEQ = 4
PAST_LEN = 2048
PAGE_SIZE = 128

N_A_LAYERS = DEPTH // 2
N_B_LAYERS = DEPTH - N_A_LAYERS
HEAD_DIM = 64
MIX_WIDTH = 3 * D_MODEL // 4
MEM_WIDTH = D_MODEL - MIX_WIDTH
MEM_HEADS = MEM_WIDTH // HEAD_DIM
N_MEM = 256
SSM_GROUP = 16
SSM_GROUPS = MIX_WIDTH // SSM_GROUP
SSM_STATE = 64
DIL_GROUPS = ((128, 1), (512, 4), (2048, 16))
N_DIL = len(DIL_GROUPS)
ATT_HEADS = MIX_WIDTH // HEAD_DIM
HEADS_PER_GROUP = ATT_HEADS // N_DIL
IN_SPLITS = (MIX_WIDTH, 2 * MIX_WIDTH, 2 * MIX_WIDTH + MEM_WIDTH)
IN_WIDTH = 2 * MIX_WIDTH + 2 * MEM_WIDTH
OUT_WIDTH = MIX_WIDTH + MEM_WIDTH
DEEPNORM_ALPHA = (2.0 * DEPTH) ** 0.25
DEEPNORM_BETA = (8.0 * DEPTH) ** -0.25
LN_EPS = 1e-5
SCALE = HEAD_DIM ** -0.5
NEG_INF = -1e30
DT_MIN, DT_MAX = 1e-3, 1e-1

kernel_name = "yoco_s5_dilated_attention_decode_step"


def layer_norm(x, g, b):
    xf = x.astype(jnp.float32)
    mu = jnp.mean(xf, axis=-1, keepdims=True)
    var = jnp.mean(jnp.square(xf - mu), axis=-1, keepdims=True)
    return ((xf - mu) * lax.rsqrt(var + LN_EPS) * g + b).astype(x.dtype)


def alibi_slopes():
    return 2.0 ** (-8.0 * jnp.arange(1, ATT_HEADS + 1, dtype=jnp.float32) / ATT_HEADS)


def _linear_combine(left, right):
    a_l, b_l = left
    a_r, b_r = right
    return a_l * a_r, a_r * b_l + b_r


def s5_branch(u, h0_re, h0_im, lam_re, lam_im, log_dt, b_re, b_im, c_re, c_im, d_skip, w_glu, b_glu):
    f32 = jnp.float32
    Bn, T, _ = u.shape
    lam = lax.complex(jnp.minimum(lam_re.astype(f32), -1e-4), lam_im.astype(f32))
    dt = jnp.exp(log_dt.astype(f32))[:, None]
    lam_bar = jnp.exp(lam * dt)
    b_bar = ((lam_bar - 1.0) / lam)[:, :, None] * lax.complex(b_re.astype(f32), b_im.astype(f32))
    c = lax.complex(c_re.astype(f32), c_im.astype(f32))
    h0 = lax.complex(h0_re.astype(f32), h0_im.astype(f32))
    ug = u.astype(f32).reshape(Bn, T, SSM_GROUPS, SSM_GROUP)
    bu = jnp.einsum('btgc,gpc->btgp', ug.astype(jnp.complex64), b_bar)
    bu = bu.at[:, 0].add(lam_bar * h0)
    a = jnp.broadcast_to(lam_bar, bu.shape)
    _, h = lax.associative_scan(_linear_combine, (a, bu), axis=1)
    y = jnp.real(jnp.einsum('btgp,gcp->btgc', h, c)) + d_skip.astype(f32) * ug
    y = jax.nn.gelu(y.reshape(Bn, T, MIX_WIDTH))
    y = y * jax.nn.sigmoid(y @ w_glu.astype(f32) + b_glu.astype(f32))
    h_last = h[:, -1]
    return y, (jnp.real(h_last), jnp.imag(h_last))


def mem_attention(q, mem_k, mem_v):
    Bn, T, _ = q.shape
    q = q.reshape(Bn, T, MEM_HEADS, HEAD_DIM)
    s = jnp.einsum('bthe,bmhe->bhtm', q, mem_k).astype(jnp.float32) * SCALE
    p = jax.nn.softmax(s, axis=-1)
    o = jnp.einsum('bhtm,bmhe->bthe', p.astype(mem_v.dtype), mem_v)
    return o.reshape(Bn, T, MEM_WIDTH)


def _dilated_group_full(q, k, v, slopes, window, dil):
    f32 = jnp.float32
    Bn, S, H, E = q.shape
    n = window // dil
    L = S // dil
    nblk = -(-L // n)
    Lp = nblk * n

    def strided(t, front):
        t = t.reshape(Bn, L, dil, H, E).transpose(0, 2, 1, 3, 4)
        return jnp.pad(t, ((0, 0), (0, 0), (front, Lp - L), (0, 0), (0, 0)))

    def banded(t):
        t = strided(t, n).reshape(Bn, dil, nblk + 1, n, H, E)
        return jnp.concatenate([t[:, :, :-1], t[:, :, 1:]], axis=3)

    qb = strided(q, 0).reshape(Bn, dil, nblk, n, H, E)
    kb, vb = banded(k), banded(v)
    i = jnp.arange(n)[:, None]
    u = jnp.arange(2 * n)[None, :]
    delta = i + n - u
    key_pos = (jnp.arange(nblk)[:, None, None] - 1) * n + u
    valid = (delta >= 0) & (delta <= n) & (key_pos >= 0)
    bias = -slopes.astype(f32)[:, None, None] * (delta * dil).astype(f32)
    s = jnp.einsum('brjihe,brjuhe->brjhiu', qb, kb).astype(f32) * SCALE + bias
    s = jnp.where(valid[:, None], s, NEG_INF)
    lse = jax.nn.logsumexp(s, axis=-1)
    p = jnp.exp(s - lse[..., None])
    o = jnp.einsum('brjhiu,brjuhe->brjihe', p.astype(vb.dtype), vb)
    o = o.reshape(Bn, dil, Lp, H, E)[:, :, :L].transpose(0, 2, 1, 3, 4).reshape(Bn, S, H, E)
    lse = lse.transpose(0, 1, 2, 4, 3).reshape(Bn, dil, Lp, H)[:, :, :L]
    lse = lse.transpose(0, 2, 1, 3).reshape(Bn, S, H)
    return o, lse


def _dilated_group_cached(q, k_new, v_new, buf_k, buf_v, slopes, window, dil):
    f32 = jnp.float32
    T = q.shape[1]
    Lb = buf_k.shape[1]
    n = window // dil
    kc = jnp.concatenate([buf_k, k_new], axis=1)
    vc = jnp.concatenate([buf_v, v_new], axis=1)
    dist = jnp.arange(n + 1) * dil
    idx = Lb + jnp.arange(T)[:, None] - dist[None, :]
    valid = idx >= 0
    idx = jnp.maximum(idx, 0)
    kg = kc[:, idx]
    vg = vc[:, idx]
    bias = -slopes.astype(f32)[:, None, None] * dist.astype(f32)
    s = jnp.einsum('bthe,btmhe->bhtm', q, kg).astype(f32) * SCALE + bias
    s = jnp.where(valid, s, NEG_INF)
    lse = jax.nn.logsumexp(s, axis=-1)
    p = jnp.exp(s - lse[..., None])
    o = jnp.einsum('bhtm,btmhe->bthe', p.astype(vg.dtype), vg)
    return o, lse.transpose(0, 2, 1)


def _merge_groups(outs, lses):
    w = jax.nn.softmax(jnp.stack(lses, axis=0), axis=0)
    o = jnp.concatenate([outs[g].astype(jnp.float32) * w[g][..., None] for g in range(N_DIL)], axis=2)
    Bn, T = o.shape[0], o.shape[1]
    return o.reshape(Bn, T, MIX_WIDTH)


def dilated_attention_prompt(q, kv):
    slopes = alibi_slopes()
    outs, lses = [], []
    for g, (win, dil) in enumerate(DIL_GROUPS):
        hs = slice(g * HEADS_PER_GROUP, (g + 1) * HEADS_PER_GROUP)
        o, lse = _dilated_group_full(q[:, :, hs], kv[:, :, 0, hs], kv[:, :, 1, hs], slopes[hs], win, dil)
        outs.append(o)
        lses.append(lse)
    return _merge_groups(outs, lses)


def dilated_attention_sample(q, kv, buffers):
    slopes = alibi_slopes()
    outs, lses = [], []
    for g, (win, dil) in enumerate(DIL_GROUPS):
        hs = slice(g * HEADS_PER_GROUP, (g + 1) * HEADS_PER_GROUP)
        buf = buffers[g]
        o, lse = _dilated_group_cached(q[:, :, hs], kv[:, :, 0, hs], kv[:, :, 1, hs],
                                       buf[:, :, 0], buf[:, :, 1], slopes[hs], win, dil)
        outs.append(o)
        lses.append(lse)
    return _merge_groups(outs, lses)


def mixer_layer(x, mem_k, mem_v, branch, w_in, w_out, ln_g, ln_b):
    proj = x @ w_in
    u, gate, mq, mgate = jnp.split(proj, IN_SPLITS, axis=-1)
    y, aux = branch(u)
    y = y.astype(x.dtype) * jax.nn.silu(gate)
    m = mem_attention(mq, mem_k, mem_v).astype(x.dtype) * jax.nn.silu(mgate)
    out = jnp.concatenate([y, m], axis=-1) @ w_out
    return layer_norm(DEEPNORM_ALPHA * x + out, ln_g, ln_b), aux


def run_trunk(x, mem_kv, h0_re, h0_im, attn_fn, w_in, w_out, ln_g, ln_b, ssm_lambda_re, ssm_lambda_im,
              ssm_log_dt, ssm_b_re, ssm_b_im, ssm_c_re, ssm_c_im, ssm_d, w_glu, b_glu, w_kv_shared):
    h_re, h_im = [], []
    kv = None
    for l in range(DEPTH):
        mem_k, mem_v = mem_kv[l][:, :, 0], mem_kv[l][:, :, 1]
        if l < N_A_LAYERS:
            def branch(u, l=l):
                return s5_branch(u, h0_re[l], h0_im[l], ssm_lambda_re[l], ssm_lambda_im[l], ssm_log_dt[l],
                                 ssm_b_re[l], ssm_b_im[l], ssm_c_re[l], ssm_c_im[l], ssm_d[l], w_glu[l], b_glu[l])
            x, (hr, hi) = mixer_layer(x, mem_k, mem_v, branch, w_in[l], w_out[l], ln_g[l], ln_b[l])
            h_re.append(hr)
            h_im.append(hi)
            if l == N_A_LAYERS - 1:
                Bn, T, _ = x.shape
                kv = (x @ w_kv_shared).reshape(Bn, T, 2, ATT_HEADS, HEAD_DIM)
        else:
            def branch(u, kv=kv):
                Bn, T, _ = u.shape
                return attn_fn(u.reshape(Bn, T, ATT_HEADS, HEAD_DIM), kv), None
            x, _ = mixer_layer(x, mem_k, mem_v, branch, w_in[l], w_out[l], ln_g[l], ln_b[l])
    return x, jnp.stack(h_re, axis=0), jnp.stack(h_im, axis=0), kv


def setup_inputs(seed: int = 0) -> dict:
    key = jax.random.key(seed)
    ks = jax.random.split(key, 28)
    f32 = jnp.float32

    def nrm(k, shape, scale=1.0):
        return jax.random.normal(k, shape, f32) * scale

    win_lens = [min(w, PAST_LEN) for (w, _) in DIL_GROUPS]
    n_idx = jnp.arange(SSM_STATE, dtype=f32)
    return {
        "x_prompt": nrm(ks[0], (BATCH, SEQ, D_MODEL)),
        "x_sample": nrm(ks[1], (DEC_BATCH, DEC_SEQ, D_MODEL)),
        "cache_mem_kv": nrm(ks[2], (DEPTH, DEC_BATCH, N_MEM, 2, MEM_HEADS, HEAD_DIM)),
        "state_ssm_re": nrm(ks[3], (N_A_LAYERS, DEC_BATCH, SSM_GROUPS, SSM_STATE), 0.3),
        "state_ssm_im": nrm(ks[4], (N_A_LAYERS, DEC_BATCH, SSM_GROUPS, SSM_STATE), 0.3),
        "cache_dil1_kv": nrm(ks[5], (DEC_BATCH, win_lens[0], 2, HEADS_PER_GROUP, HEAD_DIM)),
        "cache_dil4_kv": nrm(ks[6], (DEC_BATCH, win_lens[1], 2, HEADS_PER_GROUP, HEAD_DIM)),
        "cache_dil16_kv": nrm(ks[7], (DEC_BATCH, win_lens[2], 2, HEADS_PER_GROUP, HEAD_DIM)),
        "mem_prompt": nrm(ks[8], (BATCH, N_MEM, D_MODEL)),
        "w_in": nrm(ks[9], (DEPTH, D_MODEL, IN_WIDTH), D_MODEL ** -0.5),
        "w_out": nrm(ks[10], (DEPTH, OUT_WIDTH, D_MODEL), OUT_WIDTH ** -0.5 * DEEPNORM_BETA),
        "ln_g": 1.0 + nrm(ks[11], (DEPTH, D_MODEL), 0.01),
        "ln_b": nrm(ks[12], (DEPTH, D_MODEL), 0.01),
        "w_mem_kv": jnp.concatenate([nrm(ks[13], (DEPTH, D_MODEL, MEM_WIDTH), D_MODEL ** -0.5),
                                     nrm(ks[14], (DEPTH, D_MODEL, MEM_WIDTH), D_MODEL ** -0.5 * DEEPNORM_BETA)], axis=-1),
        "ssm_lambda_re": -0.5 + nrm(ks[15], (N_A_LAYERS, SSM_GROUPS, SSM_STATE), 0.01),
        "ssm_lambda_im": math.pi * n_idx + nrm(ks[16], (N_A_LAYERS, SSM_GROUPS, SSM_STATE), 0.01),
        "ssm_log_dt": jax.random.uniform(ks[17], (N_A_LAYERS, SSM_GROUPS), f32, math.log(DT_MIN), math.log(DT_MAX)),
        "ssm_b_re": nrm(ks[18], (N_A_LAYERS, SSM_GROUPS, SSM_STATE, SSM_GROUP), (2 * SSM_GROUP) ** -0.5),
        "ssm_b_im": nrm(ks[19], (N_A_LAYERS, SSM_GROUPS, SSM_STATE, SSM_GROUP), (2 * SSM_GROUP) ** -0.5),
        "ssm_c_re": nrm(ks[20], (N_A_LAYERS, SSM_GROUPS, SSM_GROUP, SSM_STATE), 0.5),
        "ssm_c_im": nrm(ks[21], (N_A_LAYERS, SSM_GROUPS, SSM_GROUP, SSM_STATE), 0.5),
        "ssm_d": nrm(ks[22], (N_A_LAYERS, SSM_GROUPS, SSM_GROUP)),
        "w_glu": nrm(ks[23], (N_A_LAYERS, MIX_WIDTH, MIX_WIDTH), MIX_WIDTH ** -0.5),
        "b_glu": nrm(ks[24], (N_A_LAYERS, MIX_WIDTH), 0.01),
        "w_kv_shared": jnp.concatenate([nrm(ks[25], (D_MODEL, MIX_WIDTH), D_MODEL ** -0.5),
                                        nrm(ks[26], (D_MODEL, MIX_WIDTH), D_MODEL ** -0.5 * DEEPNORM_BETA)], axis=-1),
    }


def reference(x_prompt, x_sample, cache_mem_kv, state_ssm_re, state_ssm_im, cache_dil1_kv, cache_dil4_kv,
              cache_dil16_kv, mem_prompt, w_in, w_out, ln_g, ln_b, w_mem_kv, ssm_lambda_re, ssm_lambda_im,
              ssm_log_dt, ssm_b_re, ssm_b_im, ssm_c_re, ssm_c_im, ssm_d, w_glu, b_glu, w_kv_shared):
    weights = (w_in, w_out, ln_g, ln_b, ssm_lambda_re, ssm_lambda_im, ssm_log_dt, ssm_b_re, ssm_b_im,
               ssm_c_re, ssm_c_im, ssm_d, w_glu, b_glu, w_kv_shared)

    Bp = x_prompt.shape[0]
    mem_kv_prompt = jnp.einsum('bmd,ldk->lbmk', mem_prompt, w_mem_kv).reshape(
        DEPTH, Bp, N_MEM, 2, MEM_HEADS, HEAD_DIM)
    zeros = jnp.zeros((N_A_LAYERS, Bp, SSM_GROUPS, SSM_STATE), jnp.float32)
    y_prompt, ssm_re_prompt, ssm_im_prompt, kv_p = run_trunk(
        x_prompt, mem_kv_prompt, zeros, zeros, dilated_attention_prompt, *weights)
    S = kv_p.shape[1]
    win_p = [kv_p[:, S - min(win, S):, :, g * HEADS_PER_GROUP:(g + 1) * HEADS_PER_GROUP]
             for g, (win, _) in enumerate(DIL_GROUPS)]
    dil1_kv_prompt, dil4_kv_prompt, dil16_kv_prompt = win_p

    buffers = (cache_dil1_kv, cache_dil4_kv, cache_dil16_kv)

    def attn_sample(q, kv):
        return dilated_attention_sample(q, kv, buffers)

    y_sample, ssm_re_sample, ssm_im_sample, kv_s = run_trunk(
        x_sample, cache_mem_kv, state_ssm_re, state_ssm_im, attn_sample, *weights)
    win_s = []
    for g, (win, _) in enumerate(DIL_GROUPS):
        full = jnp.concatenate([buffers[g], kv_s[:, :, :, g * HEADS_PER_GROUP:(g + 1) * HEADS_PER_GROUP]], axis=1)
        keep = min(win, full.shape[1])
        win_s.append(full[:, full.shape[1] - keep:])
    dil1_kv_sample, dil4_kv_sample, dil16_kv_sample = win_s

    return (y_prompt, y_sample, mem_kv_prompt, ssm_re_prompt, ssm_im_prompt, dil1_kv_prompt, dil4_kv_prompt,
            dil16_kv_prompt, ssm_re_sample, ssm_im_sample, dil1_kv_sample, dil4_kv_sample, dil16_kv_sample)
```

```python
import math
import os
KCUT = int(os.environ.get('KCUT', '99'))
import numpy as np
import concourse.bass as bass
import concourse.mybir as mybir
from concourse.bass_utils import run_bass_kernel_spmd
from contextlib import ExitStack
from types import SimpleNamespace

F32 = mybir.dt.float32
BF16 = mybir.dt.bfloat16
I32 = mybir.dt.int32
AF = mybir.ActivationFunctionType
ALU = mybir.AluOpType

NCORES = 8
SEQ = 2048
NS = 64
NB = 16
NTOK = SEQ + NS
ALPHA = (2.0 * 2) ** 0.25
LN_EPS = 1e-5
SCALE = 0.125
BIG = 1.0e6
TWO_PI = 2.0 * math.pi
C1_2PI = 6.28125
C2_2PI = TWO_PI - 6.28125
PI_LO = 3.1415925
SLOPES = [2.0 ** (-8.0 * (h + 1) / 12.0) for h in range(12)]
DILS = [1, 4, 16]


class _Stop(Exception):
    pass


class Sched:
    def __init__(self, nc, es, n_dma_sems=40):
        self.nc = nc
        self.eng = {'pe': nc.tensor, 'dve': nc.vector, 'act': nc.scalar, 'pool': nc.gpsimd, 'sp': nc.sync}
        self.sem = {k: es.enter_context(nc.semaphore('sem_' + k)) for k in self.eng}
        self.cnt = {k: 0 for k in self.eng}
        self.n_hw = n_dma_sems - 16
        self.dsem = [es.enter_context(nc.semaphore('dsem%d' % i)) for i in range(n_dma_sems)]
        self.dnext_sw = 0
        self.dcnt = [0] * n_dma_sems
        self.dbg = [False] * n_dma_sems
        self.dnext = 0
        self.bsem = es.enter_context(nc.semaphore('bulk'))
        self.bcnt = 0
        self.waited = {k: {} for k in self.eng}
        self.lastw = {}
        self.readers = {}
        self.nops = 0
        self.nwait = 0

    def semobj(self, sid):
        if sid == 'bulk':
            return self.bsem
        return self.sem[sid] if isinstance(sid, str) else self.dsem[sid]

    def _wait(self, e, sid, val):
        if val <= 0:
            return
        if sid == e and e == 'pe':
            return
        w = self.waited[e]
        if w.get(sid, 0) >= val:
            return
        self.eng[e].wait_ge(self.semobj(sid), val)
        self.nwait += 1
        w[sid] = val

    def _deps(self, e, reads, writes):
        for r in reads:
            if r in self.lastw:
                self._wait(e, *self.lastw[r])
        for r in writes:
            if r in self.lastw:
                self._wait(e, *self.lastw[r])
            for sid, val in self.readers.get(r, {}).items():
                self._wait(e, sid, val)

    def _commit(self, sid, val, reads, writes):
        for r in reads:
            d = self.readers.setdefault(r, {})
            d[sid] = max(val, d.get(sid, 0))
        for r in writes:
            self.lastw[r] = (sid, val)
            self.readers[r] = {}

    def op(self, e, fn, reads=(), writes=()):
        isps = lambda r: isinstance(r, str) and r.startswith('ps') and r[2:].isdigit()
        writes = list(writes) + [r for r in reads if isps(r)]
        reads = [r for r in reads if not isps(r)]
        self._deps(e, reads, writes)
        ins = fn(self.eng[e])
        self.cnt[e] += 1
        ins.then_inc(self.sem[e], 1)
        self._commit(e, self.cnt[e], reads, writes)
        self.nops += 1

    def dma(self, e, out, in_, reads=(), writes=(), bg=False, **kw):
        if e == 'pool':
            j = self.n_hw + self.dnext_sw
            self.dnext_sw = (self.dnext_sw + 1) % (len(self.dsem) - self.n_hw)
        else:
            j = self.dnext
            self.dnext = (j + 1) % self.n_hw
        self._wait(e, j, self.dcnt[j])
        self._deps(e, reads, writes)
        ins = self.eng[e].dma_start(out=out, in_=in_, **kw)
        self.dcnt[j] += 16
        self.dbg[j] = bg
        ins.then_inc(self.dsem[j], 16)
        self._commit(j, self.dcnt[j], reads, writes)
        self.nops += 1

    def dma_bulk(self, e, out, in_):
        ins = self.eng[e].dma_start(out=out, in_=in_)
        self.bcnt += 16
        ins.then_inc(self.bsem, 16)

    def barrier(self, all_dma=False):
        for e in self.eng:
            for k in self.eng:
                if k != e:
                    self._wait(e, k, self.cnt[k])
            for j in range(len(self.dsem)):
                if all_dma or not self.dbg[j]:
                    self._wait(e, j, self.dcnt[j])

    def finish(self):
        self.barrier(all_dma=True)
        self._wait('sp', 'bulk', self.bcnt)


def build(stage=99):
    nc = bass.Bass("TRN2", target_bir_lowering=False)
    NBC = NB if (stage >= 5 and not (50 <= KCUT < 60)) else 1
    NBM = NB if stage >= 5 else 1

    def din(name, shape, dt=F32):
        return nc.dram_tensor(name, list(shape), dt, kind="ExternalInput").ap()

    def dout(name, shape, dt=F32):
        return nc.dram_tensor(name, list(shape), dt, kind="ExternalOutput").ap()

    def dscr(name, shape, dt=F32):
        return nc.dram_tensor(name, list(shape), dt, kind="Internal").ap()

    x_p = din("x_p", [SEQ, 1024])
    x_s = din("x_s", [NS, 1024])
    cmk = din("cmk", [2, NBM, 256, 512])
    st_re = din("st_re", [NB * 48, 64])
    st_im = din("st_im", [NB * 48, 64])
    cd = [din("cd1", [NBC, 128, 512]), din("cd4", [NBC, 512, 512]), din("cd16", [NBC, 2048, 512])]
    memp = din("memp", [256, 1024])
    w_in = din("w_in", [2, 1024, 2048])
    w_out = din("w_out", [2, 1024, 1024])
    ln_g = din("ln_g", [2, 1024])
    ln_b = din("ln_b", [2, 1024])
    w_mem = din("w_mem", [2, 1024, 512])
    lam_re = din("lam_re", [48, 64])
    lam_im = din("lam_im", [48, 64])
    log_dt = din("log_dt", [48])
    b_re = din("b_re", [48, 64, 16])
    b_im = din("b_im", [48, 64, 16])
    c_re = din("c_re", [768, 64])
    c_im = din("c_im", [768, 64])
    ssm_d = din("ssm_d", [768])
    w_glu = din("w_glu", [768, 768])
    b_glu = din("b_glu", [768])
    w_kv = din("w_kv", [1024, 1536])
    c_ident = din("c_ident", [128, 128])
    c_swap = din("c_swap", [128, 128])
    c_maskeo = din("c_maskeo", [128, 4])
    c_jvec = din("c_jvec", [128, 128])
    c_dist = din("c_dist", [128, 256])
    c_sdist = din("c_sdist", [128, 3, 16])
    c_ndist = din("c_ndist", [64, 2, 64])

    y_p = dout("y_p", [SEQ, 1024])
    y_s = dout("y_s", [NS, 1024])
    mkv_p = dout("mkv_p", [2, 256, 512])
    sre_p = dout("sre_p", [48, 64])
    sim_p = dout("sim_p", [48, 64])
    dkv_p = [dout("d1_p", [128, 512]), dout("d4_p", [512, 512]), dout("d16_p", [2048, 512])]
    sre_s = dout("sre_s", [NB * 48, 64])
    sim_s = dout("sim_s", [NB * 48, 64])
    dkv_s = [dout("d1_s", [NBC, 128, 512]), dout("d4_s", [NBC, 512, 512]), dout("d16_s", [NBC, 2048, 512])]

    x1scr = dscr("x1scr", [NTOK, 1024], F32)
    x1Tscr = dscr("x1Tscr", [8, 128, NTOK], BF16)

    with ExitStack() as es0:
        S = Sched(nc, es0)

        def sbt(es, name, shape, dt=F32):
            return es.enter_context(nc.sbuf_tensor(name, list(shape), dt))

        ps = [es0.enter_context(nc.psum_tensor("ps%d" % i, [128, 512], F32)) for i in range(8)]
        PK = ['ps%d' % i for i in range(8)]

        identf = sbt(es0, "identf", [128, 128])
        identb = sbt(es0, "identb", [128, 128], BF16)
        onesb = sbt(es0, "onesb", [128, 128], BF16)
        S.dma('sp', identf[:], c_ident[:, :], writes=['identf'])
        S.op('dve', lambda e: e.tensor_copy(out=identb[:], in_=identf[:]), reads=['identf'], writes=['identb'])
        S.op('dve', lambda e: e.memset(onesb[:], 1.0), writes=['onesb'])
        KmT = sbt(es0, "KmT", [128, 2, 2, 256], BF16)
        Vm = sbt(es0, "Vm", [128, 2, 2, 256], BF16)
        lnG = sbt(es0, "lnG", [128, 1024])
        lnB = sbt(es0, "lnB", [128, 1024])

        wins = [128, 512, 2048]
        bulk_list = []
        for g in (range(3) if stage != 0 and stage != 2 and not (50 <= KCUT < 60) else []):
            for b in range(NBC):
                src = cd[g][b, 4:wins[g], :].rearrange("r c -> (r c)").rearrange("(a x) -> a x", a=16)
                dst = dkv_s[g][b, 0:wins[g] - 4, :].rearrange("r c -> (r c)").rearrange("(a x) -> a x", a=16)
                bulk_list.append((dst, src))

        def issue_bulk(n):
            for _ in range(min(n, len(bulk_list))):
                dst, src = bulk_list.pop()
                S.dma_bulk('act', dst, src)

        def evac(i, out, in_, reads, writes):
            if i % 2 == 0:
                S.op('act', lambda e: e.activation(out=out, in_=in_, func=AF.Copy), reads=reads, writes=writes)
            else:
                S.op('dve', lambda e: e.tensor_copy(out=out, in_=in_), reads=reads, writes=writes)

        def load_weight(es_w, dst, src_rows, ncols, stage_tiles, key, col_map=None):
            nk = dst.shape[1]
            for kc in range(nk):
                S.dma('pool', dst[:, kc, :], src_rows(kc), writes=[(key, kc)], bg=True)

        def wkeys(key, n):
            return [(key, kc) for kc in range(n)]

        def ln_part1(T, blk_i, rows, cat, wout_sb, x_src):
            c0 = blk_i * 128
            bb = blk_i % len(T.zt)
            xr = T.xres[blk_i % 2]
            xk = ['stgA', 'stgB'][blk_i % 2]
            zt = T.zt[bb]
            zk = 'zt%d' % bb
            S.dma('sp', xr[0:rows, :], x_src, writes=[xk])
            for n in range(2):
                for kc in range(8):
                    S.op('pe', lambda e: e.matmul(ps[n][0:rows, :], lhsT=cat[:, kc, c0:c0 + rows], rhs=wout_sb[:, kc, n * 512:(n + 1) * 512], start=(kc == 0), stop=(kc == 7)),
                         reads=[('catT', kc), ('w_out_sb', kc)], writes=[PK[n]])
                S.op('dve', lambda e: e.scalar_tensor_tensor(out=zt[0:rows, n * 512:(n + 1) * 512], in0=xr[0:rows, n * 512:(n + 1) * 512], scalar=ALPHA,
                                                             in1=ps[n][0:rows, :], op0=ALU.mult, op1=ALU.add),
                     reads=[xk, PK[n]], writes=[(zk, n), zk])
                S.op('dve', lambda e: e.bn_stats(out=T.stats[bb][0:rows, n, :], in_=zt[0:rows, n * 512:(n + 1) * 512]), reads=[(zk, n)], writes=[('stats', bb, n)])
            S.op('dve', lambda e: e.bn_aggr(out=T.mv[bb][0:rows, :], in_=T.stats[bb][0:rows, :, :]), reads=[('stats', bb, 0), ('stats', bb, 1)], writes=[('mv', bb)])

        def ln_part2(T, blk_i, rows, tok0, y_dst, make_T):
            bb = blk_i % len(T.zt)
            zt = T.zt[bb]
            zk = 'zt%d' % bb
            mv, rstd, nmr = T.mv[bb], T.rstd[bb], T.nmr[bb]
            S.op('act', lambda e: e.activation(out=rstd[0:rows, :], in_=mv[0:rows, 1:2], func=AF.Sqrt, bias=T.epsb[0:rows, :], scale=1.0), reads=[('mv', bb), 'epsb'], writes=[('rstd', bb)])
            S.op('dve', lambda e: e.reciprocal(out=rstd[0:rows, :], in_=rstd[0:rows, :]), reads=[('rstd', bb)], writes=[('rstd', bb)])
            S.op('dve', lambda e: e.tensor_scalar(out=nmr[0:rows, :], in0=mv[0:rows, 0:1], scalar1=rstd[0:rows, 0:1], scalar2=-1.0, op0=ALU.mult, op1=ALU.mult),
                 reads=[('mv', bb), ('rstd', bb)], writes=[('nmr', bb)])
            S.op('act', lambda e: e.activation(out=zt[0:rows, :], in_=zt[0:rows, :], func=AF.Identity, scale=rstd[0:rows, 0:1], bias=nmr[0:rows, 0:1]),
                 reads=[(zk, 0), (zk, 1), ('rstd', bb), ('nmr', bb)], writes=[zk, (zk, 0), (zk, 1)])
            S.op('dve', lambda e: e.tensor_tensor(out=zt[0:rows, :], in0=zt[0:rows, :], in1=lnG[0:rows, :], op=ALU.mult), reads=[zk, 'lnG'], writes=[zk])
            S.op('pool', lambda e: e.tensor_tensor(out=zt[0:rows, :], in0=zt[0:rows, :], in1=lnB[0:rows, :], op=ALU.add), reads=[zk, 'lnB'], writes=[zk])
            S.dma('sp', y_dst, zt[0:rows, :], reads=[zk])
            if make_T:
                S.op('act', lambda e: e.activation(out=T.x1b[0:rows, :], in_=zt[0:rows, :], func=AF.Copy), reads=[zk], writes=['x1b'])
                for kc in range(8):
                    pt = ps[2 + kc // 4]
                    S.op('pe', lambda e: e.matmul(pt[:, (kc % 4) * 128:(kc % 4) * 128 + rows], lhsT=T.x1b[0:rows, kc * 128:(kc + 1) * 128], rhs=identb[0:rows, 0:rows], start=True, stop=True),
                         reads=['x1b', 'identb'], writes=[PK[2 + kc // 4]])
                for hh in range(2):
                    evac(hh, T.x1T[:, 4 * hh:4 * hh + 4, 0:rows], ps[2 + hh][:, :].rearrange("p (k t) -> p k t", k=4)[:, :, 0:rows], [PK[2 + hh]], ['x1T'])
                S.dma('sp', x1Tscr[:, :, tok0:tok0 + rows].rearrange("k p t -> p k t"), T.x1T[:, :, 0:rows], reads=['x1T'])

        def ln_blocks(T, blocks, cat, wout_sb, make_T):
            n = len(blocks)
            for i in range(n + 1):
                if i < n:
                    bi_, rows, x_src, tok0, y_dst = blocks[i]
                    ln_part1(T, bi_, rows, cat, wout_sb, x_src)
                if i >= 1:
                    bi_, rows, x_src, tok0, y_dst = blocks[i - 1]
                    ln_part2(T, bi_, rows, tok0, y_dst, make_T)

        def mem_attention(T, l, NT, mq_t, smg_t, cat, Km, Vmm, psS, psO, psD):
            for h in range(4):
                pr, half = h // 2, h % 2
                hp = slice(64 * half, 64 * half + 64)
                for c in range(2):
                    pS = psS[c] if NT > 256 else psS[0]
                    off = 0 if NT > 256 else c * NT
                    S.op('pe', lambda e, c=c, pr=pr, hp=hp, pS=pS, off=off: e.matmul(pS[:, off:off + NT], lhsT=Km[hp, pr, c * 128:(c + 1) * 128], rhs=mq_t[hp, pr, 0:NT],
                                                                                     start=True, stop=True),
                         reads=['mq', 'KmT'], writes=[PK[ps.index(pS)]])
                if NT > 256:
                    for c in range(2):
                        S.op('act', lambda e, c=c: e.activation(out=T.PT[:, c * NT:(c + 1) * NT], in_=psS[c][:, 0:NT], func=AF.Exp, scale=SCALE),
                             reads=[PK[ps.index(psS[c])]], writes=[('PTm', c)])
                else:
                    S.op('act', lambda e: e.activation(out=T.PT[:, 0:2 * NT], in_=psS[0][:, 0:2 * NT], func=AF.Exp, scale=SCALE),
                         reads=[PK[ps.index(psS[0])]], writes=[('PTm', 0), ('PTm', 1)])
                for c in range(2):
                    S.op('pe', lambda e, c=c, h=h, pr=pr, hp=hp: e.matmul(psO[pr][hp, 0:NT], lhsT=Vmm[:, c, h * 64:(h + 1) * 64], rhs=T.PT[:, c * NT:(c + 1) * NT],
                                                                          start=(c == 0), stop=(c == 1)),
                         reads=[('PTm', c), 'Vm'], writes=[PK[ps.index(psO[pr])]])
                    S.op('pe', lambda e, c=c, pr=pr, hp=hp: e.matmul(psD[pr][hp, 0:NT], lhsT=onesb[:, 0:64], rhs=T.PT[:, c * NT:(c + 1) * NT],
                                                                     start=(c == 0), stop=(c == 1)),
                         reads=[('PTm', c), 'onesb'], writes=[PK[ps.index(psD[pr])]])
            for pr in range(2):
                S.op('dve', lambda e, pr=pr: e.reciprocal(out=T.rden[:, pr * NT:(pr + 1) * NT], in_=psD[pr][:, 0:NT]), reads=[PK[ps.index(psD[pr])]], writes=[('rden', pr)])
                S.op('dve', lambda e, pr=pr: e.tensor_tensor(out=T.rden[:, pr * NT:(pr + 1) * NT], in0=psO[pr][:, 0:NT], in1=T.rden[:, pr * NT:(pr + 1) * NT], op=ALU.mult),
                     reads=[PK[ps.index(psO[pr])], ('rden', pr)], writes=[('rden', pr)])
                S.op('pool', lambda e, pr=pr: e.tensor_tensor(out=cat[:, 6 + pr, 0:NT], in0=T.rden[:, pr * NT:(pr + 1) * NT], in1=smg_t[:, pr, 0:NT], op=ALU.mult),
                     reads=[('rden', pr), 'smg'], writes=[('catT', 6 + pr)])

        def mem_attention_sample(T, l, smg_t, mq_t, cat, psS, psT, psO, psD):
            NT = NS
            psS1 = ps[0]

            def stage1(b):
                kvb, bk = T.kvb[b % 2]
                kts, tk = T.kts[b % 2]
                S.dma('pool', kvb[:, :, :], cmk[l, b].rearrange("(c p) x -> p c x", p=128), writes=bk)
                for c in range(2):
                    for pr in range(2):
                        S.op('pe', lambda e: e.matmul(psT[:, pr * 256 + c * 128:pr * 256 + (c + 1) * 128], lhsT=kvb[:, c, pr * 128:(pr + 1) * 128], rhs=identb[:], start=True, stop=True),
                             reads=bk + ['identb'], writes=[PK[ps.index(psT)]])
                evac(b, kts[:, :, :], psT[:, 0:512].rearrange("p (a m) -> p a m", a=2), [PK[ps.index(psT)]], tk)

            def stage2(b):
                kvb, bk = T.kvb[b % 2]
                kts, tk = T.kts[b % 2]
                for h in range(4):
                    pr, half = h // 2, h % 2
                    hp = slice(64 * half, 64 * half + 64)
                    pS_h = psS if half == 0 else psS1
                    for c in range(2):
                        S.op('pe', lambda e: e.matmul(pS_h[:, (h * 2 + c) * 4:(h * 2 + c) * 4 + 4], lhsT=kts[hp, pr, c * 128:(c + 1) * 128], rhs=mq_t[hp, pr, 4 * b:4 * b + 4], start=True, stop=True),
                             reads=tk + ['mq'], writes=[PK[ps.index(pS_h)]])
                for h in range(4):
                    pS_h = psS if h % 2 == 0 else psS1
                    S.op('act', lambda e: e.activation(out=T.PTs[:, h * 8:h * 8 + 8], in_=pS_h[:, h * 8:h * 8 + 8], func=AF.Exp, scale=SCALE),
                         reads=[PK[ps.index(pS_h)]], writes=['PTs'])
                for h in range(4):
                    pr, half = h // 2, h % 2
                    hp = slice(64 * half, 64 * half + 64)
                    for c in range(2):
                        S.op('pe', lambda e: e.matmul(psO[pr][hp, 4 * b:4 * b + 4], lhsT=kvb[:, c, 256 + h * 64:256 + (h + 1) * 64], rhs=T.PTs[:, (h * 2 + c) * 4:(h * 2 + c) * 4 + 4], start=(c == 0), stop=(c == 1)),
                             reads=bk + ['PTs'], writes=[PK[ps.index(psO[pr])]])
                        S.op('pe', lambda e: e.matmul(psD[pr][hp, 4 * b:4 * b + 4], lhsT=onesb[:, 0:64], rhs=T.PTs[:, (h * 2 + c) * 4:(h * 2 + c) * 4 + 4], start=(c == 0), stop=(c == 1)),
                             reads=['onesb', 'PTs'], writes=[PK[ps.index(psD[pr])]])

            stage1(0)
            for b in range(NB):
                if b + 1 < NB:
                    stage1(b + 1)
                stage2(b)
            for pr in range(2):
                S.op('dve', lambda e, pr=pr: e.reciprocal(out=T.rden[:, pr * NT:(pr + 1) * NT], in_=psD[pr][:, 0:NT]), reads=[PK[ps.index(psD[pr])]], writes=[('rden', pr)])
                S.op('dve', lambda e, pr=pr: e.tensor_tensor(out=T.rden[:, pr * NT:(pr + 1) * NT], in0=psO[pr][:, 0:NT], in1=T.rden[:, pr * NT:(pr + 1) * NT], op=ALU.mult),
                     reads=[PK[ps.index(psO[pr])], ('rden', pr)], writes=[('rden', pr)])
                S.op('pool', lambda e, pr=pr: e.tensor_tensor(out=cat[:, 6 + pr, 0:NT], in0=T.rden[:, pr * NT:(pr + 1) * NT], in1=smg_t[:, pr, 0:NT], op=ALU.mult),
                     reads=[('rden', pr), 'smg'], writes=[('catT', 6 + pr)])

        def in_proj(T, NT, w_sb, xT_t, u_dst, l):
            for mo_ in range(16):
                pt = ps[mo_ % 2]
                pk = PK[mo_ % 2]
                for kc in range(8):
                    S.op('pe', lambda e, kc=kc, mo_=mo_, pt=pt: e.matmul(pt[:, 0:NT], lhsT=w_sb[:, kc, mo_ * 128:(mo_ + 1) * 128], rhs=xT_t[:, kc, 0:NT],
                                                                         start=(kc == 0), stop=(kc == 7)),
                         reads=[('w_in_sb', kc), ('xT', kc)], writes=[pk])
                if mo_ < 6:
                    evac(mo_, u_dst[:, mo_, 0:NT], pt[:, 0:NT], [pk], [('uTb', mo_)])
                elif mo_ < 12:
                    S.op('act', lambda e, mo_=mo_, pt=pt: e.activation(out=T.sg[:, mo_ - 6, 0:NT], in_=pt[:, 0:NT], func=AF.Silu), reads=[pk], writes=[('sg', mo_ - 6)])
                elif mo_ < 14:
                    evac(mo_ + 1, T.mq[:, mo_ - 12, 0:NT], pt[:, 0:NT], [pk], ['mq'])
                else:
                    S.op('act', lambda e, mo_=mo_, pt=pt: e.activation(out=T.smg[:, mo_ - 14, 0:NT], in_=pt[:, 0:NT], func=AF.Silu), reads=[pk], writes=['smg'])

        def load_x_dma(T, x_src_fn, NT):
            nblk = (NT + 127) // 128
            for a in range(nblk):
                rows = min(128, NT - a * 128)
                S.dma('pool', T.xb[0:rows, a, :], x_src_fn(a, rows), writes=[('xb', a)])

        def load_x_transpose(T, NT):
            nblk = (NT + 127) // 128
            for kc in range(8):
                pt = ps[kc % 2]
                for a in range(nblk):
                    rows = min(128, NT - a * 128)
                    S.op('pe', lambda e: e.matmul(pt[:, a * 128:a * 128 + rows], lhsT=T.xb[0:rows, a, kc * 128:(kc + 1) * 128], rhs=identb[0:rows, 0:rows], start=True, stop=True),
                         reads=[('xb', a), 'identb'], writes=[PK[kc % 2]])
                evac(kc, T.xT[:, kc, 0:NT], pt[:, 0:NT], [PK[kc % 2]], [('xT', kc)])

        def _phase1(es1):
              stgA = sbt(es1, "stgA", [128, 1024])
              stgB = sbt(es1, "stgB", [128, 1024])
              stages = [(stgA, 'stgA'), (stgB, 'stgB')]
              w_in_sb = sbt(es1, "w_in_sb", [128, 8, 2048], BF16)
              w_glu_sb = sbt(es1, "w_glu_sb", [128, 6, 768], BF16)
              w_out_sb = sbt(es1, "w_out_sb", [128, 8, 1024], BF16)
              xb = sbt(es1, "xb", [128, 2, 1024], BF16)

              with ExitStack() as esm:
                  memb = sbt(esm, "memb", [128, 2, 1024], BF16)
                  memT = sbt(esm, "memT", [128, 8, 256], BF16)
                  wm_sb2 = sbt(esm, "wm_sb", [128, 2, 8, 512], BF16)
                  mkv_f = sbt(esm, "mkv_f", [128, 512])
                  for c in range(2):
                      S.dma('pool', memb[:, c, :], memp[c * 128:(c + 1) * 128, :], writes=[('memb', c)])
                  if KCUT <= 1:
                      S.barrier()
                      return
                  for kc in range(8):
                      pt = ps[kc % 2]
                      for c in range(2):
                          S.op('pe', lambda e, kc=kc, c=c, pt=pt: e.matmul(pt[:, c * 128:(c + 1) * 128], lhsT=memb[:, c, kc * 128:(kc + 1) * 128],
                                                                            rhs=identb[:], start=True, stop=True),
                               reads=[('memb', c), 'identb'], writes=[PK[kc % 2]])
                      evac(kc, memT[:, kc, :], pt[:, 0:256], [PK[kc % 2]], [('memT', kc)])
                  if KCUT <= 2:
                      S.barrier()
                      return
                  for l in range(2):
                      for kc in range(8):
                          S.dma('pool', wm_sb2[:, l, kc, :], w_mem[l, kc * 128:(kc + 1) * 128, :], writes=[('wm_sb', l, kc)])
                  load_weight(es1, w_in_sb, lambda kc: w_in[0, kc * 128:(kc + 1) * 128, :], 2048, stages, 'w_in_sb')
                  load_weight(es1, w_glu_sb, lambda kc: w_glu[kc * 128:(kc + 1) * 128, :], 768, stages, 'w_glu_sb')
                  load_weight(es1, w_out_sb, lambda kc: w_out[0, kc * 128:(kc + 1) * 128, :], 1024, stages, 'w_out_sb')
                  for a in range(2):
                      S.dma('pool', xb[:, a, :], x_p[a * 128:(a + 1) * 128, :], writes=[('xb', a)], bg=True)
                  for l in range(2):
                      wm_sb = wm_sb2[:, l, :, :]
                      if KCUT == 30:
                          S.barrier()
                          return
                      for c in range(2):
                          pt = ps[c]
                          for kc in range(8):
                              S.op('pe', lambda e, kc=kc, c=c, pt=pt: e.matmul(pt[:, :], lhsT=memT[:, kc, c * 128:(c + 1) * 128], rhs=wm_sb[:, kc, :],
                                                                                start=(kc == 0), stop=(kc == 7)),
                                   reads=[('memT', kc), ('wm_sb', l, kc)], writes=[PK[c]])
                          if KCUT == 31:
                              S.barrier()
                              return
                          S.op('dve', lambda e, pt=pt: e.tensor_copy(out=mkv_f[:], in_=pt[:, :]), reads=[PK[c]], writes=['mkv_f'])
                          S.op('act', lambda e, pt=pt, l=l, c=c: e.activation(out=Vm[:, l, c, :], in_=pt[:, 256:512], func=AF.Copy),
                               reads=[PK[c]], writes=[('Vm', l)])
                          S.dma('sp', mkv_p[l, c * 128:(c + 1) * 128, :], mkv_f[:], reads=['mkv_f'])
                      if KCUT <= 3:
                          S.barrier()
                          return
                      for pr in range(2):
                          pt = ps[2 + pr]
                          for kc in range(8):
                              S.op('pe', lambda e, kc=kc, pr=pr, pt=pt: e.matmul(pt[:, 0:256], lhsT=wm_sb[:, kc, pr * 128:(pr + 1) * 128], rhs=memT[:, kc, :],
                                                                                  start=(kc == 0), stop=(kc == 7)),
                                   reads=[('memT', kc), ('wm_sb', l, kc)], writes=[PK[2 + pr]])
                          evac(pr, KmT[:, l, pr, :], pt[:, 0:256], [PK[2 + pr]], [('KmT', l)])
                  S.barrier()
              if stage <= 1:
                  return

              S.dma('sp', lnG[:], ln_g[0].partition_broadcast(128), writes=['lnG'])
              S.dma('sp', lnB[:], ln_b[0].partition_broadcast(128), writes=['lnB'])

              COS = sbt(es1, "COS", [128, 48, 128], BF16)
              SIN = sbt(es1, "SIN", [128, 48, 128], BF16)
              Bm = sbt(es1, "Bm", [128, 6, 4, 2, 128], BF16)
              Cm = sbt(es1, "Cm", [128, 48, 2, 64], BF16)
              rdec = sbt(es1, "rdec", [128, 48])
              cosL = sbt(es1, "cosL", [128, 48])
              sinL = sbt(es1, "sinL", [128, 48])
              cos1 = sbt(es1, "cos1", [128, 48])
              sin1 = sbt(es1, "sin1", [128, 48])
              cos3 = sbt(es1, "cos3", [128, 48])
              sin3 = sbt(es1, "sin3", [128, 48])
              dvec = sbt(es1, "dvec", [128, 6])
              bglu = sbt(es1, "bglu", [128, 6])
              swf = sbt(es1, "swf", [128, 128])
              carry = sbt(es1, "carry", [128, 48])
              S.dma('sp', swf[:], c_swap[:, :], writes=['swf'])
              S.dma('sp', dvec[:], ssm_d.rearrange("(m r) -> r m", r=128), writes=['dvec'], allow_slow_non_contiguous=True)
              S.dma('sp', bglu[:], b_glu.rearrange("(m r) -> r m", r=128), writes=['bglu'], allow_slow_non_contiguous=True)

              with ExitStack() as esp:
                  L48 = sbt(esp, "L48", [48, 256])
                  logdt = sbt(esp, "logdt", [128, 48])
                  maskeo = sbt(esp, "maskeo", [128, 4])
                  jvec = sbt(esp, "jvec", [128, 128])
                  v = {n: sbt(esp, "v_" + n, [128, 48]) for n in
                       ['dt', 'lr', 'li', 'a', 'th', 'nr', 'ni', 'den', 'kre', 'kim', 'A1', 'A2', 'nA2', 't0', 't1', 'th64', 'c64', 's64', 'th3']}
                  sc_k = sbt(esp, "sc_k", [128, 1024])
                  sc_i = sbt(esp, "sc_i", [128, 1024], I32)
                  sc_p = sbt(esp, "sc_p", [128, 1024])
                  sc_q = sbt(esp, "sc_q", [128, 1024])
                  PH = sbt(esp, "PH", [128, 8, 128])
                  BreT = sbt(esp, "BreT", [128, 48, 16])
                  BimT = sbt(esp, "BimT", [128, 48, 16])
                  BX1 = sbt(esp, "BX1", [128, 48, 16])
                  BX2 = sbt(esp, "BX2", [128, 48, 16])
                  btmp = sbt(esp, "btmp", [128, 48, 16])
                  Crow = sbt(esp, "Crow", [128, 6, 64])
                  Cirow = sbt(esp, "Cirow", [128, 6, 64])
                  CC1 = sbt(esp, "CC1", [128, 6, 128])
                  CC2 = sbt(esp, "CC2", [128, 6, 128])

                  def D(fn, reads, writes):
                      S.op('dve', fn, reads=reads, writes=writes)

                  def sincos(ph, F, cos_out, sin_out, rk, wk):
                      k_, i_, p_, q_ = sc_k[:, 0:F], sc_i[:, 0:F], sc_p[:, 0:F], sc_q[:, 0:F]
                      D(lambda e: e.tensor_scalar(out=k_, in0=ph, scalar1=1.0 / TWO_PI, scalar2=None, op0=ALU.mult), rk, ['sc_k'])
                      D(lambda e: e.tensor_copy(out=i_, in_=k_), ['sc_k'], ['sc_i'])
                      D(lambda e: e.tensor_copy(out=k_, in_=i_), ['sc_i'], ['sc_k'])
                      D(lambda e: e.scalar_tensor_tensor(out=p_, in0=k_, scalar=-C1_2PI, in1=ph, op0=ALU.mult, op1=ALU.add), ['sc_k'] + rk, ['sc_p'])
                      D(lambda e: e.scalar_tensor_tensor(out=p_, in0=k_, scalar=-C2_2PI, in1=p_, op0=ALU.mult, op1=ALU.add), ['sc_k', 'sc_p'], ['sc_p'])
                      D(lambda e: e.tensor_scalar(out=p_, in0=p_, scalar1=-PI_LO, scalar2=PI_LO, op0=ALU.max, op1=ALU.min), ['sc_p'], ['sc_p'])
                      S.op('act', lambda e: e.activation(out=sin_out, in_=p_, func=AF.Sin), reads=['sc_p'], writes=wk)
                      S.op('act', lambda e: e.activation(out=q_, in_=p_, func=AF.Abs), reads=['sc_p'], writes=['sc_q'])
                      D(lambda e: e.tensor_scalar(out=q_, in0=q_, scalar1=-1.0, scalar2=math.pi / 2, op0=ALU.mult, op1=ALU.add), ['sc_q'], ['sc_q'])
                      S.op('act', lambda e: e.activation(out=cos_out, in_=q_, func=AF.Sin), reads=['sc_q'], writes=wk)

                  S.dma('sp', maskeo[:], c_maskeo[:, :], writes=['maskeo'])
                  S.dma('sp', jvec[:], c_jvec[:, :], writes=['jvec'])
                  for q4, src in enumerate([lam_re, lam_re, lam_im, lam_im]):
                      S.dma('sp', L48[:, q4 * 64:(q4 + 1) * 64], src[:, :], writes=[('L48', q4)])
                  S.dma('sp', logdt[:], log_dt.partition_broadcast(128), writes=['logdt'])
                  S.op('act', lambda e: e.activation(out=v['dt'][:], in_=logdt[:], func=AF.Exp), reads=['logdt'], writes=['v_dt'])
                  S.op('pe', lambda e: e.matmul(ps[7][:, 0:48], lhsT=L48[:, 0:128], rhs=identf[0:48, 0:48], start=True, stop=True),
                       reads=[('L48', 0), ('L48', 1), 'identf'], writes=['ps7'])
                  S.op('pe', lambda e: e.matmul(ps[7][:, 64:112], lhsT=L48[:, 128:256], rhs=identf[0:48, 0:48], start=True, stop=True),
                       reads=[('L48', 2), ('L48', 3), 'identf'], writes=['ps7'])
                  D(lambda e: e.tensor_scalar(out=v['lr'][:], in0=ps[7][:, 0:48], scalar1=-1e-4, scalar2=None, op0=ALU.min), ['ps7'], ['v_lr'])
                  D(lambda e: e.tensor_copy(out=v['li'][:], in_=ps[7][:, 64:112]), ['ps7'], ['v_li'])
                  D(lambda e: e.tensor_tensor(out=v['a'][:], in0=v['lr'][:], in1=v['dt'][:], op=ALU.mult), ['v_lr', 'v_dt'], ['v_a'])
                  D(lambda e: e.tensor_tensor(out=v['th'][:], in0=v['li'][:], in1=v['dt'][:], op=ALU.mult), ['v_li', 'v_dt'], ['v_th'])
                  S.op('act', lambda e: e.activation(out=rdec[:], in_=v['a'][:], func=AF.Exp), reads=['v_a'], writes=['rdec'])
                  sincos(v['th'][:], 48, cos1[:], sin1[:], ['v_th'], ['cs1'])
                  D(lambda e: e.tensor_scalar(out=v['th64'][:], in0=v['th'][:], scalar1=64.0, scalar2=None, op0=ALU.mult), ['v_th'], ['v_th64'])
                  sincos(v['th64'][:], 48, v['c64'][:], v['s64'][:], ['v_th64'], ['cs64'])
                  D(lambda e: e.tensor_scalar(out=v['th3'][:], in0=v['th'][:], scalar1=3.0, scalar2=None, op0=ALU.mult), ['v_th'], ['v_th3'])
                  sincos(v['th3'][:], 48, cos3[:], sin3[:], ['v_th3'], ['cs3'])
                  D(lambda e: e.tensor_tensor(out=v['t0'][:], in0=v['c64'][:], in1=v['c64'][:], op=ALU.mult), ['cs64'], ['v_t0'])
                  D(lambda e: e.tensor_scalar(out=cosL[:], in0=v['t0'][:], scalar1=2.0, scalar2=-1.0, op0=ALU.mult, op1=ALU.add), ['v_t0'], ['cosL'])
                  D(lambda e: e.tensor_tensor(out=v['t0'][:], in0=v['s64'][:], in1=v['c64'][:], op=ALU.mult), ['cs64', 'cosL'], ['v_t0'])
                  D(lambda e: e.tensor_scalar(out=sinL[:], in0=v['t0'][:], scalar1=2.0, scalar2=None, op0=ALU.mult), ['v_t0'], ['sinL'])
                  D(lambda e: e.tensor_tensor(out=v['nr'][:], in0=rdec[:], in1=cos1[:], op=ALU.mult), ['rdec', 'cs1'], ['v_nr'])
                  D(lambda e: e.tensor_scalar(out=v['nr'][:], in0=v['nr'][:], scalar1=-1.0, scalar2=None, op0=ALU.add), ['v_nr'], ['v_nr'])
                  D(lambda e: e.tensor_tensor(out=v['ni'][:], in0=rdec[:], in1=sin1[:], op=ALU.mult), ['rdec', 'cs1'], ['v_ni'])
                  D(lambda e: e.tensor_tensor(out=v['den'][:], in0=v['lr'][:], in1=v['lr'][:], op=ALU.mult), ['v_lr'], ['v_den'])
                  D(lambda e: e.tensor_tensor(out=v['t0'][:], in0=v['li'][:], in1=v['li'][:], op=ALU.mult), ['v_li', 'sinL'], ['v_t0'])
                  D(lambda e: e.tensor_tensor(out=v['den'][:], in0=v['den'][:], in1=v['t0'][:], op=ALU.add), ['v_den', 'v_t0'], ['v_den'])
                  D(lambda e: e.reciprocal(out=v['den'][:], in_=v['den'][:]), ['v_den'], ['v_den'])
                  D(lambda e: e.tensor_tensor(out=v['t0'][:], in0=v['nr'][:], in1=v['lr'][:], op=ALU.mult), ['v_nr', 'v_lr', 'v_den'], ['v_t0'])
                  D(lambda e: e.tensor_tensor(out=v['t1'][:], in0=v['ni'][:], in1=v['li'][:], op=ALU.mult), ['v_ni', 'v_li'], ['v_t1'])
                  D(lambda e: e.tensor_tensor(out=v['kre'][:], in0=v['t0'][:], in1=v['t1'][:], op=ALU.add), ['v_t0', 'v_t1'], ['v_kre'])
                  D(lambda e: e.tensor_tensor(out=v['kre'][:], in0=v['kre'][:], in1=v['den'][:], op=ALU.mult), ['v_kre', 'v_den'], ['v_kre'])
                  D(lambda e: e.tensor_tensor(out=v['t0'][:], in0=v['ni'][:], in1=v['lr'][:], op=ALU.mult), ['v_ni', 'v_lr', 'v_kre'], ['v_t0'])
                  D(lambda e: e.tensor_tensor(out=v['t1'][:], in0=v['nr'][:], in1=v['li'][:], op=ALU.mult), ['v_nr', 'v_li', 'v_kre'], ['v_t1'])
                  D(lambda e: e.tensor_tensor(out=v['kim'][:], in0=v['t0'][:], in1=v['t1'][:], op=ALU.subtract), ['v_t0', 'v_t1'], ['v_kim'])
                  D(lambda e: e.tensor_tensor(out=v['kim'][:], in0=v['kim'][:], in1=v['den'][:], op=ALU.mult), ['v_kim', 'v_den'], ['v_kim'])
                  D(lambda e: e.tensor_copy(out=v['A1'][0:64, :], in_=v['kre'][0:64, :]), ['v_kre'], ['v_A1'])
                  D(lambda e: e.tensor_copy(out=v['A1'][64:128, :], in_=v['kim'][64:128, :]), ['v_kim', 'v_A1'], ['v_A1'])
                  D(lambda e: e.tensor_scalar(out=v['A2'][0:64, :], in0=v['kim'][0:64, :], scalar1=-1.0, scalar2=None, op0=ALU.mult), ['v_kim'], ['v_A2'])
                  D(lambda e: e.tensor_copy(out=v['A2'][64:128, :], in_=v['kre'][64:128, :]), ['v_kre', 'v_A2'], ['v_A2'])
                  for hh in range(2):
                      S.dma('sp', BreT[hh * 64:(hh + 1) * 64, :, :], b_re.rearrange("g p c -> p g c"), writes=[('BreT', hh)])
                      S.dma('sp', BimT[hh * 64:(hh + 1) * 64, :, :], b_im.rearrange("g p c -> p g c"), writes=[('BimT', hh)])
                  A1b = v['A1'][:].unsqueeze(2).to_broadcast([128, 48, 16])
                  A2b = v['A2'][:].unsqueeze(2).to_broadcast([128, 48, 16])
                  rB = [('BreT', 0), ('BreT', 1), ('BimT', 0), ('BimT', 1), 'v_A1', 'v_A2']
                  D(lambda e: e.tensor_tensor(out=BX1[:], in0=BreT[:], in1=A1b, op=ALU.mult), rB, ['BX1'])
                  D(lambda e: e.tensor_tensor(out=btmp[:], in0=BimT[:], in1=A2b, op=ALU.mult), rB, ['btmp'])
                  D(lambda e: e.tensor_tensor(out=BX1[:], in0=BX1[:], in1=btmp[:], op=ALU.add), ['BX1', 'btmp'], ['BX1'])
                  D(lambda e: e.tensor_tensor(out=BX2[:], in0=BimT[:], in1=A1b, op=ALU.mult), rB, ['BX2'])
                  D(lambda e: e.tensor_tensor(out=btmp[:], in0=BreT[:], in1=A2b, op=ALU.mult), rB + ['BX1'], ['btmp'])
                  D(lambda e: e.tensor_tensor(out=BX2[:], in0=BX2[:], in1=btmp[:], op=ALU.subtract), ['BX2', 'btmp'], ['BX2'])
                  for m in range(6):
                      for xi, BX in enumerate([BX1, BX2]):
                          pt = ps[(2 * m + xi) % 2]
                          pk = PK[(2 * m + xi) % 2]
                          S.op('pe', lambda e, m=m, BX=BX, pt=pt: e.matmul(pt[:, 0:128], lhsT=BX[:, 8 * m:8 * m + 8, :], rhs=identf[:], start=True, stop=True),
                               reads=['BX1', 'BX2', 'identf'], writes=[pk])
                          for mem in range(4):
                              D(lambda e, m=m, xi=xi, mem=mem, pt=pt: e.tensor_scalar(out=Bm[:, m, mem, xi, :], in0=pt[:, 0:128], scalar1=maskeo[:, mem:mem + 1],
                                                                                     scalar2=None, op0=ALU.mult), [pk, 'maskeo'], ['Bm'])
                  S.dma('sp', Crow[:], c_re.rearrange("(m r) p -> r m p", r=128), writes=['Crow'])
                  S.dma('sp', Cirow[:], c_im.rearrange("(m r) p -> r m p", r=128), writes=['Cirow'])
                  D(lambda e: e.tensor_copy(out=CC1[:, :, 0:64], in_=Crow[:]), ['Crow'], ['CC1'])
                  D(lambda e: e.tensor_scalar(out=CC1[:, :, 64:128], in0=Cirow[:], scalar1=-1.0, scalar2=None, op0=ALU.mult), ['Cirow', 'CC1'], ['CC1'])
                  D(lambda e: e.tensor_scalar(out=CC2[:, :, 0:64], in0=Cirow[:], scalar1=-1.0, scalar2=None, op0=ALU.mult), ['Cirow'], ['CC2'])
                  D(lambda e: e.tensor_scalar(out=CC2[:, :, 64:128], in0=Crow[:], scalar1=-1.0, scalar2=None, op0=ALU.mult), ['Crow', 'CC2'], ['CC2'])
                  S.op('pool', lambda e: e.memset(Cm[:], 0.0), writes=['Cm'])
                  for m in range(6):
                      for xi, CC in enumerate([CC1, CC2]):
                          pt = ps[(2 * m + xi) % 2]
                          pk = PK[(2 * m + xi) % 2]
                          S.op('pe', lambda e, m=m, CC=CC, pt=pt: e.matmul(pt[:, 0:128], lhsT=CC[:, m, :], rhs=identf[:], start=True, stop=True),
                               reads=['CC1', 'CC2', 'identf'], writes=[pk])
                          for par in range(4):
                              dst = Cm[:, 8 * m:8 * m + 8, xi, :].rearrange("p (q r) c -> p q r c", r=4)[:, :, par, par * 16:(par + 1) * 16]
                              srcv = pt[:, 0:128].rearrange("p (q r c) -> p q r c", r=4, c=16)[:, :, par, :]
                              D(lambda e, dst=dst, srcv=srcv: e.tensor_copy(out=dst, in_=srcv), [pk, 'Cm'], ['Cm'])
                  for m in range(6):
                      thb = v['th'][:, 8 * m:8 * m + 8].unsqueeze(2).to_broadcast([128, 8, 128])
                      jb = jvec[:].unsqueeze(1).to_broadcast([128, 8, 128])
                      D(lambda e, thb=thb, jb=jb: e.tensor_tensor(out=PH[:], in0=thb, in1=jb, op=ALU.mult), ['v_th', 'jvec'], ['PH'])
                      sincos(PH[:].rearrange("p g j -> p (g j)"), 1024,
                             COS[:, 8 * m:8 * m + 8, :].rearrange("p g j -> p (g j)"),
                             SIN[:, 8 * m:8 * m + 8, :].rearrange("p g j -> p (g j)"), ['PH'], [('TAB', m)])
                  S.op('dve', lambda e: e.memset(carry[:], 0.0), writes=['carry'])
                  S.barrier()
              if stage <= 2:
                  return

              NTM = 256
              xT = sbt(es1, "xT", [128, 8, NTM], BF16)
              uTb = sbt(es1, "uTb", [128, 6, NTM], BF16)
              sg = sbt(es1, "sg", [128, 6, NTM], BF16)
              mq = sbt(es1, "mq", [128, 2, NTM], BF16)
              smg = sbt(es1, "smg", [128, 2, NTM], BF16)
              ygb = sbt(es1, "ygb", [128, 6, NTM], BF16)
              t1b = [sbt(es1, "t1b%d" % i, [128, 2 * NTM]) for i in range(2)]
              t2b = [sbt(es1, "t2b%d" % i, [128, 2 * NTM]) for i in range(2)]
              mbufs = [sbt(es1, "mbufA", [128, 8, NTM]), sbt(es1, "mbufB", [128, 8, NTM])]
              d1b = [sbt(es1, "d1b%d" % i, [128, 2 * NTM], BF16) for i in range(2)]
              d2b = [sbt(es1, "d2b%d" % i, [128, 2 * NTM], BF16) for i in range(2)]
              ypre = sbt(es1, "ypre", [128, NTM])
              sig = [sbt(es1, "sig%d" % i, [128, NTM]) for i in range(2)]
              catT = sbt(es1, "catT", [128, 8, NTM], BF16)
              PT = sbt(es1, "PTm", [128, 2 * NTM], BF16)
              rden = sbt(es1, "rden", [128, 2 * NTM])
              mo = rden
              xres = [stgA, stgB]
              zt = [sbt(es1, "zt%d" % i, [128, 1024]) for i in range(2)]
              zn = zt
              stats = [sbt(es1, "stats%d" % i, [128, 2, 6]) for i in range(2)]
              mv = [sbt(es1, "mv%d" % i, [128, 2]) for i in range(2)]
              rstd = [sbt(es1, "rstd%d" % i, [128, 1]) for i in range(2)]
              nmr = [sbt(es1, "nmr%d" % i, [128, 1]) for i in range(2)]
              x1b = sbt(es1, "x1b", [128, 1024], BF16)
              x1T = sbt(es1, "x1T", [128, 8, 128], BF16)
              cl8 = sbt(es1, "cl8", [128, 8])
              ct8 = sbt(es1, "ct8", [128, 8])
              hl = sbt(es1, "hl", [128, 48])
              hlT = sbt(es1, "hlT", [48, 128])

              epsb = sbt(es1, "epsb", [128, 1])
              S.op('dve', lambda e: e.memset(epsb[:], LN_EPS), writes=['epsb'])
              PTs = sbt(es1, "PTs", [128, 32], BF16)
              mA = mbufs[0][:].rearrange("p g n -> p (g n)")
              mB = mbufs[1][:].rearrange("p g n -> p (g n)")
              h0rows = mA[:, 0:768].rearrange("p (c x) -> p c x", c=6)
              h0T = mA[:, 768:1536]
              t1s = mA[:, 1536:2048]
              ginit = mB[:, 0:768]
              g3all = mB[:, 768:1536]
              t2s = mB[:, 1536:2048]
              tm8 = sbt(es1, "tm8", [128, 128])
              d1s = sbt(es1, "d1s", [128, 512], BF16)
              d2s = sbt(es1, "d2s", [128, 512], BF16)
              T1 = SimpleNamespace(xres=xres, zt=zt, zn=zn, stats=stats, mv=mv, rstd=rstd, nmr=nmr, epsb=epsb, x1b=x1b, x1T=x1T, PT=PT, rden=rden, mo=mo,
                                   sg=sg, mq=mq, smg=smg, stages=stages, xb=xb, xT=xT, PTs=PTs,
                                   kvb=[(xb[:, i, :].rearrange("p (c x) -> p c x", c=2), [('xb', i)]) for i in range(2)],
                                   kts=[(xT[:, 2 * i:2 * i + 2, :], [('xT', 2 * i), ('xT', 2 * i + 1)]) for i in range(2)])

              tiles = [(i * NTM, NTM, False) for i in range(SEQ // NTM)]
              if stage >= 5:
                  tiles.append((SEQ, NS, True))

              def xsrc_of(ti_):
                  tk0, _, iss = tiles[ti_]
                  if iss:
                      return lambda a, rows: x_s[0:rows, :]
                  return lambda a, rows: x_p[tk0 + a * 128:tk0 + a * 128 + rows, :]

              for ti, (tok0, NT, is_s) in enumerate(tiles):
                  issue_bulk(7)
                  if is_s:
                      S.barrier()
                  load_x_transpose(T1, NT)
                  if ti + 1 < len(tiles):
                      load_x_dma(T1, xsrc_of(ti + 1), tiles[ti + 1][1])
                  in_proj(T1, NT, w_in_sb, xT, uTb, 0)
                  nchunk = NT // 128
                  if not is_s:
                      def phaseA(m, kk):
                          q = kk // 2
                          bi = kk % 2
                          rows = slice(64 * q, 64 * q + 64)
                          pX1, pX2 = ps[2 + bi], ps[4 + bi]
                          mb = mbufs[m % 2]
                          for j, gl in enumerate((2 * kk, 2 * kk + 1)):
                              mem = gl % 4
                              S.op('pe', lambda e: e.matmul(pX1[:, j * NT:(j + 1) * NT], lhsT=Bm[rows, m, mem, 0, :], rhs=uTb[rows, m, 0:NT], start=True, stop=True),
                                   reads=['Bm', ('uTb', m)], writes=[PK[2 + bi]])
                              S.op('pe', lambda e: e.matmul(pX2[:, j * NT:(j + 1) * NT], lhsT=Bm[rows, m, mem, 1, :], rhs=uTb[rows, m, 0:NT], start=True, stop=True),
                                   reads=['Bm', ('uTb', m)], writes=[PK[4 + bi]])
                          g0 = 8 * m + 2 * kk
                          cosb = COS[:, g0:g0 + 2, :].unsqueeze(2).to_broadcast([128, 2, nchunk, 128])
                          sinb = SIN[:, g0:g0 + 2, :].unsqueeze(2).to_broadcast([128, 2, nchunk, 128])
                          v = lambda ap: ap.rearrange("p (g k j) -> p g k j", g=2, j=128)
                          S.op('dve', lambda e: e.tensor_tensor(out=v(t1b[bi][:, 0:2 * NT]), in0=v(pX1[:, 0:2 * NT]), in1=cosb, op=ALU.mult),
                               reads=[PK[2 + bi], ('TAB', m)], writes=['t1b%d' % bi])
                          S.op('dve', lambda e: e.tensor_tensor(out=v(t2b[bi][:, 0:2 * NT]), in0=v(pX2[:, 0:2 * NT]), in1=sinb, op=ALU.mult),
                               reads=[PK[4 + bi], ('TAB', m)], writes=['t2b%d' % bi])
                          S.op('pool', lambda e: e.tensor_tensor(out=mb[:, 2 * kk:2 * kk + 2, 0:NT], in0=t1b[bi][:, 0:2 * NT].rearrange("p (g n) -> p g n", g=2),
                                                                                     in1=t2b[bi][:, 0:2 * NT].rearrange("p (g n) -> p g n", g=2), op=ALU.add),
                               reads=['t1b%d' % bi, 't2b%d' % bi], writes=[('mbuf', m % 2, 2 * kk), ('mbuf', m % 2, 2 * kk + 1)])

                      def phaseB_scan(m, k):
                          mb = mbufs[m % 2]
                          for gl in range(8):
                              g = 8 * m + gl
                              S.op('dve', lambda e, gl=gl, g=g: e.tensor_tensor_scan(out=mb[:, gl, k * 128:(k + 1) * 128],
                                                                                  data0=rdec[:, g:g + 1].to_broadcast([128, 128]),
                                                                                  data1=mb[:, gl, k * 128:(k + 1) * 128],
                                                                                  initial=carry[:, g:g + 1], op0=ALU.mult, op1=ALU.add),
                                   reads=[('mbuf', m % 2, gl), 'rdec', ('carry', m)], writes=[('mbuf', m % 2, gl)])
                          S.op('dve', lambda e: e.tensor_copy(out=cl8[:], in_=mb[:, :, k * 128 + 127]), reads=[('mbuf', m % 2, gl) for gl in range(8)], writes=['cl8'])
                          S.op('pe', lambda e: e.matmul(ps[7][:, 0:8], lhsT=swf[:], rhs=cl8[:], start=True, stop=True), reads=['swf', 'cl8'], writes=['ps7'])

                      def phaseB_carry(m, k):
                          S.op('dve', lambda e: e.tensor_tensor(out=ct8[:], in0=ps[7][:, 0:8], in1=sinL[:, 8 * m:8 * m + 8], op=ALU.mult), reads=['ps7', 'sinL'], writes=['ct8'])
                          S.op('dve', lambda e: e.tensor_tensor(out=cl8[:], in0=cl8[:], in1=cosL[:, 8 * m:8 * m + 8], op=ALU.mult), reads=['cl8', 'cosL'], writes=['cl8'])
                          S.op('dve', lambda e: e.tensor_tensor(out=carry[:, 8 * m:8 * m + 8], in0=cl8[:], in1=ct8[:], op=ALU.add), reads=['cl8', 'ct8'], writes=[('carry', m)])

                      def phaseC(m):
                          mb = mbufs[m % 2]
                          for kk in range(4):
                              q = kk // 2
                              bi = kk % 2
                              g0 = 8 * m + 2 * kk
                              cosb = COS[:, g0:g0 + 2, :].unsqueeze(2).to_broadcast([128, 2, nchunk, 128])
                              sinb = SIN[:, g0:g0 + 2, :].unsqueeze(2).to_broadcast([128, 2, nchunk, 128])
                              v = lambda ap: ap.rearrange("p (g k j) -> p g k j", g=2, j=128)
                              mv_ = mb[:, 2 * kk:2 * kk + 2, 0:NT].rearrange("p g (k j) -> p g k j", j=128)
                              mk = [('mbuf', m % 2, 2 * kk), ('mbuf', m % 2, 2 * kk + 1)]
                              S.op('pool', lambda e: e.tensor_tensor(out=v(d1b[bi][:, 0:2 * NT]), in0=mv_, in1=cosb, op=ALU.mult), reads=mk + [('TAB', m)], writes=['d1b%d' % bi])
                              S.op('pool', lambda e: e.tensor_tensor(out=v(d2b[bi][:, 0:2 * NT]), in0=mv_, in1=sinb, op=ALU.mult), reads=mk + [('TAB', m)], writes=['d2b%d' % bi])
                              for j, gl in enumerate((2 * kk, 2 * kk + 1)):
                                  g = 8 * m + gl
                                  S.op('pe', lambda e: e.matmul(ps[6][64 * q:64 * q + 64, 0:NT], lhsT=Cm[:, g, 0, :], rhs=d1b[bi][:, j * NT:(j + 1) * NT], start=(gl % 4 == 0), stop=False),
                                       reads=['Cm', 'd1b%d' % bi], writes=['ps6'])
                                  S.op('pe', lambda e: e.matmul(ps[6][64 * q:64 * q + 64, 0:NT], lhsT=Cm[:, g, 1, :], rhs=d2b[bi][:, j * NT:(j + 1) * NT], start=False, stop=(gl % 4 == 3)),
                                       reads=['Cm', 'd2b%d' % bi], writes=['ps6'])

                      def phaseD(m):
                          S.op('dve', lambda e: e.scalar_tensor_tensor(out=ypre[:, 0:NT], in0=uTb[:, m, 0:NT], scalar=dvec[:, m:m + 1], in1=ps[6][:, 0:NT], op0=ALU.mult, op1=ALU.add),
                               reads=[('uTb', m), 'dvec', 'ps6'], writes=['ypre'])
                          S.op('act', lambda e: e.activation(out=ygb[:, m, 0:NT], in_=ypre[:, 0:NT], func=AF.Gelu_apprx_tanh), reads=['ypre'], writes=[('ygb', m)])

                      for kk in range(4):
                          phaseA(0, kk)
                      per = 4 // nchunk
                      for m in range(6):
                          for k in range(nchunk):
                              phaseB_scan(m, k)
                              if m + 1 < 6:
                                  for kk in range(k * per, (k + 1) * per):
                                      phaseA(m + 1, kk)
                              phaseB_carry(m, k)
                              if k == nchunk - 1 and m >= 1:
                                  phaseD(m - 1)
                          phaseC(m)
                      phaseD(5)
                  elif KCUT != 51 and KCUT != 54:
                      S.dma('sp', h0rows[:, :, 0:64], st_re.rearrange("(c p) e -> p c e", p=128), writes=['h0rows'])
                      S.dma('sp', h0rows[:, :, 64:128], st_im.rearrange("(c p) e -> p c e", p=128), writes=['h0rows'])
                      for c6 in range(6):
                          pt = ps[c6 // 4]
                          S.op('pe', lambda e, c6=c6, pt=pt: e.matmul(pt[:, (c6 % 4) * 128:(c6 % 4 + 1) * 128], lhsT=h0rows[:, c6, :], rhs=identf[:], start=True, stop=True),
                               reads=['h0rows', 'identf'], writes=[PK[c6 // 4]])
                      S.op('dve', lambda e: e.tensor_copy(out=h0T[:, 0:512], in_=ps[0][:, :]), reads=['ps0'], writes=['h0T'])
                      S.op('dve', lambda e: e.tensor_copy(out=h0T[:, 512:768], in_=ps[1][:, 0:256]), reads=['ps1', 'h0T'], writes=['h0T'])

                      def rotate(dst, src, cs, sn, srck, dstk):
                          csb = cs[:].unsqueeze(1).to_broadcast([128, NB, 48])
                          snb = sn[:].unsqueeze(1).to_broadcast([128, NB, 48])
                          for hh in range(2):
                              S.op('pe', lambda e, hh=hh: e.matmul(ps[2 + hh][:, 0:384], lhsT=swf[:], rhs=src[:, hh * 384:(hh + 1) * 384], start=True, stop=True),
                                   reads=['swf', srck], writes=[PK[2 + hh]])
                              S.op('dve', lambda e, hh=hh: e.tensor_tensor(out=dst[:, hh * 384:(hh + 1) * 384].rearrange("p (b g) -> p b g", g=48),
                                                                         in0=ps[2 + hh][:, 0:384].rearrange("p (b g) -> p b g", g=48),
                                                                         in1=snb[:, 8 * hh:8 * hh + 8, :], op=ALU.mult), reads=[PK[2 + hh], 'cs1', 'cs3'], writes=[dstk])
                          S.op('pool', lambda e: e.tensor_tensor(out=src[:].rearrange("p (b g) -> p b g", g=48), in0=src[:].rearrange("p (b g) -> p b g", g=48), in1=csb, op=ALU.mult),
                               reads=[srck, 'cs1', 'cs3'], writes=[srck])
                          S.op('dve', lambda e: e.tensor_tensor(out=dst[:], in0=dst[:], in1=src[:], op=ALU.add), reads=[srck, dstk], writes=[dstk])

                      rotate(ginit, h0T, cos1, sin1, 'h0T', 'ginit')
                      gin3 = ginit[:].rearrange("p (b g) -> p b g", g=48)
                      g3v = g3all[:].rearrange("p (b g) -> p b g", g=48)
                      for m in range(6):
                          for gl in range(8):
                              q, mem = gl // 4, gl % 4
                              rows = slice(64 * q, 64 * q + 64)
                              S.op('pe', lambda e, m=m, mem=mem, rows=rows, q=q: e.matmul(ps[2 + q][:, mem * 64:(mem + 1) * 64], lhsT=Bm[rows, m, mem, 0, :], rhs=uTb[rows, m, 0:NS], start=True, stop=True),
                                   reads=['Bm', ('uTb', m)], writes=[PK[2 + q]])
                              S.op('pe', lambda e, m=m, mem=mem, rows=rows, q=q: e.matmul(ps[4 + q][:, mem * 64:(mem + 1) * 64], lhsT=Bm[rows, m, mem, 1, :], rhs=uTb[rows, m, 0:NS], start=True, stop=True),
                                   reads=['Bm', ('uTb', m)], writes=[PK[4 + q]])
                          cos4 = COS[:, 8 * m:8 * m + 8, 0:4].unsqueeze(2).to_broadcast([128, 8, NB, 4])
                          sin4 = SIN[:, 8 * m:8 * m + 8, 0:4].unsqueeze(2).to_broadcast([128, 8, NB, 4])
                          v4 = lambda ap: ap.rearrange("p (g b t) -> p g b t", g=8, t=4)
                          vh = lambda ap: ap.rearrange("p (g b t) -> p g b t", g=4, t=4)
                          for q in range(2):
                              S.op('dve', lambda e, cos4=cos4, q=q: e.tensor_tensor(out=vh(t1s[:, q * 256:(q + 1) * 256]), in0=vh(ps[2 + q][:, 0:256]), in1=cos4[:, 4 * q:4 * q + 4], op=ALU.mult),
                                   reads=[PK[2 + q], ('TAB', m)], writes=['t1s'])
                              S.op('dve', lambda e, sin4=sin4, q=q: e.tensor_tensor(out=vh(t2s[:, q * 256:(q + 1) * 256]), in0=vh(ps[4 + q][:, 0:256]), in1=sin4[:, 4 * q:4 * q + 4], op=ALU.mult),
                                   reads=[PK[4 + q], ('TAB', m)], writes=['t2s'])
                          S.op('pool', lambda e: e.tensor_tensor(out=t1s[:], in0=t1s[:], in1=t2s[:], op=ALU.add), reads=['t1s', 't2s'], writes=['t1s'])
                          rb = rdec[:, 8 * m:8 * m + 8].unsqueeze(2).to_broadcast([128, 8, NB])
                          tm3 = tm8[:].rearrange("p (g b) -> p g b", g=8)
                          for t in range(4):
                              prev = gin3[:, :, 8 * m:8 * m + 8].rearrange("p b g -> p g b") if t == 0 else v4(t1s[:])[:, :, :, t - 1]
                              S.op('dve', lambda e, prev=prev, rb=rb: e.tensor_tensor(out=tm3, in0=prev, in1=rb, op=ALU.mult), reads=['t1s', 'ginit', 'rdec'], writes=['tm8'])
                              S.op('dve', lambda e, t=t: e.tensor_tensor(out=v4(t1s[:])[:, :, :, t], in0=v4(t1s[:])[:, :, :, t], in1=tm3, op=ALU.add), reads=['t1s', 'tm8'], writes=['t1s'])
                          S.op('dve', lambda e, m=m: e.tensor_copy(out=g3v[:, :, 8 * m:8 * m + 8].rearrange("p b g -> p g b"), in_=v4(t1s[:])[:, :, :, 3]), reads=['t1s'], writes=['g3all'])
                          S.op('pool', lambda e, cos4=cos4: e.tensor_tensor(out=v4(d1s[:]), in0=v4(t1s[:]), in1=cos4, op=ALU.mult), reads=['t1s', ('TAB', m)], writes=['d1s'])
                          S.op('pool', lambda e, sin4=sin4: e.tensor_tensor(out=v4(d2s[:]), in0=v4(t1s[:]), in1=sin4, op=ALU.mult), reads=['t1s', ('TAB', m)], writes=['d2s'])
                          for gl in range(8):
                              g = 8 * m + gl
                              q = gl // 4
                              S.op('pe', lambda e, g=g, q=q, gl=gl: e.matmul(ps[6][64 * q:64 * q + 64, 0:NS], lhsT=Cm[:, g, 0, :], rhs=d1s[:, gl * 64:(gl + 1) * 64], start=(gl % 4 == 0), stop=False),
                                   reads=['Cm', 'd1s'], writes=['ps6'])
                              S.op('pe', lambda e, g=g, q=q, gl=gl: e.matmul(ps[6][64 * q:64 * q + 64, 0:NS], lhsT=Cm[:, g, 1, :], rhs=d2s[:, gl * 64:(gl + 1) * 64], start=False, stop=(gl % 4 == 3)),
                                   reads=['Cm', 'd2s'], writes=['ps6'])
                          S.op('dve', lambda e, m=m: e.scalar_tensor_tensor(out=ypre[:, 0:NS], in0=uTb[:, m, 0:NS], scalar=dvec[:, m:m + 1], in1=ps[6][:, 0:NS], op0=ALU.mult, op1=ALU.add),
                               reads=[('uTb', m), 'dvec', 'ps6'], writes=['ypre'])
                          S.op('act', lambda e, m=m: e.activation(out=ygb[:, m, 0:NS], in_=ypre[:, 0:NS], func=AF.Gelu_apprx_tanh), reads=['ypre'], writes=[('ygb', m)])
                      rotate(h0T, g3all, cos3, sin3, 'g3all', 'h0T')
                      for c6 in range(6):
                          pt = ps[c6 // 4]
                          S.op('pe', lambda e, c6=c6, pt=pt: e.matmul(pt[:, (c6 % 4) * 128:(c6 % 4 + 1) * 128], lhsT=h0T[:, c6 * 128:(c6 + 1) * 128], rhs=identf[:], start=True, stop=True),
                               reads=['h0T', 'identf'], writes=[PK[c6 // 4]])
                      S.op('dve', lambda e: e.tensor_copy(out=h0rows[:, 0:4, :], in_=ps[0][:, :].rearrange("p (c x) -> p c x", c=4)), reads=['ps0'], writes=['h0rows'])
                      S.op('dve', lambda e: e.tensor_copy(out=h0rows[:, 4:6, :], in_=ps[1][:, 0:256].rearrange("p (c x) -> p c x", c=2)), reads=['ps1', 'h0rows'], writes=['h0rows'])
                      S.dma('sp', sre_s.rearrange("(c p) e -> p c e", p=128), h0rows[:, :, 0:64], reads=['h0rows'])
                      S.dma('sp', sim_s.rearrange("(c p) e -> p c e", p=128), h0rows[:, :, 64:128], reads=['h0rows'])
                  for m2 in range(6):
                      pt = ps[m2 % 2]
                      pk = PK[m2 % 2]
                      for m in range(6):
                          S.op('pe', lambda e, m=m, m2=m2, pt=pt: e.matmul(pt[:, 0:NT], lhsT=w_glu_sb[:, m, m2 * 128:(m2 + 1) * 128], rhs=ygb[:, m, 0:NT], start=(m == 0), stop=(m == 5)),
                               reads=[('w_glu_sb', m), ('ygb', m)], writes=[pk])
                      sg_i = sig[m2 % 2]
                      sgk = 'sig%d' % (m2 % 2)
                      S.op('act', lambda e, m2=m2, pt=pt, sg_i=sg_i: e.activation(out=sg_i[:, 0:NT], in_=pt[:, 0:NT], func=AF.Sigmoid, bias=bglu[:, m2:m2 + 1], scale=1.0),
                           reads=[pk, 'bglu'], writes=[sgk])
                      S.op('dve', lambda e, m2=m2, sg_i=sg_i: e.tensor_tensor(out=sg_i[:, 0:NT], in0=sg_i[:, 0:NT], in1=ygb[:, m2, 0:NT], op=ALU.mult), reads=[sgk, ('ygb', m2)], writes=[sgk])
                      S.op('pool', lambda e, m2=m2, sg_i=sg_i: e.tensor_tensor(out=catT[:, m2, 0:NT], in0=sg_i[:, 0:NT], in1=sg[:, m2, 0:NT], op=ALU.mult),
                           reads=[sgk, ('sg', m2)], writes=[('catT', m2)])
                  if is_s and KCUT != 52 and KCUT != 54:
                      mem_attention_sample(T1, 0, smg, mq, catT, ps[2], ps[3], [ps[4], ps[6]], [ps[5], ps[7]])
                  if not is_s:
                      mem_attention(T1, 0, NT, mq, smg, catT, KmT[:, 0, :, :], Vm[:, 0, :, :], [ps[2], ps[3]], [ps[4], ps[6]], [ps[5], ps[7]])
                  nblk = (NT + 127) // 128
                  blks = []
                  for a in range(nblk):
                      rows = min(128, NT - a * 128)
                      t0 = tok0 + a * 128
                      xsrc = x_s[0:rows, :] if is_s else x_p[t0:t0 + rows, :]
                      blks.append((a, rows, xsrc, t0, x1scr[t0:t0 + rows, :]))
                  ln_blocks(T1, blks, catT, w_out_sb, True)

              S.op('pe', lambda e: e.matmul(ps[7][:, 0:48], lhsT=swf[:], rhs=carry[:], start=True, stop=True), reads=['swf'] + [('carry', m) for m in range(6)], writes=['ps7'])
              S.op('dve', lambda e: e.tensor_tensor(out=hl[:], in0=ps[7][:, 0:48], in1=sin1[:], op=ALU.mult), reads=['ps7', 'cs1'], writes=['hl'])
              S.op('dve', lambda e: e.tensor_tensor(out=carry[:], in0=carry[:], in1=cos1[:], op=ALU.mult), reads=[('carry', m) for m in range(6)] + ['cs1'], writes=[('carry', m) for m in range(6)])
              S.op('dve', lambda e: e.tensor_tensor(out=hl[:], in0=carry[:], in1=hl[:], op=ALU.subtract), reads=[('carry', m) for m in range(6)] + ['hl'], writes=['hl'])
              S.op('pe', lambda e: e.matmul(ps[7][0:48, 128:256], lhsT=hl[:], rhs=identf[:], start=True, stop=True), reads=['hl', 'identf'], writes=['ps7'])
              S.op('dve', lambda e: e.tensor_copy(out=hlT[:], in_=ps[7][0:48, 128:256]), reads=['ps7'], writes=['hlT'])
              S.dma('sp', sre_p[:, :], hlT[:, 0:64], reads=['hlT'])
              S.dma('sp', sim_p[:, :], hlT[:, 64:128], reads=['hlT'])
              S.barrier()

        with ExitStack() as es1:
            _phase1(es1)

        def _phase2(es2):
            KT = sbt(es2, "KT", [128, 6, NTOK], BF16)
            Vg = sbt(es2, "Vg", [128, 3, 16, 256], BF16)
            ntok_eff = NTOK if stage >= 5 else SEQ
            Vnew = sbt(es2, "Vnew", [64, 3, 256], BF16)
            with ExitStack() as esa:
                x1Tf = sbt(esa, "x1Tf", [128, 8, NTOK], BF16)
                w_kvt = sbt(esa, "w_kvt", [128, 8, 3, 512], BF16)
                kvo = [sbt(esa, "kvo%d" % i, [128, 512]) for i in range(2)]
                for kc in range(8):
                    S.dma('sp', x1Tf[:, kc, 0:ntok_eff], x1Tscr[kc, :, 0:ntok_eff], writes=[('x1Tf', kc)])
                for kc in range(8):
                    S.dma('pool', w_kvt[:, kc, :, 0:256], w_kv[kc * 128:(kc + 1) * 128, 0:768].rearrange("p (g c) -> p g c", g=3), writes=[('w_kvt', kc)])
                    S.dma('pool', w_kvt[:, kc, :, 256:512], w_kv[kc * 128:(kc + 1) * 128, 768:1536].rearrange("p (g c) -> p g c", g=3), writes=[('w_kvt', kc)])
                x1k = [('x1Tf', kc) for kc in range(8)]
                wk = [('w_kvt', kc) for kc in range(8)]
                cnt = 0
                for t0 in range(0, ntok_eff, 512):
                    n = min(512, ntok_eff - t0)
                    for m in range(6):
                        g, pr = m // 2, m % 2
                        pt = ps[cnt % 2]
                        for kc in range(8):
                            S.op('pe', lambda e, kc=kc, g=g, pr=pr, pt=pt, t0=t0, n=n: e.matmul(pt[:, 0:n], lhsT=w_kvt[:, kc, g, pr * 128:(pr + 1) * 128], rhs=x1Tf[:, kc, t0:t0 + n],
                                                                                              start=(kc == 0), stop=(kc == 7)),
                                 reads=[x1k[kc], wk[kc]], writes=[PK[cnt % 2]])
                        evac(cnt, KT[:, m, t0:t0 + n], pt[:, 0:n], [PK[cnt % 2]], [('KT', m)])
                        cnt += 1
                cnt = 0
                for g in range(3):
                    for bi in range(16):
                        if g == 0:
                            tsel = lambda kc, bi=bi: x1Tf[:, kc, 128 * bi:128 * bi + 128]
                            needK = (bi == 15)
                            odst = dkv_p[0][0:128, :]
                        elif g == 1:
                            jb, r = bi // 4, bi % 4
                            tsel = lambda kc, jb=jb, r=r: x1Tf[:, kc, 512 * jb:512 * jb + 512].rearrange("p (u r) -> p r u", r=4)[:, r, :]
                            needK = (jb == 3)
                            odst = dkv_p[1].rearrange("(u r) c -> r u c", r=4)[r]
                        else:
                            r = bi
                            tsel = lambda kc, r=r: x1Tf[:, kc, 0:2048].rearrange("p (u r) -> p r u", r=16)[:, r, :]
                            needK = True
                            odst = dkv_p[2].rearrange("(u r) c -> r u c", r=16)[r]
                        c0 = 0 if needK else 256
                        pt = ps[2 + cnt % 2]
                        pk = PK[2 + cnt % 2]
                        for kc in range(8):
                            S.op('pe', lambda e, kc=kc, g=g, pt=pt, tsel=tsel, c0=c0: e.matmul(pt[:, c0:512], lhsT=tsel(kc), rhs=w_kvt[:, kc, g, c0:512], start=(kc == 0), stop=(kc == 7)),
                                 reads=[x1k[kc], wk[kc]], writes=[pk])
                        S.op('act', lambda e, g=g, bi=bi, pt=pt: e.activation(out=Vg[:, g, bi, :], in_=pt[:, 256:512], func=AF.Copy), reads=[pk], writes=[('Vg', g)])
                        if needK:
                            ko = kvo[cnt % 2]
                            kk = 'kvo%d' % (cnt % 2)
                            S.op('dve', lambda e, ko=ko, pt=pt: e.tensor_copy(out=ko[:], in_=pt[:, :]), reads=[pk], writes=[kk])
                            S.dma('sp', odst, ko[:], reads=[kk])
                        cnt += 1
                if stage >= 5:
                    kvs_f = sbt(esa, "kvs_f", [64, 3, 512])
                    for g in range(3):
                        pt = ps[g % 2]
                        for kc in range(8):
                            S.op('pe', lambda e, kc=kc, g=g, pt=pt: e.matmul(pt[0:NS, :], lhsT=x1Tf[:, kc, SEQ:NTOK], rhs=w_kvt[:, kc, g, :], start=(kc == 0), stop=(kc == 7)),
                                 reads=[x1k[kc], wk[kc]], writes=[PK[g % 2]])
                        S.op('dve', lambda e, g=g, pt=pt: e.tensor_copy(out=kvs_f[:, g, :], in_=pt[0:NS, :]), reads=[PK[g % 2]], writes=[('kvs_f', g)])
                        S.op('act', lambda e, g=g, pt=pt: e.activation(out=Vnew[:, g, :], in_=pt[0:NS, 256:512], func=AF.Copy), reads=[PK[g % 2]], writes=['Vnew'])
                        for b in range(NBC):
                            S.dma('sp', dkv_s[g][b, wins[g] - 4:wins[g], :], kvs_f[4 * b:4 * b + 4, g, :], reads=[('kvs_f', g)])
                S.barrier()
            if stage <= 3:
                return
            NT2 = 512
            stgA = sbt(es2, "stgA2", [128, 1024])
            stgB = sbt(es2, "stgB2", [128, 1024])
            stages = [(stgA, 'stgA'), (stgB, 'stgB')]
            w_in_sb = sbt(es2, "w_in_sb2", [128, 8, 2048], BF16)
            w_out_sb = sbt(es2, "w_out_sb2", [128, 8, 1024], BF16)
            load_weight(es2, w_in_sb, lambda kc: w_in[1, kc * 128:(kc + 1) * 128, :], 2048, stages, 'w_in_sb')
            load_weight(es2, w_out_sb, lambda kc: w_out[1, kc * 128:(kc + 1) * 128, :], 1024, stages, 'w_out_sb')
            S.dma('sp', lnG[:], ln_g[1].partition_broadcast(128), writes=['lnG'])
            S.dma('sp', lnB[:], ln_b[1].partition_broadcast(128), writes=['lnB'])
            x1Tt = sbt(es2, "x1Tt", [128, 8, NT2], BF16)
            qT = sbt(es2, "qT", [128, 6, NT2], BF16)
            sg = sbt(es2, "sg2", [128, 6, NT2], BF16)
            mq = sbt(es2, "mq2", [128, 2, NT2], BF16)
            smg = sbt(es2, "smg2", [128, 2, NT2], BF16)
            catT = sbt(es2, "catT2", [128, 8, NT2], BF16)
            PT = sbt(es2, "PTm2", [128, 2 * NT2], BF16)
            rden = sbt(es2, "rden2", [128, 2 * NT2])
            zt = [sbt(es2, "zt2_%d" % i, [128, 1024]) for i in range(3)]
            stats = [sbt(es2, "stats2_%d" % i, [128, 2, 6]) for i in range(3)]
            mv = [sbt(es2, "mv2_%d" % i, [128, 2]) for i in range(3)]
            rstd = [sbt(es2, "rstd2_%d" % i, [128, 1]) for i in range(3)]
            nmr = [sbt(es2, "nmr2_%d" % i, [128, 1]) for i in range(3)]
            epsb = sbt(es2, "epsb2", [128, 1])
            S.op('dve', lambda e: e.memset(epsb[:], LN_EPS), writes=['epsb'])
            PTs2 = sbt(es2, "PTs2", [128, 32], BF16)
            kvb2 = sbt(es2, "kvb2", [128, 2, 2, 512], BF16)
            kts2 = sbt(es2, "kts2", [128, 2, 2, 256], BF16)
            T2 = SimpleNamespace(xres=[stgA, stgB], zt=zt, zn=zt, stats=stats, mv=mv, rstd=rstd, nmr=nmr, epsb=epsb, x1b=None, x1T=None, PT=PT, rden=rden, mo=rden,
                                 sg=sg, mq=mq, smg=smg, stages=stages, xb=None, xT=x1Tt, PTs=PTs2,
                                 kvb=[(kvb2[:, i, :, :], [('kvb2', i)]) for i in range(2)], kts=[(kts2[:, i, :, :], [('kts2', i)]) for i in range(2)])

            esq = ExitStack()
            distT = sbt(esq, "distT", [128, 256])
            S.dma('sp', distT[:], c_dist[:, :], writes=['distT'])
            scb = [sbt(esq, "scb%d" % i, [128, 256]) for i in range(4)]
            PTd = [sbt(esq, "PTd%d" % i, [128, 256], BF16) for i in range(4)]
            rtot = sbt(esq, "rtot", [128, NT2])
            atmp = [sbt(esq, "atmp%d" % i, [128, NT2]) for i in range(2)]

            def dil_attention(tt):
                NBUF = 4
                for pr in range(2):
                    psDEN = ps[7]
                    psOT = [ps[4], ps[5], ps[6]]
                    den_started = [False, False]
                    items = []
                    for g in range(3):
                        m = 2 * g + pr
                        units = []
                        if g == 0:
                            for jb in range(4):
                                J = 4 * tt + jb
                                qsel = (lambda ap, jb=jb: ap[:, 128 * jb:128 * jb + 128])
                                sub = []
                                if J >= 1:
                                    sub.append((qsel, (lambda hp, J=J, m=m: KT[hp, m, 128 * (J - 1):128 * J]), (0, J - 1), 128, 128, True))
                                sub.append((qsel, (lambda hp, J=J, m=m: KT[hp, m, 128 * J:128 * J + 128]), (0, J), 128, 128, J < 1))
                                dap = distT[:, 0:256] if J >= 1 else distT[:, 128:256]
                                units.append((sub, dap))
                        elif g == 1:
                            for r in range(4):
                                qsel = (lambda ap, r=r: ap.rearrange("p (i r) -> p r i", r=4)[:, r, :])
                                sub = []
                                if tt >= 1:
                                    sub.append((qsel, (lambda hp, r=r, m=m: KT[hp, m, 512 * (tt - 1):512 * tt].rearrange("p (u r) -> p r u", r=4)[:, r, :]), (1, 4 * (tt - 1) + r), 128, 128, True))
                                sub.append((qsel, (lambda hp, r=r, m=m: KT[hp, m, 512 * tt:512 * tt + 512].rearrange("p (u r) -> p r u", r=4)[:, r, :]), (1, 4 * tt + r), 128, 128, tt < 1))
                                dap = distT[:, 0:256] if tt >= 1 else distT[:, 128:256]
                                units.append((sub, dap))
                        else:
                            nk = 32 * (tt + 1)
                            for r4 in range(4):
                                sub = []
                                for rr in range(4):
                                    r = 4 * r4 + rr
                                    sub.append(((lambda ap, r=r: ap.rearrange("p (i r) -> p r i", r=16)[:, r, :]),
                                                (lambda hp, r=r, m=m, nk=nk: KT[hp, m, 0:2048].rearrange("p (u r) -> p r u", r=16)[:, r, 0:nk]), (2, r), nk, 32, True))
                                dap = distT[0:nk, 128 + 32 * tt:128 + 32 * tt + 32].unsqueeze(1).to_broadcast([nk, 4, 32])
                                units.append((sub, dap))
                        for (sub, dap) in units:
                            for half in range(2):
                                items.append((g, m, sub, dap, half))
                    LOOK = 3

                    def emit_qk_sm(it, ui):
                        g, m, sub, dap, half = it
                        hp = slice(64 * half, 64 * half + 64)
                        h = 4 * g + 2 * pr + half
                        cval = -SLOPES[h] * DILS[g] / SCALE
                        bi = ui % NBUF
                        pS = ps[bi]
                        pk = PK[bi]
                        nkmax = max(c[3] for c in sub)
                        ntot = sum(c[4] for c in sub)
                        off = 0
                        for (qsel, ksel, vix, nk, NQ, st) in sub:
                            S.op('pe', lambda e: e.matmul(pS[0:nk, off:off + NQ], lhsT=ksel(hp), rhs=qsel(qT[hp, m, :]), start=True, stop=True),
                                 reads=[('KT', m), ('uTb', m)], writes=[pk])
                            off += NQ
                        if g == 2:
                            o_ap = scb[bi][0:nkmax, 0:ntot].rearrange("p (a i) -> p a i", a=4)
                            i_ap = pS[0:nkmax, 0:ntot].rearrange("p (a i) -> p a i", a=4)
                        else:
                            o_ap = scb[bi][0:nkmax, 0:ntot]
                            i_ap = pS[0:nkmax, 0:ntot]
                        S.op('dve', lambda e: e.scalar_tensor_tensor(out=o_ap, in0=dap, scalar=cval, in1=i_ap, op0=ALU.mult, op1=ALU.add),
                             reads=[pk, 'distT'], writes=['scb%d' % bi])
                        S.op('act', lambda e: e.activation(out=PTd[bi][0:nkmax, 0:ntot], in_=scb[bi][0:nkmax, 0:ntot], func=AF.Exp, scale=SCALE),
                             reads=['scb%d' % bi], writes=['PTd%d' % bi])

                    def emit_pv(it, ui):
                        g, m, sub, dap, half = it
                        hp = slice(64 * half, 64 * half + 64)
                        bi = ui % NBUF
                        hc = (2 * pr + half) * 64
                        nsub = len(sub)
                        off = 0
                        for si, (qsel, ksel, vix, nk, NQ, st) in enumerate(sub):
                            last = (si == nsub - 1) or sub[si + 1][5]
                            S.op('pe', lambda e: e.matmul(qsel(psOT[g][hp, :]), lhsT=Vg[0:nk, vix[0], vix[1], hc:hc + 64], rhs=PTd[bi][0:nk, off:off + NQ], start=st, stop=last),
                                 reads=[('Vg', g), 'PTd%d' % bi], writes=[PK[4 + g]])
                            S.op('pe', lambda e: e.matmul(qsel(psDEN[hp, :]), lhsT=onesb[0:nk, 0:64], rhs=PTd[bi][0:nk, off:off + NQ], start=(not den_started[half]), stop=True,
                                                          skip_group_check=True),
                                 reads=['onesb', 'PTd%d' % bi], writes=['ps7'])
                            den_started[half] = True
                            off += NQ

                    for i in range(len(items) + LOOK):
                        if i < len(items):
                            emit_qk_sm(items[i], i)
                        if i - LOOK >= 0:
                            emit_pv(items[i - LOOK], i - LOOK)
                    S.op('dve', lambda e: e.reciprocal(out=rtot[:], in_=psDEN[:, :]), reads=['ps7'], writes=['rtot'])
                    for g in range(3):
                        m = 2 * g + pr
                        at = atmp[g % 2]
                        ak = 'atmp%d' % (g % 2)
                        S.op('dve', lambda e, g=g, at=at: e.tensor_tensor(out=at[:], in0=psOT[g][:, :], in1=rtot[:], op=ALU.mult), reads=[PK[4 + g], 'rtot'], writes=[ak])
                        S.op('pool', lambda e, m=m, at=at: e.tensor_tensor(out=catT[:, m, :], in0=at[:], in1=sg[:, m, :], op=ALU.mult), reads=[ak, ('sg', m)], writes=[('catT', m)])

            def sample_dil_attention():
              with ExitStack() as ess:
                qtok = sbt(ess, "qtok", [64, 768], BF16)
                ktile = sbt(ess, "ktile", [128, 4, 2, 256])
                vb = sbt(ess, "vb", [128, 9, 256], BF16)
                prod = [sbt(ess, "prod%d" % i, [128, 256]) for i in range(2)]
                scs = sbt(ess, "scs", [128, 48])
                scs2 = sbt(ess, "scs2", [128, 48])
                Pb = sbt(ess, "Pb", [128, 48], BF16)
                sbias = sbt(ess, "sbias", [128, 48])
                pv_sb = sbt(ess, "pv_sb", [128, NB * 48])
                den_sb = sbt(ess, "den_sb", [128, NB * 48])
                otot = sbt(ess, "otot", [128, 6, 64])
                dtot = sbt(ess, "dtot", [128, 2, 64])
                ndist = sbt(ess, "ndist", [64, 2, 64])
                scn = [sbt(ess, "scn%d" % i, [64, 64]) for i in range(2)]
                PN = [sbt(ess, "PN%d" % i, [64, 64], BF16) for i in range(2)]
                S.dma('sp', sbias[:], c_sdist.rearrange("p g x -> p (g x)"), writes=['sbias'])
                S.dma('sp', ndist[:], c_ndist[:, :, :], writes=['ndist'])
                for (pq, c0, cw) in [(ps[0], 0, 512), (ps[1], 512, 256)]:
                    for kc in range(8):
                        S.op('pe', lambda e, kc=kc, pq=pq, c0=c0, cw=cw: e.matmul(pq[0:NS, 0:cw], lhsT=x1Tt[:, kc, 0:NS], rhs=w_in_sb[:, kc, c0:c0 + cw], start=(kc == 0), stop=(kc == 7)),
                             reads=[('w_in_sb', kc), ('xT', kc)], writes=[PK[ps.index(pq)]])
                S.op('act', lambda e: e.activation(out=qtok[:, 0:512], in_=ps[0][0:NS, :], func=AF.Copy), reads=['ps0'], writes=['qtok'])
                S.op('dve', lambda e: e.tensor_copy(out=qtok[:, 512:768], in_=ps[1][0:NS, 0:256]), reads=['ps1', 'qtok'], writes=['qtok'])
                idx = 0
                for g in range(3):
                    for hh in range(4):
                        h = 4 * g + hh
                        pr, half = hh // 2, hh % 2
                        m = 2 * g + pr
                        hp = slice(64 * half, 64 * half + 64)
                        cval = -SLOPES[h] * DILS[g] / SCALE
                        bi = half
                        S.op('pe', lambda e, hp=hp, m=m, bi=bi: e.matmul(ps[2 + bi][0:NS, 0:NS], lhsT=KT[hp, m, SEQ:NTOK], rhs=qT[hp, m, 0:NS], start=True, stop=True),
                             reads=[('KT', m), ('uTb', m)], writes=[PK[2 + bi]])
                        S.op('dve', lambda e, bi=bi, g=g, cval=cval: e.scalar_tensor_tensor(out=scn[bi][:, :], in0=ndist[:, (0 if g == 0 else 1), :], scalar=cval, in1=ps[2 + bi][0:NS, 0:NS],
                                                                                          op0=ALU.mult, op1=ALU.add), reads=[PK[2 + bi], 'ndist'], writes=['scn%d' % bi])
                        S.op('act', lambda e, bi=bi: e.activation(out=PN[bi][:, :], in_=scn[bi][:, :], func=AF.Exp, scale=SCALE), reads=['scn%d' % bi], writes=['PN%d' % bi])
                        col = (g * 2 + pr) * 64
                        S.op('pe', lambda e, hp=hp, g=g, hh=hh, bi=bi, col=col: e.matmul(ps[6][hp, col:col + 64], lhsT=Vnew[0:NS, g, hh * 64:(hh + 1) * 64], rhs=PN[bi][:, :], start=True, stop=True),
                             reads=['Vnew', 'PN%d' % bi], writes=['ps6'])
                        S.op('pe', lambda e, hp=hp, bi=bi, col=col: e.matmul(ps[7][hp, col:col + 64], lhsT=onesb[0:NS, 0:64], rhs=PN[bi][:, :], start=True, stop=True),
                             reads=['onesb', 'PN%d' % bi], writes=['ps7'])
                        idx += 1
                kcnt = 0
                for b in range(NB):
                  for tp in range(2):
                    for t in (2 * tp, 2 * tp + 1):
                        s_ = 4 * b + t
                        pa, pb_ = (ps[0], ps[1]) if t % 2 == 0 else (ps[2], ps[3])
                        sel = identb[0:NS, s_:s_ + 1].to_broadcast([NS, 128])
                        S.op('pe', lambda e, pa=pa, sel=sel: e.matmul(pa[:, 0:512], lhsT=sel, rhs=qtok[:, 0:512], start=True, stop=True), reads=['qtok', 'identb'], writes=[PK[ps.index(pa)]])
                        S.op('pe', lambda e, pb_=pb_, sel=sel: e.matmul(pb_[:, 0:256], lhsT=sel, rhs=qtok[:, 512:768], start=True, stop=True), reads=['qtok', 'identb'], writes=[PK[ps.index(pb_)]])
                    for g in range(3):
                        kt = ktile[:, kcnt % 4, :, :]
                        kk = 'ktile%d' % (kcnt % 4)
                        kcnt += 1
                        if g == 0:
                            S.dma('sp', kt[:, 0, :], cd[0][b % NBC, :, 0:256], writes=[kk])
                            if tp == 0:
                                S.dma('pool', vb[:, 0, :], cd[0][b % NBC, :, 256:512], writes=[('vb', g)])
                        else:
                            r_ = 4 if g == 1 else 16
                            kb = 1 if g == 1 else 5
                            srcv = cd[g][b % NBC].rearrange("(u r) c -> u r c", r=r_)
                            S.dma('sp', kt[:, :, :], srcv[:, 2 * tp:2 * tp + 2, 0:256], writes=[kk])
                            S.dma('pool', vb[:, kb + 2 * tp:kb + 2 * tp + 2, :], srcv[:, 2 * tp:2 * tp + 2, 256:512], writes=[('vb', g)])
                        for t in (2 * tp, 2 * tp + 1):
                            pa, pb_ = (ps[0], ps[1]) if t % 2 == 0 else (ps[2], ps[3])
                            qb = pa[:, g * 256:(g + 1) * 256] if g < 2 else pb_[:, 0:256]
                            qk = PK[ps.index(pa)] if g < 2 else PK[ps.index(pb_)]
                            ki = 0 if g == 0 else t - 2 * tp
                            pi = (g * 4 + t) % 2
                            S.op('dve', lambda e, ki=ki, qb=qb, pi=pi, kt=kt: e.tensor_tensor(out=prod[pi][:, :], in0=kt[:, ki, :], in1=qb, op=ALU.mult),
                                 reads=[kk, qk], writes=['prod%d' % pi])
                            S.op('dve', lambda e, g=g, t=t, pi=pi: e.tensor_reduce(out=scs[:, (g * 4 + t) * 4:(g * 4 + t) * 4 + 4], in_=prod[pi][:, :].rearrange("p (h e) -> p h e", h=4),
                                                                                 axis=mybir.AxisListType.X, op=ALU.add),
                                 reads=['prod%d' % pi], writes=['scs'])
                  if True:
                    S.op('dve', lambda e: e.scalar_tensor_tensor(out=scs2[:], in0=scs[:], scalar=SCALE, in1=sbias[:], op0=ALU.mult, op1=ALU.add), reads=['scs', 'sbias'], writes=['scs2'])
                    S.op('act', lambda e: e.activation(out=Pb[:], in_=scs2[:], func=AF.Exp), reads=['scs2'], writes=['Pb'])
                    pB = ps[4 + b % 2]
                    pBk = PK[4 + b % 2]
                    for g in range(3):
                        for t in range(4):
                            ki = 0 if g == 0 else (1 + t if g == 1 else 5 + t)
                            for pr in range(2):
                                col = ((g * 4 + t) * 2 + pr) * 2
                                pcol = (g * 4 + t) * 4 + 2 * pr
                                S.op('pe', lambda e, ki=ki, pr=pr, col=col, pcol=pcol, pB=pB: e.matmul(pB[:, col:col + 2], lhsT=vb[:, ki, pr * 128:(pr + 1) * 128], rhs=Pb[:, pcol:pcol + 2],
                                                                                                      start=True, stop=True),
                                     reads=[('vb', g), 'Pb'], writes=[pBk])
                    S.op('pe', lambda e, pB=pB: e.matmul(pB[:, 64:112], lhsT=onesb[:, :], rhs=Pb[:, :], start=True, stop=True), reads=['onesb', 'Pb'], writes=[pBk])
                    S.op('act', lambda e, b=b, pB=pB: e.activation(out=pv_sb[:, b * 48:(b + 1) * 48], in_=pB[:, 0:48], func=AF.Copy), reads=[pBk], writes=['pv_sb'])
                    S.op('dve', lambda e, b=b, pB=pB: e.tensor_copy(out=den_sb[:, b * 48:(b + 1) * 48], in_=pB[:, 64:112]), reads=[pBk], writes=['den_sb'])
                for half in range(2):
                    hp = slice(64 * half, 64 * half + 64)
                    for g in range(3):
                        pvv = pv_sb[hp, :].rearrange("p (b g t r j) -> p g r j b t", g=3, t=4, r=2, j=2)[:, g, :, half, :, :]
                        dnv = den_sb[hp, :].rearrange("p (b g t r j) -> p g r j b t", g=3, t=4, r=2, j=2)[:, g, :, half, :, :]
                        nv = lambda pp, g=g, hp=hp: pp[hp, 2 * g * 64:(2 * g + 2) * 64].rearrange("p (r b t) -> p r b t", r=2, t=4)
                        S.op('dve', lambda e, pvv=pvv, nv=nv, g=g, hp=hp: e.tensor_tensor(out=otot[hp, 2 * g:2 * g + 2, :].rearrange("p r (b t) -> p r b t", t=4), in0=pvv, in1=nv(ps[6]), op=ALU.add),
                             reads=['pv_sb', 'ps6'], writes=['otot'])
                        dv = dtot[hp, :, :].rearrange("p r (b t) -> p r b t", t=4)
                        if g == 0:
                            S.op('dve', lambda e, dnv=dnv, nv=nv, dv=dv: e.tensor_tensor(out=dv, in0=dnv, in1=nv(ps[7]), op=ALU.add), reads=['den_sb', 'ps7'], writes=['dtot'])
                        else:
                            S.op('dve', lambda e, dnv=dnv, dv=dv: e.tensor_tensor(out=dv, in0=dv, in1=dnv, op=ALU.add), reads=['den_sb', 'dtot'], writes=['dtot'])
                            S.op('dve', lambda e, nv=nv, dv=dv: e.tensor_tensor(out=dv, in0=dv, in1=nv(ps[7]), op=ALU.add), reads=['ps7', 'dtot'], writes=['dtot'])
                S.op('dve', lambda e: e.reciprocal(out=dtot[:], in_=dtot[:]), reads=['dtot'], writes=['dtot'])
                for g in range(3):
                    S.op('dve', lambda e, g=g: e.tensor_tensor(out=otot[:, 2 * g:2 * g + 2, :], in0=otot[:, 2 * g:2 * g + 2, :], in1=dtot[:, :, :], op=ALU.mult), reads=['otot', 'dtot'], writes=['otot'])
                    S.op('pool', lambda e, g=g: e.tensor_tensor(out=catT[:, 2 * g:2 * g + 2, 0:NS], in0=otot[:, 2 * g:2 * g + 2, :], in1=sg[:, 2 * g:2 * g + 2, 0:NS], op=ALU.mult),
                         reads=['otot', ('sg', 2 * g), ('sg', 2 * g + 1)], writes=[('catT', 2 * g), ('catT', 2 * g + 1)])
                S.barrier()

            for tt in range(SEQ // NT2):
                t0 = tt * NT2
                for kc in range(8):
                    S.dma('sp', x1Tt[:, kc, :], x1Tscr[kc, :, t0:t0 + NT2], writes=[('xT', kc)])
                in_proj(T2, NT2, w_in_sb, x1Tt, qT, 1)
                dil_attention(tt)
                mem_attention(T2, 1, NT2, mq, smg, catT, KmT[:, 1, :, :], Vm[:, 1, :, :], [ps[2], ps[3]], [ps[4], ps[6]], [ps[5], ps[7]])
                blks = []
                for a in range(NT2 // 128):
                    ta = t0 + a * 128
                    blks.append((a, 128, x1scr[ta:ta + 128, :], ta, y_p[ta:ta + 128, :]))
                ln_blocks(T2, blks, catT, w_out_sb, False)
            S.barrier()
            esq.close()
            if stage >= 5 and KCUT != 53:
                for kc in range(8):
                    S.dma('sp', x1Tt[:, kc, 0:NS], x1Tscr[kc, :, SEQ:NTOK], writes=[('xT', kc)])
                in_proj(T2, NS, w_in_sb, x1Tt, qT, 1)
                if stage >= 6:
                    sample_dil_attention()
                else:
                    for m in range(6):
                        S.op('pool', lambda e, m=m: e.memset(catT[:, m, 0:NS], 0.0), writes=[('catT', m)])
                mem_attention_sample(T2, 1, smg, mq, catT, ps[2], ps[3], [ps[4], ps[6]], [ps[5], ps[7]])
                ln_blocks(T2, [(0, NS, x1scr[SEQ:NTOK, :], SEQ, y_s[0:NS, :])], catT, w_out_sb, False)
            S.barrier()

        if stage >= 3:
            with ExitStack() as es2:
                _phase2(es2)

        issue_bulk(len(bulk_list))
        S.finish()
        print("ops", S.nops, "waits", S.nwait)
    return nc


_CONST_CACHE = {}


def _consts():
    if _CONST_CACHE:
        return _CONST_CACHE
    ident = np.eye(128, dtype=np.float32)
    sw = np.zeros((128, 128), np.float32)
    for m in range(64):
        sw[64 + m, m] = -1.0
        sw[m, 64 + m] = 1.0
    p = np.arange(128)
    meo = np.zeros((128, 4), np.float32)
    for j in range(4):
        meo[:, j] = ((p // 16) % 4 == j)
    jv = np.tile(np.arange(128, dtype=np.float32)[None, :], (128, 1))
    u = np.arange(128)[:, None]
    i = np.arange(128)[None, :]
    dist = np.zeros((128, 2, 128), np.float32)
    dist[:, 0, :] = np.where(u >= i, i + 128 - u, BIG)
    dist[:, 1, :] = np.where(u <= i, i - u, BIG)
    sd = np.zeros((128, 3, 4, 4), np.float32)
    for g in range(3):
        for t in range(4):
            for hh in range(4):
                uu = np.arange(128)
                if g == 0:
                    dd = 128 + t - uu
                    sd[:, g, t, hh] = np.where(uu >= t, -SLOPES[4 * g + hh] * dd, -30000.0)
                else:
                    sd[:, g, t, hh] = -SLOPES[4 * g + hh] * ((128 - uu) * DILS[g])
    nd = np.full((64, 2, 64), BIG, np.float32)
    for sp_ in range(64):
        for s_ in range(64):
            if sp_ // 4 == s_ // 4 and sp_ % 4 <= s_ % 4:
                nd[sp_, 0, s_] = (s_ % 4) - (sp_ % 4)
            if sp_ == s_:
                nd[sp_, 1, s_] = 0.0
    _CONST_CACHE.update(dict(c_ident=ident, c_swap=sw, c_maskeo=meo, c_jvec=jv, c_dist=dist.reshape(128, 256),
                             c_sdist=sd.reshape(128, 3, 16), c_ndist=nd))
    return _CONST_CACHE


_NC_CACHE = {}


def kernel(x_prompt, x_sample, cache_mem_kv, state_ssm_re, state_ssm_im, cache_dil1_kv, cache_dil4_kv,
           cache_dil16_kv, mem_prompt, w_in, w_out, ln_g, ln_b, w_mem_kv, ssm_lambda_re, ssm_lambda_im,
           ssm_log_dt, ssm_b_re, ssm_b_im, ssm_c_re, ssm_c_im, ssm_d, w_glu, b_glu, w_kv_shared, _stage=99):
    f = lambda a: np.ascontiguousarray(np.asarray(a, dtype=np.float32))
    if _stage not in _NC_CACHE:
        _NC_CACHE[_stage] = build(_stage)
    nc = _NC_CACHE[_stage]
    shared = dict(
        w_in=f(w_in), w_out=f(w_out), ln_g=f(ln_g), ln_b=f(ln_b), w_mem=f(w_mem_kv),
        lam_re=f(ssm_lambda_re)[0], lam_im=f(ssm_lambda_im)[0], log_dt=f(ssm_log_dt)[0],
        b_re=f(ssm_b_re)[0], b_im=f(ssm_b_im)[0],
        c_re=f(ssm_c_re)[0].reshape(768, 64), c_im=f(ssm_c_im)[0].reshape(768, 64),
        ssm_d=f(ssm_d)[0].reshape(768), w_glu=f(w_glu)[0], b_glu=f(b_glu)[0], w_kv=f(w_kv_shared))
    shared.update(_consts())
    x_prompt = np.asarray(x_prompt)
    in_maps = []
    for c in range(NCORES):
        bs = slice(NB * c, NB * (c + 1))
        d = dict(shared)
        d.update(
            x_p=f(x_prompt[c]), x_s=f(np.asarray(x_sample)[bs]).reshape(NS, 1024),
            cmk=f(np.asarray(cache_mem_kv)[:, bs]).reshape(2, NB, 256, 512),
            st_re=f(np.asarray(state_ssm_re)[0, bs]).reshape(NB * 48, 64),
            st_im=f(np.asarray(state_ssm_im)[0, bs]).reshape(NB * 48, 64),
            cd1=f(np.asarray(cache_dil1_kv)[bs]).reshape(NB, 128, 512),
            cd4=f(np.asarray(cache_dil4_kv)[bs]).reshape(NB, 512, 512),
            cd16=f(np.asarray(cache_dil16_kv)[bs]).reshape(NB, 2048, 512),
            memp=f(np.asarray(mem_prompt)[c]))
        if _stage < 5:
            for k in ('cmk',):
                d[k] = np.ascontiguousarray(d[k][:, 0:1])
        if _stage < 5 or (50 <= KCUT < 60):
            for k in ('cd1', 'cd4', 'cd16'):
                d[k] = np.ascontiguousarray(d[k][0:1])
        in_maps.append(d)
    res = run_bass_kernel_spmd(nc, in_maps, core_ids=list(range(NCORES)))
    R = res.results
    cat = lambda k: np.stack([np.asarray(R[c][k]) for c in range(NCORES)], axis=0)
    y_prompt = cat("y_p")
    y_sample = cat("y_s").reshape(128, 4, 1024)
    mem_kv_prompt = cat("mkv_p").transpose(1, 0, 2, 3).reshape(2, 8, 256, 2, 4, 64)
    ssm_re_prompt = cat("sre_p")[None]
    ssm_im_prompt = cat("sim_p")[None]
    d1p = cat("d1_p").reshape(8, 128, 2, 4, 64)
    d4p = cat("d4_p").reshape(8, 512, 2, 4, 64)
    d16p = cat("d16_p").reshape(8, 2048, 2, 4, 64)
    ssm_re_sample = cat("sre_s").reshape(1, 128, 48, 64)
    ssm_im_sample = cat("sim_s").reshape(1, 128, 48, 64)
    if _stage < 5 or (50 <= KCUT < 60):
        z = lambda *sh: np.zeros(sh, np.float32)
        return (y_prompt, y_sample, mem_kv_prompt, ssm_re_prompt, ssm_im_prompt, d1p, d4p, d16p, ssm_re_sample, ssm_im_sample,
                z(128, 128, 2, 4, 64), z(128, 512, 2, 4, 64), z(128, 2048, 2, 4, 64))
    d1s = cat("d1_s").reshape(128, 128, 2, 4, 64)
    d4s = cat("d4_s").reshape(128, 512, 2, 4, 64)
    d16s = cat("d16_s").reshape(128, 2048, 2, 4, 64)
    return (y_prompt, y_sample, mem_kv_prompt, ssm_re_prompt, ssm_im_prompt, d1p, d4p, d16p,
            ssm_re_sample, ssm_im_sample, d1s, d4s, d16s)
```

```python
import math
import os
KCUT = int(os.environ.get('KCUT', '99'))
import numpy as np
import concourse.bass as bass
import concourse.mybir as mybir
from concourse.bass_utils import run_bass_kernel_spmd
from contextlib import ExitStack
from types import SimpleNamespace

F32 = mybir.dt.float32
BF16 = mybir.dt.bfloat16
I32 = mybir.dt.int32
AF = mybir.ActivationFunctionType
ALU = mybir.AluOpType

NCORES = 8
SEQ = 2048
NS = 64
NB = 16
NTOK = SEQ + NS
ALPHA = (2.0 * 2) ** 0.25
LN_EPS = 1e-5
SCALE = 0.125
BIG = 1.0e6
TWO_PI = 2.0 * math.pi
C1_2PI = 6.28125
C2_2PI = TWO_PI - 6.28125
PI_LO = 3.1415925
SLOPES = [2.0 ** (-8.0 * (h + 1) / 12.0) for h in range(12)]
DILS = [1, 4, 16]


class _Stop(Exception):
    pass


class Sched:
    def __init__(self, nc, es, n_dma_sems=40):
        self.nc = nc
        self.eng = {'pe': nc.tensor, 'dve': nc.vector, 'act': nc.scalar, 'pool': nc.gpsimd, 'sp': nc.sync}
        self.sem = {k: es.enter_context(nc.semaphore('sem_' + k)) for k in self.eng}
        self.cnt = {k: 0 for k in self.eng}
        self.n_hw = n_dma_sems - 16
        self.dsem = [es.enter_context(nc.semaphore('dsem%d' % i)) for i in range(n_dma_sems)]
        self.dnext_sw = 0
        self.dcnt = [0] * n_dma_sems
        self.dbg = [False] * n_dma_sems
        self.dnext = 0
        self.bsem = es.enter_context(nc.semaphore('bulk'))
        self.bcnt = 0
        self.waited = {k: {} for k in self.eng}
        self.lastw = {}
        self.readers = {}
        self.nops = 0
        self.nwait = 0

    def semobj(self, sid):
        if sid == 'bulk':
            return self.bsem
        return self.sem[sid] if isinstance(sid, str) else self.dsem[sid]

    def _wait(self, e, sid, val):
        if val <= 0:
            return
        if sid == e and e == 'pe':
            return
        w = self.waited[e]
        if w.get(sid, 0) >= val:
            return
        self.eng[e].wait_ge(self.semobj(sid), val)
        self.nwait += 1
        w[sid] = val

    def _deps(self, e, reads, writes):
        for r in reads:
            if r in self.lastw:
                self._wait(e, *self.lastw[r])
        for r in writes:
            if r in self.lastw:
                self._wait(e, *self.lastw[r])
            for sid, val in self.readers.get(r, {}).items():
                self._wait(e, sid, val)

    def _commit(self, sid, val, reads, writes):
        for r in reads:
            d = self.readers.setdefault(r, {})
            d[sid] = max(val, d.get(sid, 0))
        for r in writes:
            self.lastw[r] = (sid, val)
            self.readers[r] = {}

    def op(self, e, fn, reads=(), writes=()):
        isps = lambda r: isinstance(r, str) and r.startswith('ps') and r[2:].isdigit()
        writes = list(writes) + [r for r in reads if isps(r)]
        reads = [r for r in reads if not isps(r)]
        self._deps(e, reads, writes)
        ins = fn(self.eng[e])
        self.cnt[e] += 1
        ins.then_inc(self.sem[e], 1)
        self._commit(e, self.cnt[e], reads, writes)
        self.nops += 1

    def dma(self, e, out, in_, reads=(), writes=(), bg=False, **kw):
        if e == 'pool':
            j = self.n_hw + self.dnext_sw
            self.dnext_sw = (self.dnext_sw + 1) % (len(self.dsem) - self.n_hw)
        else:
            j = self.dnext
            self.dnext = (j + 1) % self.n_hw
        self._wait(e, j, self.dcnt[j])
        self._deps(e, reads, writes)
        ins = self.eng[e].dma_start(out=out, in_=in_, **kw)
        self.dcnt[j] += 16
        self.dbg[j] = bg
        ins.then_inc(self.dsem[j], 16)
        self._commit(j, self.dcnt[j], reads, writes)
        self.nops += 1

    def dma_bulk(self, e, out, in_):
        ins = self.eng[e].dma_start(out=out, in_=in_)
        self.bcnt += 16
        ins.then_inc(self.bsem, 16)

    def barrier(self, all_dma=False):
        for e in self.eng:
            for k in self.eng:
                if k != e:
                    self._wait(e, k, self.cnt[k])
            for j in range(len(self.dsem)):
                if all_dma or not self.dbg[j]:
                    self._wait(e, j, self.dcnt[j])

    def finish(self):
        self.barrier(all_dma=True)
        self._wait('sp', 'bulk', self.bcnt)


def build(stage=99):
    nc = bass.Bass("TRN2", target_bir_lowering=False)
    NBC = NB if (stage >= 5 and not (50 <= KCUT < 60)) else 1
    NBM = NB if stage >= 5 else 1

    def din(name, shape, dt=F32):
        return nc.dram_tensor(name, list(shape), dt, kind="ExternalInput").ap()

    def dout(name, shape, dt=F32):
        return nc.dram_tensor(name, list(shape), dt, kind="ExternalOutput").ap()

    def dscr(name, shape, dt=F32):
        return nc.dram_tensor(name, list(shape), dt, kind="Internal").ap()

    x_p = din("x_p", [SEQ, 1024])
    x_s = din("x_s", [NS, 1024])
    cmk = din("cmk", [2, NBM, 256, 512])
    st_re = din("st_re", [NB * 48, 64])
    st_im = din("st_im", [NB * 48, 64])
    cd = [din("cd1", [NBC, 128, 512]), din("cd4", [NBC, 512, 512]), din("cd16", [NBC, 2048, 512])]
    memp = din("memp", [256, 1024])
    w_in = din("w_in", [2, 1024, 2048])
    w_out = din("w_out", [2, 1024, 1024])
    ln_g = din("ln_g", [2, 1024])
    ln_b = din("ln_b", [2, 1024])
    w_mem = din("w_mem", [2, 1024, 512])
    lam_re = din("lam_re", [48, 64])
    lam_im = din("lam_im", [48, 64])
    log_dt = din("log_dt", [48])
    b_re = din("b_re", [48, 64, 16])
    b_im = din("b_im", [48, 64, 16])
    c_re = din("c_re", [768, 64])
    c_im = din("c_im", [768, 64])
    ssm_d = din("ssm_d", [768])
    w_glu = din("w_glu", [768, 768])
    b_glu = din("b_glu", [768])
    w_kv = din("w_kv", [1024, 1536])
    c_ident = din("c_ident", [128, 128])
    c_swap = din("c_swap", [128, 128])
    c_maskeo = din("c_maskeo", [128, 4])
    c_jvec = din("c_jvec", [128, 128])
    c_dist = din("c_dist", [128, 256])
    c_sdist = din("c_sdist", [128, 3, 16])
    c_ndist = din("c_ndist", [64, 2, 64])

    y_p = dout("y_p", [SEQ, 1024])
    y_s = dout("y_s", [NS, 1024])
    mkv_p = dout("mkv_p", [2, 256, 512])
    sre_p = dout("sre_p", [48, 64])
    sim_p = dout("sim_p", [48, 64])
    dkv_p = [dout("d1_p", [128, 512]), dout("d4_p", [512, 512]), dout("d16_p", [2048, 512])]
    sre_s = dout("sre_s", [NB * 48, 64])
    sim_s = dout("sim_s", [NB * 48, 64])
    dkv_s = [dout("d1_s", [NBC, 128, 512]), dout("d4_s", [NBC, 512, 512]), dout("d16_s", [NBC, 2048, 512])]

    x1scr = dscr("x1scr", [NTOK, 1024], F32)
    x1Tscr = dscr("x1Tscr", [8, 128, NTOK], BF16)

    with ExitStack() as es0:
        S = Sched(nc, es0)

        def sbt(es, name, shape, dt=F32):
            return es.enter_context(nc.sbuf_tensor(name, list(shape), dt))

        ps = [es0.enter_context(nc.psum_tensor("ps%d" % i, [128, 512], F32)) for i in range(8)]
        PK = ['ps%d' % i for i in range(8)]

        identf = sbt(es0, "identf", [128, 128])
        identb = sbt(es0, "identb", [128, 128], BF16)
        onesb = sbt(es0, "onesb", [128, 128], BF16)
        S.dma('sp', identf[:], c_ident[:, :], writes=['identf'])
        S.op('dve', lambda e: e.tensor_copy(out=identb[:], in_=identf[:]), reads=['identf'], writes=['identb'])
        S.op('dve', lambda e: e.memset(onesb[:], 1.0), writes=['onesb'])
        KmT = sbt(es0, "KmT", [128, 2, 2, 256], BF16)
        Vm = sbt(es0, "Vm", [128, 2, 2, 256], BF16)
        lnG = sbt(es0, "lnG", [128, 1024])
        lnB = sbt(es0, "lnB", [128, 1024])

        wins = [128, 512, 2048]
        bulk_list = []
        for g in (range(3) if stage != 0 and stage != 2 and not (50 <= KCUT < 60) else []):
            for b in range(NBC):
                src = cd[g][b, 4:wins[g], :].rearrange("r c -> (r c)").rearrange("(a x) -> a x", a=16)
                dst = dkv_s[g][b, 0:wins[g] - 4, :].rearrange("r c -> (r c)").rearrange("(a x) -> a x", a=16)
                bulk_list.append((dst, src))

        def issue_bulk(n):
            for _ in range(min(n, len(bulk_list))):
                dst, src = bulk_list.pop()
                S.dma_bulk('act', dst, src)

        def evac(i, out, in_, reads, writes):
            if i % 2 == 0:
                S.op('act', lambda e: e.activation(out=out, in_=in_, func=AF.Copy), reads=reads, writes=writes)
            else:
                S.op('dve', lambda e: e.tensor_copy(out=out, in_=in_), reads=reads, writes=writes)

        def load_weight(es_w, dst, src_rows, ncols, stage_tiles, key, col_map=None):
            nk = dst.shape[1]
            for kc in range(nk):
                S.dma('pool', dst[:, kc, :], src_rows(kc), writes=[(key, kc)], bg=True)

        def wkeys(key, n):
            return [(key, kc) for kc in range(n)]

        def ln_part1(T, blk_i, rows, cat, wout_sb, x_src):
            c0 = blk_i * 128
            bb = blk_i % len(T.zt)
            xr = T.xres[blk_i % 2]
            xk = ['stgA', 'stgB'][blk_i % 2]
            zt = T.zt[bb]
            zk = 'zt%d' % bb
            S.dma('sp', xr[0:rows, :], x_src, writes=[xk])
            for n in range(2):
                for kc in range(8):
                    S.op('pe', lambda e: e.matmul(ps[n][0:rows, :], lhsT=cat[:, kc, c0:c0 + rows], rhs=wout_sb[:, kc, n * 512:(n + 1) * 512], start=(kc == 0), stop=(kc == 7)),
                         reads=[('catT', kc), ('w_out_sb', kc)], writes=[PK[n]])
                S.op('dve', lambda e: e.scalar_tensor_tensor(out=zt[0:rows, n * 512:(n + 1) * 512], in0=xr[0:rows, n * 512:(n + 1) * 512], scalar=ALPHA,
                                                             in1=ps[n][0:rows, :], op0=ALU.mult, op1=ALU.add),
                     reads=[xk, PK[n]], writes=[(zk, n), zk])
                S.op('dve', lambda e: e.bn_stats(out=T.stats[bb][0:rows, n, :], in_=zt[0:rows, n * 512:(n + 1) * 512]), reads=[(zk, n)], writes=[('stats', bb, n)])
            S.op('dve', lambda e: e.bn_aggr(out=T.mv[bb][0:rows, :], in_=T.stats[bb][0:rows, :, :]), reads=[('stats', bb, 0), ('stats', bb, 1)], writes=[('mv', bb)])

        def ln_part2(T, blk_i, rows, tok0, y_dst, make_T):
            bb = blk_i % len(T.zt)
            zt = T.zt[bb]
            zk = 'zt%d' % bb
            mv, rstd, nmr = T.mv[bb], T.rstd[bb], T.nmr[bb]
            S.op('act', lambda e: e.activation(out=rstd[0:rows, :], in_=mv[0:rows, 1:2], func=AF.Sqrt, bias=T.epsb[0:rows, :], scale=1.0), reads=[('mv', bb), 'epsb'], writes=[('rstd', bb)])
            S.op('dve', lambda e: e.reciprocal(out=rstd[0:rows, :], in_=rstd[0:rows, :]), reads=[('rstd', bb)], writes=[('rstd', bb)])
            S.op('dve', lambda e: e.tensor_scalar(out=nmr[0:rows, :], in0=mv[0:rows, 0:1], scalar1=rstd[0:rows, 0:1], scalar2=-1.0, op0=ALU.mult, op1=ALU.mult),
                 reads=[('mv', bb), ('rstd', bb)], writes=[('nmr', bb)])
            S.op('act', lambda e: e.activation(out=zt[0:rows, :], in_=zt[0:rows, :], func=AF.Identity, scale=rstd[0:rows, 0:1], bias=nmr[0:rows, 0:1]),
                 reads=[(zk, 0), (zk, 1), ('rstd', bb), ('nmr', bb)], writes=[zk, (zk, 0), (zk, 1)])
            S.op('dve', lambda e: e.tensor_tensor(out=zt[0:rows, :], in0=zt[0:rows, :], in1=lnG[0:rows, :], op=ALU.mult), reads=[zk, 'lnG'], writes=[zk])
            S.op('pool', lambda e: e.tensor_tensor(out=zt[0:rows, :], in0=zt[0:rows, :], in1=lnB[0:rows, :], op=ALU.add), reads=[zk, 'lnB'], writes=[zk])
            S.dma('sp', y_dst, zt[0:rows, :], reads=[zk])
            if make_T:
                S.op('act', lambda e: e.activation(out=T.x1b[0:rows, :], in_=zt[0:rows, :], func=AF.Copy), reads=[zk], writes=['x1b'])
                for kc in range(8):
                    pt = ps[2 + kc // 4]
                    S.op('pe', lambda e: e.matmul(pt[:, (kc % 4) * 128:(kc % 4) * 128 + rows], lhsT=T.x1b[0:rows, kc * 128:(kc + 1) * 128], rhs=identb[0:rows, 0:rows], start=True, stop=True),
                         reads=['x1b', 'identb'], writes=[PK[2 + kc // 4]])
                for hh in range(2):
                    evac(hh, T.x1T[:, 4 * hh:4 * hh + 4, 0:rows], ps[2 + hh][:, :].rearrange("p (k t) -> p k t", k=4)[:, :, 0:rows], [PK[2 + hh]], ['x1T'])
                S.dma('sp', x1Tscr[:, :, tok0:tok0 + rows].rearrange("k p t -> p k t"), T.x1T[:, :, 0:rows], reads=['x1T'])

        def ln_blocks(T, blocks, cat, wout_sb, make_T):
            n = len(blocks)
            for i in range(n + 1):
                if i < n:
                    bi_, rows, x_src, tok0, y_dst = blocks[i]
                    ln_part1(T, bi_, rows, cat, wout_sb, x_src)
                if i >= 1:
                    bi_, rows, x_src, tok0, y_dst = blocks[i - 1]
                    ln_part2(T, bi_, rows, tok0, y_dst, make_T)

        def mem_attention(T, l, NT, mq_t, smg_t, cat, Km, Vmm, psS, psO, psD):
            for h in range(4):
                pr, half = h // 2, h % 2
                hp = slice(64 * half, 64 * half + 64)
                for c in range(2):
                    pS = psS[c] if NT > 256 else psS[0]
                    off = 0 if NT > 256 else c * NT
                    S.op('pe', lambda e, c=c, pr=pr, hp=hp, pS=pS, off=off: e.matmul(pS[:, off:off + NT], lhsT=Km[hp, pr, c * 128:(c + 1) * 128], rhs=mq_t[hp, pr, 0:NT],
                                                                                     start=True, stop=True),
                         reads=['mq', 'KmT'], writes=[PK[ps.index(pS)]])
                if NT > 256:
                    for c in range(2):
                        S.op('act', lambda e, c=c: e.activation(out=T.PT[:, c * NT:(c + 1) * NT], in_=psS[c][:, 0:NT], func=AF.Exp, scale=SCALE),
                             reads=[PK[ps.index(psS[c])]], writes=[('PTm', c)])
                else:
                    S.op('act', lambda e: e.activation(out=T.PT[:, 0:2 * NT], in_=psS[0][:, 0:2 * NT], func=AF.Exp, scale=SCALE),
                         reads=[PK[ps.index(psS[0])]], writes=[('PTm', 0), ('PTm', 1)])
                for c in range(2):
                    S.op('pe', lambda e, c=c, h=h, pr=pr, hp=hp: e.matmul(psO[pr][hp, 0:NT], lhsT=Vmm[:, c, h * 64:(h + 1) * 64], rhs=T.PT[:, c * NT:(c + 1) * NT],
                                                                          start=(c == 0), stop=(c == 1)),
                         reads=[('PTm', c), 'Vm'], writes=[PK[ps.index(psO[pr])]])
                    S.op('pe', lambda e, c=c, pr=pr, hp=hp: e.matmul(psD[pr][hp, 0:NT], lhsT=onesb[:, 0:64], rhs=T.PT[:, c * NT:(c + 1) * NT],
                                                                     start=(c == 0), stop=(c == 1)),
                         reads=[('PTm', c), 'onesb'], writes=[PK[ps.index(psD[pr])]])
            for pr in range(2):
                S.op('dve', lambda e, pr=pr: e.reciprocal(out=T.rden[:, pr * NT:(pr + 1) * NT], in_=psD[pr][:, 0:NT]), reads=[PK[ps.index(psD[pr])]], writes=[('rden', pr)])
                S.op('dve', lambda e, pr=pr: e.tensor_tensor(out=T.rden[:, pr * NT:(pr + 1) * NT], in0=psO[pr][:, 0:NT], in1=T.rden[:, pr * NT:(pr + 1) * NT], op=ALU.mult),
                     reads=[PK[ps.index(psO[pr])], ('rden', pr)], writes=[('rden', pr)])
                S.op('pool', lambda e, pr=pr: e.tensor_tensor(out=cat[:, 6 + pr, 0:NT], in0=T.rden[:, pr * NT:(pr + 1) * NT], in1=smg_t[:, pr, 0:NT], op=ALU.mult),
                     reads=[('rden', pr), 'smg'], writes=[('catT', 6 + pr)])

        def mem_attention_sample(T, l, smg_t, mq_t, cat, psS, psT, psO, psD):
            NT = NS
            for b in range(NB):
                kvb, bk = T.kvb[b % 2]
                kts, tk = T.kts[b % 2]
                S.dma('pool', kvb[:, :, :], cmk[l, b].rearrange("(c p) x -> p c x", p=128), writes=bk)
                for c in range(2):
                    for pr in range(2):
                        S.op('pe', lambda e, c=c, pr=pr, kvb=kvb: e.matmul(psT[:, pr * 256 + c * 128:pr * 256 + (c + 1) * 128], lhsT=kvb[:, c, pr * 128:(pr + 1) * 128], rhs=identb[:],
                                                                         start=True, stop=True),
                             reads=bk + ['identb'], writes=[PK[ps.index(psT)]])
                evac(b, kts[:, :, :], psT[:, 0:512].rearrange("p (a m) -> p a m", a=2), [PK[ps.index(psT)]], tk)
                for h in range(4):
                    pr, half = h // 2, h % 2
                    hp = slice(64 * half, 64 * half + 64)
                    for c in range(2):
                        pS_h = psS if half == 0 else psT
                        S.op('pe', lambda e, c=c, pr=pr, hp=hp, h=h, kts=kts, b=b, pS_h=pS_h: e.matmul(pS_h[:, (h * 2 + c) * 4:(h * 2 + c) * 4 + 4], lhsT=kts[hp, pr, c * 128:(c + 1) * 128],
                                                                                                 rhs=mq_t[hp, pr, 4 * b:4 * b + 4], start=True, stop=True),
                             reads=tk + ['mq'], writes=[PK[ps.index(pS_h)]])
                for h in range(4):
                    pS_h = psS if h % 2 == 0 else psT
                    S.op('act', lambda e, h=h, pS_h=pS_h: e.activation(out=T.PTs[:, h * 8:h * 8 + 8], in_=pS_h[:, h * 8:h * 8 + 8], func=AF.Exp, scale=SCALE),
                         reads=[PK[ps.index(pS_h)]], writes=['PTs'])
                for h in range(4):
                    pr, half = h // 2, h % 2
                    hp = slice(64 * half, 64 * half + 64)
                    for c in range(2):
                        S.op('pe', lambda e, c=c, h=h, pr=pr, hp=hp, kvb=kvb, b=b: e.matmul(psO[pr][hp, 4 * b:4 * b + 4], lhsT=kvb[:, c, 256 + h * 64:256 + (h + 1) * 64],
                                                                                       rhs=T.PTs[:, (h * 2 + c) * 4:(h * 2 + c) * 4 + 4], start=(c == 0), stop=(c == 1)),
                             reads=bk + ['PTs'], writes=[PK[ps.index(psO[pr])]])
                        S.op('pe', lambda e, c=c, h=h, pr=pr, hp=hp, b=b: e.matmul(psD[pr][hp, 4 * b:4 * b + 4], lhsT=onesb[:, 0:64],
                                                                              rhs=T.PTs[:, (h * 2 + c) * 4:(h * 2 + c) * 4 + 4], start=(c == 0), stop=(c == 1)),
                             reads=['onesb', 'PTs'], writes=[PK[ps.index(psD[pr])]])
            for pr in range(2):
                S.op('dve', lambda e, pr=pr: e.reciprocal(out=T.rden[:, pr * NT:(pr + 1) * NT], in_=psD[pr][:, 0:NT]), reads=[PK[ps.index(psD[pr])]], writes=[('rden', pr)])
                S.op('dve', lambda e, pr=pr: e.tensor_tensor(out=T.rden[:, pr * NT:(pr + 1) * NT], in0=psO[pr][:, 0:NT], in1=T.rden[:, pr * NT:(pr + 1) * NT], op=ALU.mult),
                     reads=[PK[ps.index(psO[pr])], ('rden', pr)], writes=[('rden', pr)])
                S.op('pool', lambda e, pr=pr: e.tensor_tensor(out=cat[:, 6 + pr, 0:NT], in0=T.rden[:, pr * NT:(pr + 1) * NT], in1=smg_t[:, pr, 0:NT], op=ALU.mult),
                     reads=[('rden', pr), 'smg'], writes=[('catT', 6 + pr)])

        def in_proj(T, NT, w_sb, xT_t, u_dst, l):
            for mo_ in range(16):
                pt = ps[mo_ % 2]
                pk = PK[mo_ % 2]
                for kc in range(8):
                    S.op('pe', lambda e, kc=kc, mo_=mo_, pt=pt: e.matmul(pt[:, 0:NT], lhsT=w_sb[:, kc, mo_ * 128:(mo_ + 1) * 128], rhs=xT_t[:, kc, 0:NT],
                                                                         start=(kc == 0), stop=(kc == 7)),
                         reads=[('w_in_sb', kc), ('xT', kc)], writes=[pk])
                if mo_ < 6:
                    evac(mo_, u_dst[:, mo_, 0:NT], pt[:, 0:NT], [pk], [('uTb', mo_)])
                elif mo_ < 12:
                    S.op('act', lambda e, mo_=mo_, pt=pt: e.activation(out=T.sg[:, mo_ - 6, 0:NT], in_=pt[:, 0:NT], func=AF.Silu), reads=[pk], writes=[('sg', mo_ - 6)])
                elif mo_ < 14:
                    evac(mo_ + 1, T.mq[:, mo_ - 12, 0:NT], pt[:, 0:NT], [pk], ['mq'])
                else:
                    S.op('act', lambda e, mo_=mo_, pt=pt: e.activation(out=T.smg[:, mo_ - 14, 0:NT], in_=pt[:, 0:NT], func=AF.Silu), reads=[pk], writes=['smg'])

        def load_x_dma(T, x_src_fn, NT):
            nblk = (NT + 127) // 128
            for a in range(nblk):
                rows = min(128, NT - a * 128)
                S.dma('pool', T.xb[0:rows, a, :], x_src_fn(a, rows), writes=[('xb', a)])

        def load_x_transpose(T, NT):
            nblk = (NT + 127) // 128
            for kc in range(8):
                pt = ps[kc % 2]
                for a in range(nblk):
                    rows = min(128, NT - a * 128)
                    S.op('pe', lambda e: e.matmul(pt[:, a * 128:a * 128 + rows], lhsT=T.xb[0:rows, a, kc * 128:(kc + 1) * 128], rhs=identb[0:rows, 0:rows], start=True, stop=True),
                         reads=[('xb', a), 'identb'], writes=[PK[kc % 2]])
                evac(kc, T.xT[:, kc, 0:NT], pt[:, 0:NT], [PK[kc % 2]], [('xT', kc)])

        def _phase1(es1):
              stgA = sbt(es1, "stgA", [128, 1024])
              stgB = sbt(es1, "stgB", [128, 1024])
              stages = [(stgA, 'stgA'), (stgB, 'stgB')]
              w_in_sb = sbt(es1, "w_in_sb", [128, 8, 2048], BF16)
              w_glu_sb = sbt(es1, "w_glu_sb", [128, 6, 768], BF16)
              w_out_sb = sbt(es1, "w_out_sb", [128, 8, 1024], BF16)
              xb = sbt(es1, "xb", [128, 2, 1024], BF16)

              with ExitStack() as esm:
                  memb = sbt(esm, "memb", [128, 2, 1024], BF16)
                  memT = sbt(esm, "memT", [128, 8, 256], BF16)
                  wm_sb2 = sbt(esm, "wm_sb", [128, 2, 8, 512], BF16)
                  mkv_f = sbt(esm, "mkv_f", [128, 512])
                  for c in range(2):
                      S.dma('pool', memb[:, c, :], memp[c * 128:(c + 1) * 128, :], writes=[('memb', c)])
                  if KCUT <= 1:
                      S.barrier()
                      return
                  for kc in range(8):
                      pt = ps[kc % 2]
                      for c in range(2):
                          S.op('pe', lambda e, kc=kc, c=c, pt=pt: e.matmul(pt[:, c * 128:(c + 1) * 128], lhsT=memb[:, c, kc * 128:(kc + 1) * 128],
                                                                            rhs=identb[:], start=True, stop=True),
                               reads=[('memb', c), 'identb'], writes=[PK[kc % 2]])
                      evac(kc, memT[:, kc, :], pt[:, 0:256], [PK[kc % 2]], [('memT', kc)])
                  if KCUT <= 2:
                      S.barrier()
                      return
                  for l in range(2):
                      for kc in range(8):
                          S.dma('pool', wm_sb2[:, l, kc, :], w_mem[l, kc * 128:(kc + 1) * 128, :], writes=[('wm_sb', l, kc)])
                  load_weight(es1, w_in_sb, lambda kc: w_in[0, kc * 128:(kc + 1) * 128, :], 2048, stages, 'w_in_sb')
                  load_weight(es1, w_glu_sb, lambda kc: w_glu[kc * 128:(kc + 1) * 128, :], 768, stages, 'w_glu_sb')
                  load_weight(es1, w_out_sb, lambda kc: w_out[0, kc * 128:(kc + 1) * 128, :], 1024, stages, 'w_out_sb')
                  for a in range(2):
                      S.dma('pool', xb[:, a, :], x_p[a * 128:(a + 1) * 128, :], writes=[('xb', a)], bg=True)
                  for l in range(2):
                      wm_sb = wm_sb2[:, l, :, :]
                      if KCUT == 30:
                          S.barrier()
                          return
                      for c in range(2):
                          pt = ps[c]
                          for kc in range(8):
                              S.op('pe', lambda e, kc=kc, c=c, pt=pt: e.matmul(pt[:, :], lhsT=memT[:, kc, c * 128:(c + 1) * 128], rhs=wm_sb[:, kc, :],
                                                                                start=(kc == 0), stop=(kc == 7)),
                                   reads=[('memT', kc), ('wm_sb', l, kc)], writes=[PK[c]])
                          if KCUT == 31:
                              S.barrier()
                              return
                          S.op('dve', lambda e, pt=pt: e.tensor_copy(out=mkv_f[:], in_=pt[:, :]), reads=[PK[c]], writes=['mkv_f'])
                          S.op('act', lambda e, pt=pt, l=l, c=c: e.activation(out=Vm[:, l, c, :], in_=pt[:, 256:512], func=AF.Copy),
                               reads=[PK[c]], writes=[('Vm', l)])
                          S.dma('sp', mkv_p[l, c * 128:(c + 1) * 128, :], mkv_f[:], reads=['mkv_f'])
                      if KCUT <= 3:
                          S.barrier()
                          return
                      for pr in range(2):
                          pt = ps[2 + pr]
                          for kc in range(8):
                              S.op('pe', lambda e, kc=kc, pr=pr, pt=pt: e.matmul(pt[:, 0:256], lhsT=wm_sb[:, kc, pr * 128:(pr + 1) * 128], rhs=memT[:, kc, :],
                                                                                  start=(kc == 0), stop=(kc == 7)),
                                   reads=[('memT', kc), ('wm_sb', l, kc)], writes=[PK[2 + pr]])
                          evac(pr, KmT[:, l, pr, :], pt[:, 0:256], [PK[2 + pr]], [('KmT', l)])
                  S.barrier()
              if stage <= 1:
                  return

              S.dma('sp', lnG[:], ln_g[0].partition_broadcast(128), writes=['lnG'])
              S.dma('sp', lnB[:], ln_b[0].partition_broadcast(128), writes=['lnB'])

              COS = sbt(es1, "COS", [128, 48, 128], BF16)
              SIN = sbt(es1, "SIN", [128, 48, 128], BF16)
              Bm = sbt(es1, "Bm", [128, 6, 4, 2, 128], BF16)
              Cm = sbt(es1, "Cm", [128, 48, 2, 64], BF16)
              rdec = sbt(es1, "rdec", [128, 48])
              cosL = sbt(es1, "cosL", [128, 48])
              sinL = sbt(es1, "sinL", [128, 48])
              cos1 = sbt(es1, "cos1", [128, 48])
              sin1 = sbt(es1, "sin1", [128, 48])
              cos3 = sbt(es1, "cos3", [128, 48])
              sin3 = sbt(es1, "sin3", [128, 48])
              dvec = sbt(es1, "dvec", [128, 6])
              bglu = sbt(es1, "bglu", [128, 6])
              swf = sbt(es1, "swf", [128, 128])
              carry = sbt(es1, "carry", [128, 48])
              S.dma('sp', swf[:], c_swap[:, :], writes=['swf'])
              S.dma('sp', dvec[:], ssm_d.rearrange("(m r) -> r m", r=128), writes=['dvec'], allow_slow_non_contiguous=True)
              S.dma('sp', bglu[:], b_glu.rearrange("(m r) -> r m", r=128), writes=['bglu'], allow_slow_non_contiguous=True)

              with ExitStack() as esp:
                  L48 = sbt(esp, "L48", [48, 256])
                  logdt = sbt(esp, "logdt", [128, 48])
                  maskeo = sbt(esp, "maskeo", [128, 4])
                  jvec = sbt(esp, "jvec", [128, 128])
                  v = {n: sbt(esp, "v_" + n, [128, 48]) for n in
                       ['dt', 'lr', 'li', 'a', 'th', 'nr', 'ni', 'den', 'kre', 'kim', 'A1', 'A2', 'nA2', 't0', 't1', 'th64', 'c64', 's64', 'th3']}
                  sc_k = sbt(esp, "sc_k", [128, 1024])
                  sc_i = sbt(esp, "sc_i", [128, 1024], I32)
                  sc_p = sbt(esp, "sc_p", [128, 1024])
                  sc_q = sbt(esp, "sc_q", [128, 1024])
                  PH = sbt(esp, "PH", [128, 8, 128])
                  BreT = sbt(esp, "BreT", [128, 48, 16])
                  BimT = sbt(esp, "BimT", [128, 48, 16])
                  BX1 = sbt(esp, "BX1", [128, 48, 16])
                  BX2 = sbt(esp, "BX2", [128, 48, 16])
                  btmp = sbt(esp, "btmp", [128, 48, 16])
                  Crow = sbt(esp, "Crow", [128, 6, 64])
                  Cirow = sbt(esp, "Cirow", [128, 6, 64])
                  CC1 = sbt(esp, "CC1", [128, 6, 128])
                  CC2 = sbt(esp, "CC2", [128, 6, 128])

                  def D(fn, reads, writes):
                      S.op('dve', fn, reads=reads, writes=writes)

                  def sincos(ph, F, cos_out, sin_out, rk, wk):
                      k_, i_, p_, q_ = sc_k[:, 0:F], sc_i[:, 0:F], sc_p[:, 0:F], sc_q[:, 0:F]
                      D(lambda e: e.tensor_scalar(out=k_, in0=ph, scalar1=1.0 / TWO_PI, scalar2=None, op0=ALU.mult), rk, ['sc_k'])
                      D(lambda e: e.tensor_copy(out=i_, in_=k_), ['sc_k'], ['sc_i'])
                      D(lambda e: e.tensor_copy(out=k_, in_=i_), ['sc_i'], ['sc_k'])
                      D(lambda e: e.scalar_tensor_tensor(out=p_, in0=k_, scalar=-C1_2PI, in1=ph, op0=ALU.mult, op1=ALU.add), ['sc_k'] + rk, ['sc_p'])
                      D(lambda e: e.scalar_tensor_tensor(out=p_, in0=k_, scalar=-C2_2PI, in1=p_, op0=ALU.mult, op1=ALU.add), ['sc_k', 'sc_p'], ['sc_p'])
                      D(lambda e: e.tensor_scalar(out=p_, in0=p_, scalar1=-PI_LO, scalar2=PI_LO, op0=ALU.max, op1=ALU.min), ['sc_p'], ['sc_p'])
                      S.op('act', lambda e: e.activation(out=sin_out, in_=p_, func=AF.Sin), reads=['sc_p'], writes=wk)
                      S.op('act', lambda e: e.activation(out=q_, in_=p_, func=AF.Abs), reads=['sc_p'], writes=['sc_q'])
                      D(lambda e: e.tensor_scalar(out=q_, in0=q_, scalar1=-1.0, scalar2=math.pi / 2, op0=ALU.mult, op1=ALU.add), ['sc_q'], ['sc_q'])
                      S.op('act', lambda e: e.activation(out=cos_out, in_=q_, func=AF.Sin), reads=['sc_q'], writes=wk)

                  S.dma('sp', maskeo[:], c_maskeo[:, :], writes=['maskeo'])
                  S.dma('sp', jvec[:], c_jvec[:, :], writes=['jvec'])
                  for q4, src in enumerate([lam_re, lam_re, lam_im, lam_im]):
                      S.dma('sp', L48[:, q4 * 64:(q4 + 1) * 64], src[:, :], writes=[('L48', q4)])
                  S.dma('sp', logdt[:], log_dt.partition_broadcast(128), writes=['logdt'])
                  S.op('act', lambda e: e.activation(out=v['dt'][:], in_=logdt[:], func=AF.Exp), reads=['logdt'], writes=['v_dt'])
                  S.op('pe', lambda e: e.matmul(ps[7][:, 0:48], lhsT=L48[:, 0:128], rhs=identf[0:48, 0:48], start=True, stop=True),
                       reads=[('L48', 0), ('L48', 1), 'identf'], writes=['ps7'])
                  S.op('pe', lambda e: e.matmul(ps[7][:, 64:112], lhsT=L48[:, 128:256], rhs=identf[0:48, 0:48], start=True, stop=True),
                       reads=[('L48', 2), ('L48', 3), 'identf'], writes=['ps7'])
                  D(lambda e: e.tensor_scalar(out=v['lr'][:], in0=ps[7][:, 0:48], scalar1=-1e-4, scalar2=None, op0=ALU.min), ['ps7'], ['v_lr'])
                  D(lambda e: e.tensor_copy(out=v['li'][:], in_=ps[7][:, 64:112]), ['ps7'], ['v_li'])
                  D(lambda e: e.tensor_tensor(out=v['a'][:], in0=v['lr'][:], in1=v['dt'][:], op=ALU.mult), ['v_lr', 'v_dt'], ['v_a'])
                  D(lambda e: e.tensor_tensor(out=v['th'][:], in0=v['li'][:], in1=v['dt'][:], op=ALU.mult), ['v_li', 'v_dt'], ['v_th'])
                  S.op('act', lambda e: e.activation(out=rdec[:], in_=v['a'][:], func=AF.Exp), reads=['v_a'], writes=['rdec'])
                  sincos(v['th'][:], 48, cos1[:], sin1[:], ['v_th'], ['cs1'])
                  D(lambda e: e.tensor_scalar(out=v['th64'][:], in0=v['th'][:], scalar1=64.0, scalar2=None, op0=ALU.mult), ['v_th'], ['v_th64'])
                  sincos(v['th64'][:], 48, v['c64'][:], v['s64'][:], ['v_th64'], ['cs64'])
                  D(lambda e: e.tensor_scalar(out=v['th3'][:], in0=v['th'][:], scalar1=3.0, scalar2=None, op0=ALU.mult), ['v_th'], ['v_th3'])
                  sincos(v['th3'][:], 48, cos3[:], sin3[:], ['v_th3'], ['cs3'])
                  D(lambda e: e.tensor_tensor(out=v['t0'][:], in0=v['c64'][:], in1=v['c64'][:], op=ALU.mult), ['cs64'], ['v_t0'])
                  D(lambda e: e.tensor_scalar(out=cosL[:], in0=v['t0'][:], scalar1=2.0, scalar2=-1.0, op0=ALU.mult, op1=ALU.add), ['v_t0'], ['cosL'])
                  D(lambda e: e.tensor_tensor(out=v['t0'][:], in0=v['s64'][:], in1=v['c64'][:], op=ALU.mult), ['cs64', 'cosL'], ['v_t0'])
                  D(lambda e: e.tensor_scalar(out=sinL[:], in0=v['t0'][:], scalar1=2.0, scalar2=None, op0=ALU.mult), ['v_t0'], ['sinL'])
                  D(lambda e: e.tensor_tensor(out=v['nr'][:], in0=rdec[:], in1=cos1[:], op=ALU.mult), ['rdec', 'cs1'], ['v_nr'])
                  D(lambda e: e.tensor_scalar(out=v['nr'][:], in0=v['nr'][:], scalar1=-1.0, scalar2=None, op0=ALU.add), ['v_nr'], ['v_nr'])
                  D(lambda e: e.tensor_tensor(out=v['ni'][:], in0=rdec[:], in1=sin1[:], op=ALU.mult), ['rdec', 'cs1'], ['v_ni'])
                  D(lambda e: e.tensor_tensor(out=v['den'][:], in0=v['lr'][:], in1=v['lr'][:], op=ALU.mult), ['v_lr'], ['v_den'])
                  D(lambda e: e.tensor_tensor(out=v['t0'][:], in0=v['li'][:], in1=v['li'][:], op=ALU.mult), ['v_li', 'sinL'], ['v_t0'])
                  D(lambda e: e.tensor_tensor(out=v['den'][:], in0=v['den'][:], in1=v['t0'][:], op=ALU.add), ['v_den', 'v_t0'], ['v_den'])
                  D(lambda e: e.reciprocal(out=v['den'][:], in_=v['den'][:]), ['v_den'], ['v_den'])
                  D(lambda e: e.tensor_tensor(out=v['t0'][:], in0=v['nr'][:], in1=v['lr'][:], op=ALU.mult), ['v_nr', 'v_lr', 'v_den'], ['v_t0'])
                  D(lambda e: e.tensor_tensor(out=v['t1'][:], in0=v['ni'][:], in1=v['li'][:], op=ALU.mult), ['v_ni', 'v_li'], ['v_t1'])
                  D(lambda e: e.tensor_tensor(out=v['kre'][:], in0=v['t0'][:], in1=v['t1'][:], op=ALU.add), ['v_t0', 'v_t1'], ['v_kre'])
                  D(lambda e: e.tensor_tensor(out=v['kre'][:], in0=v['kre'][:], in1=v['den'][:], op=ALU.mult), ['v_kre', 'v_den'], ['v_kre'])
                  D(lambda e: e.tensor_tensor(out=v['t0'][:], in0=v['ni'][:], in1=v['lr'][:], op=ALU.mult), ['v_ni', 'v_lr', 'v_kre'], ['v_t0'])
                  D(lambda e: e.tensor_tensor(out=v['t1'][:], in0=v['nr'][:], in1=v['li'][:], op=ALU.mult), ['v_nr', 'v_li', 'v_kre'], ['v_t1'])
                  D(lambda e: e.tensor_tensor(out=v['kim'][:], in0=v['t0'][:], in1=v['t1'][:], op=ALU.subtract), ['v_t0', 'v_t1'], ['v_kim'])
                  D(lambda e: e.tensor_tensor(out=v['kim'][:], in0=v['kim'][:], in1=v['den'][:], op=ALU.mult), ['v_kim', 'v_den'], ['v_kim'])
                  D(lambda e: e.tensor_copy(out=v['A1'][0:64, :], in_=v['kre'][0:64, :]), ['v_kre'], ['v_A1'])
                  D(lambda e: e.tensor_copy(out=v['A1'][64:128, :], in_=v['kim'][64:128, :]), ['v_kim', 'v_A1'], ['v_A1'])
                  D(lambda e: e.tensor_scalar(out=v['A2'][0:64, :], in0=v['kim'][0:64, :], scalar1=-1.0, scalar2=None, op0=ALU.mult), ['v_kim'], ['v_A2'])
                  D(lambda e: e.tensor_copy(out=v['A2'][64:128, :], in_=v['kre'][64:128, :]), ['v_kre', 'v_A2'], ['v_A2'])
                  for hh in range(2):
                      S.dma('sp', BreT[hh * 64:(hh + 1) * 64, :, :], b_re.rearrange("g p c -> p g c"), writes=[('BreT', hh)])
                      S.dma('sp', BimT[hh * 64:(hh + 1) * 64, :, :], b_im.rearrange("g p c -> p g c"), writes=[('BimT', hh)])
                  A1b = v['A1'][:].unsqueeze(2).to_broadcast([128, 48, 16])
                  A2b = v['A2'][:].unsqueeze(2).to_broadcast([128, 48, 16])
                  rB = [('BreT', 0), ('BreT', 1), ('BimT', 0), ('BimT', 1), 'v_A1', 'v_A2']
                  D(lambda e: e.tensor_tensor(out=BX1[:], in0=BreT[:], in1=A1b, op=ALU.mult), rB, ['BX1'])
                  D(lambda e: e.tensor_tensor(out=btmp[:], in0=BimT[:], in1=A2b, op=ALU.mult), rB, ['btmp'])
                  D(lambda e: e.tensor_tensor(out=BX1[:], in0=BX1[:], in1=btmp[:], op=ALU.add), ['BX1', 'btmp'], ['BX1'])
                  D(lambda e: e.tensor_tensor(out=BX2[:], in0=BimT[:], in1=A1b, op=ALU.mult), rB, ['BX2'])
                  D(lambda e: e.tensor_tensor(out=btmp[:], in0=BreT[:], in1=A2b, op=ALU.mult), rB + ['BX1'], ['btmp'])
                  D(lambda e: e.tensor_tensor(out=BX2[:], in0=BX2[:], in1=btmp[:], op=ALU.subtract), ['BX2', 'btmp'], ['BX2'])
                  for m in range(6):
                      for xi, BX in enumerate([BX1, BX2]):
                          pt = ps[(2 * m + xi) % 2]
                          pk = PK[(2 * m + xi) % 2]
                          S.op('pe', lambda e, m=m, BX=BX, pt=pt: e.matmul(pt[:, 0:128], lhsT=BX[:, 8 * m:8 * m + 8, :], rhs=identf[:], start=True, stop=True),
                               reads=['BX1', 'BX2', 'identf'], writes=[pk])
                          for mem in range(4):
                              D(lambda e, m=m, xi=xi, mem=mem, pt=pt: e.tensor_scalar(out=Bm[:, m, mem, xi, :], in0=pt[:, 0:128], scalar1=maskeo[:, mem:mem + 1],
                                                                                     scalar2=None, op0=ALU.mult), [pk, 'maskeo'], ['Bm'])
                  S.dma('sp', Crow[:], c_re.rearrange("(m r) p -> r m p", r=128), writes=['Crow'])
                  S.dma('sp', Cirow[:], c_im.rearrange("(m r) p -> r m p", r=128), writes=['Cirow'])
                  D(lambda e: e.tensor_copy(out=CC1[:, :, 0:64], in_=Crow[:]), ['Crow'], ['CC1'])
                  D(lambda e: e.tensor_scalar(out=CC1[:, :, 64:128], in0=Cirow[:], scalar1=-1.0, scalar2=None, op0=ALU.mult), ['Cirow', 'CC1'], ['CC1'])
                  D(lambda e: e.tensor_scalar(out=CC2[:, :, 0:64], in0=Cirow[:], scalar1=-1.0, scalar2=None, op0=ALU.mult), ['Cirow'], ['CC2'])
                  D(lambda e: e.tensor_scalar(out=CC2[:, :, 64:128], in0=Crow[:], scalar1=-1.0, scalar2=None, op0=ALU.mult), ['Crow', 'CC2'], ['CC2'])
                  S.op('pool', lambda e: e.memset(Cm[:], 0.0), writes=['Cm'])
                  for m in range(6):
                      for xi, CC in enumerate([CC1, CC2]):
                          pt = ps[(2 * m + xi) % 2]
                          pk = PK[(2 * m + xi) % 2]
                          S.op('pe', lambda e, m=m, CC=CC, pt=pt: e.matmul(pt[:, 0:128], lhsT=CC[:, m, :], rhs=identf[:], start=True, stop=True),
                               reads=['CC1', 'CC2', 'identf'], writes=[pk])
                          for par in range(4):
                              dst = Cm[:, 8 * m:8 * m + 8, xi, :].rearrange("p (q r) c -> p q r c", r=4)[:, :, par, par * 16:(par + 1) * 16]
                              srcv = pt[:, 0:128].rearrange("p (q r c) -> p q r c", r=4, c=16)[:, :, par, :]
                              D(lambda e, dst=dst, srcv=srcv: e.tensor_copy(out=dst, in_=srcv), [pk, 'Cm'], ['Cm'])
                  for m in range(6):
                      thb = v['th'][:, 8 * m:8 * m + 8].unsqueeze(2).to_broadcast([128, 8, 128])
                      jb = jvec[:].unsqueeze(1).to_broadcast([128, 8, 128])
                      D(lambda e, thb=thb, jb=jb: e.tensor_tensor(out=PH[:], in0=thb, in1=jb, op=ALU.mult), ['v_th', 'jvec'], ['PH'])
                      sincos(PH[:].rearrange("p g j -> p (g j)"), 1024,
                             COS[:, 8 * m:8 * m + 8, :].rearrange("p g j -> p (g j)"),
                             SIN[:, 8 * m:8 * m + 8, :].rearrange("p g j -> p (g j)"), ['PH'], [('TAB', m)])
                  S.op('dve', lambda e: e.memset(carry[:], 0.0), writes=['carry'])
                  S.barrier()
              if stage <= 2:
                  return

              NTM = 256
              xT = sbt(es1, "xT", [128, 8, NTM], BF16)
              uTb = sbt(es1, "uTb", [128, 6, NTM], BF16)
              sg = sbt(es1, "sg", [128, 6, NTM], BF16)
              mq = sbt(es1, "mq", [128, 2, NTM], BF16)
              smg = sbt(es1, "smg", [128, 2, NTM], BF16)
              ygb = sbt(es1, "ygb", [128, 6, NTM], BF16)
              t1b = [sbt(es1, "t1b%d" % i, [128, 2 * NTM]) for i in range(2)]
              t2b = [sbt(es1, "t2b%d" % i, [128, 2 * NTM]) for i in range(2)]
              mbufs = [sbt(es1, "mbufA", [128, 8, NTM]), sbt(es1, "mbufB", [128, 8, NTM])]
              d1b = [sbt(es1, "d1b%d" % i, [128, 2 * NTM], BF16) for i in range(2)]
              d2b = [sbt(es1, "d2b%d" % i, [128, 2 * NTM], BF16) for i in range(2)]
              ypre = sbt(es1, "ypre", [128, NTM])
              sig = [sbt(es1, "sig%d" % i, [128, NTM]) for i in range(2)]
              catT = sbt(es1, "catT", [128, 8, NTM], BF16)
              PT = sbt(es1, "PTm", [128, 2 * NTM], BF16)
              rden = sbt(es1, "rden", [128, 2 * NTM])
              mo = rden
              xres = [stgA, stgB]
              zt = [sbt(es1, "zt%d" % i, [128, 1024]) for i in range(2)]
              zn = zt
              stats = [sbt(es1, "stats%d" % i, [128, 2, 6]) for i in range(2)]
              mv = [sbt(es1, "mv%d" % i, [128, 2]) for i in range(2)]
              rstd = [sbt(es1, "rstd%d" % i, [128, 1]) for i in range(2)]
              nmr = [sbt(es1, "nmr%d" % i, [128, 1]) for i in range(2)]
              x1b = sbt(es1, "x1b", [128, 1024], BF16)
              x1T = sbt(es1, "x1T", [128, 8, 128], BF16)
              cl8 = sbt(es1, "cl8", [128, 8])
              ct8 = sbt(es1, "ct8", [128, 8])
              hl = sbt(es1, "hl", [128, 48])
              hlT = sbt(es1, "hlT", [48, 128])

              epsb = sbt(es1, "epsb", [128, 1])
              S.op('dve', lambda e: e.memset(epsb[:], LN_EPS), writes=['epsb'])
              PTs = sbt(es1, "PTs", [128, 32], BF16)
              mA = mbufs[0][:].rearrange("p g n -> p (g n)")
              mB = mbufs[1][:].rearrange("p g n -> p (g n)")
              h0rows = mA[:, 0:768].rearrange("p (c x) -> p c x", c=6)
              h0T = mA[:, 768:1536]
              t1s = mA[:, 1536:2048]
              ginit = mB[:, 0:768]
              g3all = mB[:, 768:1536]
              t2s = mB[:, 1536:2048]
              tm8 = sbt(es1, "tm8", [128, 128])
              d1s = sbt(es1, "d1s", [128, 512], BF16)
              d2s = sbt(es1, "d2s", [128, 512], BF16)
              T1 = SimpleNamespace(xres=xres, zt=zt, zn=zn, stats=stats, mv=mv, rstd=rstd, nmr=nmr, epsb=epsb, x1b=x1b, x1T=x1T, PT=PT, rden=rden, mo=mo,
                                   sg=sg, mq=mq, smg=smg, stages=stages, xb=xb, xT=xT, PTs=PTs,
                                   kvb=[(xb[:, i, :].rearrange("p (c x) -> p c x", c=2), [('xb', i)]) for i in range(2)],
                                   kts=[(xT[:, 2 * i:2 * i + 2, :], [('xT', 2 * i), ('xT', 2 * i + 1)]) for i in range(2)])

              tiles = [(i * NTM, NTM, False) for i in range(SEQ // NTM)]
              if stage >= 5:
                  tiles.append((SEQ, NS, True))

              def xsrc_of(ti_):
                  tk0, _, iss = tiles[ti_]
                  if iss:
                      return lambda a, rows: x_s[0:rows, :]
                  return lambda a, rows: x_p[tk0 + a * 128:tk0 + a * 128 + rows, :]

              for ti, (tok0, NT, is_s) in enumerate(tiles):
                  issue_bulk(7)
                  if is_s:
                      S.barrier()
                  load_x_transpose(T1, NT)
                  if ti + 1 < len(tiles):
                      load_x_dma(T1, xsrc_of(ti + 1), tiles[ti + 1][1])
                  in_proj(T1, NT, w_in_sb, xT, uTb, 0)
                  nchunk = NT // 128
                  if not is_s:
                      def phaseA(m, kk):
                          q = kk // 2
                          bi = kk % 2
                          rows = slice(64 * q, 64 * q + 64)
                          pX1, pX2 = ps[2 + bi], ps[4 + bi]
                          mb = mbufs[m % 2]
                          for j, gl in enumerate((2 * kk, 2 * kk + 1)):
                              mem = gl % 4
                              S.op('pe', lambda e: e.matmul(pX1[:, j * NT:(j + 1) * NT], lhsT=Bm[rows, m, mem, 0, :], rhs=uTb[rows, m, 0:NT], start=True, stop=True),
                                   reads=['Bm', ('uTb', m)], writes=[PK[2 + bi]])
                              S.op('pe', lambda e: e.matmul(pX2[:, j * NT:(j + 1) * NT], lhsT=Bm[rows, m, mem, 1, :], rhs=uTb[rows, m, 0:NT], start=True, stop=True),
                                   reads=['Bm', ('uTb', m)], writes=[PK[4 + bi]])
                          g0 = 8 * m + 2 * kk
                          cosb = COS[:, g0:g0 + 2, :].unsqueeze(2).to_broadcast([128, 2, nchunk, 128])
                          sinb = SIN[:, g0:g0 + 2, :].unsqueeze(2).to_broadcast([128, 2, nchunk, 128])
                          v = lambda ap: ap.rearrange("p (g k j) -> p g k j", g=2, j=128)
                          S.op('dve', lambda e: e.tensor_tensor(out=v(t1b[bi][:, 0:2 * NT]), in0=v(pX1[:, 0:2 * NT]), in1=cosb, op=ALU.mult),
                               reads=[PK[2 + bi], ('TAB', m)], writes=['t1b%d' % bi])
                          S.op('dve', lambda e: e.tensor_tensor(out=v(t2b[bi][:, 0:2 * NT]), in0=v(pX2[:, 0:2 * NT]), in1=sinb, op=ALU.mult),
                               reads=[PK[4 + bi], ('TAB', m)], writes=['t2b%d' % bi])
                          S.op('pool', lambda e: e.tensor_tensor(out=mb[:, 2 * kk:2 * kk + 2, 0:NT], in0=t1b[bi][:, 0:2 * NT].rearrange("p (g n) -> p g n", g=2),
                                                                                     in1=t2b[bi][:, 0:2 * NT].rearrange("p (g n) -> p g n", g=2), op=ALU.add),
                               reads=['t1b%d' % bi, 't2b%d' % bi], writes=[('mbuf', m % 2, 2 * kk), ('mbuf', m % 2, 2 * kk + 1)])

                      def phaseB_scan(m, k):
                          mb = mbufs[m % 2]
                          for gl in range(8):
                              g = 8 * m + gl
                              S.op('dve', lambda e, gl=gl, g=g: e.tensor_tensor_scan(out=mb[:, gl, k * 128:(k + 1) * 128],
                                                                                  data0=rdec[:, g:g + 1].to_broadcast([128, 128]),
                                                                                  data1=mb[:, gl, k * 128:(k + 1) * 128],
                                                                                  initial=carry[:, g:g + 1], op0=ALU.mult, op1=ALU.add),
                                   reads=[('mbuf', m % 2, gl), 'rdec', ('carry', m)], writes=[('mbuf', m % 2, gl)])
                          S.op('dve', lambda e: e.tensor_copy(out=cl8[:], in_=mb[:, :, k * 128 + 127]), reads=[('mbuf', m % 2, gl) for gl in range(8)], writes=['cl8'])
                          S.op('pe', lambda e: e.matmul(ps[7][:, 0:8], lhsT=swf[:], rhs=cl8[:], start=True, stop=True), reads=['swf', 'cl8'], writes=['ps7'])

                      def phaseB_carry(m, k):
                          S.op('dve', lambda e: e.tensor_tensor(out=ct8[:], in0=ps[7][:, 0:8], in1=sinL[:, 8 * m:8 * m + 8], op=ALU.mult), reads=['ps7', 'sinL'], writes=['ct8'])
                          S.op('dve', lambda e: e.tensor_tensor(out=cl8[:], in0=cl8[:], in1=cosL[:, 8 * m:8 * m + 8], op=ALU.mult), reads=['cl8', 'cosL'], writes=['cl8'])
                          S.op('dve', lambda e: e.tensor_tensor(out=carry[:, 8 * m:8 * m + 8], in0=cl8[:], in1=ct8[:], op=ALU.add), reads=['cl8', 'ct8'], writes=[('carry', m)])

                      def phaseC(m):
                          mb = mbufs[m % 2]
                          for kk in range(4):
                              q = kk // 2
                              bi = kk % 2
                              g0 = 8 * m + 2 * kk
                              cosb = COS[:, g0:g0 + 2, :].unsqueeze(2).to_broadcast([128, 2, nchunk, 128])
                              sinb = SIN[:, g0:g0 + 2, :].unsqueeze(2).to_broadcast([128, 2, nchunk, 128])
                              v = lambda ap: ap.rearrange("p (g k j) -> p g k j", g=2, j=128)
                              mv_ = mb[:, 2 * kk:2 * kk + 2, 0:NT].rearrange("p g (k j) -> p g k j", j=128)
                              mk = [('mbuf', m % 2, 2 * kk), ('mbuf', m % 2, 2 * kk + 1)]
                              S.op('pool', lambda e: e.tensor_tensor(out=v(d1b[bi][:, 0:2 * NT]), in0=mv_, in1=cosb, op=ALU.mult), reads=mk + [('TAB', m)], writes=['d1b%d' % bi])
                              S.op('pool', lambda e: e.tensor_tensor(out=v(d2b[bi][:, 0:2 * NT]), in0=mv_, in1=sinb, op=ALU.mult), reads=mk + [('TAB', m)], writes=['d2b%d' % bi])
                              for j, gl in enumerate((2 * kk, 2 * kk + 1)):
                                  g = 8 * m + gl
                                  S.op('pe', lambda e: e.matmul(ps[6][64 * q:64 * q + 64, 0:NT], lhsT=Cm[:, g, 0, :], rhs=d1b[bi][:, j * NT:(j + 1) * NT], start=(gl % 4 == 0), stop=False),
                                       reads=['Cm', 'd1b%d' % bi], writes=['ps6'])
                                  S.op('pe', lambda e: e.matmul(ps[6][64 * q:64 * q + 64, 0:NT], lhsT=Cm[:, g, 1, :], rhs=d2b[bi][:, j * NT:(j + 1) * NT], start=False, stop=(gl % 4 == 3)),
                                       reads=['Cm', 'd2b%d' % bi], writes=['ps6'])

                      def phaseD(m):
                          S.op('dve', lambda e: e.scalar_tensor_tensor(out=ypre[:, 0:NT], in0=uTb[:, m, 0:NT], scalar=dvec[:, m:m + 1], in1=ps[6][:, 0:NT], op0=ALU.mult, op1=ALU.add),
                               reads=[('uTb', m), 'dvec', 'ps6'], writes=['ypre'])
                          S.op('act', lambda e: e.activation(out=ygb[:, m, 0:NT], in_=ypre[:, 0:NT], func=AF.Gelu_apprx_tanh), reads=['ypre'], writes=[('ygb', m)])

                      for kk in range(4):
                          phaseA(0, kk)
                      per = 4 // nchunk
                      for m in range(6):
                          for k in range(nchunk):
                              phaseB_scan(m, k)
                              if m + 1 < 6:
                                  for kk in range(k * per, (k + 1) * per):
                                      phaseA(m + 1, kk)
                              phaseB_carry(m, k)
                              if k == nchunk - 1 and m >= 1:
                                  phaseD(m - 1)
                          phaseC(m)
                      phaseD(5)
                  elif KCUT != 51 and KCUT != 54:
                      S.dma('sp', h0rows[:, :, 0:64], st_re.rearrange("(c p) e -> p c e", p=128), writes=['h0rows'])
                      S.dma('sp', h0rows[:, :, 64:128], st_im.rearrange("(c p) e -> p c e", p=128), writes=['h0rows'])
                      for c6 in range(6):
                          pt = ps[c6 // 4]
                          S.op('pe', lambda e, c6=c6, pt=pt: e.matmul(pt[:, (c6 % 4) * 128:(c6 % 4 + 1) * 128], lhsT=h0rows[:, c6, :], rhs=identf[:], start=True, stop=True),
                               reads=['h0rows', 'identf'], writes=[PK[c6 // 4]])
                      S.op('dve', lambda e: e.tensor_copy(out=h0T[:, 0:512], in_=ps[0][:, :]), reads=['ps0'], writes=['h0T'])
                      S.op('dve', lambda e: e.tensor_copy(out=h0T[:, 512:768], in_=ps[1][:, 0:256]), reads=['ps1', 'h0T'], writes=['h0T'])

                      def rotate(dst, src, cs, sn, srck, dstk):
                          csb = cs[:].unsqueeze(1).to_broadcast([128, NB, 48])
                          snb = sn[:].unsqueeze(1).to_broadcast([128, NB, 48])
                          for hh in range(2):
                              S.op('pe', lambda e, hh=hh: e.matmul(ps[2 + hh][:, 0:384], lhsT=swf[:], rhs=src[:, hh * 384:(hh + 1) * 384], start=True, stop=True),
                                   reads=['swf', srck], writes=[PK[2 + hh]])
                              S.op('dve', lambda e, hh=hh: e.tensor_tensor(out=dst[:, hh * 384:(hh + 1) * 384].rearrange("p (b g) -> p b g", g=48),
                                                                         in0=ps[2 + hh][:, 0:384].rearrange("p (b g) -> p b g", g=48),
                                                                         in1=snb[:, 8 * hh:8 * hh + 8, :], op=ALU.mult), reads=[PK[2 + hh], 'cs1', 'cs3'], writes=[dstk])
                          S.op('pool', lambda e: e.tensor_tensor(out=src[:].rearrange("p (b g) -> p b g", g=48), in0=src[:].rearrange("p (b g) -> p b g", g=48), in1=csb, op=ALU.mult),
                               reads=[srck, 'cs1', 'cs3'], writes=[srck])
                          S.op('dve', lambda e: e.tensor_tensor(out=dst[:], in0=dst[:], in1=src[:], op=ALU.add), reads=[srck, dstk], writes=[dstk])

                      rotate(ginit, h0T, cos1, sin1, 'h0T', 'ginit')
                      gin3 = ginit[:].rearrange("p (b g) -> p b g", g=48)
                      g3v = g3all[:].rearrange("p (b g) -> p b g", g=48)
                      for m in range(6):
                          for gl in range(8):
                              q, mem = gl // 4, gl % 4
                              rows = slice(64 * q, 64 * q + 64)
                              S.op('pe', lambda e, m=m, mem=mem, rows=rows, q=q: e.matmul(ps[2 + q][:, mem * 64:(mem + 1) * 64], lhsT=Bm[rows, m, mem, 0, :], rhs=uTb[rows, m, 0:NS], start=True, stop=True),
                                   reads=['Bm', ('uTb', m)], writes=[PK[2 + q]])
                              S.op('pe', lambda e, m=m, mem=mem, rows=rows, q=q: e.matmul(ps[4 + q][:, mem * 64:(mem + 1) * 64], lhsT=Bm[rows, m, mem, 1, :], rhs=uTb[rows, m, 0:NS], start=True, stop=True),
                                   reads=['Bm', ('uTb', m)], writes=[PK[4 + q]])
                          cos4 = COS[:, 8 * m:8 * m + 8, 0:4].unsqueeze(2).to_broadcast([128, 8, NB, 4])
                          sin4 = SIN[:, 8 * m:8 * m + 8, 0:4].unsqueeze(2).to_broadcast([128, 8, NB, 4])
                          v4 = lambda ap: ap.rearrange("p (g b t) -> p g b t", g=8, t=4)
                          vh = lambda ap: ap.rearrange("p (g b t) -> p g b t", g=4, t=4)
                          for q in range(2):
                              S.op('dve', lambda e, cos4=cos4, q=q: e.tensor_tensor(out=vh(t1s[:, q * 256:(q + 1) * 256]), in0=vh(ps[2 + q][:, 0:256]), in1=cos4[:, 4 * q:4 * q + 4], op=ALU.mult),
                                   reads=[PK[2 + q], ('TAB', m)], writes=['t1s'])
                              S.op('dve', lambda e, sin4=sin4, q=q: e.tensor_tensor(out=vh(t2s[:, q * 256:(q + 1) * 256]), in0=vh(ps[4 + q][:, 0:256]), in1=sin4[:, 4 * q:4 * q + 4], op=ALU.mult),
                                   reads=[PK[4 + q], ('TAB', m)], writes=['t2s'])
                          S.op('pool', lambda e: e.tensor_tensor(out=t1s[:], in0=t1s[:], in1=t2s[:], op=ALU.add), reads=['t1s', 't2s'], writes=['t1s'])
                          rb = rdec[:, 8 * m:8 * m + 8].unsqueeze(2).to_broadcast([128, 8, NB])
                          tm3 = tm8[:].rearrange("p (g b) -> p g b", g=8)
                          for t in range(4):
                              prev = gin3[:, :, 8 * m:8 * m + 8].rearrange("p b g -> p g b") if t == 0 else v4(t1s[:])[:, :, :, t - 1]
                              S.op('dve', lambda e, prev=prev, rb=rb: e.tensor_tensor(out=tm3, in0=prev, in1=rb, op=ALU.mult), reads=['t1s', 'ginit', 'rdec'], writes=['tm8'])
                              S.op('dve', lambda e, t=t: e.tensor_tensor(out=v4(t1s[:])[:, :, :, t], in0=v4(t1s[:])[:, :, :, t], in1=tm3, op=ALU.add), reads=['t1s', 'tm8'], writes=['t1s'])
                          S.op('dve', lambda e, m=m: e.tensor_copy(out=g3v[:, :, 8 * m:8 * m + 8].rearrange("p b g -> p g b"), in_=v4(t1s[:])[:, :, :, 3]), reads=['t1s'], writes=['g3all'])
                          S.op('pool', lambda e, cos4=cos4: e.tensor_tensor(out=v4(d1s[:]), in0=v4(t1s[:]), in1=cos4, op=ALU.mult), reads=['t1s', ('TAB', m)], writes=['d1s'])
                          S.op('pool', lambda e, sin4=sin4: e.tensor_tensor(out=v4(d2s[:]), in0=v4(t1s[:]), in1=sin4, op=ALU.mult), reads=['t1s', ('TAB', m)], writes=['d2s'])
                          for gl in range(8):
                              g = 8 * m + gl
                              q = gl // 4
                              S.op('pe', lambda e, g=g, q=q, gl=gl: e.matmul(ps[6][64 * q:64 * q + 64, 0:NS], lhsT=Cm[:, g, 0, :], rhs=d1s[:, gl * 64:(gl + 1) * 64], start=(gl % 4 == 0), stop=False),
                                   reads=['Cm', 'd1s'], writes=['ps6'])
                              S.op('pe', lambda e, g=g, q=q, gl=gl: e.matmul(ps[6][64 * q:64 * q + 64, 0:NS], lhsT=Cm[:, g, 1, :], rhs=d2s[:, gl * 64:(gl + 1) * 64], start=False, stop=(gl % 4 == 3)),
                                   reads=['Cm', 'd2s'], writes=['ps6'])
                          S.op('dve', lambda e, m=m: e.scalar_tensor_tensor(out=ypre[:, 0:NS], in0=uTb[:, m, 0:NS], scalar=dvec[:, m:m + 1], in1=ps[6][:, 0:NS], op0=ALU.mult, op1=ALU.add),
                               reads=[('uTb', m), 'dvec', 'ps6'], writes=['ypre'])
                          S.op('act', lambda e, m=m: e.activation(out=ygb[:, m, 0:NS], in_=ypre[:, 0:NS], func=AF.Gelu_apprx_tanh), reads=['ypre'], writes=[('ygb', m)])
                      rotate(h0T, g3all, cos3, sin3, 'g3all', 'h0T')
                      for c6 in range(6):
                          pt = ps[c6 // 4]
                          S.op('pe', lambda e, c6=c6, pt=pt: e.matmul(pt[:, (c6 % 4) * 128:(c6 % 4 + 1) * 128], lhsT=h0T[:, c6 * 128:(c6 + 1) * 128], rhs=identf[:], start=True, stop=True),
                               reads=['h0T', 'identf'], writes=[PK[c6 // 4]])
                      S.op('dve', lambda e: e.tensor_copy(out=h0rows[:, 0:4, :], in_=ps[0][:, :].rearrange("p (c x) -> p c x", c=4)), reads=['ps0'], writes=['h0rows'])
                      S.op('dve', lambda e: e.tensor_copy(out=h0rows[:, 4:6, :], in_=ps[1][:, 0:256].rearrange("p (c x) -> p c x", c=2)), reads=['ps1', 'h0rows'], writes=['h0rows'])
                      S.dma('sp', sre_s.rearrange("(c p) e -> p c e", p=128), h0rows[:, :, 0:64], reads=['h0rows'])
                      S.dma('sp', sim_s.rearrange("(c p) e -> p c e", p=128), h0rows[:, :, 64:128], reads=['h0rows'])
                  for m2 in range(6):
                      pt = ps[m2 % 2]
                      pk = PK[m2 % 2]
                      for m in range(6):
                          S.op('pe', lambda e, m=m, m2=m2, pt=pt: e.matmul(pt[:, 0:NT], lhsT=w_glu_sb[:, m, m2 * 128:(m2 + 1) * 128], rhs=ygb[:, m, 0:NT], start=(m == 0), stop=(m == 5)),
                               reads=[('w_glu_sb', m), ('ygb', m)], writes=[pk])
                      sg_i = sig[m2 % 2]
                      sgk = 'sig%d' % (m2 % 2)
                      S.op('act', lambda e, m2=m2, pt=pt, sg_i=sg_i: e.activation(out=sg_i[:, 0:NT], in_=pt[:, 0:NT], func=AF.Sigmoid, bias=bglu[:, m2:m2 + 1], scale=1.0),
                           reads=[pk, 'bglu'], writes=[sgk])
                      S.op('dve', lambda e, m2=m2, sg_i=sg_i: e.tensor_tensor(out=sg_i[:, 0:NT], in0=sg_i[:, 0:NT], in1=ygb[:, m2, 0:NT], op=ALU.mult), reads=[sgk, ('ygb', m2)], writes=[sgk])
                      S.op('pool', lambda e, m2=m2, sg_i=sg_i: e.tensor_tensor(out=catT[:, m2, 0:NT], in0=sg_i[:, 0:NT], in1=sg[:, m2, 0:NT], op=ALU.mult),
                           reads=[sgk, ('sg', m2)], writes=[('catT', m2)])
                  if is_s and KCUT != 52 and KCUT != 54:
                      mem_attention_sample(T1, 0, smg, mq, catT, ps[2], ps[3], [ps[4], ps[6]], [ps[5], ps[7]])
                  if not is_s:
                      mem_attention(T1, 0, NT, mq, smg, catT, KmT[:, 0, :, :], Vm[:, 0, :, :], [ps[2], ps[3]], [ps[4], ps[6]], [ps[5], ps[7]])
                  nblk = (NT + 127) // 128
                  blks = []
                  for a in range(nblk):
                      rows = min(128, NT - a * 128)
                      t0 = tok0 + a * 128
                      xsrc = x_s[0:rows, :] if is_s else x_p[t0:t0 + rows, :]
                      blks.append((a, rows, xsrc, t0, x1scr[t0:t0 + rows, :]))
                  ln_blocks(T1, blks, catT, w_out_sb, True)

              S.op('pe', lambda e: e.matmul(ps[7][:, 0:48], lhsT=swf[:], rhs=carry[:], start=True, stop=True), reads=['swf'] + [('carry', m) for m in range(6)], writes=['ps7'])
              S.op('dve', lambda e: e.tensor_tensor(out=hl[:], in0=ps[7][:, 0:48], in1=sin1[:], op=ALU.mult), reads=['ps7', 'cs1'], writes=['hl'])
              S.op('dve', lambda e: e.tensor_tensor(out=carry[:], in0=carry[:], in1=cos1[:], op=ALU.mult), reads=[('carry', m) for m in range(6)] + ['cs1'], writes=[('carry', m) for m in range(6)])
              S.op('dve', lambda e: e.tensor_tensor(out=hl[:], in0=carry[:], in1=hl[:], op=ALU.subtract), reads=[('carry', m) for m in range(6)] + ['hl'], writes=['hl'])
              S.op('pe', lambda e: e.matmul(ps[7][0:48, 128:256], lhsT=hl[:], rhs=identf[:], start=True, stop=True), reads=['hl', 'identf'], writes=['ps7'])
              S.op('dve', lambda e: e.tensor_copy(out=hlT[:], in_=ps[7][0:48, 128:256]), reads=['ps7'], writes=['hlT'])
              S.dma('sp', sre_p[:, :], hlT[:, 0:64], reads=['hlT'])
              S.dma('sp', sim_p[:, :], hlT[:, 64:128], reads=['hlT'])
              S.barrier()

        with ExitStack() as es1:
            _phase1(es1)

        def _phase2(es2):
            KT = sbt(es2, "KT", [128, 6, NTOK], BF16)
            Vg = sbt(es2, "Vg", [128, 3, 16, 256], BF16)
            ntok_eff = NTOK if stage >= 5 else SEQ
            Vnew = sbt(es2, "Vnew", [64, 3, 256], BF16)
            with ExitStack() as esa:
                x1Tf = sbt(esa, "x1Tf", [128, 8, NTOK], BF16)
                w_kvt = sbt(esa, "w_kvt", [128, 8, 3, 512], BF16)
                kvo = [sbt(esa, "kvo%d" % i, [128, 512]) for i in range(2)]
                for kc in range(8):
                    S.dma('sp', x1Tf[:, kc, 0:ntok_eff], x1Tscr[kc, :, 0:ntok_eff], writes=[('x1Tf', kc)])
                for kc in range(8):
                    S.dma('pool', w_kvt[:, kc, :, 0:256], w_kv[kc * 128:(kc + 1) * 128, 0:768].rearrange("p (g c) -> p g c", g=3), writes=[('w_kvt', kc)])
                    S.dma('pool', w_kvt[:, kc, :, 256:512], w_kv[kc * 128:(kc + 1) * 128, 768:1536].rearrange("p (g c) -> p g c", g=3), writes=[('w_kvt', kc)])
                x1k = [('x1Tf', kc) for kc in range(8)]
                wk = [('w_kvt', kc) for kc in range(8)]
                cnt = 0
                for t0 in range(0, ntok_eff, 512):
                    n = min(512, ntok_eff - t0)
                    for m in range(6):
                        g, pr = m // 2, m % 2
                        pt = ps[cnt % 2]
                        for kc in range(8):
                            S.op('pe', lambda e, kc=kc, g=g, pr=pr, pt=pt, t0=t0, n=n: e.matmul(pt[:, 0:n], lhsT=w_kvt[:, kc, g, pr * 128:(pr + 1) * 128], rhs=x1Tf[:, kc, t0:t0 + n],
                                                                                              start=(kc == 0), stop=(kc == 7)),
                                 reads=[x1k[kc], wk[kc]], writes=[PK[cnt % 2]])
                        evac(cnt, KT[:, m, t0:t0 + n], pt[:, 0:n], [PK[cnt % 2]], [('KT', m)])
                        cnt += 1
                cnt = 0
                for g in range(3):
                    for bi in range(16):
                        if g == 0:
                            tsel = lambda kc, bi=bi: x1Tf[:, kc, 128 * bi:128 * bi + 128]
                            needK = (bi == 15)
                            odst = dkv_p[0][0:128, :]
                        elif g == 1:
                            jb, r = bi // 4, bi % 4
                            tsel = lambda kc, jb=jb, r=r: x1Tf[:, kc, 512 * jb:512 * jb + 512].rearrange("p (u r) -> p r u", r=4)[:, r, :]
                            needK = (jb == 3)
                            odst = dkv_p[1].rearrange("(u r) c -> r u c", r=4)[r]
                        else:
                            r = bi
                            tsel = lambda kc, r=r: x1Tf[:, kc, 0:2048].rearrange("p (u r) -> p r u", r=16)[:, r, :]
                            needK = True
                            odst = dkv_p[2].rearrange("(u r) c -> r u c", r=16)[r]
                        c0 = 0 if needK else 256
                        pt = ps[2 + cnt % 2]
                        pk = PK[2 + cnt % 2]
                        for kc in range(8):
                            S.op('pe', lambda e, kc=kc, g=g, pt=pt, tsel=tsel, c0=c0: e.matmul(pt[:, c0:512], lhsT=tsel(kc), rhs=w_kvt[:, kc, g, c0:512], start=(kc == 0), stop=(kc == 7)),
                                 reads=[x1k[kc], wk[kc]], writes=[pk])
                        S.op('act', lambda e, g=g, bi=bi, pt=pt: e.activation(out=Vg[:, g, bi, :], in_=pt[:, 256:512], func=AF.Copy), reads=[pk], writes=[('Vg', g)])
                        if needK:
                            ko = kvo[cnt % 2]
                            kk = 'kvo%d' % (cnt % 2)
                            S.op('dve', lambda e, ko=ko, pt=pt: e.tensor_copy(out=ko[:], in_=pt[:, :]), reads=[pk], writes=[kk])
                            S.dma('sp', odst, ko[:], reads=[kk])
                        cnt += 1
                if stage >= 5:
                    kvs_f = sbt(esa, "kvs_f", [64, 3, 512])
                    for g in range(3):
                        pt = ps[g % 2]
                        for kc in range(8):
                            S.op('pe', lambda e, kc=kc, g=g, pt=pt: e.matmul(pt[0:NS, :], lhsT=x1Tf[:, kc, SEQ:NTOK], rhs=w_kvt[:, kc, g, :], start=(kc == 0), stop=(kc == 7)),
                                 reads=[x1k[kc], wk[kc]], writes=[PK[g % 2]])
                        S.op('dve', lambda e, g=g, pt=pt: e.tensor_copy(out=kvs_f[:, g, :], in_=pt[0:NS, :]), reads=[PK[g % 2]], writes=[('kvs_f', g)])
                        S.op('act', lambda e, g=g, pt=pt: e.activation(out=Vnew[:, g, :], in_=pt[0:NS, 256:512], func=AF.Copy), reads=[PK[g % 2]], writes=['Vnew'])
                        for b in range(NBC):
                            S.dma('sp', dkv_s[g][b, wins[g] - 4:wins[g], :], kvs_f[4 * b:4 * b + 4, g, :], reads=[('kvs_f', g)])
                S.barrier()
            if stage <= 3:
                return
            NT2 = 512
            stgA = sbt(es2, "stgA2", [128, 1024])
            stgB = sbt(es2, "stgB2", [128, 1024])
            stages = [(stgA, 'stgA'), (stgB, 'stgB')]
            w_in_sb = sbt(es2, "w_in_sb2", [128, 8, 2048], BF16)
            w_out_sb = sbt(es2, "w_out_sb2", [128, 8, 1024], BF16)
            load_weight(es2, w_in_sb, lambda kc: w_in[1, kc * 128:(kc + 1) * 128, :], 2048, stages, 'w_in_sb')
            load_weight(es2, w_out_sb, lambda kc: w_out[1, kc * 128:(kc + 1) * 128, :], 1024, stages, 'w_out_sb')
            S.dma('sp', lnG[:], ln_g[1].partition_broadcast(128), writes=['lnG'])
            S.dma('sp', lnB[:], ln_b[1].partition_broadcast(128), writes=['lnB'])
            x1Tt = sbt(es2, "x1Tt", [128, 8, NT2], BF16)
            qT = sbt(es2, "qT", [128, 6, NT2], BF16)
            sg = sbt(es2, "sg2", [128, 6, NT2], BF16)
            mq = sbt(es2, "mq2", [128, 2, NT2], BF16)
            smg = sbt(es2, "smg2", [128, 2, NT2], BF16)
            catT = sbt(es2, "catT2", [128, 8, NT2], BF16)
            PT = sbt(es2, "PTm2", [128, 2 * NT2], BF16)
            rden = sbt(es2, "rden2", [128, 2 * NT2])
            zt = [sbt(es2, "zt2_%d" % i, [128, 1024]) for i in range(3)]
            stats = [sbt(es2, "stats2_%d" % i, [128, 2, 6]) for i in range(3)]
            mv = [sbt(es2, "mv2_%d" % i, [128, 2]) for i in range(3)]
            rstd = [sbt(es2, "rstd2_%d" % i, [128, 1]) for i in range(3)]
            nmr = [sbt(es2, "nmr2_%d" % i, [128, 1]) for i in range(3)]
            epsb = sbt(es2, "epsb2", [128, 1])
            S.op('dve', lambda e: e.memset(epsb[:], LN_EPS), writes=['epsb'])
            PTs2 = sbt(es2, "PTs2", [128, 32], BF16)
            kvb2 = sbt(es2, "kvb2", [128, 2, 2, 512], BF16)
            kts2 = sbt(es2, "kts2", [128, 2, 2, 256], BF16)
            T2 = SimpleNamespace(xres=[stgA, stgB], zt=zt, zn=zt, stats=stats, mv=mv, rstd=rstd, nmr=nmr, epsb=epsb, x1b=None, x1T=None, PT=PT, rden=rden, mo=rden,
                                 sg=sg, mq=mq, smg=smg, stages=stages, xb=None, xT=x1Tt, PTs=PTs2,
                                 kvb=[(kvb2[:, i, :, :], [('kvb2', i)]) for i in range(2)], kts=[(kts2[:, i, :, :], [('kts2', i)]) for i in range(2)])

            esq = ExitStack()
            distT = sbt(esq, "distT", [128, 256])
            S.dma('sp', distT[:], c_dist[:, :], writes=['distT'])
            scb = [sbt(esq, "scb%d" % i, [128, 256]) for i in range(4)]
            PTd = [sbt(esq, "PTd%d" % i, [128, 256], BF16) for i in range(4)]
            rtot = sbt(esq, "rtot", [128, NT2])
            atmp = [sbt(esq, "atmp%d" % i, [128, NT2]) for i in range(2)]

            def dil_attention(tt):
                NBUF = 4
                for pr in range(2):
                    psDEN = ps[7]
                    psOT = [ps[4], ps[5], ps[6]]
                    den_started = [False, False]
                    items = []
                    for g in range(3):
                        m = 2 * g + pr
                        units = []
                        if g == 0:
                            for jb in range(4):
                                J = 4 * tt + jb
                                qsel = (lambda ap, jb=jb: ap[:, 128 * jb:128 * jb + 128])
                                sub = []
                                if J >= 1:
                                    sub.append((qsel, (lambda hp, J=J, m=m: KT[hp, m, 128 * (J - 1):128 * J]), (0, J - 1), 128, 128, True))
                                sub.append((qsel, (lambda hp, J=J, m=m: KT[hp, m, 128 * J:128 * J + 128]), (0, J), 128, 128, J < 1))
                                dap = distT[:, 0:256] if J >= 1 else distT[:, 128:256]
                                units.append((sub, dap))
                        elif g == 1:
                            for r in range(4):
                                qsel = (lambda ap, r=r: ap.rearrange("p (i r) -> p r i", r=4)[:, r, :])
                                sub = []
                                if tt >= 1:
                                    sub.append((qsel, (lambda hp, r=r, m=m: KT[hp, m, 512 * (tt - 1):512 * tt].rearrange("p (u r) -> p r u", r=4)[:, r, :]), (1, 4 * (tt - 1) + r), 128, 128, True))
                                sub.append((qsel, (lambda hp, r=r, m=m: KT[hp, m, 512 * tt:512 * tt + 512].rearrange("p (u r) -> p r u", r=4)[:, r, :]), (1, 4 * tt + r), 128, 128, tt < 1))
                                dap = distT[:, 0:256] if tt >= 1 else distT[:, 128:256]
                                units.append((sub, dap))
                        else:
                            nk = 32 * (tt + 1)
                            for r4 in range(4):
                                sub = []
                                for rr in range(4):
                                    r = 4 * r4 + rr
                                    sub.append(((lambda ap, r=r: ap.rearrange("p (i r) -> p r i", r=16)[:, r, :]),
                                                (lambda hp, r=r, m=m, nk=nk: KT[hp, m, 0:2048].rearrange("p (u r) -> p r u", r=16)[:, r, 0:nk]), (2, r), nk, 32, True))
                                dap = distT[0:nk, 128 + 32 * tt:128 + 32 * tt + 32].unsqueeze(1).to_broadcast([nk, 4, 32])
                                units.append((sub, dap))
                        for (sub, dap) in units:
                            for half in range(2):
                                items.append((g, m, sub, dap, half))
                    LOOK = 3

                    def emit_qk_sm(it, ui):
                        g, m, sub, dap, half = it
                        hp = slice(64 * half, 64 * half + 64)
                        h = 4 * g + 2 * pr + half
                        cval = -SLOPES[h] * DILS[g] / SCALE
                        bi = ui % NBUF
                        pS = ps[bi]
                        pk = PK[bi]
                        nkmax = max(c[3] for c in sub)
                        ntot = sum(c[4] for c in sub)
                        off = 0
                        for (qsel, ksel, vix, nk, NQ, st) in sub:
                            S.op('pe', lambda e: e.matmul(pS[0:nk, off:off + NQ], lhsT=ksel(hp), rhs=qsel(qT[hp, m, :]), start=True, stop=True),
                                 reads=[('KT', m), ('uTb', m)], writes=[pk])
                            off += NQ
                        if g == 2:
                            o_ap = scb[bi][0:nkmax, 0:ntot].rearrange("p (a i) -> p a i", a=4)
                            i_ap = pS[0:nkmax, 0:ntot].rearrange("p (a i) -> p a i", a=4)
                        else:
                            o_ap = scb[bi][0:nkmax, 0:ntot]
                            i_ap = pS[0:nkmax, 0:ntot]
                        S.op('dve', lambda e: e.scalar_tensor_tensor(out=o_ap, in0=dap, scalar=cval, in1=i_ap, op0=ALU.mult, op1=ALU.add),
                             reads=[pk, 'distT'], writes=['scb%d' % bi])
                        S.op('act', lambda e: e.activation(out=PTd[bi][0:nkmax, 0:ntot], in_=scb[bi][0:nkmax, 0:ntot], func=AF.Exp, scale=SCALE),
                             reads=['scb%d' % bi], writes=['PTd%d' % bi])

                    def emit_pv(it, ui):
                        g, m, sub, dap, half = it
                        hp = slice(64 * half, 64 * half + 64)
                        bi = ui % NBUF
                        hc = (2 * pr + half) * 64
                        nsub = len(sub)
                        off = 0
                        for si, (qsel, ksel, vix, nk, NQ, st) in enumerate(sub):
                            last = (si == nsub - 1) or sub[si + 1][5]
                            S.op('pe', lambda e: e.matmul(qsel(psOT[g][hp, :]), lhsT=Vg[0:nk, vix[0], vix[1], hc:hc + 64], rhs=PTd[bi][0:nk, off:off + NQ], start=st, stop=last),
                                 reads=[('Vg', g), 'PTd%d' % bi], writes=[PK[4 + g]])
                            S.op('pe', lambda e: e.matmul(qsel(psDEN[hp, :]), lhsT=onesb[0:nk, 0:64], rhs=PTd[bi][0:nk, off:off + NQ], start=(not den_started[half]), stop=True,
                                                          skip_group_check=True),
                                 reads=['onesb', 'PTd%d' % bi], writes=['ps7'])
                            den_started[half] = True
                            off += NQ

                    for i in range(len(items) + LOOK):
                        if i < len(items):
                            emit_qk_sm(items[i], i)
                        if i - LOOK >= 0:
                            emit_pv(items[i - LOOK], i - LOOK)
                    S.op('dve', lambda e: e.reciprocal(out=rtot[:], in_=psDEN[:, :]), reads=['ps7'], writes=['rtot'])
                    for g in range(3):
                        m = 2 * g + pr
                        at = atmp[g % 2]
                        ak = 'atmp%d' % (g % 2)
                        S.op('dve', lambda e, g=g, at=at: e.tensor_tensor(out=at[:], in0=psOT[g][:, :], in1=rtot[:], op=ALU.mult), reads=[PK[4 + g], 'rtot'], writes=[ak])
                        S.op('pool', lambda e, m=m, at=at: e.tensor_tensor(out=catT[:, m, :], in0=at[:], in1=sg[:, m, :], op=ALU.mult), reads=[ak, ('sg', m)], writes=[('catT', m)])

            def sample_dil_attention():
              with ExitStack() as ess:
                qtok = sbt(ess, "qtok", [64, 768], BF16)
                ktile = sbt(ess, "ktile", [128, 4, 2, 256])
                vb = sbt(ess, "vb", [128, 9, 256], BF16)
                prod = [sbt(ess, "prod%d" % i, [128, 256]) for i in range(2)]
                scs = sbt(ess, "scs", [128, 48])
                scs2 = sbt(ess, "scs2", [128, 48])
                Pb = sbt(ess, "Pb", [128, 48], BF16)
                sbias = sbt(ess, "sbias", [128, 48])
                pv_sb = sbt(ess, "pv_sb", [128, NB * 48])
                den_sb = sbt(ess, "den_sb", [128, NB * 48])
                otot = sbt(ess, "otot", [128, 6, 64])
                dtot = sbt(ess, "dtot", [128, 2, 64])
                ndist = sbt(ess, "ndist", [64, 2, 64])
                scn = [sbt(ess, "scn%d" % i, [64, 64]) for i in range(2)]
                PN = [sbt(ess, "PN%d" % i, [64, 64], BF16) for i in range(2)]
                S.dma('sp', sbias[:], c_sdist.rearrange("p g x -> p (g x)"), writes=['sbias'])
                S.dma('sp', ndist[:], c_ndist[:, :, :], writes=['ndist'])
                for (pq, c0, cw) in [(ps[0], 0, 512), (ps[1], 512, 256)]:
                    for kc in range(8):
                        S.op('pe', lambda e, kc=kc, pq=pq, c0=c0, cw=cw: e.matmul(pq[0:NS, 0:cw], lhsT=x1Tt[:, kc, 0:NS], rhs=w_in_sb[:, kc, c0:c0 + cw], start=(kc == 0), stop=(kc == 7)),
                             reads=[('w_in_sb', kc), ('xT', kc)], writes=[PK[ps.index(pq)]])
                S.op('act', lambda e: e.activation(out=qtok[:, 0:512], in_=ps[0][0:NS, :], func=AF.Copy), reads=['ps0'], writes=['qtok'])
                S.op('dve', lambda e: e.tensor_copy(out=qtok[:, 512:768], in_=ps[1][0:NS, 0:256]), reads=['ps1', 'qtok'], writes=['qtok'])
                idx = 0
                for g in range(3):
                    for hh in range(4):
                        h = 4 * g + hh
                        pr, half = hh // 2, hh % 2
                        m = 2 * g + pr
                        hp = slice(64 * half, 64 * half + 64)
                        cval = -SLOPES[h] * DILS[g] / SCALE
                        bi = half
                        S.op('pe', lambda e, hp=hp, m=m, bi=bi: e.matmul(ps[2 + bi][0:NS, 0:NS], lhsT=KT[hp, m, SEQ:NTOK], rhs=qT[hp, m, 0:NS], start=True, stop=True),
                             reads=[('KT', m), ('uTb', m)], writes=[PK[2 + bi]])
                        S.op('dve', lambda e, bi=bi, g=g, cval=cval: e.scalar_tensor_tensor(out=scn[bi][:, :], in0=ndist[:, (0 if g == 0 else 1), :], scalar=cval, in1=ps[2 + bi][0:NS, 0:NS],
                                                                                          op0=ALU.mult, op1=ALU.add), reads=[PK[2 + bi], 'ndist'], writes=['scn%d' % bi])
                        S.op('act', lambda e, bi=bi: e.activation(out=PN[bi][:, :], in_=scn[bi][:, :], func=AF.Exp, scale=SCALE), reads=['scn%d' % bi], writes=['PN%d' % bi])
                        col = (g * 2 + pr) * 64
                        S.op('pe', lambda e, hp=hp, g=g, hh=hh, bi=bi, col=col: e.matmul(ps[6][hp, col:col + 64], lhsT=Vnew[0:NS, g, hh * 64:(hh + 1) * 64], rhs=PN[bi][:, :], start=True, stop=True),
                             reads=['Vnew', 'PN%d' % bi], writes=['ps6'])
                        S.op('pe', lambda e, hp=hp, bi=bi, col=col: e.matmul(ps[7][hp, col:col + 64], lhsT=onesb[0:NS, 0:64], rhs=PN[bi][:, :], start=True, stop=True),
                             reads=['onesb', 'PN%d' % bi], writes=['ps7'])
                        idx += 1
                kcnt = 0
                for b in range(NB):
                  for tp in range(2):
                    for t in (2 * tp, 2 * tp + 1):
                        s_ = 4 * b + t
                        pa, pb_ = (ps[0], ps[1]) if t % 2 == 0 else (ps[2], ps[3])
                        sel = identb[0:NS, s_:s_ + 1].to_broadcast([NS, 128])
                        S.op('pe', lambda e, pa=pa, sel=sel: e.matmul(pa[:, 0:512], lhsT=sel, rhs=qtok[:, 0:512], start=True, stop=True), reads=['qtok', 'identb'], writes=[PK[ps.index(pa)]])
                        S.op('pe', lambda e, pb_=pb_, sel=sel: e.matmul(pb_[:, 0:256], lhsT=sel, rhs=qtok[:, 512:768], start=True, stop=True), reads=['qtok', 'identb'], writes=[PK[ps.index(pb_)]])
                    for g in range(3):
                        kt = ktile[:, kcnt % 4, :, :]
                        kk = 'ktile%d' % (kcnt % 4)
                        kcnt += 1
                        if g == 0:
                            S.dma('sp', kt[:, 0, :], cd[0][b % NBC, :, 0:256], writes=[kk])
                            if tp == 0:
                                S.dma('pool', vb[:, 0, :], cd[0][b % NBC, :, 256:512], writes=[('vb', g)])
                        else:
                            r_ = 4 if g == 1 else 16
                            kb = 1 if g == 1 else 5
                            srcv = cd[g][b % NBC].rearrange("(u r) c -> u r c", r=r_)
                            S.dma('sp', kt[:, :, :], srcv[:, 2 * tp:2 * tp + 2, 0:256], writes=[kk])
                            S.dma('pool', vb[:, kb + 2 * tp:kb + 2 * tp + 2, :], srcv[:, 2 * tp:2 * tp + 2, 256:512], writes=[('vb', g)])
                        for t in (2 * tp, 2 * tp + 1):
                            pa, pb_ = (ps[0], ps[1]) if t % 2 == 0 else (ps[2], ps[3])
                            qb = pa[:, g * 256:(g + 1) * 256] if g < 2 else pb_[:, 0:256]
                            qk = PK[ps.index(pa)] if g < 2 else PK[ps.index(pb_)]
                            ki = 0 if g == 0 else t - 2 * tp
                            pi = (g * 4 + t) % 2
                            S.op('dve', lambda e, ki=ki, qb=qb, pi=pi, kt=kt: e.tensor_tensor(out=prod[pi][:, :], in0=kt[:, ki, :], in1=qb, op=ALU.mult),
                                 reads=[kk, qk], writes=['prod%d' % pi])
                            S.op('dve', lambda e, g=g, t=t, pi=pi: e.tensor_reduce(out=scs[:, (g * 4 + t) * 4:(g * 4 + t) * 4 + 4], in_=prod[pi][:, :].rearrange("p (h e) -> p h e", h=4),
                                                                                 axis=mybir.AxisListType.X, op=ALU.add),
                                 reads=['prod%d' % pi], writes=['scs'])
                  if True:
                    S.op('dve', lambda e: e.scalar_tensor_tensor(out=scs2[:], in0=scs[:], scalar=SCALE, in1=sbias[:], op0=ALU.mult, op1=ALU.add), reads=['scs', 'sbias'], writes=['scs2'])
                    S.op('act', lambda e: e.activation(out=Pb[:], in_=scs2[:], func=AF.Exp), reads=['scs2'], writes=['Pb'])
                    pB = ps[4 + b % 2]
                    pBk = PK[4 + b % 2]
                    for g in range(3):
                        for t in range(4):
                            ki = 0 if g == 0 else (1 + t if g == 1 else 5 + t)
                            for pr in range(2):
                                col = ((g * 4 + t) * 2 + pr) * 2
                                pcol = (g * 4 + t) * 4 + 2 * pr
                                S.op('pe', lambda e, ki=ki, pr=pr, col=col, pcol=pcol, pB=pB: e.matmul(pB[:, col:col + 2], lhsT=vb[:, ki, pr * 128:(pr + 1) * 128], rhs=Pb[:, pcol:pcol + 2],
                                                                                                      start=True, stop=True),
                                     reads=[('vb', g), 'Pb'], writes=[pBk])
                    S.op('pe', lambda e, pB=pB: e.matmul(pB[:, 64:112], lhsT=onesb[:, :], rhs=Pb[:, :], start=True, stop=True), reads=['onesb', 'Pb'], writes=[pBk])
                    S.op('act', lambda e, b=b, pB=pB: e.activation(out=pv_sb[:, b * 48:(b + 1) * 48], in_=pB[:, 0:48], func=AF.Copy), reads=[pBk], writes=['pv_sb'])
                    S.op('dve', lambda e, b=b, pB=pB: e.tensor_copy(out=den_sb[:, b * 48:(b + 1) * 48], in_=pB[:, 64:112]), reads=[pBk], writes=['den_sb'])
                for half in range(2):
                    hp = slice(64 * half, 64 * half + 64)
                    for g in range(3):
                        pvv = pv_sb[hp, :].rearrange("p (b g t r j) -> p g r j b t", g=3, t=4, r=2, j=2)[:, g, :, half, :, :]
                        dnv = den_sb[hp, :].rearrange("p (b g t r j) -> p g r j b t", g=3, t=4, r=2, j=2)[:, g, :, half, :, :]
                        nv = lambda pp, g=g, hp=hp: pp[hp, 2 * g * 64:(2 * g + 2) * 64].rearrange("p (r b t) -> p r b t", r=2, t=4)
                        S.op('dve', lambda e, pvv=pvv, nv=nv, g=g, hp=hp: e.tensor_tensor(out=otot[hp, 2 * g:2 * g + 2, :].rearrange("p r (b t) -> p r b t", t=4), in0=pvv, in1=nv(ps[6]), op=ALU.add),
                             reads=['pv_sb', 'ps6'], writes=['otot'])
                        dv = dtot[hp, :, :].rearrange("p r (b t) -> p r b t", t=4)
                        if g == 0:
                            S.op('dve', lambda e, dnv=dnv, nv=nv, dv=dv: e.tensor_tensor(out=dv, in0=dnv, in1=nv(ps[7]), op=ALU.add), reads=['den_sb', 'ps7'], writes=['dtot'])
                        else:
                            S.op('dve', lambda e, dnv=dnv, dv=dv: e.tensor_tensor(out=dv, in0=dv, in1=dnv, op=ALU.add), reads=['den_sb', 'dtot'], writes=['dtot'])
                            S.op('dve', lambda e, nv=nv, dv=dv: e.tensor_tensor(out=dv, in0=dv, in1=nv(ps[7]), op=ALU.add), reads=['ps7', 'dtot'], writes=['dtot'])
                S.op('dve', lambda e: e.reciprocal(out=dtot[:], in_=dtot[:]), reads=['dtot'], writes=['dtot'])
                for g in range(3):
                    S.op('dve', lambda e, g=g: e.tensor_tensor(out=otot[:, 2 * g:2 * g + 2, :], in0=otot[:, 2 * g:2 * g + 2, :], in1=dtot[:, :, :], op=ALU.mult), reads=['otot', 'dtot'], writes=['otot'])
                    S.op('pool', lambda e, g=g: e.tensor_tensor(out=catT[:, 2 * g:2 * g + 2, 0:NS], in0=otot[:, 2 * g:2 * g + 2, :], in1=sg[:, 2 * g:2 * g + 2, 0:NS], op=ALU.mult),
                         reads=['otot', ('sg', 2 * g), ('sg', 2 * g + 1)], writes=[('catT', 2 * g), ('catT', 2 * g + 1)])
                S.barrier()

            for tt in range(SEQ // NT2):
                t0 = tt * NT2
                for kc in range(8):
                    S.dma('sp', x1Tt[:, kc, :], x1Tscr[kc, :, t0:t0 + NT2], writes=[('xT', kc)])
                in_proj(T2, NT2, w_in_sb, x1Tt, qT, 1)
                dil_attention(tt)
                mem_attention(T2, 1, NT2, mq, smg, catT, KmT[:, 1, :, :], Vm[:, 1, :, :], [ps[2], ps[3]], [ps[4], ps[6]], [ps[5], ps[7]])
                blks = []
                for a in range(NT2 // 128):
                    ta = t0 + a * 128
                    blks.append((a, 128, x1scr[ta:ta + 128, :], ta, y_p[ta:ta + 128, :]))
                ln_blocks(T2, blks, catT, w_out_sb, False)
            S.barrier()
            esq.close()
            if stage >= 5 and KCUT != 53:
                for kc in range(8):
                    S.dma('sp', x1Tt[:, kc, 0:NS], x1Tscr[kc, :, SEQ:NTOK], writes=[('xT', kc)])
                in_proj(T2, NS, w_in_sb, x1Tt, qT, 1)
                if stage >= 6:
                    sample_dil_attention()
                else:
                    for m in range(6):
                        S.op('pool', lambda e, m=m: e.memset(catT[:, m, 0:NS], 0.0), writes=[('catT', m)])
                mem_attention_sample(T2, 1, smg, mq, catT, ps[2], ps[3], [ps[4], ps[6]], [ps[5], ps[7]])
                ln_blocks(T2, [(0, NS, x1scr[SEQ:NTOK, :], SEQ, y_s[0:NS, :])], catT, w_out_sb, False)
            S.barrier()

        if stage >= 3:
            with ExitStack() as es2:
                _phase2(es2)

        issue_bulk(len(bulk_list))
        S.finish()
        print("ops", S.nops, "waits", S.nwait)
    return nc


_CONST_CACHE = {}


def _consts():
    if _CONST_CACHE:
        return _CONST_CACHE
    ident = np.eye(128, dtype=np.float32)
    sw = np.zeros((128, 128), np.float32)
    for m in range(64):
        sw[64 + m, m] = -1.0
        sw[m, 64 + m] = 1.0
    p = np.arange(128)
    meo = np.zeros((128, 4), np.float32)
    for j in range(4):
        meo[:, j] = ((p // 16) % 4 == j)
    jv = np.tile(np.arange(128, dtype=np.float32)[None, :], (128, 1))
    u = np.arange(128)[:, None]
    i = np.arange(128)[None, :]
    dist = np.zeros((128, 2, 128), np.float32)
    dist[:, 0, :] = np.where(u >= i, i + 128 - u, BIG)
    dist[:, 1, :] = np.where(u <= i, i - u, BIG)
    sd = np.zeros((128, 3, 4, 4), np.float32)
    for g in range(3):
        for t in range(4):
            for hh in range(4):
                uu = np.arange(128)
                if g == 0:
                    dd = 128 + t - uu
                    sd[:, g, t, hh] = np.where(uu >= t, -SLOPES[4 * g + hh] * dd, -30000.0)
                else:
                    sd[:, g, t, hh] = -SLOPES[4 * g + hh] * ((128 - uu) * DILS[g])
    nd = np.full((64, 2, 64), BIG, np.float32)
    for sp_ in range(64):
        for s_ in range(64):
            if sp_ // 4 == s_ // 4 and sp_ % 4 <= s_ % 4:
                nd[sp_, 0, s_] = (s_ % 4) - (sp_ % 4)
            if sp_ == s_:
                nd[sp_, 1, s_] = 0.0
    _CONST_CACHE.update(dict(c_ident=ident, c_swap=sw, c_maskeo=meo, c_jvec=jv, c_dist=dist.reshape(128, 256),
                             c_sdist=sd.reshape(128, 3, 16), c_ndist=nd))
    return _CONST_CACHE


_NC_CACHE = {}


def kernel(x_prompt, x_sample, cache_mem_kv, state_ssm_re, state_ssm_im, cache_dil1_kv, cache_dil4_kv,
           cache_dil16_kv, mem_prompt, w_in, w_out, ln_g, ln_b, w_mem_kv, ssm_lambda_re, ssm_lambda_im,
           ssm_log_dt, ssm_b_re, ssm_b_im, ssm_c_re, ssm_c_im, ssm_d, w_glu, b_glu, w_kv_shared, _stage=99):
    f = lambda a: np.ascontiguousarray(np.asarray(a, dtype=np.float32))
    if _stage not in _NC_CACHE:
        _NC_CACHE[_stage] = build(_stage)
    nc = _NC_CACHE[_stage]
    shared = dict(
        w_in=f(w_in), w_out=f(w_out), ln_g=f(ln_g), ln_b=f(ln_b), w_mem=f(w_mem_kv),
        lam_re=f(ssm_lambda_re)[0], lam_im=f(ssm_lambda_im)[0], log_dt=f(ssm_log_dt)[0],
        b_re=f(ssm_b_re)[0], b_im=f(ssm_b_im)[0],
        c_re=f(ssm_c_re)[0].reshape(768, 64), c_im=f(ssm_c_im)[0].reshape(768, 64),
        ssm_d=f(ssm_d)[0].reshape(768), w_glu=f(w_glu)[0], b_glu=f(b_glu)[0], w_kv=f(w_kv_shared))
    shared.update(_consts())
    x_prompt = np.asarray(x_prompt)
    in_maps = []
    for c in range(NCORES):
        bs = slice(NB * c, NB * (c + 1))
        d = dict(shared)
        d.update(
            x_p=f(x_prompt[c]), x_s=f(np.asarray(x_sample)[bs]).reshape(NS, 1024),
            cmk=f(np.asarray(cache_mem_kv)[:, bs]).reshape(2, NB, 256, 512),
            st_re=f(np.asarray(state_ssm_re)[0, bs]).reshape(NB * 48, 64),
            st_im=f(np.asarray(state_ssm_im)[0, bs]).reshape(NB * 48, 64),
            cd1=f(np.asarray(cache_dil1_kv)[bs]).reshape(NB, 128, 512),
            cd4=f(np.asarray(cache_dil4_kv)[bs]).reshape(NB, 512, 512),
            cd16=f(np.asarray(cache_dil16_kv)[bs]).reshape(NB, 2048, 512),
            memp=f(np.asarray(mem_prompt)[c]))
        if _stage < 5:
            for k in ('cmk',):
                d[k] = np.ascontiguousarray(d[k][:, 0:1])
        if _stage < 5 or (50 <= KCUT < 60):
            for k in ('cd1', 'cd4', 'cd16'):
                d[k] = np.ascontiguousarray(d[k][0:1])
        in_maps.append(d)
    res = run_bass_kernel_spmd(nc, in_maps, core_ids=list(range(NCORES)))
    R = res.results
    cat = lambda k: np.stack([np.asarray(R[c][k]) for c in range(NCORES)], axis=0)
    y_prompt = cat("y_p")
    y_sample = cat("y_s").reshape(128, 4, 1024)
    mem_kv_prompt = cat("mkv_p").transpose(1, 0, 2, 3).reshape(2, 8, 256, 2, 4, 64)
    ssm_re_prompt = cat("sre_p")[None]
    ssm_im_prompt = cat("sim_p")[None]
    d1p = cat("d1_p").reshape(8, 128, 2, 4, 64)
    d4p = cat("d4_p").reshape(8, 512, 2, 4, 64)
    d16p = cat("d16_p").reshape(8, 2048, 2, 4, 64)
    ssm_re_sample = cat("sre_s").reshape(1, 128, 48, 64)
    ssm_im_sample = cat("sim_s").reshape(1, 128, 48, 64)
    if _stage < 5 or (50 <= KCUT < 60):
        z = lambda *sh: np.zeros(sh, np.float32)
        return (y_prompt, y_sample, mem_kv_prompt, ssm_re_prompt, ssm_im_prompt, d1p, d4p, d16p, ssm_re_sample, ssm_im_sample,
                z(128, 128, 2, 4, 64), z(128, 512, 2, 4, 64), z(128, 2048, 2, 4, 64))
    d1s = cat("d1_s").reshape(128, 128, 2, 4, 64)
    d4s = cat("d4_s").reshape(128, 512, 2, 4, 64)
    d16s = cat("d16_s").reshape(128, 2048, 2, 4, 64)
    return (y_prompt, y_sample, mem_kv_prompt, ssm_re_prompt, ssm_im_prompt, d1p, d4p, d16p,
            ssm_re_sample, ssm_im_sample, d1s, d4s, d16s)
```

```python
import math
import os
KCUT = int(os.environ.get('KCUT', '99'))
import numpy as np
import concourse.bass as bass
import concourse.mybir as mybir
from concourse.bass_utils import run_bass_kernel_spmd
from contextlib import ExitStack
from types import SimpleNamespace

F32 = mybir.dt.float32
BF16 = mybir.dt.bfloat16
I32 = mybir.dt.int32
AF = mybir.ActivationFunctionType
ALU = mybir.AluOpType

NCORES = 8
SEQ = 2048
NS = 64
NB = 16
NTOK = SEQ + NS
ALPHA = (2.0 * 2) ** 0.25
LN_EPS = 1e-5
SCALE = 0.125
BIG = 1.0e6
TWO_PI = 2.0 * math.pi
C1_2PI = 6.28125
C2_2PI = TWO_PI - 6.28125
PI_LO = 3.1415925
SLOPES = [2.0 ** (-8.0 * (h + 1) / 12.0) for h in range(12)]
DILS = [1, 4, 16]


class _Stop(Exception):
    pass


class Sched:
    def __init__(self, nc, es, n_dma_sems=40):
        self.nc = nc
        self.eng = {'pe': nc.tensor, 'dve': nc.vector, 'act': nc.scalar, 'pool': nc.gpsimd, 'sp': nc.sync}
        self.sem = {k: es.enter_context(nc.semaphore('sem_' + k)) for k in self.eng}
        self.cnt = {k: 0 for k in self.eng}
        self.n_hw = n_dma_sems - 16
        self.dsem = [es.enter_context(nc.semaphore('dsem%d' % i)) for i in range(n_dma_sems)]
        self.dnext_sw = 0
        self.dcnt = [0] * n_dma_sems
        self.dbg = [False] * n_dma_sems
        self.dnext = 0
        self.bsem = es.enter_context(nc.semaphore('bulk'))
        self.bcnt = 0
        self.waited = {k: {} for k in self.eng}
        self.snap = {}
        self.lastw = {}
        self.readers = {}
        self.nops = 0
        self.nwait = 0

    def semobj(self, sid):
        if sid == 'bulk':
            return self.bsem
        return self.sem[sid] if isinstance(sid, str) else self.dsem[sid]

    def _wait(self, e, sid, val):
        if val <= 0:
            return
        if sid == e and e == 'pe':
            return
        w = self.waited[e]
        if w.get(sid, 0) >= val:
            return
        self.eng[e].wait_ge(self.semobj(sid), val)
        self.nwait += 1
        w[sid] = val
        sn = self.snap.get((sid, val))
        if sn:
            for s2, v2 in sn.items():
                if s2 != e and w.get(s2, 0) < v2:
                    w[s2] = v2

    def _deps(self, e, reads, writes):
        for r in reads:
            if r in self.lastw:
                self._wait(e, *self.lastw[r])
        for r in writes:
            if r in self.lastw:
                self._wait(e, *self.lastw[r])
            for sid, val in self.readers.get(r, {}).items():
                self._wait(e, sid, val)

    def _commit(self, sid, val, reads, writes):
        for r in reads:
            d = self.readers.setdefault(r, {})
            d[sid] = max(val, d.get(sid, 0))
        for r in writes:
            self.lastw[r] = (sid, val)
            self.readers[r] = {}

    def op(self, e, fn, reads=(), writes=()):
        isps = lambda r: isinstance(r, str) and r.startswith('ps') and r[2:].isdigit()
        writes = list(writes) + [r for r in reads if isps(r)]
        reads = [r for r in reads if not isps(r)]
        self._deps(e, reads, writes)
        ins = fn(self.eng[e])
        self.cnt[e] += 1
        self.snap[(e, self.cnt[e])] = dict(self.waited[e])
        ins.then_inc(self.sem[e], 1)
        self._commit(e, self.cnt[e], reads, writes)
        self.nops += 1

    def dma(self, e, out, in_, reads=(), writes=(), bg=False, **kw):
        if e == 'pool':
            j = self.n_hw + self.dnext_sw
            self.dnext_sw = (self.dnext_sw + 1) % (len(self.dsem) - self.n_hw)
        else:
            j = self.dnext
            self.dnext = (j + 1) % self.n_hw
        self._wait(e, j, self.dcnt[j])
        self._deps(e, reads, writes)
        ins = self.eng[e].dma_start(out=out, in_=in_, **kw)
        self.dcnt[j] += 16
        self.snap[(j, self.dcnt[j])] = dict(self.waited[e])
        self.dbg[j] = bg
        ins.then_inc(self.dsem[j], 16)
        self._commit(j, self.dcnt[j], reads, writes)
        self.nops += 1

    def dma_bulk(self, e, out, in_):
        ins = self.eng[e].dma_start(out=out, in_=in_)
        self.bcnt += 16
        ins.then_inc(self.bsem, 16)

    def barrier(self, all_dma=False):
        for e in self.eng:
            for k in self.eng:
                if k != e:
                    self._wait(e, k, self.cnt[k])
            for j in range(len(self.dsem)):
                if all_dma or not self.dbg[j]:
                    self._wait(e, j, self.dcnt[j])

    def finish(self):
        self.barrier(all_dma=True)
        self._wait('sp', 'bulk', self.bcnt)


def build(stage=99):
    nc = bass.Bass("TRN2", target_bir_lowering=False)
    NBC = NB if (stage >= 5 and not (50 <= KCUT < 60)) else 1
    NBM = NB if stage >= 5 else 1

    def din(name, shape, dt=F32):
        return nc.dram_tensor(name, list(shape), dt, kind="ExternalInput").ap()

    def dout(name, shape, dt=F32):
        return nc.dram_tensor(name, list(shape), dt, kind="ExternalOutput").ap()

    def dscr(name, shape, dt=F32):
        return nc.dram_tensor(name, list(shape), dt, kind="Internal").ap()

    x_p = din("x_p", [SEQ, 1024])
    x_s = din("x_s", [NS, 1024])
    cmk = din("cmk", [2, NBM, 256, 512])
    st_re = din("st_re", [NB * 48, 64])
    st_im = din("st_im", [NB * 48, 64])
    cd = [din("cd1", [NBC, 128, 512]), din("cd4", [NBC, 512, 512]), din("cd16", [NBC, 2048, 512])]
    memp = din("memp", [256, 1024])
    w_in = din("w_in", [2, 1024, 2048])
    w_out = din("w_out", [2, 1024, 1024])
    ln_g = din("ln_g", [2, 1024])
    ln_b = din("ln_b", [2, 1024])
    w_mem = din("w_mem", [2, 1024, 512])
    lam_re = din("lam_re", [48, 64])
    lam_im = din("lam_im", [48, 64])
    log_dt = din("log_dt", [48])
    b_re = din("b_re", [48, 64, 16])
    b_im = din("b_im", [48, 64, 16])
    c_re = din("c_re", [768, 64])
    c_im = din("c_im", [768, 64])
    ssm_d = din("ssm_d", [768])
    w_glu = din("w_glu", [768, 768])
    b_glu = din("b_glu", [768])
    w_kv = din("w_kv", [1024, 1536])
    c_ident = din("c_ident", [128, 128])
    c_swap = din("c_swap", [128, 128])
    c_maskeo = din("c_maskeo", [128, 4])
    c_jvec = din("c_jvec", [128, 128])
    c_dist = din("c_dist", [128, 256])
    c_sdist = din("c_sdist", [128, 3, 16])
    c_ndist = din("c_ndist", [64, 2, 64])

    y_p = dout("y_p", [SEQ, 1024])
    y_s = dout("y_s", [NS, 1024])
    mkv_p = dout("mkv_p", [2, 256, 512])
    sre_p = dout("sre_p", [48, 64])
    sim_p = dout("sim_p", [48, 64])
    dkv_p = [dout("d1_p", [128, 512]), dout("d4_p", [512, 512]), dout("d16_p", [2048, 512])]
    sre_s = dout("sre_s", [NB * 48, 64])
    sim_s = dout("sim_s", [NB * 48, 64])
    dkv_s = [dout("d1_s", [NBC, 128, 512]), dout("d4_s", [NBC, 512, 512]), dout("d16_s", [NBC, 2048, 512])]

    x1scr = dscr("x1scr", [NTOK, 1024], F32)
    x1Tscr = dscr("x1Tscr", [8, 128, NTOK], BF16)

    with ExitStack() as es0:
        S = Sched(nc, es0)

        def sbt(es, name, shape, dt=F32):
            return es.enter_context(nc.sbuf_tensor(name, list(shape), dt))

        ps = [es0.enter_context(nc.psum_tensor("ps%d" % i, [128, 512], F32)) for i in range(8)]
        PK = ['ps%d' % i for i in range(8)]

        identf = sbt(es0, "identf", [128, 128])
        identb = sbt(es0, "identb", [128, 128], BF16)
        onesb = sbt(es0, "onesb", [128, 128], BF16)
        S.dma('sp', identf[:], c_ident[:, :], writes=['identf'])
        S.op('dve', lambda e: e.tensor_copy(out=identb[:], in_=identf[:]), reads=['identf'], writes=['identb'])
        S.op('dve', lambda e: e.memset(onesb[:], 1.0), writes=['onesb'])
        KmT = sbt(es0, "KmT", [128, 2, 2, 256], BF16)
        Vm = sbt(es0, "Vm", [128, 2, 2, 256], BF16)
        lnG = sbt(es0, "lnG", [128, 1024])
        lnB = sbt(es0, "lnB", [128, 1024])

        wins = [128, 512, 2048]
        bulk_list = []
        for g in (range(3) if stage != 0 and stage != 2 and not (50 <= KCUT < 60) else []):
            for b in range(NBC):
                src = cd[g][b, 4:wins[g], :].rearrange("r c -> (r c)").rearrange("(a x) -> a x", a=16)
                dst = dkv_s[g][b, 0:wins[g] - 4, :].rearrange("r c -> (r c)").rearrange("(a x) -> a x", a=16)
                bulk_list.append((dst, src))

        def issue_bulk(n):
            for _ in range(min(n, len(bulk_list))):
                dst, src = bulk_list.pop()
                S.dma_bulk('act', dst, src)

        def evac(i, out, in_, reads, writes):
            if i % 2 == 0:
                S.op('act', lambda e: e.activation(out=out, in_=in_, func=AF.Copy), reads=reads, writes=writes)
            else:
                S.op('dve', lambda e: e.tensor_copy(out=out, in_=in_), reads=reads, writes=writes)

        def load_weight(es_w, dst, src_rows, ncols, stage_tiles, key, col_map=None):
            nk = dst.shape[1]
            for kc in range(nk):
                S.dma('pool', dst[:, kc, :], src_rows(kc), writes=[(key, kc)], bg=True)

        def wkeys(key, n):
            return [(key, kc) for kc in range(n)]

        def ln_part1(T, blk_i, rows, cat, wout_sb, x_src):
            c0 = blk_i * 128
            bb = blk_i % 2
            xr = T.xres[bb]
            xk = ['stgA', 'stgB'][bb]
            zt = T.zt[bb]
            zk = 'zt%d' % bb
            S.dma('sp', xr[0:rows, :], x_src, writes=[xk])
            for n in range(2):
                for kc in range(8):
                    S.op('pe', lambda e: e.matmul(ps[n][0:rows, :], lhsT=cat[:, kc, c0:c0 + rows], rhs=wout_sb[:, kc, n * 512:(n + 1) * 512], start=(kc == 0), stop=(kc == 7)),
                         reads=[('catT', kc), ('w_out_sb', kc)], writes=[PK[n]])
                S.op('dve', lambda e: e.scalar_tensor_tensor(out=zt[0:rows, n * 512:(n + 1) * 512], in0=xr[0:rows, n * 512:(n + 1) * 512], scalar=ALPHA,
                                                             in1=ps[n][0:rows, :], op0=ALU.mult, op1=ALU.add),
                     reads=[xk, PK[n]], writes=[(zk, n), zk])
                S.op('dve', lambda e: e.bn_stats(out=T.stats[bb][0:rows, n, :], in_=zt[0:rows, n * 512:(n + 1) * 512]), reads=[(zk, n)], writes=[('stats', bb, n)])
            S.op('dve', lambda e: e.bn_aggr(out=T.mv[bb][0:rows, :], in_=T.stats[bb][0:rows, :, :]), reads=[('stats', bb, 0), ('stats', bb, 1)], writes=[('mv', bb)])

        def ln_part2(T, blk_i, rows, tok0, y_dst, make_T):
            bb = blk_i % 2
            zt = T.zt[bb]
            zk = 'zt%d' % bb
            mv, rstd, nmr = T.mv[bb], T.rstd[bb], T.nmr[bb]
            S.op('act', lambda e: e.activation(out=rstd[0:rows, :], in_=mv[0:rows, 1:2], func=AF.Sqrt, bias=T.epsb[0:rows, :], scale=1.0), reads=[('mv', bb), 'epsb'], writes=[('rstd', bb)])
            S.op('dve', lambda e: e.reciprocal(out=rstd[0:rows, :], in_=rstd[0:rows, :]), reads=[('rstd', bb)], writes=[('rstd', bb)])
            S.op('dve', lambda e: e.tensor_scalar(out=nmr[0:rows, :], in0=mv[0:rows, 0:1], scalar1=rstd[0:rows, 0:1], scalar2=-1.0, op0=ALU.mult, op1=ALU.mult),
                 reads=[('mv', bb), ('rstd', bb)], writes=[('nmr', bb)])
            S.op('act', lambda e: e.activation(out=zt[0:rows, :], in_=zt[0:rows, :], func=AF.Identity, scale=rstd[0:rows, 0:1], bias=nmr[0:rows, 0:1]),
                 reads=[(zk, 0), (zk, 1), ('rstd', bb), ('nmr', bb)], writes=[zk, (zk, 0), (zk, 1)])
            S.op('dve', lambda e: e.tensor_tensor(out=zt[0:rows, :], in0=zt[0:rows, :], in1=lnG[0:rows, :], op=ALU.mult), reads=[zk, 'lnG'], writes=[zk])
            S.op('pool', lambda e: e.tensor_tensor(out=zt[0:rows, :], in0=zt[0:rows, :], in1=lnB[0:rows, :], op=ALU.add), reads=[zk, 'lnB'], writes=[zk])
            S.dma('sp', y_dst, zt[0:rows, :], reads=[zk])
            if make_T:
                S.op('act', lambda e: e.activation(out=T.x1b[0:rows, :], in_=zt[0:rows, :], func=AF.Copy), reads=[zk], writes=['x1b'])
                for kc in range(8):
                    pt = ps[2 + kc // 4]
                    S.op('pe', lambda e: e.matmul(pt[:, (kc % 4) * 128:(kc % 4) * 128 + rows], lhsT=T.x1b[0:rows, kc * 128:(kc + 1) * 128], rhs=identb[0:rows, 0:rows], start=True, stop=True),
                         reads=['x1b', 'identb'], writes=[PK[2 + kc // 4]])
                for hh in range(2):
                    evac(hh, T.x1T[:, 4 * hh:4 * hh + 4, 0:rows], ps[2 + hh][:, :].rearrange("p (k t) -> p k t", k=4)[:, :, 0:rows], [PK[2 + hh]], ['x1T'])
                S.dma('sp', x1Tscr[:, :, tok0:tok0 + rows].rearrange("k p t -> p k t"), T.x1T[:, :, 0:rows], reads=['x1T'])

        def ln_blocks(T, blocks, cat, wout_sb, make_T):
            n = len(blocks)
            for i in range(n + 1):
                if i < n:
                    bi_, rows, x_src, tok0, y_dst = blocks[i]
                    ln_part1(T, bi_, rows, cat, wout_sb, x_src)
                if i >= 1:
                    bi_, rows, x_src, tok0, y_dst = blocks[i - 1]
                    ln_part2(T, bi_, rows, tok0, y_dst, make_T)

        def mem_attention(T, l, NT, mq_t, smg_t, cat, Km, Vmm, psS, psO, psD):
            for h in range(4):
                pr, half = h // 2, h % 2
                hp = slice(64 * half, 64 * half + 64)
                for c in range(2):
                    pS = psS[c] if NT > 256 else psS[0]
                    off = 0 if NT > 256 else c * NT
                    S.op('pe', lambda e, c=c, pr=pr, hp=hp, pS=pS, off=off: e.matmul(pS[:, off:off + NT], lhsT=Km[hp, pr, c * 128:(c + 1) * 128], rhs=mq_t[hp, pr, 0:NT],
                                                                                     start=True, stop=True),
                         reads=['mq', 'KmT'], writes=[PK[ps.index(pS)]])
                if NT > 256:
                    for c in range(2):
                        S.op('act', lambda e, c=c: e.activation(out=T.PT[:, c * NT:(c + 1) * NT], in_=psS[c][:, 0:NT], func=AF.Exp, scale=SCALE),
                             reads=[PK[ps.index(psS[c])]], writes=[('PTm', c)])
                else:
                    S.op('act', lambda e: e.activation(out=T.PT[:, 0:2 * NT], in_=psS[0][:, 0:2 * NT], func=AF.Exp, scale=SCALE),
                         reads=[PK[ps.index(psS[0])]], writes=[('PTm', 0), ('PTm', 1)])
                for c in range(2):
                    S.op('pe', lambda e, c=c, h=h, pr=pr, hp=hp: e.matmul(psO[pr][hp, 0:NT], lhsT=Vmm[:, c, h * 64:(h + 1) * 64], rhs=T.PT[:, c * NT:(c + 1) * NT],
                                                                          start=(c == 0), stop=(c == 1)),
                         reads=[('PTm', c), 'Vm'], writes=[PK[ps.index(psO[pr])]])
                    S.op('pe', lambda e, c=c, pr=pr, hp=hp: e.matmul(psD[pr][hp, 0:NT], lhsT=onesb[:, 0:64], rhs=T.PT[:, c * NT:(c + 1) * NT],
                                                                     start=(c == 0), stop=(c == 1)),
                         reads=[('PTm', c), 'onesb'], writes=[PK[ps.index(psD[pr])]])
            for pr in range(2):
                S.op('dve', lambda e, pr=pr: e.reciprocal(out=T.rden[:, pr * NT:(pr + 1) * NT], in_=psD[pr][:, 0:NT]), reads=[PK[ps.index(psD[pr])]], writes=[('rden', pr)])
                S.op('dve', lambda e, pr=pr: e.tensor_tensor(out=T.rden[:, pr * NT:(pr + 1) * NT], in0=psO[pr][:, 0:NT], in1=T.rden[:, pr * NT:(pr + 1) * NT], op=ALU.mult),
                     reads=[PK[ps.index(psO[pr])], ('rden', pr)], writes=[('rden', pr)])
                S.op('pool', lambda e, pr=pr: e.tensor_tensor(out=cat[:, 6 + pr, 0:NT], in0=T.rden[:, pr * NT:(pr + 1) * NT], in1=smg_t[:, pr, 0:NT], op=ALU.mult),
                     reads=[('rden', pr), 'smg'], writes=[('catT', 6 + pr)])

        def mem_attention_sample(T, l, smg_t, mq_t, cat, psS, psT, psO, psD):
            NT = NS
            for b in range(NB):
                kvb, bk = T.kvb[b % 2]
                kts, tk = T.kts[b % 2]
                S.dma('pool', kvb[:, :, :], cmk[l, b].rearrange("(c p) x -> p c x", p=128), writes=bk)
                for c in range(2):
                    for pr in range(2):
                        S.op('pe', lambda e, c=c, pr=pr, kvb=kvb: e.matmul(psT[:, pr * 256 + c * 128:pr * 256 + (c + 1) * 128], lhsT=kvb[:, c, pr * 128:(pr + 1) * 128], rhs=identb[:],
                                                                         start=True, stop=True),
                             reads=bk + ['identb'], writes=[PK[ps.index(psT)]])
                evac(b, kts[:, :, :], psT[:, 0:512].rearrange("p (a m) -> p a m", a=2), [PK[ps.index(psT)]], tk)
                for h in range(4):
                    pr, half = h // 2, h % 2
                    hp = slice(64 * half, 64 * half + 64)
                    for c in range(2):
                        pS_h = psS if half == 0 else psT
                        S.op('pe', lambda e, c=c, pr=pr, hp=hp, h=h, kts=kts, b=b, pS_h=pS_h: e.matmul(pS_h[:, (h * 2 + c) * 4:(h * 2 + c) * 4 + 4], lhsT=kts[hp, pr, c * 128:(c + 1) * 128],
                                                                                                 rhs=mq_t[hp, pr, 4 * b:4 * b + 4], start=True, stop=True),
                             reads=tk + ['mq'], writes=[PK[ps.index(pS_h)]])
                for h in range(4):
                    pS_h = psS if h % 2 == 0 else psT
                    S.op('act', lambda e, h=h, pS_h=pS_h: e.activation(out=T.PTs[:, h * 8:h * 8 + 8], in_=pS_h[:, h * 8:h * 8 + 8], func=AF.Exp, scale=SCALE),
                         reads=[PK[ps.index(pS_h)]], writes=['PTs'])
                for h in range(4):
                    pr, half = h // 2, h % 2
                    hp = slice(64 * half, 64 * half + 64)
                    for c in range(2):
                        S.op('pe', lambda e, c=c, h=h, pr=pr, hp=hp, kvb=kvb, b=b: e.matmul(psO[pr][hp, 4 * b:4 * b + 4], lhsT=kvb[:, c, 256 + h * 64:256 + (h + 1) * 64],
                                                                                       rhs=T.PTs[:, (h * 2 + c) * 4:(h * 2 + c) * 4 + 4], start=(c == 0), stop=(c == 1)),
                             reads=bk + ['PTs'], writes=[PK[ps.index(psO[pr])]])
                        S.op('pe', lambda e, c=c, h=h, pr=pr, hp=hp, b=b: e.matmul(psD[pr][hp, 4 * b:4 * b + 4], lhsT=onesb[:, 0:64],
                                                                              rhs=T.PTs[:, (h * 2 + c) * 4:(h * 2 + c) * 4 + 4], start=(c == 0), stop=(c == 1)),
                             reads=['onesb', 'PTs'], writes=[PK[ps.index(psD[pr])]])
            for pr in range(2):
                S.op('dve', lambda e, pr=pr: e.reciprocal(out=T.rden[:, pr * NT:(pr + 1) * NT], in_=psD[pr][:, 0:NT]), reads=[PK[ps.index(psD[pr])]], writes=[('rden', pr)])
                S.op('dve', lambda e, pr=pr: e.tensor_tensor(out=T.rden[:, pr * NT:(pr + 1) * NT], in0=psO[pr][:, 0:NT], in1=T.rden[:, pr * NT:(pr + 1) * NT], op=ALU.mult),
                     reads=[PK[ps.index(psO[pr])], ('rden', pr)], writes=[('rden', pr)])
                S.op('pool', lambda e, pr=pr: e.tensor_tensor(out=cat[:, 6 + pr, 0:NT], in0=T.rden[:, pr * NT:(pr + 1) * NT], in1=smg_t[:, pr, 0:NT], op=ALU.mult),
                     reads=[('rden', pr), 'smg'], writes=[('catT', 6 + pr)])

        def in_proj(T, NT, w_sb, xT_t, u_dst, l):
            for mo_ in range(16):
                pt = ps[mo_ % 2]
                pk = PK[mo_ % 2]
                for kc in range(8):
                    S.op('pe', lambda e, kc=kc, mo_=mo_, pt=pt: e.matmul(pt[:, 0:NT], lhsT=w_sb[:, kc, mo_ * 128:(mo_ + 1) * 128], rhs=xT_t[:, kc, 0:NT],
                                                                         start=(kc == 0), stop=(kc == 7)),
                         reads=[('w_in_sb', kc), ('xT', kc)], writes=[pk])
                if mo_ < 6:
                    evac(mo_, u_dst[:, mo_, 0:NT], pt[:, 0:NT], [pk], [('uTb', mo_)])
                elif mo_ < 12:
                    S.op('act', lambda e, mo_=mo_, pt=pt: e.activation(out=T.sg[:, mo_ - 6, 0:NT], in_=pt[:, 0:NT], func=AF.Silu), reads=[pk], writes=[('sg', mo_ - 6)])
                elif mo_ < 14:
                    evac(mo_ + 1, T.mq[:, mo_ - 12, 0:NT], pt[:, 0:NT], [pk], ['mq'])
                else:
                    S.op('act', lambda e, mo_=mo_, pt=pt: e.activation(out=T.smg[:, mo_ - 14, 0:NT], in_=pt[:, 0:NT], func=AF.Silu), reads=[pk], writes=['smg'])

        def load_x_dma(T, x_src_fn, NT):
            nblk = (NT + 127) // 128
            for a in range(nblk):
                rows = min(128, NT - a * 128)
                S.dma('pool', T.xb[0:rows, a, :], x_src_fn(a, rows), writes=[('xb', a)])

        def load_x_transpose(T, NT):
            nblk = (NT + 127) // 128
            for kc in range(8):
                pt = ps[kc % 2]
                for a in range(nblk):
                    rows = min(128, NT - a * 128)
                    S.op('pe', lambda e: e.matmul(pt[:, a * 128:a * 128 + rows], lhsT=T.xb[0:rows, a, kc * 128:(kc + 1) * 128], rhs=identb[0:rows, 0:rows], start=True, stop=True),
                         reads=[('xb', a), 'identb'], writes=[PK[kc % 2]])
                evac(kc, T.xT[:, kc, 0:NT], pt[:, 0:NT], [PK[kc % 2]], [('xT', kc)])

        def _phase1(es1):
              stgA = sbt(es1, "stgA", [128, 1024])
              stgB = sbt(es1, "stgB", [128, 1024])
              stages = [(stgA, 'stgA'), (stgB, 'stgB')]
              w_in_sb = sbt(es1, "w_in_sb", [128, 8, 2048], BF16)
              w_glu_sb = sbt(es1, "w_glu_sb", [128, 6, 768], BF16)
              w_out_sb = sbt(es1, "w_out_sb", [128, 8, 1024], BF16)
              xb = sbt(es1, "xb", [128, 2, 1024], BF16)

              with ExitStack() as esm:
                  memb = sbt(esm, "memb", [128, 2, 1024], BF16)
                  memT = sbt(esm, "memT", [128, 8, 256], BF16)
                  wm_sb2 = sbt(esm, "wm_sb", [128, 2, 8, 512], BF16)
                  mkv_f = sbt(esm, "mkv_f", [128, 512])
                  for c in range(2):
                      S.dma('pool', memb[:, c, :], memp[c * 128:(c + 1) * 128, :], writes=[('memb', c)])
                  if KCUT <= 1:
                      S.barrier()
                      return
                  for kc in range(8):
                      pt = ps[kc % 2]
                      for c in range(2):
                          S.op('pe', lambda e, kc=kc, c=c, pt=pt: e.matmul(pt[:, c * 128:(c + 1) * 128], lhsT=memb[:, c, kc * 128:(kc + 1) * 128],
                                                                            rhs=identb[:], start=True, stop=True),
                               reads=[('memb', c), 'identb'], writes=[PK[kc % 2]])
                      evac(kc, memT[:, kc, :], pt[:, 0:256], [PK[kc % 2]], [('memT', kc)])
                  if KCUT <= 2:
                      S.barrier()
                      return
                  for l in range(2):
                      for kc in range(8):
                          S.dma('pool', wm_sb2[:, l, kc, :], w_mem[l, kc * 128:(kc + 1) * 128, :], writes=[('wm_sb', l, kc)])
                  load_weight(es1, w_in_sb, lambda kc: w_in[0, kc * 128:(kc + 1) * 128, :], 2048, stages, 'w_in_sb')
                  load_weight(es1, w_glu_sb, lambda kc: w_glu[kc * 128:(kc + 1) * 128, :], 768, stages, 'w_glu_sb')
                  load_weight(es1, w_out_sb, lambda kc: w_out[0, kc * 128:(kc + 1) * 128, :], 1024, stages, 'w_out_sb')
                  for a in range(2):
                      S.dma('pool', xb[:, a, :], x_p[a * 128:(a + 1) * 128, :], writes=[('xb', a)], bg=True)
                  for l in range(2):
                      wm_sb = wm_sb2[:, l, :, :]
                      if KCUT == 30:
                          S.barrier()
                          return
                      for c in range(2):
                          pt = ps[c]
                          for kc in range(8):
                              S.op('pe', lambda e, kc=kc, c=c, pt=pt: e.matmul(pt[:, :], lhsT=memT[:, kc, c * 128:(c + 1) * 128], rhs=wm_sb[:, kc, :],
                                                                                start=(kc == 0), stop=(kc == 7)),
                                   reads=[('memT', kc), ('wm_sb', l, kc)], writes=[PK[c]])
                          if KCUT == 31:
                              S.barrier()
                              return
                          S.op('dve', lambda e, pt=pt: e.tensor_copy(out=mkv_f[:], in_=pt[:, :]), reads=[PK[c]], writes=['mkv_f'])
                          S.op('act', lambda e, pt=pt, l=l, c=c: e.activation(out=Vm[:, l, c, :], in_=pt[:, 256:512], func=AF.Copy),
                               reads=[PK[c]], writes=[('Vm', l)])
                          S.dma('sp', mkv_p[l, c * 128:(c + 1) * 128, :], mkv_f[:], reads=['mkv_f'])
                      if KCUT <= 3:
                          S.barrier()
                          return
                      for pr in range(2):
                          pt = ps[2 + pr]
                          for kc in range(8):
                              S.op('pe', lambda e, kc=kc, pr=pr, pt=pt: e.matmul(pt[:, 0:256], lhsT=wm_sb[:, kc, pr * 128:(pr + 1) * 128], rhs=memT[:, kc, :],
                                                                                  start=(kc == 0), stop=(kc == 7)),
                                   reads=[('memT', kc), ('wm_sb', l, kc)], writes=[PK[2 + pr]])
                          evac(pr, KmT[:, l, pr, :], pt[:, 0:256], [PK[2 + pr]], [('KmT', l)])
                  S.barrier()
              if stage <= 1:
                  return

              S.dma('sp', lnG[:], ln_g[0].partition_broadcast(128), writes=['lnG'])
              S.dma('sp', lnB[:], ln_b[0].partition_broadcast(128), writes=['lnB'])

              COS = sbt(es1, "COS", [128, 48, 128], BF16)
              SIN = sbt(es1, "SIN", [128, 48, 128], BF16)
              Bm = sbt(es1, "Bm", [128, 6, 4, 2, 128], BF16)
              Cm = sbt(es1, "Cm", [128, 48, 2, 64], BF16)
              rdec = sbt(es1, "rdec", [128, 48])
              cosL = sbt(es1, "cosL", [128, 48])
              sinL = sbt(es1, "sinL", [128, 48])
              cos1 = sbt(es1, "cos1", [128, 48])
              sin1 = sbt(es1, "sin1", [128, 48])
              cos3 = sbt(es1, "cos3", [128, 48])
              sin3 = sbt(es1, "sin3", [128, 48])
              dvec = sbt(es1, "dvec", [128, 6])
              bglu = sbt(es1, "bglu", [128, 6])
              swf = sbt(es1, "swf", [128, 128])
              carry = sbt(es1, "carry", [128, 48])
              S.dma('sp', swf[:], c_swap[:, :], writes=['swf'])
              S.dma('sp', dvec[:], ssm_d.rearrange("(m r) -> r m", r=128), writes=['dvec'], allow_slow_non_contiguous=True)
              S.dma('sp', bglu[:], b_glu.rearrange("(m r) -> r m", r=128), writes=['bglu'], allow_slow_non_contiguous=True)

              with ExitStack() as esp:
                  L48 = sbt(esp, "L48", [48, 256])
                  logdt = sbt(esp, "logdt", [128, 48])
                  maskeo = sbt(esp, "maskeo", [128, 4])
                  jvec = sbt(esp, "jvec", [128, 128])
                  v = {n: sbt(esp, "v_" + n, [128, 48]) for n in
                       ['dt', 'lr', 'li', 'a', 'th', 'nr', 'ni', 'den', 'kre', 'kim', 'A1', 'A2', 'nA2', 't0', 't1', 'th64', 'c64', 's64', 'th3']}
                  sc_k = sbt(esp, "sc_k", [128, 1024])
                  sc_i = sbt(esp, "sc_i", [128, 1024], I32)
                  sc_p = sbt(esp, "sc_p", [128, 1024])
                  sc_q = sbt(esp, "sc_q", [128, 1024])
                  PH = sbt(esp, "PH", [128, 8, 128])
                  BreT = sbt(esp, "BreT", [128, 48, 16])
                  BimT = sbt(esp, "BimT", [128, 48, 16])
                  BX1 = sbt(esp, "BX1", [128, 48, 16])
                  BX2 = sbt(esp, "BX2", [128, 48, 16])
                  btmp = sbt(esp, "btmp", [128, 48, 16])
                  Crow = sbt(esp, "Crow", [128, 6, 64])
                  Cirow = sbt(esp, "Cirow", [128, 6, 64])
                  CC1 = sbt(esp, "CC1", [128, 6, 128])
                  CC2 = sbt(esp, "CC2", [128, 6, 128])

                  def D(fn, reads, writes):
                      S.op('dve', fn, reads=reads, writes=writes)

                  def sincos(ph, F, cos_out, sin_out, rk, wk):
                      k_, i_, p_, q_ = sc_k[:, 0:F], sc_i[:, 0:F], sc_p[:, 0:F], sc_q[:, 0:F]
                      D(lambda e: e.tensor_scalar(out=k_, in0=ph, scalar1=1.0 / TWO_PI, scalar2=None, op0=ALU.mult), rk, ['sc_k'])
                      D(lambda e: e.tensor_copy(out=i_, in_=k_), ['sc_k'], ['sc_i'])
                      D(lambda e: e.tensor_copy(out=k_, in_=i_), ['sc_i'], ['sc_k'])
                      D(lambda e: e.scalar_tensor_tensor(out=p_, in0=k_, scalar=-C1_2PI, in1=ph, op0=ALU.mult, op1=ALU.add), ['sc_k'] + rk, ['sc_p'])
                      D(lambda e: e.scalar_tensor_tensor(out=p_, in0=k_, scalar=-C2_2PI, in1=p_, op0=ALU.mult, op1=ALU.add), ['sc_k', 'sc_p'], ['sc_p'])
                      D(lambda e: e.tensor_scalar(out=p_, in0=p_, scalar1=-PI_LO, scalar2=PI_LO, op0=ALU.max, op1=ALU.min), ['sc_p'], ['sc_p'])
                      S.op('act', lambda e: e.activation(out=sin_out, in_=p_, func=AF.Sin), reads=['sc_p'], writes=wk)
                      S.op('act', lambda e: e.activation(out=q_, in_=p_, func=AF.Abs), reads=['sc_p'], writes=['sc_q'])
                      D(lambda e: e.tensor_scalar(out=q_, in0=q_, scalar1=-1.0, scalar2=math.pi / 2, op0=ALU.mult, op1=ALU.add), ['sc_q'], ['sc_q'])
                      S.op('act', lambda e: e.activation(out=cos_out, in_=q_, func=AF.Sin), reads=['sc_q'], writes=wk)

                  S.dma('sp', maskeo[:], c_maskeo[:, :], writes=['maskeo'])
                  S.dma('sp', jvec[:], c_jvec[:, :], writes=['jvec'])
                  for q4, src in enumerate([lam_re, lam_re, lam_im, lam_im]):
                      S.dma('sp', L48[:, q4 * 64:(q4 + 1) * 64], src[:, :], writes=[('L48', q4)])
                  S.dma('sp', logdt[:], log_dt.partition_broadcast(128), writes=['logdt'])
                  S.op('act', lambda e: e.activation(out=v['dt'][:], in_=logdt[:], func=AF.Exp), reads=['logdt'], writes=['v_dt'])
                  S.op('pe', lambda e: e.matmul(ps[7][:, 0:48], lhsT=L48[:, 0:128], rhs=identf[0:48, 0:48], start=True, stop=True),
                       reads=[('L48', 0), ('L48', 1), 'identf'], writes=['ps7'])
                  S.op('pe', lambda e: e.matmul(ps[7][:, 64:112], lhsT=L48[:, 128:256], rhs=identf[0:48, 0:48], start=True, stop=True),
                       reads=[('L48', 2), ('L48', 3), 'identf'], writes=['ps7'])
                  D(lambda e: e.tensor_scalar(out=v['lr'][:], in0=ps[7][:, 0:48], scalar1=-1e-4, scalar2=None, op0=ALU.min), ['ps7'], ['v_lr'])
                  D(lambda e: e.tensor_copy(out=v['li'][:], in_=ps[7][:, 64:112]), ['ps7'], ['v_li'])
                  D(lambda e: e.tensor_tensor(out=v['a'][:], in0=v['lr'][:], in1=v['dt'][:], op=ALU.mult), ['v_lr', 'v_dt'], ['v_a'])
                  D(lambda e: e.tensor_tensor(out=v['th'][:], in0=v['li'][:], in1=v['dt'][:], op=ALU.mult), ['v_li', 'v_dt'], ['v_th'])
                  S.op('act', lambda e: e.activation(out=rdec[:], in_=v['a'][:], func=AF.Exp), reads=['v_a'], writes=['rdec'])
                  sincos(v['th'][:], 48, cos1[:], sin1[:], ['v_th'], ['cs1'])
                  D(lambda e: e.tensor_scalar(out=v['th64'][:], in0=v['th'][:], scalar1=64.0, scalar2=None, op0=ALU.mult), ['v_th'], ['v_th64'])
                  sincos(v['th64'][:], 48, v['c64'][:], v['s64'][:], ['v_th64'], ['cs64'])
                  D(lambda e: e.tensor_scalar(out=v['th3'][:], in0=v['th'][:], scalar1=3.0, scalar2=None, op0=ALU.mult), ['v_th'], ['v_th3'])
                  sincos(v['th3'][:], 48, cos3[:], sin3[:], ['v_th3'], ['cs3'])
                  D(lambda e: e.tensor_tensor(out=v['t0'][:], in0=v['c64'][:], in1=v['c64'][:], op=ALU.mult), ['cs64'], ['v_t0'])
                  D(lambda e: e.tensor_scalar(out=cosL[:], in0=v['t0'][:], scalar1=2.0, scalar2=-1.0, op0=ALU.mult, op1=ALU.add), ['v_t0'], ['cosL'])
                  D(lambda e: e.tensor_tensor(out=v['t0'][:], in0=v['s64'][:], in1=v['c64'][:], op=ALU.mult), ['cs64', 'cosL'], ['v_t0'])
                  D(lambda e: e.tensor_scalar(out=sinL[:], in0=v['t0'][:], scalar1=2.0, scalar2=None, op0=ALU.mult), ['v_t0'], ['sinL'])
                  D(lambda e: e.tensor_tensor(out=v['nr'][:], in0=rdec[:], in1=cos1[:], op=ALU.mult), ['rdec', 'cs1'], ['v_nr'])
                  D(lambda e: e.tensor_scalar(out=v['nr'][:], in0=v['nr'][:], scalar1=-1.0, scalar2=None, op0=ALU.add), ['v_nr'], ['v_nr'])
                  D(lambda e: e.tensor_tensor(out=v['ni'][:], in0=rdec[:], in1=sin1[:], op=ALU.mult), ['rdec', 'cs1'], ['v_ni'])
                  D(lambda e: e.tensor_tensor(out=v['den'][:], in0=v['lr'][:], in1=v['lr'][:], op=ALU.mult), ['v_lr'], ['v_den'])
                  D(lambda e: e.tensor_tensor(out=v['t0'][:], in0=v['li'][:], in1=v['li'][:], op=ALU.mult), ['v_li', 'sinL'], ['v_t0'])
                  D(lambda e: e.tensor_tensor(out=v['den'][:], in0=v['den'][:], in1=v['t0'][:], op=ALU.add), ['v_den', 'v_t0'], ['v_den'])
                  D(lambda e: e.reciprocal(out=v['den'][:], in_=v['den'][:]), ['v_den'], ['v_den'])
                  D(lambda e: e.tensor_tensor(out=v['t0'][:], in0=v['nr'][:], in1=v['lr'][:], op=ALU.mult), ['v_nr', 'v_lr', 'v_den'], ['v_t0'])
                  D(lambda e: e.tensor_tensor(out=v['t1'][:], in0=v['ni'][:], in1=v['li'][:], op=ALU.mult), ['v_ni', 'v_li'], ['v_t1'])
                  D(lambda e: e.tensor_tensor(out=v['kre'][:], in0=v['t0'][:], in1=v['t1'][:], op=ALU.add), ['v_t0', 'v_t1'], ['v_kre'])
                  D(lambda e: e.tensor_tensor(out=v['kre'][:], in0=v['kre'][:], in1=v['den'][:], op=ALU.mult), ['v_kre', 'v_den'], ['v_kre'])
                  D(lambda e: e.tensor_tensor(out=v['t0'][:], in0=v['ni'][:], in1=v['lr'][:], op=ALU.mult), ['v_ni', 'v_lr', 'v_kre'], ['v_t0'])
                  D(lambda e: e.tensor_tensor(out=v['t1'][:], in0=v['nr'][:], in1=v['li'][:], op=ALU.mult), ['v_nr', 'v_li', 'v_kre'], ['v_t1'])
                  D(lambda e: e.tensor_tensor(out=v['kim'][:], in0=v['t0'][:], in1=v['t1'][:], op=ALU.subtract), ['v_t0', 'v_t1'], ['v_kim'])
                  D(lambda e: e.tensor_tensor(out=v['kim'][:], in0=v['kim'][:], in1=v['den'][:], op=ALU.mult), ['v_kim', 'v_den'], ['v_kim'])
                  D(lambda e: e.tensor_copy(out=v['A1'][0:64, :], in_=v['kre'][0:64, :]), ['v_kre'], ['v_A1'])
                  D(lambda e: e.tensor_copy(out=v['A1'][64:128, :], in_=v['kim'][64:128, :]), ['v_kim', 'v_A1'], ['v_A1'])
                  D(lambda e: e.tensor_scalar(out=v['A2'][0:64, :], in0=v['kim'][0:64, :], scalar1=-1.0, scalar2=None, op0=ALU.mult), ['v_kim'], ['v_A2'])
                  D(lambda e: e.tensor_copy(out=v['A2'][64:128, :], in_=v['kre'][64:128, :]), ['v_kre', 'v_A2'], ['v_A2'])
                  for hh in range(2):
                      S.dma('sp', BreT[hh * 64:(hh + 1) * 64, :, :], b_re.rearrange("g p c -> p g c"), writes=[('BreT', hh)])
                      S.dma('sp', BimT[hh * 64:(hh + 1) * 64, :, :], b_im.rearrange("g p c -> p g c"), writes=[('BimT', hh)])
                  A1b = v['A1'][:].unsqueeze(2).to_broadcast([128, 48, 16])
                  A2b = v['A2'][:].unsqueeze(2).to_broadcast([128, 48, 16])
                  rB = [('BreT', 0), ('BreT', 1), ('BimT', 0), ('BimT', 1), 'v_A1', 'v_A2']
                  D(lambda e: e.tensor_tensor(out=BX1[:], in0=BreT[:], in1=A1b, op=ALU.mult), rB, ['BX1'])
                  D(lambda e: e.tensor_tensor(out=btmp[:], in0=BimT[:], in1=A2b, op=ALU.mult), rB, ['btmp'])
                  D(lambda e: e.tensor_tensor(out=BX1[:], in0=BX1[:], in1=btmp[:], op=ALU.add), ['BX1', 'btmp'], ['BX1'])
                  D(lambda e: e.tensor_tensor(out=BX2[:], in0=BimT[:], in1=A1b, op=ALU.mult), rB, ['BX2'])
                  D(lambda e: e.tensor_tensor(out=btmp[:], in0=BreT[:], in1=A2b, op=ALU.mult), rB + ['BX1'], ['btmp'])
                  D(lambda e: e.tensor_tensor(out=BX2[:], in0=BX2[:], in1=btmp[:], op=ALU.subtract), ['BX2', 'btmp'], ['BX2'])
                  for m in range(6):
                      for xi, BX in enumerate([BX1, BX2]):
                          pt = ps[(2 * m + xi) % 2]
                          pk = PK[(2 * m + xi) % 2]
                          S.op('pe', lambda e, m=m, BX=BX, pt=pt: e.matmul(pt[:, 0:128], lhsT=BX[:, 8 * m:8 * m + 8, :], rhs=identf[:], start=True, stop=True),
                               reads=['BX1', 'BX2', 'identf'], writes=[pk])
                          for mem in range(4):
                              D(lambda e, m=m, xi=xi, mem=mem, pt=pt: e.tensor_scalar(out=Bm[:, m, mem, xi, :], in0=pt[:, 0:128], scalar1=maskeo[:, mem:mem + 1],
                                                                                     scalar2=None, op0=ALU.mult), [pk, 'maskeo'], ['Bm'])
                  S.dma('sp', Crow[:], c_re.rearrange("(m r) p -> r m p", r=128), writes=['Crow'])
                  S.dma('sp', Cirow[:], c_im.rearrange("(m r) p -> r m p", r=128), writes=['Cirow'])
                  D(lambda e: e.tensor_copy(out=CC1[:, :, 0:64], in_=Crow[:]), ['Crow'], ['CC1'])
                  D(lambda e: e.tensor_scalar(out=CC1[:, :, 64:128], in0=Cirow[:], scalar1=-1.0, scalar2=None, op0=ALU.mult), ['Cirow', 'CC1'], ['CC1'])
                  D(lambda e: e.tensor_scalar(out=CC2[:, :, 0:64], in0=Cirow[:], scalar1=-1.0, scalar2=None, op0=ALU.mult), ['Cirow'], ['CC2'])
                  D(lambda e: e.tensor_scalar(out=CC2[:, :, 64:128], in0=Crow[:], scalar1=-1.0, scalar2=None, op0=ALU.mult), ['Crow', 'CC2'], ['CC2'])
                  S.op('pool', lambda e: e.memset(Cm[:], 0.0), writes=['Cm'])
                  for m in range(6):
                      for xi, CC in enumerate([CC1, CC2]):
                          pt = ps[(2 * m + xi) % 2]
                          pk = PK[(2 * m + xi) % 2]
                          S.op('pe', lambda e, m=m, CC=CC, pt=pt: e.matmul(pt[:, 0:128], lhsT=CC[:, m, :], rhs=identf[:], start=True, stop=True),
                               reads=['CC1', 'CC2', 'identf'], writes=[pk])
                          for par in range(4):
                              dst = Cm[:, 8 * m:8 * m + 8, xi, :].rearrange("p (q r) c -> p q r c", r=4)[:, :, par, par * 16:(par + 1) * 16]
                              srcv = pt[:, 0:128].rearrange("p (q r c) -> p q r c", r=4, c=16)[:, :, par, :]
                              D(lambda e, dst=dst, srcv=srcv: e.tensor_copy(out=dst, in_=srcv), [pk, 'Cm'], ['Cm'])
                  for m in range(6):
                      thb = v['th'][:, 8 * m:8 * m + 8].unsqueeze(2).to_broadcast([128, 8, 128])
                      jb = jvec[:].unsqueeze(1).to_broadcast([128, 8, 128])
                      D(lambda e, thb=thb, jb=jb: e.tensor_tensor(out=PH[:], in0=thb, in1=jb, op=ALU.mult), ['v_th', 'jvec'], ['PH'])
                      sincos(PH[:].rearrange("p g j -> p (g j)"), 1024,
                             COS[:, 8 * m:8 * m + 8, :].rearrange("p g j -> p (g j)"),
                             SIN[:, 8 * m:8 * m + 8, :].rearrange("p g j -> p (g j)"), ['PH'], [('TAB', m)])
                  S.op('dve', lambda e: e.memset(carry[:], 0.0), writes=['carry'])
                  S.barrier()
              if stage <= 2:
                  return

              NTM = 256
              xT = sbt(es1, "xT", [128, 8, NTM], BF16)
              uTb = sbt(es1, "uTb", [128, 6, NTM], BF16)
              sg = sbt(es1, "sg", [128, 6, NTM], BF16)
              mq = sbt(es1, "mq", [128, 2, NTM], BF16)
              smg = sbt(es1, "smg", [128, 2, NTM], BF16)
              ygb = sbt(es1, "ygb", [128, 6, NTM], BF16)
              t1b = [sbt(es1, "t1b%d" % i, [128, 2 * NTM]) for i in range(2)]
              t2b = [sbt(es1, "t2b%d" % i, [128, 2 * NTM]) for i in range(2)]
              mbufs = [sbt(es1, "mbufA", [128, 8, NTM]), sbt(es1, "mbufB", [128, 8, NTM])]
              d1b = [sbt(es1, "d1b%d" % i, [128, 2 * NTM], BF16) for i in range(2)]
              d2b = [sbt(es1, "d2b%d" % i, [128, 2 * NTM], BF16) for i in range(2)]
              ypre = sbt(es1, "ypre", [128, NTM])
              sig = [sbt(es1, "sig%d" % i, [128, NTM]) for i in range(2)]
              catT = sbt(es1, "catT", [128, 8, NTM], BF16)
              PT = sbt(es1, "PTm", [128, 2 * NTM], BF16)
              rden = sbt(es1, "rden", [128, 2 * NTM])
              mo = rden
              xres = [stgA, stgB]
              zt = [sbt(es1, "zt%d" % i, [128, 1024]) for i in range(2)]
              zn = zt
              stats = [sbt(es1, "stats%d" % i, [128, 2, 6]) for i in range(2)]
              mv = [sbt(es1, "mv%d" % i, [128, 2]) for i in range(2)]
              rstd = [sbt(es1, "rstd%d" % i, [128, 1]) for i in range(2)]
              nmr = [sbt(es1, "nmr%d" % i, [128, 1]) for i in range(2)]
              x1b = sbt(es1, "x1b", [128, 1024], BF16)
              x1T = sbt(es1, "x1T", [128, 8, 128], BF16)
              cl8 = sbt(es1, "cl8", [128, 8])
              ct8 = sbt(es1, "ct8", [128, 8])
              hl = sbt(es1, "hl", [128, 48])
              hlT = sbt(es1, "hlT", [48, 128])

              epsb = sbt(es1, "epsb", [128, 1])
              S.op('dve', lambda e: e.memset(epsb[:], LN_EPS), writes=['epsb'])
              PTs = sbt(es1, "PTs", [128, 32], BF16)
              mA = mbufs[0][:].rearrange("p g n -> p (g n)")
              mB = mbufs[1][:].rearrange("p g n -> p (g n)")
              h0rows = mA[:, 0:768].rearrange("p (c x) -> p c x", c=6)
              h0T = mA[:, 768:1536]
              t1s = mA[:, 1536:2048]
              ginit = mB[:, 0:768]
              g3all = mB[:, 768:1536]
              t2s = mB[:, 1536:2048]
              tm8 = sbt(es1, "tm8", [128, 128])
              d1s = sbt(es1, "d1s", [128, 512], BF16)
              d2s = sbt(es1, "d2s", [128, 512], BF16)
              T1 = SimpleNamespace(xres=xres, zt=zt, zn=zn, stats=stats, mv=mv, rstd=rstd, nmr=nmr, epsb=epsb, x1b=x1b, x1T=x1T, PT=PT, rden=rden, mo=mo,
                                   sg=sg, mq=mq, smg=smg, stages=stages, xb=xb, xT=xT, PTs=PTs,
                                   kvb=[(xb[:, i, :].rearrange("p (c x) -> p c x", c=2), [('xb', i)]) for i in range(2)],
                                   kts=[(xT[:, 2 * i:2 * i + 2, :], [('xT', 2 * i), ('xT', 2 * i + 1)]) for i in range(2)])

              tiles = [(i * NTM, NTM, False) for i in range(SEQ // NTM)]
              if stage >= 5:
                  tiles.append((SEQ, NS, True))

              def xsrc_of(ti_):
                  tk0, _, iss = tiles[ti_]
                  if iss:
                      return lambda a, rows: x_s[0:rows, :]
                  return lambda a, rows: x_p[tk0 + a * 128:tk0 + a * 128 + rows, :]

              for ti, (tok0, NT, is_s) in enumerate(tiles):
                  issue_bulk(7)
                  if is_s:
                      S.barrier()
                  load_x_transpose(T1, NT)
                  if ti + 1 < len(tiles):
                      load_x_dma(T1, xsrc_of(ti + 1), tiles[ti + 1][1])
                  in_proj(T1, NT, w_in_sb, xT, uTb, 0)
                  nchunk = NT // 128
                  if not is_s:
                      def phaseA(m, kk):
                          q = kk // 2
                          bi = kk % 2
                          rows = slice(64 * q, 64 * q + 64)
                          pX1, pX2 = ps[2 + bi], ps[4 + bi]
                          mb = mbufs[m % 2]
                          for j, gl in enumerate((2 * kk, 2 * kk + 1)):
                              mem = gl % 4
                              S.op('pe', lambda e: e.matmul(pX1[:, j * NT:(j + 1) * NT], lhsT=Bm[rows, m, mem, 0, :], rhs=uTb[rows, m, 0:NT], start=True, stop=True),
                                   reads=['Bm', ('uTb', m)], writes=[PK[2 + bi]])
                              S.op('pe', lambda e: e.matmul(pX2[:, j * NT:(j + 1) * NT], lhsT=Bm[rows, m, mem, 1, :], rhs=uTb[rows, m, 0:NT], start=True, stop=True),
                                   reads=['Bm', ('uTb', m)], writes=[PK[4 + bi]])
                          g0 = 8 * m + 2 * kk
                          cosb = COS[:, g0:g0 + 2, :].unsqueeze(2).to_broadcast([128, 2, nchunk, 128])
                          sinb = SIN[:, g0:g0 + 2, :].unsqueeze(2).to_broadcast([128, 2, nchunk, 128])
                          v = lambda ap: ap.rearrange("p (g k j) -> p g k j", g=2, j=128)
                          S.op('dve', lambda e: e.tensor_tensor(out=v(t1b[bi][:, 0:2 * NT]), in0=v(pX1[:, 0:2 * NT]), in1=cosb, op=ALU.mult),
                               reads=[PK[2 + bi], ('TAB', m)], writes=['t1b%d' % bi])
                          S.op('dve', lambda e: e.tensor_tensor(out=v(t2b[bi][:, 0:2 * NT]), in0=v(pX2[:, 0:2 * NT]), in1=sinb, op=ALU.mult),
                               reads=[PK[4 + bi], ('TAB', m)], writes=['t2b%d' % bi])
                          S.op('pool', lambda e: e.tensor_tensor(out=mb[:, 2 * kk:2 * kk + 2, 0:NT], in0=t1b[bi][:, 0:2 * NT].rearrange("p (g n) -> p g n", g=2),
                                                                                     in1=t2b[bi][:, 0:2 * NT].rearrange("p (g n) -> p g n", g=2), op=ALU.add),
                               reads=['t1b%d' % bi, 't2b%d' % bi], writes=[('mbuf', m % 2, 2 * kk), ('mbuf', m % 2, 2 * kk + 1)])

                      def phaseB_scan(m, k):
                          mb = mbufs[m % 2]
                          for gl in range(8):
                              g = 8 * m + gl
                              S.op('dve', lambda e, gl=gl, g=g: e.tensor_tensor_scan(out=mb[:, gl, k * 128:(k + 1) * 128],
                                                                                  data0=rdec[:, g:g + 1].to_broadcast([128, 128]),
                                                                                  data1=mb[:, gl, k * 128:(k + 1) * 128],
                                                                                  initial=carry[:, g:g + 1], op0=ALU.mult, op1=ALU.add),
                                   reads=[('mbuf', m % 2, gl), 'rdec', ('carry', m)], writes=[('mbuf', m % 2, gl)])
                          S.op('dve', lambda e: e.tensor_copy(out=cl8[:], in_=mb[:, :, k * 128 + 127]), reads=[('mbuf', m % 2, gl) for gl in range(8)], writes=['cl8'])
                          S.op('pe', lambda e: e.matmul(ps[7][:, 0:8], lhsT=swf[:], rhs=cl8[:], start=True, stop=True), reads=['swf', 'cl8'], writes=['ps7'])

                      def phaseB_carry(m, k):
                          S.op('dve', lambda e: e.tensor_tensor(out=ct8[:], in0=ps[7][:, 0:8], in1=sinL[:, 8 * m:8 * m + 8], op=ALU.mult), reads=['ps7', 'sinL'], writes=['ct8'])
                          S.op('dve', lambda e: e.tensor_tensor(out=cl8[:], in0=cl8[:], in1=cosL[:, 8 * m:8 * m + 8], op=ALU.mult), reads=['cl8', 'cosL'], writes=['cl8'])
                          S.op('dve', lambda e: e.tensor_tensor(out=carry[:, 8 * m:8 * m + 8], in0=cl8[:], in1=ct8[:], op=ALU.add), reads=['cl8', 'ct8'], writes=[('carry', m)])

                      def phaseC(m):
                          mb = mbufs[m % 2]
                          for kk in range(4):
                              q = kk // 2
                              bi = kk % 2
                              g0 = 8 * m + 2 * kk
                              cosb = COS[:, g0:g0 + 2, :].unsqueeze(2).to_broadcast([128, 2, nchunk, 128])
                              sinb = SIN[:, g0:g0 + 2, :].unsqueeze(2).to_broadcast([128, 2, nchunk, 128])
                              v = lambda ap: ap.rearrange("p (g k j) -> p g k j", g=2, j=128)
                              mv_ = mb[:, 2 * kk:2 * kk + 2, 0:NT].rearrange("p g (k j) -> p g k j", j=128)
                              mk = [('mbuf', m % 2, 2 * kk), ('mbuf', m % 2, 2 * kk + 1)]
                              S.op('pool', lambda e: e.tensor_tensor(out=v(d1b[bi][:, 0:2 * NT]), in0=mv_, in1=cosb, op=ALU.mult), reads=mk + [('TAB', m)], writes=['d1b%d' % bi])
                              S.op('pool', lambda e: e.tensor_tensor(out=v(d2b[bi][:, 0:2 * NT]), in0=mv_, in1=sinb, op=ALU.mult), reads=mk + [('TAB', m)], writes=['d2b%d' % bi])
                              for j, gl in enumerate((2 * kk, 2 * kk + 1)):
                                  g = 8 * m + gl
                                  S.op('pe', lambda e: e.matmul(ps[6][64 * q:64 * q + 64, 0:NT], lhsT=Cm[:, g, 0, :], rhs=d1b[bi][:, j * NT:(j + 1) * NT], start=(gl % 4 == 0), stop=False),
                                       reads=['Cm', 'd1b%d' % bi], writes=['ps6'])
                                  S.op('pe', lambda e: e.matmul(ps[6][64 * q:64 * q + 64, 0:NT], lhsT=Cm[:, g, 1, :], rhs=d2b[bi][:, j * NT:(j + 1) * NT], start=False, stop=(gl % 4 == 3)),
                                       reads=['Cm', 'd2b%d' % bi], writes=['ps6'])

                      def phaseD(m):
                          S.op('dve', lambda e: e.scalar_tensor_tensor(out=ypre[:, 0:NT], in0=uTb[:, m, 0:NT], scalar=dvec[:, m:m + 1], in1=ps[6][:, 0:NT], op0=ALU.mult, op1=ALU.add),
                               reads=[('uTb', m), 'dvec', 'ps6'], writes=['ypre'])
                          S.op('act', lambda e: e.activation(out=ygb[:, m, 0:NT], in_=ypre[:, 0:NT], func=AF.Gelu_apprx_tanh), reads=['ypre'], writes=[('ygb', m)])

                      for kk in range(4):
                          phaseA(0, kk)
                      per = 4 // nchunk
                      for m in range(6):
                          for k in range(nchunk):
                              phaseB_scan(m, k)
                              if m + 1 < 6:
                                  for kk in range(k * per, (k + 1) * per):
                                      phaseA(m + 1, kk)
                              phaseB_carry(m, k)
                              if k == nchunk - 1 and m >= 1:
                                  phaseD(m - 1)
                          phaseC(m)
                      phaseD(5)
                  elif KCUT != 51 and KCUT != 54:
                      S.dma('sp', h0rows[:, :, 0:64], st_re.rearrange("(c p) e -> p c e", p=128), writes=['h0rows'])
                      S.dma('sp', h0rows[:, :, 64:128], st_im.rearrange("(c p) e -> p c e", p=128), writes=['h0rows'])
                      for c6 in range(6):
                          pt = ps[c6 // 4]
                          S.op('pe', lambda e, c6=c6, pt=pt: e.matmul(pt[:, (c6 % 4) * 128:(c6 % 4 + 1) * 128], lhsT=h0rows[:, c6, :], rhs=identf[:], start=True, stop=True),
                               reads=['h0rows', 'identf'], writes=[PK[c6 // 4]])
                      S.op('dve', lambda e: e.tensor_copy(out=h0T[:, 0:512], in_=ps[0][:, :]), reads=['ps0'], writes=['h0T'])
                      S.op('dve', lambda e: e.tensor_copy(out=h0T[:, 512:768], in_=ps[1][:, 0:256]), reads=['ps1', 'h0T'], writes=['h0T'])

                      def rotate(dst, src, cs, sn, srck, dstk):
                          csb = cs[:].unsqueeze(1).to_broadcast([128, NB, 48])
                          snb = sn[:].unsqueeze(1).to_broadcast([128, NB, 48])
                          for hh in range(2):
                              S.op('pe', lambda e, hh=hh: e.matmul(ps[2 + hh][:, 0:384], lhsT=swf[:], rhs=src[:, hh * 384:(hh + 1) * 384], start=True, stop=True),
                                   reads=['swf', srck], writes=[PK[2 + hh]])
                              S.op('dve', lambda e, hh=hh: e.tensor_tensor(out=dst[:, hh * 384:(hh + 1) * 384].rearrange("p (b g) -> p b g", g=48),
                                                                         in0=ps[2 + hh][:, 0:384].rearrange("p (b g) -> p b g", g=48),
                                                                         in1=snb[:, 8 * hh:8 * hh + 8, :], op=ALU.mult), reads=[PK[2 + hh], 'cs1', 'cs3'], writes=[dstk])
                          S.op('pool', lambda e: e.tensor_tensor(out=src[:].rearrange("p (b g) -> p b g", g=48), in0=src[:].rearrange("p (b g) -> p b g", g=48), in1=csb, op=ALU.mult),
                               reads=[srck, 'cs1', 'cs3'], writes=[srck])
                          S.op('dve', lambda e: e.tensor_tensor(out=dst[:], in0=dst[:], in1=src[:], op=ALU.add), reads=[srck, dstk], writes=[dstk])

                      rotate(ginit, h0T, cos1, sin1, 'h0T', 'ginit')
                      gin3 = ginit[:].rearrange("p (b g) -> p b g", g=48)
                      g3v = g3all[:].rearrange("p (b g) -> p b g", g=48)
                      for m in range(6):
                          for gl in range(8):
                              q, mem = gl // 4, gl % 4
                              rows = slice(64 * q, 64 * q + 64)
                              S.op('pe', lambda e, m=m, mem=mem, rows=rows, q=q: e.matmul(ps[2 + q][:, mem * 64:(mem + 1) * 64], lhsT=Bm[rows, m, mem, 0, :], rhs=uTb[rows, m, 0:NS], start=True, stop=True),
                                   reads=['Bm', ('uTb', m)], writes=[PK[2 + q]])
                              S.op('pe', lambda e, m=m, mem=mem, rows=rows, q=q: e.matmul(ps[4 + q][:, mem * 64:(mem + 1) * 64], lhsT=Bm[rows, m, mem, 1, :], rhs=uTb[rows, m, 0:NS], start=True, stop=True),
                                   reads=['Bm', ('uTb', m)], writes=[PK[4 + q]])
                          cos4 = COS[:, 8 * m:8 * m + 8, 0:4].unsqueeze(2).to_broadcast([128, 8, NB, 4])
                          sin4 = SIN[:, 8 * m:8 * m + 8, 0:4].unsqueeze(2).to_broadcast([128, 8, NB, 4])
                          v4 = lambda ap: ap.rearrange("p (g b t) -> p g b t", g=8, t=4)
                          vh = lambda ap: ap.rearrange("p (g b t) -> p g b t", g=4, t=4)
                          for q in range(2):
                              S.op('dve', lambda e, cos4=cos4, q=q: e.tensor_tensor(out=vh(t1s[:, q * 256:(q + 1) * 256]), in0=vh(ps[2 + q][:, 0:256]), in1=cos4[:, 4 * q:4 * q + 4], op=ALU.mult),
                                   reads=[PK[2 + q], ('TAB', m)], writes=['t1s'])
                              S.op('dve', lambda e, sin4=sin4, q=q: e.tensor_tensor(out=vh(t2s[:, q * 256:(q + 1) * 256]), in0=vh(ps[4 + q][:, 0:256]), in1=sin4[:, 4 * q:4 * q + 4], op=ALU.mult),
                                   reads=[PK[4 + q], ('TAB', m)], writes=['t2s'])
                          S.op('pool', lambda e: e.tensor_tensor(out=t1s[:], in0=t1s[:], in1=t2s[:], op=ALU.add), reads=['t1s', 't2s'], writes=['t1s'])
                          rb = rdec[:, 8 * m:8 * m + 8].unsqueeze(2).to_broadcast([128, 8, NB])
                          tm3 = tm8[:].rearrange("p (g b) -> p g b", g=8)
                          for t in range(4):
                              prev = gin3[:, :, 8 * m:8 * m + 8].rearrange("p b g -> p g b") if t == 0 else v4(t1s[:])[:, :, :, t - 1]
                              S.op('dve', lambda e, prev=prev, rb=rb: e.tensor_tensor(out=tm3, in0=prev, in1=rb, op=ALU.mult), reads=['t1s', 'ginit', 'rdec'], writes=['tm8'])
                              S.op('dve', lambda e, t=t: e.tensor_tensor(out=v4(t1s[:])[:, :, :, t], in0=v4(t1s[:])[:, :, :, t], in1=tm3, op=ALU.add), reads=['t1s', 'tm8'], writes=['t1s'])
                          S.op('dve', lambda e, m=m: e.tensor_copy(out=g3v[:, :, 8 * m:8 * m + 8].rearrange("p b g -> p g b"), in_=v4(t1s[:])[:, :, :, 3]), reads=['t1s'], writes=['g3all'])
                          S.op('pool', lambda e, cos4=cos4: e.tensor_tensor(out=v4(d1s[:]), in0=v4(t1s[:]), in1=cos4, op=ALU.mult), reads=['t1s', ('TAB', m)], writes=['d1s'])
                          S.op('pool', lambda e, sin4=sin4: e.tensor_tensor(out=v4(d2s[:]), in0=v4(t1s[:]), in1=sin4, op=ALU.mult), reads=['t1s', ('TAB', m)], writes=['d2s'])
                          for gl in range(8):
                              g = 8 * m + gl
                              q = gl // 4
                              S.op('pe', lambda e, g=g, q=q, gl=gl: e.matmul(ps[6][64 * q:64 * q + 64, 0:NS], lhsT=Cm[:, g, 0, :], rhs=d1s[:, gl * 64:(gl + 1) * 64], start=(gl % 4 == 0), stop=False),
                                   reads=['Cm', 'd1s'], writes=['ps6'])
                              S.op('pe', lambda e, g=g, q=q, gl=gl: e.matmul(ps[6][64 * q:64 * q + 64, 0:NS], lhsT=Cm[:, g, 1, :], rhs=d2s[:, gl * 64:(gl + 1) * 64], start=False, stop=(gl % 4 == 3)),
                                   reads=['Cm', 'd2s'], writes=['ps6'])
                          S.op('dve', lambda e, m=m: e.scalar_tensor_tensor(out=ypre[:, 0:NS], in0=uTb[:, m, 0:NS], scalar=dvec[:, m:m + 1], in1=ps[6][:, 0:NS], op0=ALU.mult, op1=ALU.add),
                               reads=[('uTb', m), 'dvec', 'ps6'], writes=['ypre'])
                          S.op('act', lambda e, m=m: e.activation(out=ygb[:, m, 0:NS], in_=ypre[:, 0:NS], func=AF.Gelu_apprx_tanh), reads=['ypre'], writes=[('ygb', m)])
                      rotate(h0T, g3all, cos3, sin3, 'g3all', 'h0T')
                      for c6 in range(6):
                          pt = ps[c6 // 4]
                          S.op('pe', lambda e, c6=c6, pt=pt: e.matmul(pt[:, (c6 % 4) * 128:(c6 % 4 + 1) * 128], lhsT=h0T[:, c6 * 128:(c6 + 1) * 128], rhs=identf[:], start=True, stop=True),
                               reads=['h0T', 'identf'], writes=[PK[c6 // 4]])
                      S.op('dve', lambda e: e.tensor_copy(out=h0rows[:, 0:4, :], in_=ps[0][:, :].rearrange("p (c x) -> p c x", c=4)), reads=['ps0'], writes=['h0rows'])
                      S.op('dve', lambda e: e.tensor_copy(out=h0rows[:, 4:6, :], in_=ps[1][:, 0:256].rearrange("p (c x) -> p c x", c=2)), reads=['ps1', 'h0rows'], writes=['h0rows'])
                      S.dma('sp', sre_s.rearrange("(c p) e -> p c e", p=128), h0rows[:, :, 0:64], reads=['h0rows'])
                      S.dma('sp', sim_s.rearrange("(c p) e -> p c e", p=128), h0rows[:, :, 64:128], reads=['h0rows'])
                  for m2 in range(6):
                      pt = ps[m2 % 2]
                      pk = PK[m2 % 2]
                      for m in range(6):
                          S.op('pe', lambda e, m=m, m2=m2, pt=pt: e.matmul(pt[:, 0:NT], lhsT=w_glu_sb[:, m, m2 * 128:(m2 + 1) * 128], rhs=ygb[:, m, 0:NT], start=(m == 0), stop=(m == 5)),
                               reads=[('w_glu_sb', m), ('ygb', m)], writes=[pk])
                      sg_i = sig[m2 % 2]
                      sgk = 'sig%d' % (m2 % 2)
                      S.op('act', lambda e, m2=m2, pt=pt, sg_i=sg_i: e.activation(out=sg_i[:, 0:NT], in_=pt[:, 0:NT], func=AF.Sigmoid, bias=bglu[:, m2:m2 + 1], scale=1.0),
                           reads=[pk, 'bglu'], writes=[sgk])
                      S.op('dve', lambda e, m2=m2, sg_i=sg_i: e.tensor_tensor(out=sg_i[:, 0:NT], in0=sg_i[:, 0:NT], in1=ygb[:, m2, 0:NT], op=ALU.mult), reads=[sgk, ('ygb', m2)], writes=[sgk])
                      S.op('pool', lambda e, m2=m2, sg_i=sg_i: e.tensor_tensor(out=catT[:, m2, 0:NT], in0=sg_i[:, 0:NT], in1=sg[:, m2, 0:NT], op=ALU.mult),
                           reads=[sgk, ('sg', m2)], writes=[('catT', m2)])
                  if is_s and KCUT != 52 and KCUT != 54:
                      mem_attention_sample(T1, 0, smg, mq, catT, ps[2], ps[3], [ps[4], ps[6]], [ps[5], ps[7]])
                  if not is_s:
                      mem_attention(T1, 0, NT, mq, smg, catT, KmT[:, 0, :, :], Vm[:, 0, :, :], [ps[2], ps[3]], [ps[4], ps[6]], [ps[5], ps[7]])
                  nblk = (NT + 127) // 128
                  blks = []
                  for a in range(nblk):
                      rows = min(128, NT - a * 128)
                      t0 = tok0 + a * 128
                      xsrc = x_s[0:rows, :] if is_s else x_p[t0:t0 + rows, :]
                      blks.append((a, rows, xsrc, t0, x1scr[t0:t0 + rows, :]))
                  ln_blocks(T1, blks, catT, w_out_sb, True)

              S.op('pe', lambda e: e.matmul(ps[7][:, 0:48], lhsT=swf[:], rhs=carry[:], start=True, stop=True), reads=['swf'] + [('carry', m) for m in range(6)], writes=['ps7'])
              S.op('dve', lambda e: e.tensor_tensor(out=hl[:], in0=ps[7][:, 0:48], in1=sin1[:], op=ALU.mult), reads=['ps7', 'cs1'], writes=['hl'])
              S.op('dve', lambda e: e.tensor_tensor(out=carry[:], in0=carry[:], in1=cos1[:], op=ALU.mult), reads=[('carry', m) for m in range(6)] + ['cs1'], writes=[('carry', m) for m in range(6)])
              S.op('dve', lambda e: e.tensor_tensor(out=hl[:], in0=carry[:], in1=hl[:], op=ALU.subtract), reads=[('carry', m) for m in range(6)] + ['hl'], writes=['hl'])
              S.op('pe', lambda e: e.matmul(ps[7][0:48, 128:256], lhsT=hl[:], rhs=identf[:], start=True, stop=True), reads=['hl', 'identf'], writes=['ps7'])
              S.op('dve', lambda e: e.tensor_copy(out=hlT[:], in_=ps[7][0:48, 128:256]), reads=['ps7'], writes=['hlT'])
              S.dma('sp', sre_p[:, :], hlT[:, 0:64], reads=['hlT'])
              S.dma('sp', sim_p[:, :], hlT[:, 64:128], reads=['hlT'])
              S.barrier()

        with ExitStack() as es1:
            _phase1(es1)

        def _phase2(es2):
            KT = sbt(es2, "KT", [128, 6, NTOK], BF16)
            Vg = sbt(es2, "Vg", [128, 3, 16, 256], BF16)
            ntok_eff = NTOK if stage >= 5 else SEQ
            Vnew = sbt(es2, "Vnew", [64, 3, 256], BF16)
            with ExitStack() as esa:
                x1Tf = sbt(esa, "x1Tf", [128, 8, NTOK], BF16)
                w_kvt = sbt(esa, "w_kvt", [128, 8, 3, 512], BF16)
                kvo = [sbt(esa, "kvo%d" % i, [128, 512]) for i in range(2)]
                for kc in range(8):
                    S.dma('sp', x1Tf[:, kc, 0:ntok_eff], x1Tscr[kc, :, 0:ntok_eff], writes=[('x1Tf', kc)])
                for kc in range(8):
                    S.dma('pool', w_kvt[:, kc, :, 0:256], w_kv[kc * 128:(kc + 1) * 128, 0:768].rearrange("p (g c) -> p g c", g=3), writes=[('w_kvt', kc)])
                    S.dma('pool', w_kvt[:, kc, :, 256:512], w_kv[kc * 128:(kc + 1) * 128, 768:1536].rearrange("p (g c) -> p g c", g=3), writes=[('w_kvt', kc)])
                x1k = [('x1Tf', kc) for kc in range(8)]
                wk = [('w_kvt', kc) for kc in range(8)]
                cnt = 0
                for t0 in range(0, ntok_eff, 512):
                    n = min(512, ntok_eff - t0)
                    for m in range(6):
                        g, pr = m // 2, m % 2
                        pt = ps[cnt % 2]
                        for kc in range(8):
                            S.op('pe', lambda e, kc=kc, g=g, pr=pr, pt=pt, t0=t0, n=n: e.matmul(pt[:, 0:n], lhsT=w_kvt[:, kc, g, pr * 128:(pr + 1) * 128], rhs=x1Tf[:, kc, t0:t0 + n],
                                                                                              start=(kc == 0), stop=(kc == 7)),
                                 reads=[x1k[kc], wk[kc]], writes=[PK[cnt % 2]])
                        evac(cnt, KT[:, m, t0:t0 + n], pt[:, 0:n], [PK[cnt % 2]], [('KT', m)])
                        cnt += 1
                cnt = 0
                for g in range(3):
                    for bi in range(16):
                        if g == 0:
                            tsel = lambda kc, bi=bi: x1Tf[:, kc, 128 * bi:128 * bi + 128]
                            needK = (bi == 15)
                            odst = dkv_p[0][0:128, :]
                        elif g == 1:
                            jb, r = bi // 4, bi % 4
                            tsel = lambda kc, jb=jb, r=r: x1Tf[:, kc, 512 * jb:512 * jb + 512].rearrange("p (u r) -> p r u", r=4)[:, r, :]
                            needK = (jb == 3)
                            odst = dkv_p[1].rearrange("(u r) c -> r u c", r=4)[r]
                        else:
                            r = bi
                            tsel = lambda kc, r=r: x1Tf[:, kc, 0:2048].rearrange("p (u r) -> p r u", r=16)[:, r, :]
                            needK = True
                            odst = dkv_p[2].rearrange("(u r) c -> r u c", r=16)[r]
                        c0 = 0 if needK else 256
                        pt = ps[2 + cnt % 2]
                        pk = PK[2 + cnt % 2]
                        for kc in range(8):
                            S.op('pe', lambda e, kc=kc, g=g, pt=pt, tsel=tsel, c0=c0: e.matmul(pt[:, c0:512], lhsT=tsel(kc), rhs=w_kvt[:, kc, g, c0:512], start=(kc == 0), stop=(kc == 7)),
                                 reads=[x1k[kc], wk[kc]], writes=[pk])
                        S.op('act', lambda e, g=g, bi=bi, pt=pt: e.activation(out=Vg[:, g, bi, :], in_=pt[:, 256:512], func=AF.Copy), reads=[pk], writes=[('Vg', g)])
                        if needK:
                            ko = kvo[cnt % 2]
                            kk = 'kvo%d' % (cnt % 2)
                            S.op('dve', lambda e, ko=ko, pt=pt: e.tensor_copy(out=ko[:], in_=pt[:, :]), reads=[pk], writes=[kk])
                            S.dma('sp', odst, ko[:], reads=[kk])
                        cnt += 1
                if stage >= 5:
                    kvs_f = sbt(esa, "kvs_f", [64, 3, 512])
                    for g in range(3):
                        pt = ps[g % 2]
                        for kc in range(8):
                            S.op('pe', lambda e, kc=kc, g=g, pt=pt: e.matmul(pt[0:NS, :], lhsT=x1Tf[:, kc, SEQ:NTOK], rhs=w_kvt[:, kc, g, :], start=(kc == 0), stop=(kc == 7)),
                                 reads=[x1k[kc], wk[kc]], writes=[PK[g % 2]])
                        S.op('dve', lambda e, g=g, pt=pt: e.tensor_copy(out=kvs_f[:, g, :], in_=pt[0:NS, :]), reads=[PK[g % 2]], writes=[('kvs_f', g)])
                        S.op('act', lambda e, g=g, pt=pt: e.activation(out=Vnew[:, g, :], in_=pt[0:NS, 256:512], func=AF.Copy), reads=[PK[g % 2]], writes=['Vnew'])
                        for b in range(NBC):
                            S.dma('sp', dkv_s[g][b, wins[g] - 4:wins[g], :], kvs_f[4 * b:4 * b + 4, g, :], reads=[('kvs_f', g)])
                S.barrier()
            if stage <= 3:
                return
            NT2 = 512
            stgA = sbt(es2, "stgA2", [128, 1024])
            stgB = sbt(es2, "stgB2", [128, 1024])
            stages = [(stgA, 'stgA'), (stgB, 'stgB')]
            w_in_sb = sbt(es2, "w_in_sb2", [128, 8, 2048], BF16)
            w_out_sb = sbt(es2, "w_out_sb2", [128, 8, 1024], BF16)
            load_weight(es2, w_in_sb, lambda kc: w_in[1, kc * 128:(kc + 1) * 128, :], 2048, stages, 'w_in_sb')
            load_weight(es2, w_out_sb, lambda kc: w_out[1, kc * 128:(kc + 1) * 128, :], 1024, stages, 'w_out_sb')
            S.dma('sp', lnG[:], ln_g[1].partition_broadcast(128), writes=['lnG'])
            S.dma('sp', lnB[:], ln_b[1].partition_broadcast(128), writes=['lnB'])
            x1Tt = sbt(es2, "x1Tt", [128, 8, NT2], BF16)
            qT = sbt(es2, "qT", [128, 6, NT2], BF16)
            sg = sbt(es2, "sg2", [128, 6, NT2], BF16)
            mq = sbt(es2, "mq2", [128, 2, NT2], BF16)
            smg = sbt(es2, "smg2", [128, 2, NT2], BF16)
            catT = sbt(es2, "catT2", [128, 8, NT2], BF16)
            PT = sbt(es2, "PTm2", [128, 2 * NT2], BF16)
            rden = sbt(es2, "rden2", [128, 2 * NT2])
            zt = [sbt(es2, "zt2_%d" % i, [128, 1024]) for i in range(2)]
            stats = [sbt(es2, "stats2_%d" % i, [128, 2, 6]) for i in range(2)]
            mv = [sbt(es2, "mv2_%d" % i, [128, 2]) for i in range(2)]
            rstd = [sbt(es2, "rstd2_%d" % i, [128, 1]) for i in range(2)]
            nmr = [sbt(es2, "nmr2_%d" % i, [128, 1]) for i in range(2)]
            epsb = sbt(es2, "epsb2", [128, 1])
            S.op('dve', lambda e: e.memset(epsb[:], LN_EPS), writes=['epsb'])
            PTs2 = sbt(es2, "PTs2", [128, 32], BF16)
            kvb2 = sbt(es2, "kvb2", [128, 2, 2, 512], BF16)
            kts2 = sbt(es2, "kts2", [128, 2, 2, 256], BF16)
            T2 = SimpleNamespace(xres=[stgA, stgB], zt=zt, zn=zt, stats=stats, mv=mv, rstd=rstd, nmr=nmr, epsb=epsb, x1b=None, x1T=None, PT=PT, rden=rden, mo=rden,
                                 sg=sg, mq=mq, smg=smg, stages=stages, xb=None, xT=x1Tt, PTs=PTs2,
                                 kvb=[(kvb2[:, i, :, :], [('kvb2', i)]) for i in range(2)], kts=[(kts2[:, i, :, :], [('kts2', i)]) for i in range(2)])

            esq = ExitStack()
            distT = sbt(esq, "distT", [128, 256])
            S.dma('sp', distT[:], c_dist[:, :], writes=['distT'])
            scb = [sbt(esq, "scb%d" % i, [128, 256]) for i in range(4)]
            PTd = [sbt(esq, "PTd%d" % i, [128, 256], BF16) for i in range(4)]
            rtot = sbt(esq, "rtot", [128, NT2])
            atmp = [sbt(esq, "atmp%d" % i, [128, NT2]) for i in range(2)]

            def dil_attention(tt):
                NBUF = 4
                for pr in range(2):
                    psDEN = ps[7]
                    psOT = [ps[4], ps[5], ps[6]]
                    den_started = [False, False]
                    items = []
                    for g in range(3):
                        m = 2 * g + pr
                        units = []
                        if g == 0:
                            for jb in range(4):
                                J = 4 * tt + jb
                                qsel = (lambda ap, jb=jb: ap[:, 128 * jb:128 * jb + 128])
                                sub = []
                                if J >= 1:
                                    sub.append((qsel, (lambda hp, J=J, m=m: KT[hp, m, 128 * (J - 1):128 * J]), (0, J - 1), 128, 128, True))
                                sub.append((qsel, (lambda hp, J=J, m=m: KT[hp, m, 128 * J:128 * J + 128]), (0, J), 128, 128, J < 1))
                                dap = distT[:, 0:256] if J >= 1 else distT[:, 128:256]
                                units.append((sub, dap))
                        elif g == 1:
                            for r in range(4):
                                qsel = (lambda ap, r=r: ap.rearrange("p (i r) -> p r i", r=4)[:, r, :])
                                sub = []
                                if tt >= 1:
                                    sub.append((qsel, (lambda hp, r=r, m=m: KT[hp, m, 512 * (tt - 1):512 * tt].rearrange("p (u r) -> p r u", r=4)[:, r, :]), (1, 4 * (tt - 1) + r), 128, 128, True))
                                sub.append((qsel, (lambda hp, r=r, m=m: KT[hp, m, 512 * tt:512 * tt + 512].rearrange("p (u r) -> p r u", r=4)[:, r, :]), (1, 4 * tt + r), 128, 128, tt < 1))
                                dap = distT[:, 0:256] if tt >= 1 else distT[:, 128:256]
                                units.append((sub, dap))
                        else:
                            nk = 32 * (tt + 1)
                            for r4 in range(4):
                                sub = []
                                for rr in range(4):
                                    r = 4 * r4 + rr
                                    sub.append(((lambda ap, r=r: ap.rearrange("p (i r) -> p r i", r=16)[:, r, :]),
                                                (lambda hp, r=r, m=m, nk=nk: KT[hp, m, 0:2048].rearrange("p (u r) -> p r u", r=16)[:, r, 0:nk]), (2, r), nk, 32, True))
                                dap = distT[0:nk, 128 + 32 * tt:128 + 32 * tt + 32].unsqueeze(1).to_broadcast([nk, 4, 32])
                                units.append((sub, dap))
                        for (sub, dap) in units:
                            for half in range(2):
                                items.append((g, m, sub, dap, half))
                    LOOK = 3

                    def emit_qk_sm(it, ui):
                        g, m, sub, dap, half = it
                        hp = slice(64 * half, 64 * half + 64)
                        h = 4 * g + 2 * pr + half
                        cval = -SLOPES[h] * DILS[g] / SCALE
                        bi = ui % NBUF
                        pS = ps[bi]
                        pk = PK[bi]
                        nkmax = max(c[3] for c in sub)
                        ntot = sum(c[4] for c in sub)
                        off = 0
                        for (qsel, ksel, vix, nk, NQ, st) in sub:
                            S.op('pe', lambda e: e.matmul(pS[0:nk, off:off + NQ], lhsT=ksel(hp), rhs=qsel(qT[hp, m, :]), start=True, stop=True),
                                 reads=[('KT', m), ('uTb', m)], writes=[pk])
                            off += NQ
                        if g == 2:
                            o_ap = scb[bi][0:nkmax, 0:ntot].rearrange("p (a i) -> p a i", a=4)
                            i_ap = pS[0:nkmax, 0:ntot].rearrange("p (a i) -> p a i", a=4)
                        else:
                            o_ap = scb[bi][0:nkmax, 0:ntot]
                            i_ap = pS[0:nkmax, 0:ntot]
                        S.op('dve', lambda e: e.scalar_tensor_tensor(out=o_ap, in0=dap, scalar=cval, in1=i_ap, op0=ALU.mult, op1=ALU.add),
                             reads=[pk, 'distT'], writes=['scb%d' % bi])
                        S.op('act', lambda e: e.activation(out=PTd[bi][0:nkmax, 0:ntot], in_=scb[bi][0:nkmax, 0:ntot], func=AF.Exp, scale=SCALE),
                             reads=['scb%d' % bi], writes=['PTd%d' % bi])

                    def emit_pv(it, ui):
                        g, m, sub, dap, half = it
                        hp = slice(64 * half, 64 * half + 64)
                        bi = ui % NBUF
                        hc = (2 * pr + half) * 64
                        nsub = len(sub)
                        off = 0
                        for si, (qsel, ksel, vix, nk, NQ, st) in enumerate(sub):
                            last = (si == nsub - 1) or sub[si + 1][5]
                            S.op('pe', lambda e: e.matmul(qsel(psOT[g][hp, :]), lhsT=Vg[0:nk, vix[0], vix[1], hc:hc + 64], rhs=PTd[bi][0:nk, off:off + NQ], start=st, stop=last),
                                 reads=[('Vg', g), 'PTd%d' % bi], writes=[PK[4 + g]])
                            S.op('pe', lambda e: e.matmul(qsel(psDEN[hp, :]), lhsT=onesb[0:nk, 0:64], rhs=PTd[bi][0:nk, off:off + NQ], start=(not den_started[half]), stop=True,
                                                          skip_group_check=True),
                                 reads=['onesb', 'PTd%d' % bi], writes=['ps7'])
                            den_started[half] = True
                            off += NQ

                    for i in range(len(items) + LOOK):
                        if i < len(items):
                            emit_qk_sm(items[i], i)
                        if i - LOOK >= 0:
                            emit_pv(items[i - LOOK], i - LOOK)
                    S.op('dve', lambda e: e.reciprocal(out=rtot[:], in_=psDEN[:, :]), reads=['ps7'], writes=['rtot'])
                    for g in range(3):
                        m = 2 * g + pr
                        at = atmp[g % 2]
                        ak = 'atmp%d' % (g % 2)
                        S.op('dve', lambda e, g=g, at=at: e.tensor_tensor(out=at[:], in0=psOT[g][:, :], in1=rtot[:], op=ALU.mult), reads=[PK[4 + g], 'rtot'], writes=[ak])
                        S.op('pool', lambda e, m=m, at=at: e.tensor_tensor(out=catT[:, m, :], in0=at[:], in1=sg[:, m, :], op=ALU.mult), reads=[ak, ('sg', m)], writes=[('catT', m)])

            def sample_dil_attention():
              with ExitStack() as ess:
                qtok = sbt(ess, "qtok", [64, 768], BF16)
                ktile = sbt(ess, "ktile", [128, 4, 2, 256])
                vb = sbt(ess, "vb", [128, 9, 256], BF16)
                prod = [sbt(ess, "prod%d" % i, [128, 256]) for i in range(2)]
                scs = sbt(ess, "scs", [128, 48])
                scs2 = sbt(ess, "scs2", [128, 48])
                Pb = sbt(ess, "Pb", [128, 48], BF16)
                sbias = sbt(ess, "sbias", [128, 48])
                pv_sb = sbt(ess, "pv_sb", [128, NB * 48])
                den_sb = sbt(ess, "den_sb", [128, NB * 48])
                otot = sbt(ess, "otot", [128, 6, 64])
                dtot = sbt(ess, "dtot", [128, 2, 64])
                ndist = sbt(ess, "ndist", [64, 2, 64])
                scn = [sbt(ess, "scn%d" % i, [64, 64]) for i in range(2)]
                PN = [sbt(ess, "PN%d" % i, [64, 64], BF16) for i in range(2)]
                S.dma('sp', sbias[:], c_sdist.rearrange("p g x -> p (g x)"), writes=['sbias'])
                S.dma('sp', ndist[:], c_ndist[:, :, :], writes=['ndist'])
                for (pq, c0, cw) in [(ps[0], 0, 512), (ps[1], 512, 256)]:
                    for kc in range(8):
                        S.op('pe', lambda e, kc=kc, pq=pq, c0=c0, cw=cw: e.matmul(pq[0:NS, 0:cw], lhsT=x1Tt[:, kc, 0:NS], rhs=w_in_sb[:, kc, c0:c0 + cw], start=(kc == 0), stop=(kc == 7)),
                             reads=[('w_in_sb', kc), ('xT', kc)], writes=[PK[ps.index(pq)]])
                S.op('act', lambda e: e.activation(out=qtok[:, 0:512], in_=ps[0][0:NS, :], func=AF.Copy), reads=['ps0'], writes=['qtok'])
                S.op('dve', lambda e: e.tensor_copy(out=qtok[:, 512:768], in_=ps[1][0:NS, 0:256]), reads=['ps1', 'qtok'], writes=['qtok'])
                idx = 0
                for g in range(3):
                    for hh in range(4):
                        h = 4 * g + hh
                        pr, half = hh // 2, hh % 2
                        m = 2 * g + pr
                        hp = slice(64 * half, 64 * half + 64)
                        cval = -SLOPES[h] * DILS[g] / SCALE
                        bi = half
                        S.op('pe', lambda e, hp=hp, m=m, bi=bi: e.matmul(ps[2 + bi][0:NS, 0:NS], lhsT=KT[hp, m, SEQ:NTOK], rhs=qT[hp, m, 0:NS], start=True, stop=True),
                             reads=[('KT', m), ('uTb', m)], writes=[PK[2 + bi]])
                        S.op('dve', lambda e, bi=bi, g=g, cval=cval: e.scalar_tensor_tensor(out=scn[bi][:, :], in0=ndist[:, (0 if g == 0 else 1), :], scalar=cval, in1=ps[2 + bi][0:NS, 0:NS],
                                                                                          op0=ALU.mult, op1=ALU.add), reads=[PK[2 + bi], 'ndist'], writes=['scn%d' % bi])
                        S.op('act', lambda e, bi=bi: e.activation(out=PN[bi][:, :], in_=scn[bi][:, :], func=AF.Exp, scale=SCALE), reads=['scn%d' % bi], writes=['PN%d' % bi])
                        col = (g * 2 + pr) * 64
                        S.op('pe', lambda e, hp=hp, g=g, hh=hh, bi=bi, col=col: e.matmul(ps[6][hp, col:col + 64], lhsT=Vnew[0:NS, g, hh * 64:(hh + 1) * 64], rhs=PN[bi][:, :], start=True, stop=True),
                             reads=['Vnew', 'PN%d' % bi], writes=['ps6'])
                        S.op('pe', lambda e, hp=hp, bi=bi, col=col: e.matmul(ps[7][hp, col:col + 64], lhsT=onesb[0:NS, 0:64], rhs=PN[bi][:, :], start=True, stop=True),
                             reads=['onesb', 'PN%d' % bi], writes=['ps7'])
                        idx += 1
                kcnt = 0
                for b in range(NB):
                  for tp in range(2):
                    for t in (2 * tp, 2 * tp + 1):
                        s_ = 4 * b + t
                        pa, pb_ = (ps[0], ps[1]) if t % 2 == 0 else (ps[2], ps[3])
                        sel = identb[0:NS, s_:s_ + 1].to_broadcast([NS, 128])
                        S.op('pe', lambda e, pa=pa, sel=sel: e.matmul(pa[:, 0:512], lhsT=sel, rhs=qtok[:, 0:512], start=True, stop=True), reads=['qtok', 'identb'], writes=[PK[ps.index(pa)]])
                        S.op('pe', lambda e, pb_=pb_, sel=sel: e.matmul(pb_[:, 0:256], lhsT=sel, rhs=qtok[:, 512:768], start=True, stop=True), reads=['qtok', 'identb'], writes=[PK[ps.index(pb_)]])
                    for g in range(3):
                        kt = ktile[:, kcnt % 4, :, :]
                        kk = 'ktile%d' % (kcnt % 4)
                        kcnt += 1
                        if g == 0:
                            S.dma('sp', kt[:, 0, :], cd[0][b % NBC, :, 0:256], writes=[kk])
                            if tp == 0:
                                S.dma('pool', vb[:, 0, :], cd[0][b % NBC, :, 256:512], writes=[('vb', g)])
                        else:
                            r_ = 4 if g == 1 else 16
                            kb = 1 if g == 1 else 5
                            srcv = cd[g][b % NBC].rearrange("(u r) c -> u r c", r=r_)
                            S.dma('sp', kt[:, :, :], srcv[:, 2 * tp:2 * tp + 2, 0:256], writes=[kk])
                            S.dma('pool', vb[:, kb + 2 * tp:kb + 2 * tp + 2, :], srcv[:, 2 * tp:2 * tp + 2, 256:512], writes=[('vb', g)])
                        for t in (2 * tp, 2 * tp + 1):
                            pa, pb_ = (ps[0], ps[1]) if t % 2 == 0 else (ps[2], ps[3])
                            qb = pa[:, g * 256:(g + 1) * 256] if g < 2 else pb_[:, 0:256]
                            qk = PK[ps.index(pa)] if g < 2 else PK[ps.index(pb_)]
                            ki = 0 if g == 0 else t - 2 * tp
                            pi = (g * 4 + t) % 2
                            S.op('dve', lambda e, ki=ki, qb=qb, pi=pi, kt=kt: e.tensor_tensor(out=prod[pi][:, :], in0=kt[:, ki, :], in1=qb, op=ALU.mult),
                                 reads=[kk, qk], writes=['prod%d' % pi])
                            S.op('dve', lambda e, g=g, t=t, pi=pi: e.tensor_reduce(out=scs[:, (g * 4 + t) * 4:(g * 4 + t) * 4 + 4], in_=prod[pi][:, :].rearrange("p (h e) -> p h e", h=4),
                                                                                 axis=mybir.AxisListType.X, op=ALU.add),
                                 reads=['prod%d' % pi], writes=['scs'])
                  if True:
                    S.op('dve', lambda e: e.scalar_tensor_tensor(out=scs2[:], in0=scs[:], scalar=SCALE, in1=sbias[:], op0=ALU.mult, op1=ALU.add), reads=['scs', 'sbias'], writes=['scs2'])
                    S.op('act', lambda e: e.activation(out=Pb[:], in_=scs2[:], func=AF.Exp), reads=['scs2'], writes=['Pb'])
                    pB = ps[4 + b % 2]
                    pBk = PK[4 + b % 2]
                    for g in range(3):
                        for t in range(4):
                            ki = 0 if g == 0 else (1 + t if g == 1 else 5 + t)
                            for pr in range(2):
                                col = ((g * 4 + t) * 2 + pr) * 2
                                pcol = (g * 4 + t) * 4 + 2 * pr
                                S.op('pe', lambda e, ki=ki, pr=pr, col=col, pcol=pcol, pB=pB: e.matmul(pB[:, col:col + 2], lhsT=vb[:, ki, pr * 128:(pr + 1) * 128], rhs=Pb[:, pcol:pcol + 2],
                                                                                                      start=True, stop=True),
                                     reads=[('vb', g), 'Pb'], writes=[pBk])
                    S.op('pe', lambda e, pB=pB: e.matmul(pB[:, 64:112], lhsT=onesb[:, :], rhs=Pb[:, :], start=True, stop=True), reads=['onesb', 'Pb'], writes=[pBk])
                    S.op('act', lambda e, b=b, pB=pB: e.activation(out=pv_sb[:, b * 48:(b + 1) * 48], in_=pB[:, 0:48], func=AF.Copy), reads=[pBk], writes=['pv_sb'])
                    S.op('dve', lambda e, b=b, pB=pB: e.tensor_copy(out=den_sb[:, b * 48:(b + 1) * 48], in_=pB[:, 64:112]), reads=[pBk], writes=['den_sb'])
                for half in range(2):
                    hp = slice(64 * half, 64 * half + 64)
                    for g in range(3):
                        pvv = pv_sb[hp, :].rearrange("p (b g t r j) -> p g r j b t", g=3, t=4, r=2, j=2)[:, g, :, half, :, :]
                        dnv = den_sb[hp, :].rearrange("p (b g t r j) -> p g r j b t", g=3, t=4, r=2, j=2)[:, g, :, half, :, :]
                        nv = lambda pp, g=g, hp=hp: pp[hp, 2 * g * 64:(2 * g + 2) * 64].rearrange("p (r b t) -> p r b t", r=2, t=4)
                        S.op('dve', lambda e, pvv=pvv, nv=nv, g=g, hp=hp: e.tensor_tensor(out=otot[hp, 2 * g:2 * g + 2, :].rearrange("p r (b t) -> p r b t", t=4), in0=pvv, in1=nv(ps[6]), op=ALU.add),
                             reads=['pv_sb', 'ps6'], writes=['otot'])
                        dv = dtot[hp, :, :].rearrange("p r (b t) -> p r b t", t=4)
                        if g == 0:
                            S.op('dve', lambda e, dnv=dnv, nv=nv, dv=dv: e.tensor_tensor(out=dv, in0=dnv, in1=nv(ps[7]), op=ALU.add), reads=['den_sb', 'ps7'], writes=['dtot'])
                        else:
                            S.op('dve', lambda e, dnv=dnv, dv=dv: e.tensor_tensor(out=dv, in0=dv, in1=dnv, op=ALU.add), reads=['den_sb', 'dtot'], writes=['dtot'])
                            S.op('dve', lambda e, nv=nv, dv=dv: e.tensor_tensor(out=dv, in0=dv, in1=nv(ps[7]), op=ALU.add), reads=['ps7', 'dtot'], writes=['dtot'])
                S.op('dve', lambda e: e.reciprocal(out=dtot[:], in_=dtot[:]), reads=['dtot'], writes=['dtot'])
                for g in range(3):
                    S.op('dve', lambda e, g=g: e.tensor_tensor(out=otot[:, 2 * g:2 * g + 2, :], in0=otot[:, 2 * g:2 * g + 2, :], in1=dtot[:, :, :], op=ALU.mult), reads=['otot', 'dtot'], writes=['otot'])
                    S.op('pool', lambda e, g=g: e.tensor_tensor(out=catT[:, 2 * g:2 * g + 2, 0:NS], in0=otot[:, 2 * g:2 * g + 2, :], in1=sg[:, 2 * g:2 * g + 2, 0:NS], op=ALU.mult),
                         reads=['otot', ('sg', 2 * g), ('sg', 2 * g + 1)], writes=[('catT', 2 * g), ('catT', 2 * g + 1)])
                S.barrier()

            for tt in range(SEQ // NT2):
                t0 = tt * NT2
                for kc in range(8):
                    S.dma('sp', x1Tt[:, kc, :], x1Tscr[kc, :, t0:t0 + NT2], writes=[('xT', kc)])
                in_proj(T2, NT2, w_in_sb, x1Tt, qT, 1)
                dil_attention(tt)
                mem_attention(T2, 1, NT2, mq, smg, catT, KmT[:, 1, :, :], Vm[:, 1, :, :], [ps[2], ps[3]], [ps[4], ps[6]], [ps[5], ps[7]])
                blks = []
                for a in range(NT2 // 128):
                    ta = t0 + a * 128
                    blks.append((a, 128, x1scr[ta:ta + 128, :], ta, y_p[ta:ta + 128, :]))
                ln_blocks(T2, blks, catT, w_out_sb, False)
            S.barrier()
            esq.close()
            if stage >= 5 and KCUT != 53:
                for kc in range(8):
                    S.dma('sp', x1Tt[:, kc, 0:NS], x1Tscr[kc, :, SEQ:NTOK], writes=[('xT', kc)])
                in_proj(T2, NS, w_in_sb, x1Tt, qT, 1)
                if stage >= 6:
                    sample_dil_attention()
                else:
                    for m in range(6):
                        S.op('pool', lambda e, m=m: e.memset(catT[:, m, 0:NS], 0.0), writes=[('catT', m)])
                mem_attention_sample(T2, 1, smg, mq, catT, ps[2], ps[3], [ps[4], ps[6]], [ps[5], ps[7]])
                ln_blocks(T2, [(0, NS, x1scr[SEQ:NTOK, :], SEQ, y_s[0:NS, :])], catT, w_out_sb, False)
            S.barrier()

        if stage >= 3:
            with ExitStack() as es2:
                _phase2(es2)

        issue_bulk(len(bulk_list))
        S.finish()
        print("ops", S.nops, "waits", S.nwait)
    return nc


_CONST_CACHE = {}


def _consts():
    if _CONST_CACHE:
        return _CONST_CACHE
    ident = np.eye(128, dtype=np.float32)
    sw = np.zeros((128, 128), np.float32)
    for m in range(64):
        sw[64 + m, m] = -1.0
        sw[m, 64 + m] = 1.0
    p = np.arange(128)
    meo = np.zeros((128, 4), np.float32)
    for j in range(4):
        meo[:, j] = ((p // 16) % 4 == j)
    jv = np.tile(np.arange(128, dtype=np.float32)[None, :], (128, 1))
    u = np.arange(128)[:, None]
    i = np.arange(128)[None, :]
    dist = np.zeros((128, 2, 128), np.float32)
    dist[:, 0, :] = np.where(u >= i, i + 128 - u, BIG)
    dist[:, 1, :] = np.where(u <= i, i - u, BIG)
    sd = np.zeros((128, 3, 4, 4), np.float32)
    for g in range(3):
        for t in range(4):
            for hh in range(4):
                uu = np.arange(128)
                if g == 0:
                    dd = 128 + t - uu
                    sd[:, g, t, hh] = np.where(uu >= t, -SLOPES[4 * g + hh] * dd, -30000.0)
                else:
                    sd[:, g, t, hh] = -SLOPES[4 * g + hh] * ((128 - uu) * DILS[g])
    nd = np.full((64, 2, 64), BIG, np.float32)
    for sp_ in range(64):
        for s_ in range(64):
            if sp_ // 4 == s_ // 4 and sp_ % 4 <= s_ % 4:
                nd[sp_, 0, s_] = (s_ % 4) - (sp_ % 4)
            if sp_ == s_:
                nd[sp_, 1, s_] = 0.0
    _CONST_CACHE.update(dict(c_ident=ident, c_swap=sw, c_maskeo=meo, c_jvec=jv, c_dist=dist.reshape(128, 256),
                             c_sdist=sd.reshape(128, 3, 16), c_ndist=nd))
    return _CONST_CACHE


_NC_CACHE = {}


def kernel(x_prompt, x_sample, cache_mem_kv, state_ssm_re, state_ssm_im, cache_dil1_kv, cache_dil4_kv,
           cache_dil16_kv, mem_prompt, w_in, w_out, ln_g, ln_b, w_mem_kv, ssm_lambda_re, ssm_lambda_im,
           ssm_log_dt, ssm_b_re, ssm_b_im, ssm_c_re, ssm_c_im, ssm_d, w_glu, b_glu, w_kv_shared, _stage=99):
    f = lambda a: np.ascontiguousarray(np.asarray(a, dtype=np.float32))
    if _stage not in _NC_CACHE:
        _NC_CACHE[_stage] = build(_stage)
    nc = _NC_CACHE[_stage]
    shared = dict(
        w_in=f(w_in), w_out=f(w_out), ln_g=f(ln_g), ln_b=f(ln_b), w_mem=f(w_mem_kv),
        lam_re=f(ssm_lambda_re)[0], lam_im=f(ssm_lambda_im)[0], log_dt=f(ssm_log_dt)[0],
        b_re=f(ssm_b_re)[0], b_im=f(ssm_b_im)[0],
        c_re=f(ssm_c_re)[0].reshape(768, 64), c_im=f(ssm_c_im)[0].reshape(768, 64),
        ssm_d=f(ssm_d)[0].reshape(768), w_glu=f(w_glu)[0], b_glu=f(b_glu)[0], w_kv=f(w_kv_shared))
    shared.update(_consts())
    x_prompt = np.asarray(x_prompt)
    in_maps = []
    for c in range(NCORES):
        bs = slice(NB * c, NB * (c + 1))
        d = dict(shared)
        d.update(
            x_p=f(x_prompt[c]), x_s=f(np.asarray(x_sample)[bs]).reshape(NS, 1024),
            cmk=f(np.asarray(cache_mem_kv)[:, bs]).reshape(2, NB, 256, 512),
            st_re=f(np.asarray(state_ssm_re)[0, bs]).reshape(NB * 48, 64),
            st_im=f(np.asarray(state_ssm_im)[0, bs]).reshape(NB * 48, 64),
            cd1=f(np.asarray(cache_dil1_kv)[bs]).reshape(NB, 128, 512),
            cd4=f(np.asarray(cache_dil4_kv)[bs]).reshape(NB, 512, 512),
            cd16=f(np.asarray(cache_dil16_kv)[bs]).reshape(NB, 2048, 512),
            memp=f(np.asarray(mem_prompt)[c]))
        if _stage < 5:
            for k in ('cmk',):
                d[k] = np.ascontiguousarray(d[k][:, 0:1])
        if _stage < 5 or (50 <= KCUT < 60):
            for k in ('cd1', 'cd4', 'cd16'):
                d[k] = np.ascontiguousarray(d[k][0:1])
        in_maps.append(d)
    res = run_bass_kernel_spmd(nc, in_maps, core_ids=list(range(NCORES)))
    R = res.results
    cat = lambda k: np.stack([np.asarray(R[c][k]) for c in range(NCORES)], axis=0)
    y_prompt = cat("y_p")
    y_sample = cat("y_s").reshape(128, 4, 1024)
    mem_kv_prompt = cat("mkv_p").transpose(1, 0, 2, 3).reshape(2, 8, 256, 2, 4, 64)
    ssm_re_prompt = cat("sre_p")[None]
    ssm_im_prompt = cat("sim_p")[None]
    d1p = cat("d1_p").reshape(8, 128, 2, 4, 64)
    d4p = cat("d4_p").reshape(8, 512, 2, 4, 64)
    d16p = cat("d16_p").reshape(8, 2048, 2, 4, 64)
    ssm_re_sample = cat("sre_s").reshape(1, 128, 48, 64)
    ssm_im_sample = cat("sim_s").reshape(1, 128, 48, 64)
    if _stage < 5 or (50 <= KCUT < 60):
        z = lambda *sh: np.zeros(sh, np.float32)
        return (y_prompt, y_sample, mem_kv_prompt, ssm_re_prompt, ssm_im_prompt, d1p, d4p, d16p, ssm_re_sample, ssm_im_sample,
                z(128, 128, 2, 4, 64), z(128, 512, 2, 4, 64), z(128, 2048, 2, 4, 64))
    d1s = cat("d1_s").reshape(128, 128, 2, 4, 64)
    d4s = cat("d4_s").reshape(128, 512, 2, 4, 64)
    d16s = cat("d16_s").reshape(128, 2048, 2, 4, 64)
    return (y_prompt, y_sample, mem_kv_prompt, ssm_re_prompt, ssm_im_prompt, d1p, d4p, d16p,
            ssm_re_sample, ssm_im_sample, d1s, d4s, d16s)
```

```python
import math
import os
KCUT = int(os.environ.get('KCUT', '99'))
import numpy as np
import concourse.bass as bass
import concourse.mybir as mybir
from concourse.bass_utils import run_bass_kernel_spmd
from contextlib import ExitStack
from types import SimpleNamespace

F32 = mybir.dt.float32
BF16 = mybir.dt.bfloat16
I32 = mybir.dt.int32
AF = mybir.ActivationFunctionType
ALU = mybir.AluOpType

NCORES = 8
SEQ = 2048
NS = 64
NB = 16
NTOK = SEQ + NS
ALPHA = (2.0 * 2) ** 0.25
LN_EPS = 1e-5
SCALE = 0.125
BIG = 1.0e6
TWO_PI = 2.0 * math.pi
C1_2PI = 6.28125
C2_2PI = TWO_PI - 6.28125
PI_LO = 3.1415925
SLOPES = [2.0 ** (-8.0 * (h + 1) / 12.0) for h in range(12)]
DILS = [1, 4, 16]


class _Stop(Exception):
    pass


class Sched:
    def __init__(self, nc, es, n_dma_sems=40):
        self.nc = nc
        self.eng = {'pe': nc.tensor, 'dve': nc.vector, 'act': nc.scalar, 'pool': nc.gpsimd, 'sp': nc.sync}
        self.sem = {k: es.enter_context(nc.semaphore('sem_' + k)) for k in self.eng}
        self.cnt = {k: 0 for k in self.eng}
        self.n_hw = n_dma_sems - 16
        self.dsem = [es.enter_context(nc.semaphore('dsem%d' % i)) for i in range(n_dma_sems)]
        self.dnext_sw = 0
        self.dcnt = [0] * n_dma_sems
        self.dbg = [False] * n_dma_sems
        self.dnext = 0
        self.bsem = es.enter_context(nc.semaphore('bulk'))
        self.bcnt = 0
        self.waited = {k: {} for k in self.eng}
        self.snap = {}
        self.lastw = {}
        self.readers = {}
        self.nops = 0
        self.nwait = 0

    def semobj(self, sid):
        if sid == 'bulk':
            return self.bsem
        return self.sem[sid] if isinstance(sid, str) else self.dsem[sid]

    def _wait(self, e, sid, val):
        if val <= 0:
            return
        if sid == e and e == 'pe':
            return
        w = self.waited[e]
        if w.get(sid, 0) >= val:
            return
        self.eng[e].wait_ge(self.semobj(sid), val)
        self.nwait += 1
        w[sid] = val
        sn = self.snap.get((sid, val))
        if sn:
            for s2, v2 in sn.items():
                if s2 != e and w.get(s2, 0) < v2:
                    w[s2] = v2

    def _deps(self, e, reads, writes):
        for r in reads:
            if r in self.lastw:
                self._wait(e, *self.lastw[r])
        for r in writes:
            if r in self.lastw:
                self._wait(e, *self.lastw[r])
            for sid, val in self.readers.get(r, {}).items():
                self._wait(e, sid, val)

    def _commit(self, sid, val, reads, writes):
        for r in reads:
            d = self.readers.setdefault(r, {})
            d[sid] = max(val, d.get(sid, 0))
        for r in writes:
            self.lastw[r] = (sid, val)
            self.readers[r] = {}

    def op(self, e, fn, reads=(), writes=()):
        isps = lambda r: isinstance(r, str) and r.startswith('ps') and r[2:].isdigit()
        writes = list(writes) + [r for r in reads if isps(r)]
        reads = [r for r in reads if not isps(r)]
        self._deps(e, reads, writes)
        ins = fn(self.eng[e])
        self.cnt[e] += 1
        self.snap[(e, self.cnt[e])] = dict(self.waited[e])
        ins.then_inc(self.sem[e], 1)
        self._commit(e, self.cnt[e], reads, writes)
        self.nops += 1

    def dma(self, e, out, in_, reads=(), writes=(), bg=False, **kw):
        if e == 'pool':
            j = self.n_hw + self.dnext_sw
            self.dnext_sw = (self.dnext_sw + 1) % (len(self.dsem) - self.n_hw)
        else:
            j = self.dnext
            self.dnext = (j + 1) % self.n_hw
        self._wait(e, j, self.dcnt[j])
        self._deps(e, reads, writes)
        ins = self.eng[e].dma_start(out=out, in_=in_, **kw)
        self.dcnt[j] += 16
        self.snap[(j, self.dcnt[j])] = dict(self.waited[e])
        self.dbg[j] = bg
        ins.then_inc(self.dsem[j], 16)
        self._commit(j, self.dcnt[j], reads, writes)
        self.nops += 1

    def dma_bulk(self, e, out, in_):
        ins = self.eng[e].dma_start(out=out, in_=in_)
        self.bcnt += 16
        ins.then_inc(self.bsem, 16)

    def barrier(self, all_dma=False):
        for e in self.eng:
            for k in self.eng:
                if k != e:
                    self._wait(e, k, self.cnt[k])
            for j in range(len(self.dsem)):
                if all_dma or not self.dbg[j]:
                    self._wait(e, j, self.dcnt[j])

    def finish(self):
        self.barrier(all_dma=True)
        self._wait('sp', 'bulk', self.bcnt)


def build(stage=99):
    nc = bass.Bass("TRN2", target_bir_lowering=False)
    NBC = NB if (stage >= 5 and not (50 <= KCUT < 60)) else 1
    NBM = NB if stage >= 5 else 1

    def din(name, shape, dt=F32):
        return nc.dram_tensor(name, list(shape), dt, kind="ExternalInput").ap()

    def dout(name, shape, dt=F32):
        return nc.dram_tensor(name, list(shape), dt, kind="ExternalOutput").ap()

    def dscr(name, shape, dt=F32):
        return nc.dram_tensor(name, list(shape), dt, kind="Internal").ap()

    x_p = din("x_p", [SEQ, 1024])
    x_s = din("x_s", [NS, 1024])
    cmk = din("cmk", [2, NBM, 256, 512])
    st_re = din("st_re", [NB * 48, 64])
    st_im = din("st_im", [NB * 48, 64])
    cd = [din("cd1", [NBC, 128, 512]), din("cd4", [NBC, 512, 512]), din("cd16", [NBC, 2048, 512])]
    memp = din("memp", [256, 1024])
    w_in = din("w_in", [2, 1024, 2048])
    w_out = din("w_out", [2, 1024, 1024])
    ln_g = din("ln_g", [2, 1024])
    ln_b = din("ln_b", [2, 1024])
    w_mem = din("w_mem", [2, 1024, 512])
    lam_re = din("lam_re", [48, 64])
    lam_im = din("lam_im", [48, 64])
    log_dt = din("log_dt", [48])
    b_re = din("b_re", [48, 64, 16])
    b_im = din("b_im", [48, 64, 16])
    c_re = din("c_re", [768, 64])
    c_im = din("c_im", [768, 64])
    ssm_d = din("ssm_d", [768])
    w_glu = din("w_glu", [768, 768])
    b_glu = din("b_glu", [768])
    w_kv = din("w_kv", [1024, 1536])
    c_ident = din("c_ident", [128, 128])
    c_swap = din("c_swap", [128, 128])
    c_maskeo = din("c_maskeo", [128, 4])
    c_jvec = din("c_jvec", [128, 128])
    c_dist = din("c_dist", [128, 256])
    c_sdist = din("c_sdist", [128, 3, 16])
    c_ndist = din("c_ndist", [64, 2, 64])

    y_p = dout("y_p", [SEQ, 1024])
    y_s = dout("y_s", [NS, 1024])
    mkv_p = dout("mkv_p", [2, 256, 512])
    sre_p = dout("sre_p", [48, 64])
    sim_p = dout("sim_p", [48, 64])
    dkv_p = [dout("d1_p", [128, 512]), dout("d4_p", [512, 512]), dout("d16_p", [2048, 512])]
    sre_s = dout("sre_s", [NB * 48, 64])
    sim_s = dout("sim_s", [NB * 48, 64])
    dkv_s = [dout("d1_s", [NBC, 128, 512]), dout("d4_s", [NBC, 512, 512]), dout("d16_s", [NBC, 2048, 512])]

    x1scr = dscr("x1scr", [NTOK, 1024], F32)
    x1Tscr = dscr("x1Tscr", [8, 128, NTOK], BF16)

    with ExitStack() as es0:
        S = Sched(nc, es0)

        def sbt(es, name, shape, dt=F32):
            return es.enter_context(nc.sbuf_tensor(name, list(shape), dt))

        ps = [es0.enter_context(nc.psum_tensor("ps%d" % i, [128, 512], F32)) for i in range(8)]
        PK = ['ps%d' % i for i in range(8)]

        identf = sbt(es0, "identf", [128, 128])
        identb = sbt(es0, "identb", [128, 128], BF16)
        onesb = sbt(es0, "onesb", [128, 128], BF16)
        S.dma('sp', identf[:], c_ident[:, :], writes=['identf'])
        S.op('dve', lambda e: e.tensor_copy(out=identb[:], in_=identf[:]), reads=['identf'], writes=['identb'])
        S.op('dve', lambda e: e.memset(onesb[:], 1.0), writes=['onesb'])
        KmT = sbt(es0, "KmT", [128, 2, 2, 256], BF16)
        Vm = sbt(es0, "Vm", [128, 2, 2, 256], BF16)
        lnG = sbt(es0, "lnG", [128, 1024])
        lnB = sbt(es0, "lnB", [128, 1024])

        wins = [128, 512, 2048]
        bulk_list = []
        for g in (range(3) if stage != 0 and stage != 2 and not (50 <= KCUT < 60) else []):
            for b in range(NBC):
                src = cd[g][b, 4:wins[g], :].rearrange("r c -> (r c)").rearrange("(a x) -> a x", a=16)
                dst = dkv_s[g][b, 0:wins[g] - 4, :].rearrange("r c -> (r c)").rearrange("(a x) -> a x", a=16)
                bulk_list.append((dst, src))

        def issue_bulk(n):
            for _ in range(min(n, len(bulk_list))):
                dst, src = bulk_list.pop()
                S.dma_bulk('act', dst, src)

        def evac(i, out, in_, reads, writes):
            if i % 2 == 0:
                S.op('act', lambda e: e.activation(out=out, in_=in_, func=AF.Copy), reads=reads, writes=writes)
            else:
                S.op('dve', lambda e: e.tensor_copy(out=out, in_=in_), reads=reads, writes=writes)

        def load_weight(es_w, dst, src_rows, ncols, stage_tiles, key, col_map=None):
            nk = dst.shape[1]
            for kc in range(nk):
                S.dma('pool', dst[:, kc, :], src_rows(kc), writes=[(key, kc)], bg=True)

        def wkeys(key, n):
            return [(key, kc) for kc in range(n)]

        def ln_part1(T, blk_i, rows, cat, wout_sb, x_src):
            c0 = blk_i * 128
            bb = blk_i % 2
            xr = T.xres[bb]
            xk = ['stgA', 'stgB'][bb]
            zt = T.zt[bb]
            zk = 'zt%d' % bb
            S.dma('sp', xr[0:rows, :], x_src, writes=[xk])
            for n in range(2):
                for kc in range(8):
                    S.op('pe', lambda e: e.matmul(ps[n][0:rows, :], lhsT=cat[:, kc, c0:c0 + rows], rhs=wout_sb[:, kc, n * 512:(n + 1) * 512], start=(kc == 0), stop=(kc == 7)),
                         reads=[('catT', kc), ('w_out_sb', kc)], writes=[PK[n]])
                S.op('dve', lambda e: e.scalar_tensor_tensor(out=zt[0:rows, n * 512:(n + 1) * 512], in0=xr[0:rows, n * 512:(n + 1) * 512], scalar=ALPHA,
                                                             in1=ps[n][0:rows, :], op0=ALU.mult, op1=ALU.add),
                     reads=[xk, PK[n]], writes=[(zk, n), zk])
                S.op('dve', lambda e: e.bn_stats(out=T.stats[bb][0:rows, n, :], in_=zt[0:rows, n * 512:(n + 1) * 512]), reads=[(zk, n)], writes=[('stats', bb, n)])
            S.op('dve', lambda e: e.bn_aggr(out=T.mv[bb][0:rows, :], in_=T.stats[bb][0:rows, :, :]), reads=[('stats', bb, 0), ('stats', bb, 1)], writes=[('mv', bb)])

        def ln_part2_stage(T, st, blk_i, rows, tok0, y_dst, make_T):
            bb = blk_i % len(T.zt)
            zt = T.zt[bb]
            zk = 'zt%d' % bb
            mv, rstd, nmr = T.mv[bb], T.rstd[bb], T.nmr[bb]
            if st == 0:
                S.op('act', lambda e: e.activation(out=rstd[0:rows, :], in_=mv[0:rows, 1:2], func=AF.Sqrt, bias=T.epsb[0:rows, :], scale=1.0), reads=[('mv', bb), 'epsb'], writes=[('rstd', bb)])
                S.op('dve', lambda e: e.reciprocal(out=rstd[0:rows, :], in_=rstd[0:rows, :]), reads=[('rstd', bb)], writes=[('rstd', bb)])
                S.op('dve', lambda e: e.tensor_scalar(out=nmr[0:rows, :], in0=mv[0:rows, 0:1], scalar1=rstd[0:rows, 0:1], scalar2=-1.0, op0=ALU.mult, op1=ALU.mult),
                     reads=[('mv', bb), ('rstd', bb)], writes=[('nmr', bb)])
            elif st == 1:
                S.op('act', lambda e: e.activation(out=zt[0:rows, :], in_=zt[0:rows, :], func=AF.Identity, scale=rstd[0:rows, 0:1], bias=nmr[0:rows, 0:1]),
                     reads=[(zk, 0), (zk, 1), ('rstd', bb), ('nmr', bb)], writes=[zk, (zk, 0), (zk, 1)])
            elif st == 2:
                S.op('dve', lambda e: e.tensor_tensor(out=zt[0:rows, :], in0=zt[0:rows, :], in1=lnG[0:rows, :], op=ALU.mult), reads=[zk, 'lnG'], writes=[zk])
            elif st == 3:
                S.op('pool', lambda e: e.tensor_tensor(out=zt[0:rows, :], in0=zt[0:rows, :], in1=lnB[0:rows, :], op=ALU.add), reads=[zk, 'lnB'], writes=[zk])
                S.dma('sp', y_dst, zt[0:rows, :], reads=[zk])
            elif st == 4 and make_T:
                xb_ = T.x1b[bb]
                xk_ = 'x1b%d' % bb
                S.op('act', lambda e: e.activation(out=xb_[0:rows, :], in_=zt[0:rows, :], func=AF.Copy), reads=[zk], writes=[xk_])
                for kc in range(8):
                    pt = ps[2 + kc // 4]
                    S.op('pe', lambda e: e.matmul(pt[:, (kc % 4) * 128:(kc % 4) * 128 + rows], lhsT=xb_[0:rows, kc * 128:(kc + 1) * 128], rhs=identb[0:rows, 0:rows], start=True, stop=True),
                         reads=[xk_, 'identb'], writes=[PK[2 + kc // 4]])
                for hh in range(2):
                    evac(hh, T.x1T[:, 4 * hh:4 * hh + 4, 0:rows], ps[2 + hh][:, :].rearrange("p (k t) -> p k t", k=4)[:, :, 0:rows], [PK[2 + hh]], ['x1T'])
                S.dma('sp', x1Tscr[:, :, tok0:tok0 + rows].rearrange("k p t -> p k t"), T.x1T[:, :, 0:rows], reads=['x1T'])

        def ln_blocks(T, blocks, cat, wout_sb, make_T):
            n = len(blocks)
            for p0 in range(0, n, 2):
                pair = blocks[p0:p0 + 2]
                for (bi_, rows, x_src, tok0, y_dst) in pair:
                    ln_part1(T, bi_, rows, cat, wout_sb, x_src)
                for st in range(5):
                    for (bi_, rows, x_src, tok0, y_dst) in pair:
                        ln_part2_stage(T, st, bi_, rows, tok0, y_dst, make_T)

        def mem_attention(T, l, NT, mq_t, smg_t, cat, Km, Vmm, psS, psO, psD):
            for h in range(4):
                pr, half = h // 2, h % 2
                hp = slice(64 * half, 64 * half + 64)
                for c in range(2):
                    pS = psS[c] if NT > 256 else psS[0]
                    off = 0 if NT > 256 else c * NT
                    S.op('pe', lambda e, c=c, pr=pr, hp=hp, pS=pS, off=off: e.matmul(pS[:, off:off + NT], lhsT=Km[hp, pr, c * 128:(c + 1) * 128], rhs=mq_t[hp, pr, 0:NT],
                                                                                     start=True, stop=True),
                         reads=['mq', 'KmT'], writes=[PK[ps.index(pS)]])
                if NT > 256:
                    for c in range(2):
                        S.op('act', lambda e, c=c: e.activation(out=T.PT[:, c * NT:(c + 1) * NT], in_=psS[c][:, 0:NT], func=AF.Exp, scale=SCALE),
                             reads=[PK[ps.index(psS[c])]], writes=[('PTm', c)])
                else:
                    S.op('act', lambda e: e.activation(out=T.PT[:, 0:2 * NT], in_=psS[0][:, 0:2 * NT], func=AF.Exp, scale=SCALE),
                         reads=[PK[ps.index(psS[0])]], writes=[('PTm', 0), ('PTm', 1)])
                for c in range(2):
                    S.op('pe', lambda e, c=c, h=h, pr=pr, hp=hp: e.matmul(psO[pr][hp, 0:NT], lhsT=Vmm[:, c, h * 64:(h + 1) * 64], rhs=T.PT[:, c * NT:(c + 1) * NT],
                                                                          start=(c == 0), stop=(c == 1)),
                         reads=[('PTm', c), 'Vm'], writes=[PK[ps.index(psO[pr])]])
                    S.op('pe', lambda e, c=c, pr=pr, hp=hp: e.matmul(psD[pr][hp, 0:NT], lhsT=onesb[:, 0:64], rhs=T.PT[:, c * NT:(c + 1) * NT],
                                                                     start=(c == 0), stop=(c == 1)),
                         reads=[('PTm', c), 'onesb'], writes=[PK[ps.index(psD[pr])]])
            for pr in range(2):
                S.op('dve', lambda e, pr=pr: e.reciprocal(out=T.rden[:, pr * NT:(pr + 1) * NT], in_=psD[pr][:, 0:NT]), reads=[PK[ps.index(psD[pr])]], writes=[('rden', pr)])
                S.op('dve', lambda e, pr=pr: e.tensor_tensor(out=T.rden[:, pr * NT:(pr + 1) * NT], in0=psO[pr][:, 0:NT], in1=T.rden[:, pr * NT:(pr + 1) * NT], op=ALU.mult),
                     reads=[PK[ps.index(psO[pr])], ('rden', pr)], writes=[('rden', pr)])
                S.op('pool', lambda e, pr=pr: e.tensor_tensor(out=cat[:, 6 + pr, 0:NT], in0=T.rden[:, pr * NT:(pr + 1) * NT], in1=smg_t[:, pr, 0:NT], op=ALU.mult),
                     reads=[('rden', pr), 'smg'], writes=[('catT', 6 + pr)])

        def mem_attention_sample(T, l, smg_t, mq_t, cat, psS, psT, psO, psD):
            NT = NS
            for b in range(NB):
                kvb, bk = T.kvb[b % 2]
                kts, tk = T.kts[b % 2]
                S.dma('pool', kvb[:, :, :], cmk[l, b].rearrange("(c p) x -> p c x", p=128), writes=bk)
                for c in range(2):
                    for pr in range(2):
                        S.op('pe', lambda e, c=c, pr=pr, kvb=kvb: e.matmul(psT[:, pr * 256 + c * 128:pr * 256 + (c + 1) * 128], lhsT=kvb[:, c, pr * 128:(pr + 1) * 128], rhs=identb[:],
                                                                         start=True, stop=True),
                             reads=bk + ['identb'], writes=[PK[ps.index(psT)]])
                evac(b, kts[:, :, :], psT[:, 0:512].rearrange("p (a m) -> p a m", a=2), [PK[ps.index(psT)]], tk)
                for h in range(4):
                    pr, half = h // 2, h % 2
                    hp = slice(64 * half, 64 * half + 64)
                    for c in range(2):
                        pS_h = psS if half == 0 else psT
                        S.op('pe', lambda e, c=c, pr=pr, hp=hp, h=h, kts=kts, b=b, pS_h=pS_h: e.matmul(pS_h[:, (h * 2 + c) * 4:(h * 2 + c) * 4 + 4], lhsT=kts[hp, pr, c * 128:(c + 1) * 128],
                                                                                                 rhs=mq_t[hp, pr, 4 * b:4 * b + 4], start=True, stop=True),
                             reads=tk + ['mq'], writes=[PK[ps.index(pS_h)]])
                for h in range(4):
                    pS_h = psS if h % 2 == 0 else psT
                    S.op('act', lambda e, h=h, pS_h=pS_h: e.activation(out=T.PTs[:, h * 8:h * 8 + 8], in_=pS_h[:, h * 8:h * 8 + 8], func=AF.Exp, scale=SCALE),
                         reads=[PK[ps.index(pS_h)]], writes=['PTs'])
                for h in range(4):
                    pr, half = h // 2, h % 2
                    hp = slice(64 * half, 64 * half + 64)
                    for c in range(2):
                        S.op('pe', lambda e, c=c, h=h, pr=pr, hp=hp, kvb=kvb, b=b: e.matmul(psO[pr][hp, 4 * b:4 * b + 4], lhsT=kvb[:, c, 256 + h * 64:256 + (h + 1) * 64],
                                                                                       rhs=T.PTs[:, (h * 2 + c) * 4:(h * 2 + c) * 4 + 4], start=(c == 0), stop=(c == 1)),
                             reads=bk + ['PTs'], writes=[PK[ps.index(psO[pr])]])
                        S.op('pe', lambda e, c=c, h=h, pr=pr, hp=hp, b=b: e.matmul(psD[pr][hp, 4 * b:4 * b + 4], lhsT=onesb[:, 0:64],
                                                                              rhs=T.PTs[:, (h * 2 + c) * 4:(h * 2 + c) * 4 + 4], start=(c == 0), stop=(c == 1)),
                             reads=['onesb', 'PTs'], writes=[PK[ps.index(psD[pr])]])
            for pr in range(2):
                S.op('dve', lambda e, pr=pr: e.reciprocal(out=T.rden[:, pr * NT:(pr + 1) * NT], in_=psD[pr][:, 0:NT]), reads=[PK[ps.index(psD[pr])]], writes=[('rden', pr)])
                S.op('dve', lambda e, pr=pr: e.tensor_tensor(out=T.rden[:, pr * NT:(pr + 1) * NT], in0=psO[pr][:, 0:NT], in1=T.rden[:, pr * NT:(pr + 1) * NT], op=ALU.mult),
                     reads=[PK[ps.index(psO[pr])], ('rden', pr)], writes=[('rden', pr)])
                S.op('pool', lambda e, pr=pr: e.tensor_tensor(out=cat[:, 6 + pr, 0:NT], in0=T.rden[:, pr * NT:(pr + 1) * NT], in1=smg_t[:, pr, 0:NT], op=ALU.mult),
                     reads=[('rden', pr), 'smg'], writes=[('catT', 6 + pr)])

        def in_proj(T, NT, w_sb, xT_t, u_dst, l):
            for mo_ in range(16):
                pt = ps[mo_ % 2]
                pk = PK[mo_ % 2]
                for kc in range(8):
                    S.op('pe', lambda e, kc=kc, mo_=mo_, pt=pt: e.matmul(pt[:, 0:NT], lhsT=w_sb[:, kc, mo_ * 128:(mo_ + 1) * 128], rhs=xT_t[:, kc, 0:NT],
                                                                         start=(kc == 0), stop=(kc == 7)),
                         reads=[('w_in_sb', kc), ('xT', kc)], writes=[pk])
                if mo_ < 6:
                    evac(mo_, u_dst[:, mo_, 0:NT], pt[:, 0:NT], [pk], [('uTb', mo_)])
                elif mo_ < 12:
                    S.op('act', lambda e, mo_=mo_, pt=pt: e.activation(out=T.sg[:, mo_ - 6, 0:NT], in_=pt[:, 0:NT], func=AF.Silu), reads=[pk], writes=[('sg', mo_ - 6)])
                elif mo_ < 14:
                    evac(mo_ + 1, T.mq[:, mo_ - 12, 0:NT], pt[:, 0:NT], [pk], ['mq'])
                else:
                    S.op('act', lambda e, mo_=mo_, pt=pt: e.activation(out=T.smg[:, mo_ - 14, 0:NT], in_=pt[:, 0:NT], func=AF.Silu), reads=[pk], writes=['smg'])

        def load_x_dma(T, x_src_fn, NT):
            nblk = (NT + 127) // 128
            for a in range(nblk):
                rows = min(128, NT - a * 128)
                S.dma('pool', T.xb[0:rows, a, :], x_src_fn(a, rows), writes=[('xb', a)])

        def load_x_transpose(T, NT):
            nblk = (NT + 127) // 128
            for kc in range(8):
                pt = ps[kc % 2]
                for a in range(nblk):
                    rows = min(128, NT - a * 128)
                    S.op('pe', lambda e: e.matmul(pt[:, a * 128:a * 128 + rows], lhsT=T.xb[0:rows, a, kc * 128:(kc + 1) * 128], rhs=identb[0:rows, 0:rows], start=True, stop=True),
                         reads=[('xb', a), 'identb'], writes=[PK[kc % 2]])
                evac(kc, T.xT[:, kc, 0:NT], pt[:, 0:NT], [PK[kc % 2]], [('xT', kc)])

        def _phase1(es1):
              stgA = sbt(es1, "stgA", [128, 1024])
              stgB = sbt(es1, "stgB", [128, 1024])
              stages = [(stgA, 'stgA'), (stgB, 'stgB')]
              w_in_sb = sbt(es1, "w_in_sb", [128, 8, 2048], BF16)
              w_glu_sb = sbt(es1, "w_glu_sb", [128, 6, 768], BF16)
              w_out_sb = sbt(es1, "w_out_sb", [128, 8, 1024], BF16)
              xb = sbt(es1, "xb", [128, 2, 1024], BF16)

              with ExitStack() as esm:
                  memb = sbt(esm, "memb", [128, 2, 1024], BF16)
                  memT = sbt(esm, "memT", [128, 8, 256], BF16)
                  wm_sb2 = sbt(esm, "wm_sb", [128, 2, 8, 512], BF16)
                  mkv_f = sbt(esm, "mkv_f", [128, 512])
                  for c in range(2):
                      S.dma('pool', memb[:, c, :], memp[c * 128:(c + 1) * 128, :], writes=[('memb', c)])
                  if KCUT <= 1:
                      S.barrier()
                      return
                  for kc in range(8):
                      pt = ps[kc % 2]
                      for c in range(2):
                          S.op('pe', lambda e, kc=kc, c=c, pt=pt: e.matmul(pt[:, c * 128:(c + 1) * 128], lhsT=memb[:, c, kc * 128:(kc + 1) * 128],
                                                                            rhs=identb[:], start=True, stop=True),
                               reads=[('memb', c), 'identb'], writes=[PK[kc % 2]])
                      evac(kc, memT[:, kc, :], pt[:, 0:256], [PK[kc % 2]], [('memT', kc)])
                  if KCUT <= 2:
                      S.barrier()
                      return
                  for l in range(2):
                      for kc in range(8):
                          S.dma('pool', wm_sb2[:, l, kc, :], w_mem[l, kc * 128:(kc + 1) * 128, :], writes=[('wm_sb', l, kc)])
                  load_weight(es1, w_in_sb, lambda kc: w_in[0, kc * 128:(kc + 1) * 128, :], 2048, stages, 'w_in_sb')
                  load_weight(es1, w_glu_sb, lambda kc: w_glu[kc * 128:(kc + 1) * 128, :], 768, stages, 'w_glu_sb')
                  load_weight(es1, w_out_sb, lambda kc: w_out[0, kc * 128:(kc + 1) * 128, :], 1024, stages, 'w_out_sb')
                  for a in range(2):
                      S.dma('pool', xb[:, a, :], x_p[a * 128:(a + 1) * 128, :], writes=[('xb', a)], bg=True)
                  for l in range(2):
                      wm_sb = wm_sb2[:, l, :, :]
                      if KCUT == 30:
                          S.barrier()
                          return
                      for c in range(2):
                          pt = ps[c]
                          for kc in range(8):
                              S.op('pe', lambda e, kc=kc, c=c, pt=pt: e.matmul(pt[:, :], lhsT=memT[:, kc, c * 128:(c + 1) * 128], rhs=wm_sb[:, kc, :],
                                                                                start=(kc == 0), stop=(kc == 7)),
                                   reads=[('memT', kc), ('wm_sb', l, kc)], writes=[PK[c]])
                          if KCUT == 31:
                              S.barrier()
                              return
                          S.op('dve', lambda e, pt=pt: e.tensor_copy(out=mkv_f[:], in_=pt[:, :]), reads=[PK[c]], writes=['mkv_f'])
                          S.op('act', lambda e, pt=pt, l=l, c=c: e.activation(out=Vm[:, l, c, :], in_=pt[:, 256:512], func=AF.Copy),
                               reads=[PK[c]], writes=[('Vm', l)])
                          S.dma('sp', mkv_p[l, c * 128:(c + 1) * 128, :], mkv_f[:], reads=['mkv_f'])
                      if KCUT <= 3:
                          S.barrier()
                          return
                      for pr in range(2):
                          pt = ps[2 + pr]
                          for kc in range(8):
                              S.op('pe', lambda e, kc=kc, pr=pr, pt=pt: e.matmul(pt[:, 0:256], lhsT=wm_sb[:, kc, pr * 128:(pr + 1) * 128], rhs=memT[:, kc, :],
                                                                                  start=(kc == 0), stop=(kc == 7)),
                                   reads=[('memT', kc), ('wm_sb', l, kc)], writes=[PK[2 + pr]])
                          evac(pr, KmT[:, l, pr, :], pt[:, 0:256], [PK[2 + pr]], [('KmT', l)])
                  S.barrier()
              if stage <= 1:
                  return

              S.dma('sp', lnG[:], ln_g[0].partition_broadcast(128), writes=['lnG'])
              S.dma('sp', lnB[:], ln_b[0].partition_broadcast(128), writes=['lnB'])

              COS = sbt(es1, "COS", [128, 48, 128], BF16)
              SIN = sbt(es1, "SIN", [128, 48, 128], BF16)
              Bm = sbt(es1, "Bm", [128, 6, 4, 2, 128], BF16)
              Cm = sbt(es1, "Cm", [128, 48, 2, 64], BF16)
              rdec = sbt(es1, "rdec", [128, 48])
              cosL = sbt(es1, "cosL", [128, 48])
              sinL = sbt(es1, "sinL", [128, 48])
              cos1 = sbt(es1, "cos1", [128, 48])
              sin1 = sbt(es1, "sin1", [128, 48])
              cos3 = sbt(es1, "cos3", [128, 48])
              sin3 = sbt(es1, "sin3", [128, 48])
              dvec = sbt(es1, "dvec", [128, 6])
              bglu = sbt(es1, "bglu", [128, 6])
              swf = sbt(es1, "swf", [128, 128])
              carry = sbt(es1, "carry", [128, 48])
              S.dma('sp', swf[:], c_swap[:, :], writes=['swf'])
              S.dma('sp', dvec[:], ssm_d.rearrange("(m r) -> r m", r=128), writes=['dvec'], allow_slow_non_contiguous=True)
              S.dma('sp', bglu[:], b_glu.rearrange("(m r) -> r m", r=128), writes=['bglu'], allow_slow_non_contiguous=True)

              with ExitStack() as esp:
                  L48 = sbt(esp, "L48", [48, 256])
                  logdt = sbt(esp, "logdt", [128, 48])
                  maskeo = sbt(esp, "maskeo", [128, 4])
                  jvec = sbt(esp, "jvec", [128, 128])
                  v = {n: sbt(esp, "v_" + n, [128, 48]) for n in
                       ['dt', 'lr', 'li', 'a', 'th', 'nr', 'ni', 'den', 'kre', 'kim', 'A1', 'A2', 'nA2', 't0', 't1', 'th64', 'c64', 's64', 'th3']}
                  sc_k = sbt(esp, "sc_k", [128, 1024])
                  sc_i = sbt(esp, "sc_i", [128, 1024], I32)
                  sc_p = sbt(esp, "sc_p", [128, 1024])
                  sc_q = sbt(esp, "sc_q", [128, 1024])
                  PH = sbt(esp, "PH", [128, 8, 128])
                  BreT = sbt(esp, "BreT", [128, 48, 16])
                  BimT = sbt(esp, "BimT", [128, 48, 16])
                  BX1 = sbt(esp, "BX1", [128, 48, 16])
                  BX2 = sbt(esp, "BX2", [128, 48, 16])
                  btmp = sbt(esp, "btmp", [128, 48, 16])
                  Crow = sbt(esp, "Crow", [128, 6, 64])
                  Cirow = sbt(esp, "Cirow", [128, 6, 64])
                  CC1 = sbt(esp, "CC1", [128, 6, 128])
                  CC2 = sbt(esp, "CC2", [128, 6, 128])

                  def D(fn, reads, writes):
                      S.op('dve', fn, reads=reads, writes=writes)

                  def sincos(ph, F, cos_out, sin_out, rk, wk):
                      k_, i_, p_, q_ = sc_k[:, 0:F], sc_i[:, 0:F], sc_p[:, 0:F], sc_q[:, 0:F]
                      D(lambda e: e.tensor_scalar(out=k_, in0=ph, scalar1=1.0 / TWO_PI, scalar2=None, op0=ALU.mult), rk, ['sc_k'])
                      D(lambda e: e.tensor_copy(out=i_, in_=k_), ['sc_k'], ['sc_i'])
                      D(lambda e: e.tensor_copy(out=k_, in_=i_), ['sc_i'], ['sc_k'])
                      D(lambda e: e.scalar_tensor_tensor(out=p_, in0=k_, scalar=-C1_2PI, in1=ph, op0=ALU.mult, op1=ALU.add), ['sc_k'] + rk, ['sc_p'])
                      D(lambda e: e.scalar_tensor_tensor(out=p_, in0=k_, scalar=-C2_2PI, in1=p_, op0=ALU.mult, op1=ALU.add), ['sc_k', 'sc_p'], ['sc_p'])
                      D(lambda e: e.tensor_scalar(out=p_, in0=p_, scalar1=-PI_LO, scalar2=PI_LO, op0=ALU.max, op1=ALU.min), ['sc_p'], ['sc_p'])
                      S.op('act', lambda e: e.activation(out=sin_out, in_=p_, func=AF.Sin), reads=['sc_p'], writes=wk)
                      S.op('act', lambda e: e.activation(out=q_, in_=p_, func=AF.Abs), reads=['sc_p'], writes=['sc_q'])
                      D(lambda e: e.tensor_scalar(out=q_, in0=q_, scalar1=-1.0, scalar2=math.pi / 2, op0=ALU.mult, op1=ALU.add), ['sc_q'], ['sc_q'])
                      S.op('act', lambda e: e.activation(out=cos_out, in_=q_, func=AF.Sin), reads=['sc_q'], writes=wk)

                  S.dma('sp', maskeo[:], c_maskeo[:, :], writes=['maskeo'])
                  S.dma('sp', jvec[:], c_jvec[:, :], writes=['jvec'])
                  for q4, src in enumerate([lam_re, lam_re, lam_im, lam_im]):
                      S.dma('sp', L48[:, q4 * 64:(q4 + 1) * 64], src[:, :], writes=[('L48', q4)])
                  S.dma('sp', logdt[:], log_dt.partition_broadcast(128), writes=['logdt'])
                  S.op('act', lambda e: e.activation(out=v['dt'][:], in_=logdt[:], func=AF.Exp), reads=['logdt'], writes=['v_dt'])
                  S.op('pe', lambda e: e.matmul(ps[7][:, 0:48], lhsT=L48[:, 0:128], rhs=identf[0:48, 0:48], start=True, stop=True),
                       reads=[('L48', 0), ('L48', 1), 'identf'], writes=['ps7'])
                  S.op('pe', lambda e: e.matmul(ps[7][:, 64:112], lhsT=L48[:, 128:256], rhs=identf[0:48, 0:48], start=True, stop=True),
                       reads=[('L48', 2), ('L48', 3), 'identf'], writes=['ps7'])
                  D(lambda e: e.tensor_scalar(out=v['lr'][:], in0=ps[7][:, 0:48], scalar1=-1e-4, scalar2=None, op0=ALU.min), ['ps7'], ['v_lr'])
                  D(lambda e: e.tensor_copy(out=v['li'][:], in_=ps[7][:, 64:112]), ['ps7'], ['v_li'])
                  D(lambda e: e.tensor_tensor(out=v['a'][:], in0=v['lr'][:], in1=v['dt'][:], op=ALU.mult), ['v_lr', 'v_dt'], ['v_a'])
                  D(lambda e: e.tensor_tensor(out=v['th'][:], in0=v['li'][:], in1=v['dt'][:], op=ALU.mult), ['v_li', 'v_dt'], ['v_th'])
                  S.op('act', lambda e: e.activation(out=rdec[:], in_=v['a'][:], func=AF.Exp), reads=['v_a'], writes=['rdec'])
                  sincos(v['th'][:], 48, cos1[:], sin1[:], ['v_th'], ['cs1'])
                  D(lambda e: e.tensor_scalar(out=v['th64'][:], in0=v['th'][:], scalar1=64.0, scalar2=None, op0=ALU.mult), ['v_th'], ['v_th64'])
                  sincos(v['th64'][:], 48, v['c64'][:], v['s64'][:], ['v_th64'], ['cs64'])
                  D(lambda e: e.tensor_scalar(out=v['th3'][:], in0=v['th'][:], scalar1=3.0, scalar2=None, op0=ALU.mult), ['v_th'], ['v_th3'])
                  sincos(v['th3'][:], 48, cos3[:], sin3[:], ['v_th3'], ['cs3'])
                  D(lambda e: e.tensor_tensor(out=v['t0'][:], in0=v['c64'][:], in1=v['c64'][:], op=ALU.mult), ['cs64'], ['v_t0'])
                  D(lambda e: e.tensor_scalar(out=cosL[:], in0=v['t0'][:], scalar1=2.0, scalar2=-1.0, op0=ALU.mult, op1=ALU.add), ['v_t0'], ['cosL'])
                  D(lambda e: e.tensor_tensor(out=v['t0'][:], in0=v['s64'][:], in1=v['c64'][:], op=ALU.mult), ['cs64', 'cosL'], ['v_t0'])
                  D(lambda e: e.tensor_scalar(out=sinL[:], in0=v['t0'][:], scalar1=2.0, scalar2=None, op0=ALU.mult), ['v_t0'], ['sinL'])
                  D(lambda e: e.tensor_tensor(out=v['nr'][:], in0=rdec[:], in1=cos1[:], op=ALU.mult), ['rdec', 'cs1'], ['v_nr'])
                  D(lambda e: e.tensor_scalar(out=v['nr'][:], in0=v['nr'][:], scalar1=-1.0, scalar2=None, op0=ALU.add), ['v_nr'], ['v_nr'])
                  D(lambda e: e.tensor_tensor(out=v['ni'][:], in0=rdec[:], in1=sin1[:], op=ALU.mult), ['rdec', 'cs1'], ['v_ni'])
                  D(lambda e: e.tensor_tensor(out=v['den'][:], in0=v['lr'][:], in1=v['lr'][:], op=ALU.mult), ['v_lr'], ['v_den'])
                  D(lambda e: e.tensor_tensor(out=v['t0'][:], in0=v['li'][:], in1=v['li'][:], op=ALU.mult), ['v_li', 'sinL'], ['v_t0'])
                  D(lambda e: e.tensor_tensor(out=v['den'][:], in0=v['den'][:], in1=v['t0'][:], op=ALU.add), ['v_den', 'v_t0'], ['v_den'])
                  D(lambda e: e.reciprocal(out=v['den'][:], in_=v['den'][:]), ['v_den'], ['v_den'])
                  D(lambda e: e.tensor_tensor(out=v['t0'][:], in0=v['nr'][:], in1=v['lr'][:], op=ALU.mult), ['v_nr', 'v_lr', 'v_den'], ['v_t0'])
                  D(lambda e: e.tensor_tensor(out=v['t1'][:], in0=v['ni'][:], in1=v['li'][:], op=ALU.mult), ['v_ni', 'v_li'], ['v_t1'])
                  D(lambda e: e.tensor_tensor(out=v['kre'][:], in0=v['t0'][:], in1=v['t1'][:], op=ALU.add), ['v_t0', 'v_t1'], ['v_kre'])
                  D(lambda e: e.tensor_tensor(out=v['kre'][:], in0=v['kre'][:], in1=v['den'][:], op=ALU.mult), ['v_kre', 'v_den'], ['v_kre'])
                  D(lambda e: e.tensor_tensor(out=v['t0'][:], in0=v['ni'][:], in1=v['lr'][:], op=ALU.mult), ['v_ni', 'v_lr', 'v_kre'], ['v_t0'])
                  D(lambda e: e.tensor_tensor(out=v['t1'][:], in0=v['nr'][:], in1=v['li'][:], op=ALU.mult), ['v_nr', 'v_li', 'v_kre'], ['v_t1'])
                  D(lambda e: e.tensor_tensor(out=v['kim'][:], in0=v['t0'][:], in1=v['t1'][:], op=ALU.subtract), ['v_t0', 'v_t1'], ['v_kim'])
                  D(lambda e: e.tensor_tensor(out=v['kim'][:], in0=v['kim'][:], in1=v['den'][:], op=ALU.mult), ['v_kim', 'v_den'], ['v_kim'])
                  D(lambda e: e.tensor_copy(out=v['A1'][0:64, :], in_=v['kre'][0:64, :]), ['v_kre'], ['v_A1'])
                  D(lambda e: e.tensor_copy(out=v['A1'][64:128, :], in_=v['kim'][64:128, :]), ['v_kim', 'v_A1'], ['v_A1'])
                  D(lambda e: e.tensor_scalar(out=v['A2'][0:64, :], in0=v['kim'][0:64, :], scalar1=-1.0, scalar2=None, op0=ALU.mult), ['v_kim'], ['v_A2'])
                  D(lambda e: e.tensor_copy(out=v['A2'][64:128, :], in_=v['kre'][64:128, :]), ['v_kre', 'v_A2'], ['v_A2'])
                  for hh in range(2):
                      S.dma('sp', BreT[hh * 64:(hh + 1) * 64, :, :], b_re.rearrange("g p c -> p g c"), writes=[('BreT', hh)])
                      S.dma('sp', BimT[hh * 64:(hh + 1) * 64, :, :], b_im.rearrange("g p c -> p g c"), writes=[('BimT', hh)])
                  A1b = v['A1'][:].unsqueeze(2).to_broadcast([128, 48, 16])
                  A2b = v['A2'][:].unsqueeze(2).to_broadcast([128, 48, 16])
                  rB = [('BreT', 0), ('BreT', 1), ('BimT', 0), ('BimT', 1), 'v_A1', 'v_A2']
                  D(lambda e: e.tensor_tensor(out=BX1[:], in0=BreT[:], in1=A1b, op=ALU.mult), rB, ['BX1'])
                  D(lambda e: e.tensor_tensor(out=btmp[:], in0=BimT[:], in1=A2b, op=ALU.mult), rB, ['btmp'])
                  D(lambda e: e.tensor_tensor(out=BX1[:], in0=BX1[:], in1=btmp[:], op=ALU.add), ['BX1', 'btmp'], ['BX1'])
                  D(lambda e: e.tensor_tensor(out=BX2[:], in0=BimT[:], in1=A1b, op=ALU.mult), rB, ['BX2'])
                  D(lambda e: e.tensor_tensor(out=btmp[:], in0=BreT[:], in1=A2b, op=ALU.mult), rB + ['BX1'], ['btmp'])
                  D(lambda e: e.tensor_tensor(out=BX2[:], in0=BX2[:], in1=btmp[:], op=ALU.subtract), ['BX2', 'btmp'], ['BX2'])
                  for m in range(6):
                      for xi, BX in enumerate([BX1, BX2]):
                          pt = ps[(2 * m + xi) % 2]
                          pk = PK[(2 * m + xi) % 2]
                          S.op('pe', lambda e, m=m, BX=BX, pt=pt: e.matmul(pt[:, 0:128], lhsT=BX[:, 8 * m:8 * m + 8, :], rhs=identf[:], start=True, stop=True),
                               reads=['BX1', 'BX2', 'identf'], writes=[pk])
                          for mem in range(4):
                              D(lambda e, m=m, xi=xi, mem=mem, pt=pt: e.tensor_scalar(out=Bm[:, m, mem, xi, :], in0=pt[:, 0:128], scalar1=maskeo[:, mem:mem + 1],
                                                                                     scalar2=None, op0=ALU.mult), [pk, 'maskeo'], ['Bm'])
                  S.dma('sp', Crow[:], c_re.rearrange("(m r) p -> r m p", r=128), writes=['Crow'])
                  S.dma('sp', Cirow[:], c_im.rearrange("(m r) p -> r m p", r=128), writes=['Cirow'])
                  D(lambda e: e.tensor_copy(out=CC1[:, :, 0:64], in_=Crow[:]), ['Crow'], ['CC1'])
                  D(lambda e: e.tensor_scalar(out=CC1[:, :, 64:128], in0=Cirow[:], scalar1=-1.0, scalar2=None, op0=ALU.mult), ['Cirow', 'CC1'], ['CC1'])
                  D(lambda e: e.tensor_scalar(out=CC2[:, :, 0:64], in0=Cirow[:], scalar1=-1.0, scalar2=None, op0=ALU.mult), ['Cirow'], ['CC2'])
                  D(lambda e: e.tensor_scalar(out=CC2[:, :, 64:128], in0=Crow[:], scalar1=-1.0, scalar2=None, op0=ALU.mult), ['Crow', 'CC2'], ['CC2'])
                  S.op('pool', lambda e: e.memset(Cm[:], 0.0), writes=['Cm'])
                  for m in range(6):
                      for xi, CC in enumerate([CC1, CC2]):
                          pt = ps[(2 * m + xi) % 2]
                          pk = PK[(2 * m + xi) % 2]
                          S.op('pe', lambda e, m=m, CC=CC, pt=pt: e.matmul(pt[:, 0:128], lhsT=CC[:, m, :], rhs=identf[:], start=True, stop=True),
                               reads=['CC1', 'CC2', 'identf'], writes=[pk])
                          for par in range(4):
                              dst = Cm[:, 8 * m:8 * m + 8, xi, :].rearrange("p (q r) c -> p q r c", r=4)[:, :, par, par * 16:(par + 1) * 16]
                              srcv = pt[:, 0:128].rearrange("p (q r c) -> p q r c", r=4, c=16)[:, :, par, :]
                              D(lambda e, dst=dst, srcv=srcv: e.tensor_copy(out=dst, in_=srcv), [pk, 'Cm'], ['Cm'])
                  for m in range(6):
                      thb = v['th'][:, 8 * m:8 * m + 8].unsqueeze(2).to_broadcast([128, 8, 128])
                      jb = jvec[:].unsqueeze(1).to_broadcast([128, 8, 128])
                      D(lambda e, thb=thb, jb=jb: e.tensor_tensor(out=PH[:], in0=thb, in1=jb, op=ALU.mult), ['v_th', 'jvec'], ['PH'])
                      sincos(PH[:].rearrange("p g j -> p (g j)"), 1024,
                             COS[:, 8 * m:8 * m + 8, :].rearrange("p g j -> p (g j)"),
                             SIN[:, 8 * m:8 * m + 8, :].rearrange("p g j -> p (g j)"), ['PH'], [('TAB', m)])
                  S.op('dve', lambda e: e.memset(carry[:], 0.0), writes=['carry'])
                  S.barrier()
              if stage <= 2:
                  return

              NTM = 256
              xT = sbt(es1, "xT", [128, 8, NTM], BF16)
              uTb = sbt(es1, "uTb", [128, 6, NTM], BF16)
              sg = sbt(es1, "sg", [128, 6, NTM], BF16)
              mq = sbt(es1, "mq", [128, 2, NTM], BF16)
              smg = sbt(es1, "smg", [128, 2, NTM], BF16)
              ygb = sbt(es1, "ygb", [128, 6, NTM], BF16)
              t1b = [sbt(es1, "t1b%d" % i, [128, 2 * NTM]) for i in range(2)]
              t2b = [sbt(es1, "t2b%d" % i, [128, 2 * NTM]) for i in range(2)]
              mbufs = [sbt(es1, "mbufA", [128, 8, NTM]), sbt(es1, "mbufB", [128, 8, NTM])]
              d1b = [sbt(es1, "d1b%d" % i, [128, 2 * NTM], BF16) for i in range(2)]
              d2b = [sbt(es1, "d2b%d" % i, [128, 2 * NTM], BF16) for i in range(2)]
              ypre = sbt(es1, "ypre", [128, NTM])
              sig = [sbt(es1, "sig%d" % i, [128, NTM]) for i in range(2)]
              catT = sbt(es1, "catT", [128, 8, NTM], BF16)
              PT = sbt(es1, "PTm", [128, 2 * NTM], BF16)
              rden = sbt(es1, "rden", [128, 2 * NTM])
              mo = rden
              xres = [stgA, stgB]
              zt = [sbt(es1, "zt%d" % i, [128, 1024]) for i in range(2)]
              zn = zt
              stats = [sbt(es1, "stats%d" % i, [128, 2, 6]) for i in range(2)]
              mv = [sbt(es1, "mv%d" % i, [128, 2]) for i in range(2)]
              rstd = [sbt(es1, "rstd%d" % i, [128, 1]) for i in range(2)]
              nmr = [sbt(es1, "nmr%d" % i, [128, 1]) for i in range(2)]
              x1b = [sbt(es1, "x1b%d" % i, [128, 1024], BF16) for i in range(2)]
              x1T = sbt(es1, "x1T", [128, 8, 128], BF16)
              cl8 = sbt(es1, "cl8", [128, 8])
              ct8 = sbt(es1, "ct8", [128, 8])
              hl = sbt(es1, "hl", [128, 48])
              hlT = sbt(es1, "hlT", [48, 128])

              epsb = sbt(es1, "epsb", [128, 1])
              S.op('dve', lambda e: e.memset(epsb[:], LN_EPS), writes=['epsb'])
              PTs = sbt(es1, "PTs", [128, 32], BF16)
              mA = mbufs[0][:].rearrange("p g n -> p (g n)")
              mB = mbufs[1][:].rearrange("p g n -> p (g n)")
              h0rows = mA[:, 0:768].rearrange("p (c x) -> p c x", c=6)
              h0T = mA[:, 768:1536]
              t1s = mA[:, 1536:2048]
              ginit = mB[:, 0:768]
              g3all = mB[:, 768:1536]
              t2s = mB[:, 1536:2048]
              tm8 = sbt(es1, "tm8", [128, 128])
              d1s = sbt(es1, "d1s", [128, 512], BF16)
              d2s = sbt(es1, "d2s", [128, 512], BF16)
              T1 = SimpleNamespace(xres=xres, zt=zt, zn=zn, stats=stats, mv=mv, rstd=rstd, nmr=nmr, epsb=epsb, x1b=x1b, x1T=x1T, PT=PT, rden=rden, mo=mo,
                                   sg=sg, mq=mq, smg=smg, stages=stages, xb=xb, xT=xT, PTs=PTs,
                                   kvb=[(xb[:, i, :].rearrange("p (c x) -> p c x", c=2), [('xb', i)]) for i in range(2)],
                                   kts=[(xT[:, 2 * i:2 * i + 2, :], [('xT', 2 * i), ('xT', 2 * i + 1)]) for i in range(2)])

              tiles = [(i * NTM, NTM, False) for i in range(SEQ // NTM)]
              if stage >= 5:
                  tiles.append((SEQ, NS, True))

              def xsrc_of(ti_):
                  tk0, _, iss = tiles[ti_]
                  if iss:
                      return lambda a, rows: x_s[0:rows, :]
                  return lambda a, rows: x_p[tk0 + a * 128:tk0 + a * 128 + rows, :]

              for ti, (tok0, NT, is_s) in enumerate(tiles):
                  issue_bulk(7)
                  if is_s:
                      S.barrier()
                  load_x_transpose(T1, NT)
                  if ti + 1 < len(tiles):
                      load_x_dma(T1, xsrc_of(ti + 1), tiles[ti + 1][1])
                  in_proj(T1, NT, w_in_sb, xT, uTb, 0)
                  nchunk = NT // 128
                  if not is_s:
                      def phaseA(m, kk):
                          q = kk // 2
                          bi = kk % 2
                          rows = slice(64 * q, 64 * q + 64)
                          pX1, pX2 = ps[2 + bi], ps[4 + bi]
                          mb = mbufs[m % 2]
                          for j, gl in enumerate((2 * kk, 2 * kk + 1)):
                              mem = gl % 4
                              S.op('pe', lambda e: e.matmul(pX1[:, j * NT:(j + 1) * NT], lhsT=Bm[rows, m, mem, 0, :], rhs=uTb[rows, m, 0:NT], start=True, stop=True),
                                   reads=['Bm', ('uTb', m)], writes=[PK[2 + bi]])
                              S.op('pe', lambda e: e.matmul(pX2[:, j * NT:(j + 1) * NT], lhsT=Bm[rows, m, mem, 1, :], rhs=uTb[rows, m, 0:NT], start=True, stop=True),
                                   reads=['Bm', ('uTb', m)], writes=[PK[4 + bi]])
                          g0 = 8 * m + 2 * kk
                          cosb = COS[:, g0:g0 + 2, :].unsqueeze(2).to_broadcast([128, 2, nchunk, 128])
                          sinb = SIN[:, g0:g0 + 2, :].unsqueeze(2).to_broadcast([128, 2, nchunk, 128])
                          v = lambda ap: ap.rearrange("p (g k j) -> p g k j", g=2, j=128)
                          S.op('dve', lambda e: e.tensor_tensor(out=v(t1b[bi][:, 0:2 * NT]), in0=v(pX1[:, 0:2 * NT]), in1=cosb, op=ALU.mult),
                               reads=[PK[2 + bi], ('TAB', m)], writes=['t1b%d' % bi])
                          S.op('dve', lambda e: e.tensor_tensor(out=v(t2b[bi][:, 0:2 * NT]), in0=v(pX2[:, 0:2 * NT]), in1=sinb, op=ALU.mult),
                               reads=[PK[4 + bi], ('TAB', m)], writes=['t2b%d' % bi])
                          S.op('pool', lambda e: e.tensor_tensor(out=mb[:, 2 * kk:2 * kk + 2, 0:NT], in0=t1b[bi][:, 0:2 * NT].rearrange("p (g n) -> p g n", g=2),
                                                                                     in1=t2b[bi][:, 0:2 * NT].rearrange("p (g n) -> p g n", g=2), op=ALU.add),
                               reads=['t1b%d' % bi, 't2b%d' % bi], writes=[('mbuf', m % 2, 2 * kk), ('mbuf', m % 2, 2 * kk + 1)])

                      def phaseB_scan(m, k):
                          mb = mbufs[m % 2]
                          for gl in range(8):
                              g = 8 * m + gl
                              S.op('dve', lambda e, gl=gl, g=g: e.tensor_tensor_scan(out=mb[:, gl, k * 128:(k + 1) * 128],
                                                                                  data0=rdec[:, g:g + 1].to_broadcast([128, 128]),
                                                                                  data1=mb[:, gl, k * 128:(k + 1) * 128],
                                                                                  initial=carry[:, g:g + 1], op0=ALU.mult, op1=ALU.add),
                                   reads=[('mbuf', m % 2, gl), 'rdec', ('carry', m)], writes=[('mbuf', m % 2, gl)])
                          S.op('dve', lambda e: e.tensor_copy(out=cl8[:], in_=mb[:, :, k * 128 + 127]), reads=[('mbuf', m % 2, gl) for gl in range(8)], writes=['cl8'])
                          S.op('pe', lambda e: e.matmul(ps[7][:, 0:8], lhsT=swf[:], rhs=cl8[:], start=True, stop=True), reads=['swf', 'cl8'], writes=['ps7'])

                      def phaseB_carry(m, k):
                          S.op('dve', lambda e: e.tensor_tensor(out=ct8[:], in0=ps[7][:, 0:8], in1=sinL[:, 8 * m:8 * m + 8], op=ALU.mult), reads=['ps7', 'sinL'], writes=['ct8'])
                          S.op('dve', lambda e: e.tensor_tensor(out=cl8[:], in0=cl8[:], in1=cosL[:, 8 * m:8 * m + 8], op=ALU.mult), reads=['cl8', 'cosL'], writes=['cl8'])
                          S.op('dve', lambda e: e.tensor_tensor(out=carry[:, 8 * m:8 * m + 8], in0=cl8[:], in1=ct8[:], op=ALU.add), reads=['cl8', 'ct8'], writes=[('carry', m)])

                      def phaseC(m):
                          mb = mbufs[m % 2]
                          for kk in range(4):
                              q = kk // 2
                              bi = kk % 2
                              g0 = 8 * m + 2 * kk
                              cosb = COS[:, g0:g0 + 2, :].unsqueeze(2).to_broadcast([128, 2, nchunk, 128])
                              sinb = SIN[:, g0:g0 + 2, :].unsqueeze(2).to_broadcast([128, 2, nchunk, 128])
                              v = lambda ap: ap.rearrange("p (g k j) -> p g k j", g=2, j=128)
                              mv_ = mb[:, 2 * kk:2 * kk + 2, 0:NT].rearrange("p g (k j) -> p g k j", j=128)
                              mk = [('mbuf', m % 2, 2 * kk), ('mbuf', m % 2, 2 * kk + 1)]
                              S.op('pool', lambda e: e.tensor_tensor(out=v(d1b[bi][:, 0:2 * NT]), in0=mv_, in1=cosb, op=ALU.mult), reads=mk + [('TAB', m)], writes=['d1b%d' % bi])
                              S.op('pool', lambda e: e.tensor_tensor(out=v(d2b[bi][:, 0:2 * NT]), in0=mv_, in1=sinb, op=ALU.mult), reads=mk + [('TAB', m)], writes=['d2b%d' % bi])
                              for j, gl in enumerate((2 * kk, 2 * kk + 1)):
                                  g = 8 * m + gl
                                  S.op('pe', lambda e: e.matmul(ps[6][64 * q:64 * q + 64, 0:NT], lhsT=Cm[:, g, 0, :], rhs=d1b[bi][:, j * NT:(j + 1) * NT], start=(gl % 4 == 0), stop=False),
                                       reads=['Cm', 'd1b%d' % bi], writes=['ps6'])
                                  S.op('pe', lambda e: e.matmul(ps[6][64 * q:64 * q + 64, 0:NT], lhsT=Cm[:, g, 1, :], rhs=d2b[bi][:, j * NT:(j + 1) * NT], start=False, stop=(gl % 4 == 3)),
                                       reads=['Cm', 'd2b%d' % bi], writes=['ps6'])

                      def phaseD(m):
                          S.op('dve', lambda e: e.scalar_tensor_tensor(out=ypre[:, 0:NT], in0=uTb[:, m, 0:NT], scalar=dvec[:, m:m + 1], in1=ps[6][:, 0:NT], op0=ALU.mult, op1=ALU.add),
                               reads=[('uTb', m), 'dvec', 'ps6'], writes=['ypre'])
                          S.op('act', lambda e: e.activation(out=ygb[:, m, 0:NT], in_=ypre[:, 0:NT], func=AF.Gelu_apprx_tanh), reads=['ypre'], writes=[('ygb', m)])

                      for kk in range(4):
                          phaseA(0, kk)
                      per = 4 // nchunk
                      for m in range(6):
                          for k in range(nchunk):
                              phaseB_scan(m, k)
                              if m + 1 < 6:
                                  for kk in range(k * per, (k + 1) * per):
                                      phaseA(m + 1, kk)
                              phaseB_carry(m, k)
                              if k == nchunk - 1 and m >= 1:
                                  phaseD(m - 1)
                          phaseC(m)
                      phaseD(5)
                  elif KCUT != 51 and KCUT != 54:
                      S.dma('sp', h0rows[:, :, 0:64], st_re.rearrange("(c p) e -> p c e", p=128), writes=['h0rows'])
                      S.dma('sp', h0rows[:, :, 64:128], st_im.rearrange("(c p) e -> p c e", p=128), writes=['h0rows'])
                      for c6 in range(6):
                          pt = ps[c6 // 4]
                          S.op('pe', lambda e, c6=c6, pt=pt: e.matmul(pt[:, (c6 % 4) * 128:(c6 % 4 + 1) * 128], lhsT=h0rows[:, c6, :], rhs=identf[:], start=True, stop=True),
                               reads=['h0rows', 'identf'], writes=[PK[c6 // 4]])
                      S.op('dve', lambda e: e.tensor_copy(out=h0T[:, 0:512], in_=ps[0][:, :]), reads=['ps0'], writes=['h0T'])
                      S.op('dve', lambda e: e.tensor_copy(out=h0T[:, 512:768], in_=ps[1][:, 0:256]), reads=['ps1', 'h0T'], writes=['h0T'])

                      def rotate(dst, src, cs, sn, srck, dstk):
                          csb = cs[:].unsqueeze(1).to_broadcast([128, NB, 48])
                          snb = sn[:].unsqueeze(1).to_broadcast([128, NB, 48])
                          for hh in range(2):
                              S.op('pe', lambda e, hh=hh: e.matmul(ps[2 + hh][:, 0:384], lhsT=swf[:], rhs=src[:, hh * 384:(hh + 1) * 384], start=True, stop=True),
                                   reads=['swf', srck], writes=[PK[2 + hh]])
                              S.op('dve', lambda e, hh=hh: e.tensor_tensor(out=dst[:, hh * 384:(hh + 1) * 384].rearrange("p (b g) -> p b g", g=48),
                                                                         in0=ps[2 + hh][:, 0:384].rearrange("p (b g) -> p b g", g=48),
                                                                         in1=snb[:, 8 * hh:8 * hh + 8, :], op=ALU.mult), reads=[PK[2 + hh], 'cs1', 'cs3'], writes=[dstk])
                          S.op('pool', lambda e: e.tensor_tensor(out=src[:].rearrange("p (b g) -> p b g", g=48), in0=src[:].rearrange("p (b g) -> p b g", g=48), in1=csb, op=ALU.mult),
                               reads=[srck, 'cs1', 'cs3'], writes=[srck])
                          S.op('dve', lambda e: e.tensor_tensor(out=dst[:], in0=dst[:], in1=src[:], op=ALU.add), reads=[srck, dstk], writes=[dstk])

                      rotate(ginit, h0T, cos1, sin1, 'h0T', 'ginit')
                      gin3 = ginit[:].rearrange("p (b g) -> p b g", g=48)
                      g3v = g3all[:].rearrange("p (b g) -> p b g", g=48)
                      for m in range(6):
                          for gl in range(8):
                              q, mem = gl // 4, gl % 4
                              rows = slice(64 * q, 64 * q + 64)
                              S.op('pe', lambda e, m=m, mem=mem, rows=rows, q=q: e.matmul(ps[2 + q][:, mem * 64:(mem + 1) * 64], lhsT=Bm[rows, m, mem, 0, :], rhs=uTb[rows, m, 0:NS], start=True, stop=True),
                                   reads=['Bm', ('uTb', m)], writes=[PK[2 + q]])
                              S.op('pe', lambda e, m=m, mem=mem, rows=rows, q=q: e.matmul(ps[4 + q][:, mem * 64:(mem + 1) * 64], lhsT=Bm[rows, m, mem, 1, :], rhs=uTb[rows, m, 0:NS], start=True, stop=True),
                                   reads=['Bm', ('uTb', m)], writes=[PK[4 + q]])
                          cos4 = COS[:, 8 * m:8 * m + 8, 0:4].unsqueeze(2).to_broadcast([128, 8, NB, 4])
                          sin4 = SIN[:, 8 * m:8 * m + 8, 0:4].unsqueeze(2).to_broadcast([128, 8, NB, 4])
                          v4 = lambda ap: ap.rearrange("p (g b t) -> p g b t", g=8, t=4)
                          vh = lambda ap: ap.rearrange("p (g b t) -> p g b t", g=4, t=4)
                          for q in range(2):
                              S.op('dve', lambda e, cos4=cos4, q=q: e.tensor_tensor(out=vh(t1s[:, q * 256:(q + 1) * 256]), in0=vh(ps[2 + q][:, 0:256]), in1=cos4[:, 4 * q:4 * q + 4], op=ALU.mult),
                                   reads=[PK[2 + q], ('TAB', m)], writes=['t1s'])
                              S.op('dve', lambda e, sin4=sin4, q=q: e.tensor_tensor(out=vh(t2s[:, q * 256:(q + 1) * 256]), in0=vh(ps[4 + q][:, 0:256]), in1=sin4[:, 4 * q:4 * q + 4], op=ALU.mult),
                                   reads=[PK[4 + q], ('TAB', m)], writes=['t2s'])
                          S.op('pool', lambda e: e.tensor_tensor(out=t1s[:], in0=t1s[:], in1=t2s[:], op=ALU.add), reads=['t1s', 't2s'], writes=['t1s'])
                          rb = rdec[:, 8 * m:8 * m + 8].unsqueeze(2).to_broadcast([128, 8, NB])
                          tm3 = tm8[:].rearrange("p (g b) -> p g b", g=8)
                          for t in range(4):
                              prev = gin3[:, :, 8 * m:8 * m + 8].rearrange("p b g -> p g b") if t == 0 else v4(t1s[:])[:, :, :, t - 1]
                              S.op('dve', lambda e, prev=prev, rb=rb: e.tensor_tensor(out=tm3, in0=prev, in1=rb, op=ALU.mult), reads=['t1s', 'ginit', 'rdec'], writes=['tm8'])
                              S.op('dve', lambda e, t=t: e.tensor_tensor(out=v4(t1s[:])[:, :, :, t], in0=v4(t1s[:])[:, :, :, t], in1=tm3, op=ALU.add), reads=['t1s', 'tm8'], writes=['t1s'])
                          S.op('dve', lambda e, m=m: e.tensor_copy(out=g3v[:, :, 8 * m:8 * m + 8].rearrange("p b g -> p g b"), in_=v4(t1s[:])[:, :, :, 3]), reads=['t1s'], writes=['g3all'])
                          S.op('pool', lambda e, cos4=cos4: e.tensor_tensor(out=v4(d1s[:]), in0=v4(t1s[:]), in1=cos4, op=ALU.mult), reads=['t1s', ('TAB', m)], writes=['d1s'])
                          S.op('pool', lambda e, sin4=sin4: e.tensor_tensor(out=v4(d2s[:]), in0=v4(t1s[:]), in1=sin4, op=ALU.mult), reads=['t1s', ('TAB', m)], writes=['d2s'])
                          for gl in range(8):
                              g = 8 * m + gl
                              q = gl // 4
                              S.op('pe', lambda e, g=g, q=q, gl=gl: e.matmul(ps[6][64 * q:64 * q + 64, 0:NS], lhsT=Cm[:, g, 0, :], rhs=d1s[:, gl * 64:(gl + 1) * 64], start=(gl % 4 == 0), stop=False),
                                   reads=['Cm', 'd1s'], writes=['ps6'])
                              S.op('pe', lambda e, g=g, q=q, gl=gl: e.matmul(ps[6][64 * q:64 * q + 64, 0:NS], lhsT=Cm[:, g, 1, :], rhs=d2s[:, gl * 64:(gl + 1) * 64], start=False, stop=(gl % 4 == 3)),
                                   reads=['Cm', 'd2s'], writes=['ps6'])
                          S.op('dve', lambda e, m=m: e.scalar_tensor_tensor(out=ypre[:, 0:NS], in0=uTb[:, m, 0:NS], scalar=dvec[:, m:m + 1], in1=ps[6][:, 0:NS], op0=ALU.mult, op1=ALU.add),
                               reads=[('uTb', m), 'dvec', 'ps6'], writes=['ypre'])
                          S.op('act', lambda e, m=m: e.activation(out=ygb[:, m, 0:NS], in_=ypre[:, 0:NS], func=AF.Gelu_apprx_tanh), reads=['ypre'], writes=[('ygb', m)])
                      rotate(h0T, g3all, cos3, sin3, 'g3all', 'h0T')
                      for c6 in range(6):
                          pt = ps[c6 // 4]
                          S.op('pe', lambda e, c6=c6, pt=pt: e.matmul(pt[:, (c6 % 4) * 128:(c6 % 4 + 1) * 128], lhsT=h0T[:, c6 * 128:(c6 + 1) * 128], rhs=identf[:], start=True, stop=True),
                               reads=['h0T', 'identf'], writes=[PK[c6 // 4]])
                      S.op('dve', lambda e: e.tensor_copy(out=h0rows[:, 0:4, :], in_=ps[0][:, :].rearrange("p (c x) -> p c x", c=4)), reads=['ps0'], writes=['h0rows'])
                      S.op('dve', lambda e: e.tensor_copy(out=h0rows[:, 4:6, :], in_=ps[1][:, 0:256].rearrange("p (c x) -> p c x", c=2)), reads=['ps1', 'h0rows'], writes=['h0rows'])
                      S.dma('sp', sre_s.rearrange("(c p) e -> p c e", p=128), h0rows[:, :, 0:64], reads=['h0rows'])
                      S.dma('sp', sim_s.rearrange("(c p) e -> p c e", p=128), h0rows[:, :, 64:128], reads=['h0rows'])
                  for m2 in range(6):
                      pt = ps[m2 % 2]
                      pk = PK[m2 % 2]
                      for m in range(6):
                          S.op('pe', lambda e, m=m, m2=m2, pt=pt: e.matmul(pt[:, 0:NT], lhsT=w_glu_sb[:, m, m2 * 128:(m2 + 1) * 128], rhs=ygb[:, m, 0:NT], start=(m == 0), stop=(m == 5)),
                               reads=[('w_glu_sb', m), ('ygb', m)], writes=[pk])
                      sg_i = sig[m2 % 2]
                      sgk = 'sig%d' % (m2 % 2)
                      S.op('act', lambda e, m2=m2, pt=pt, sg_i=sg_i: e.activation(out=sg_i[:, 0:NT], in_=pt[:, 0:NT], func=AF.Sigmoid, bias=bglu[:, m2:m2 + 1], scale=1.0),
                           reads=[pk, 'bglu'], writes=[sgk])
                      S.op('dve', lambda e, m2=m2, sg_i=sg_i: e.tensor_tensor(out=sg_i[:, 0:NT], in0=sg_i[:, 0:NT], in1=ygb[:, m2, 0:NT], op=ALU.mult), reads=[sgk, ('ygb', m2)], writes=[sgk])
                      S.op('pool', lambda e, m2=m2, sg_i=sg_i: e.tensor_tensor(out=catT[:, m2, 0:NT], in0=sg_i[:, 0:NT], in1=sg[:, m2, 0:NT], op=ALU.mult),
                           reads=[sgk, ('sg', m2)], writes=[('catT', m2)])
                  if is_s and KCUT != 52 and KCUT != 54:
                      mem_attention_sample(T1, 0, smg, mq, catT, ps[2], ps[3], [ps[4], ps[6]], [ps[5], ps[7]])
                  if not is_s:
                      mem_attention(T1, 0, NT, mq, smg, catT, KmT[:, 0, :, :], Vm[:, 0, :, :], [ps[2], ps[3]], [ps[4], ps[6]], [ps[5], ps[7]])
                  nblk = (NT + 127) // 128
                  blks = []
                  for a in range(nblk):
                      rows = min(128, NT - a * 128)
                      t0 = tok0 + a * 128
                      xsrc = x_s[0:rows, :] if is_s else x_p[t0:t0 + rows, :]
                      blks.append((a, rows, xsrc, t0, x1scr[t0:t0 + rows, :]))
                  ln_blocks(T1, blks, catT, w_out_sb, True)

              S.op('pe', lambda e: e.matmul(ps[7][:, 0:48], lhsT=swf[:], rhs=carry[:], start=True, stop=True), reads=['swf'] + [('carry', m) for m in range(6)], writes=['ps7'])
              S.op('dve', lambda e: e.tensor_tensor(out=hl[:], in0=ps[7][:, 0:48], in1=sin1[:], op=ALU.mult), reads=['ps7', 'cs1'], writes=['hl'])
              S.op('dve', lambda e: e.tensor_tensor(out=carry[:], in0=carry[:], in1=cos1[:], op=ALU.mult), reads=[('carry', m) for m in range(6)] + ['cs1'], writes=[('carry', m) for m in range(6)])
              S.op('dve', lambda e: e.tensor_tensor(out=hl[:], in0=carry[:], in1=hl[:], op=ALU.subtract), reads=[('carry', m) for m in range(6)] + ['hl'], writes=['hl'])
              S.op('pe', lambda e: e.matmul(ps[7][0:48, 128:256], lhsT=hl[:], rhs=identf[:], start=True, stop=True), reads=['hl', 'identf'], writes=['ps7'])
              S.op('dve', lambda e: e.tensor_copy(out=hlT[:], in_=ps[7][0:48, 128:256]), reads=['ps7'], writes=['hlT'])
              S.dma('sp', sre_p[:, :], hlT[:, 0:64], reads=['hlT'])
              S.dma('sp', sim_p[:, :], hlT[:, 64:128], reads=['hlT'])
              S.barrier()

        with ExitStack() as es1:
            _phase1(es1)

        def _phase2(es2):
            KT = sbt(es2, "KT", [128, 6, NTOK], BF16)
            Vg = sbt(es2, "Vg", [128, 3, 16, 256], BF16)
            ntok_eff = NTOK if stage >= 5 else SEQ
            Vnew = sbt(es2, "Vnew", [64, 3, 256], BF16)
            with ExitStack() as esa:
                x1Tf = sbt(esa, "x1Tf", [128, 8, NTOK], BF16)
                w_kvt = sbt(esa, "w_kvt", [128, 8, 3, 512], BF16)
                kvo = [sbt(esa, "kvo%d" % i, [128, 512]) for i in range(2)]
                for kc in range(8):
                    S.dma('sp', x1Tf[:, kc, 0:ntok_eff], x1Tscr[kc, :, 0:ntok_eff], writes=[('x1Tf', kc)])
                for kc in range(8):
                    S.dma('pool', w_kvt[:, kc, :, 0:256], w_kv[kc * 128:(kc + 1) * 128, 0:768].rearrange("p (g c) -> p g c", g=3), writes=[('w_kvt', kc)])
                    S.dma('pool', w_kvt[:, kc, :, 256:512], w_kv[kc * 128:(kc + 1) * 128, 768:1536].rearrange("p (g c) -> p g c", g=3), writes=[('w_kvt', kc)])
                x1k = [('x1Tf', kc) for kc in range(8)]
                wk = [('w_kvt', kc) for kc in range(8)]
                cnt = 0
                for t0 in range(0, ntok_eff, 512):
                    n = min(512, ntok_eff - t0)
                    for m in range(6):
                        g, pr = m // 2, m % 2
                        pt = ps[cnt % 2]
                        for kc in range(8):
                            S.op('pe', lambda e, kc=kc, g=g, pr=pr, pt=pt, t0=t0, n=n: e.matmul(pt[:, 0:n], lhsT=w_kvt[:, kc, g, pr * 128:(pr + 1) * 128], rhs=x1Tf[:, kc, t0:t0 + n],
                                                                                              start=(kc == 0), stop=(kc == 7)),
                                 reads=[x1k[kc], wk[kc]], writes=[PK[cnt % 2]])
                        evac(cnt, KT[:, m, t0:t0 + n], pt[:, 0:n], [PK[cnt % 2]], [('KT', m)])
                        cnt += 1
                cnt = 0
                for g in range(3):
                    for bi in range(16):
                        if g == 0:
                            tsel = lambda kc, bi=bi: x1Tf[:, kc, 128 * bi:128 * bi + 128]
                            needK = (bi == 15)
                            odst = dkv_p[0][0:128, :]
                        elif g == 1:
                            jb, r = bi // 4, bi % 4
                            tsel = lambda kc, jb=jb, r=r: x1Tf[:, kc, 512 * jb:512 * jb + 512].rearrange("p (u r) -> p r u", r=4)[:, r, :]
                            needK = (jb == 3)
                            odst = dkv_p[1].rearrange("(u r) c -> r u c", r=4)[r]
                        else:
                            r = bi
                            tsel = lambda kc, r=r: x1Tf[:, kc, 0:2048].rearrange("p (u r) -> p r u", r=16)[:, r, :]
                            needK = True
                            odst = dkv_p[2].rearrange("(u r) c -> r u c", r=16)[r]
                        c0 = 0 if needK else 256
                        pt = ps[2 + cnt % 2]
                        pk = PK[2 + cnt % 2]
                        for kc in range(8):
                            S.op('pe', lambda e, kc=kc, g=g, pt=pt, tsel=tsel, c0=c0: e.matmul(pt[:, c0:512], lhsT=tsel(kc), rhs=w_kvt[:, kc, g, c0:512], start=(kc == 0), stop=(kc == 7)),
                                 reads=[x1k[kc], wk[kc]], writes=[pk])
                        S.op('act', lambda e, g=g, bi=bi, pt=pt: e.activation(out=Vg[:, g, bi, :], in_=pt[:, 256:512], func=AF.Copy), reads=[pk], writes=[('Vg', g)])
                        if needK:
                            ko = kvo[cnt % 2]
                            kk = 'kvo%d' % (cnt % 2)
                            S.op('dve', lambda e, ko=ko, pt=pt: e.tensor_copy(out=ko[:], in_=pt[:, :]), reads=[pk], writes=[kk])
                            S.dma('sp', odst, ko[:], reads=[kk])
                        cnt += 1
                if stage >= 5:
                    kvs_f = sbt(esa, "kvs_f", [64, 3, 512])
                    for g in range(3):
                        pt = ps[g % 2]
                        for kc in range(8):
                            S.op('pe', lambda e, kc=kc, g=g, pt=pt: e.matmul(pt[0:NS, :], lhsT=x1Tf[:, kc, SEQ:NTOK], rhs=w_kvt[:, kc, g, :], start=(kc == 0), stop=(kc == 7)),
                                 reads=[x1k[kc], wk[kc]], writes=[PK[g % 2]])
                        S.op('dve', lambda e, g=g, pt=pt: e.tensor_copy(out=kvs_f[:, g, :], in_=pt[0:NS, :]), reads=[PK[g % 2]], writes=[('kvs_f', g)])
                        S.op('act', lambda e, g=g, pt=pt: e.activation(out=Vnew[:, g, :], in_=pt[0:NS, 256:512], func=AF.Copy), reads=[PK[g % 2]], writes=['Vnew'])
                        for b in range(NBC):
                            S.dma('sp', dkv_s[g][b, wins[g] - 4:wins[g], :], kvs_f[4 * b:4 * b + 4, g, :], reads=[('kvs_f', g)])
                S.barrier()
            if stage <= 3:
                return
            NT2 = 512
            stgA = sbt(es2, "stgA2", [128, 1024])
            stgB = sbt(es2, "stgB2", [128, 1024])
            stages = [(stgA, 'stgA'), (stgB, 'stgB')]
            w_in_sb = sbt(es2, "w_in_sb2", [128, 8, 2048], BF16)
            w_out_sb = sbt(es2, "w_out_sb2", [128, 8, 1024], BF16)
            load_weight(es2, w_in_sb, lambda kc: w_in[1, kc * 128:(kc + 1) * 128, :], 2048, stages, 'w_in_sb')
            load_weight(es2, w_out_sb, lambda kc: w_out[1, kc * 128:(kc + 1) * 128, :], 1024, stages, 'w_out_sb')
            S.dma('sp', lnG[:], ln_g[1].partition_broadcast(128), writes=['lnG'])
            S.dma('sp', lnB[:], ln_b[1].partition_broadcast(128), writes=['lnB'])
            x1Tt = sbt(es2, "x1Tt", [128, 8, NT2], BF16)
            qT = sbt(es2, "qT", [128, 6, NT2], BF16)
            sg = sbt(es2, "sg2", [128, 6, NT2], BF16)
            mq = sbt(es2, "mq2", [128, 2, NT2], BF16)
            smg = sbt(es2, "smg2", [128, 2, NT2], BF16)
            catT = sbt(es2, "catT2", [128, 8, NT2], BF16)
            PT = sbt(es2, "PTm2", [128, 2 * NT2], BF16)
            rden = sbt(es2, "rden2", [128, 2 * NT2])
            zt = [sbt(es2, "zt2_%d" % i, [128, 1024]) for i in range(2)]
            stats = [sbt(es2, "stats2_%d" % i, [128, 2, 6]) for i in range(2)]
            mv = [sbt(es2, "mv2_%d" % i, [128, 2]) for i in range(2)]
            rstd = [sbt(es2, "rstd2_%d" % i, [128, 1]) for i in range(2)]
            nmr = [sbt(es2, "nmr2_%d" % i, [128, 1]) for i in range(2)]
            epsb = sbt(es2, "epsb2", [128, 1])
            S.op('dve', lambda e: e.memset(epsb[:], LN_EPS), writes=['epsb'])
            PTs2 = sbt(es2, "PTs2", [128, 32], BF16)
            kvb2 = sbt(es2, "kvb2", [128, 2, 2, 512], BF16)
            kts2 = sbt(es2, "kts2", [128, 2, 2, 256], BF16)
            T2 = SimpleNamespace(xres=[stgA, stgB], zt=zt, zn=zt, stats=stats, mv=mv, rstd=rstd, nmr=nmr, epsb=epsb, x1b=None, x1T=None, PT=PT, rden=rden, mo=rden,
                                 sg=sg, mq=mq, smg=smg, stages=stages, xb=None, xT=x1Tt, PTs=PTs2,
                                 kvb=[(kvb2[:, i, :, :], [('kvb2', i)]) for i in range(2)], kts=[(kts2[:, i, :, :], [('kts2', i)]) for i in range(2)])

            esq = ExitStack()
            distT = sbt(esq, "distT", [128, 256])
            S.dma('sp', distT[:], c_dist[:, :], writes=['distT'])
            scb = [sbt(esq, "scb%d" % i, [128, 256]) for i in range(4)]
            PTd = [sbt(esq, "PTd%d" % i, [128, 256], BF16) for i in range(4)]
            rtot = sbt(esq, "rtot", [128, NT2])
            atmp = [sbt(esq, "atmp%d" % i, [128, NT2]) for i in range(2)]

            def dil_attention(tt):
                NBUF = 4
                for pr in range(2):
                    psDEN = ps[7]
                    psOT = [ps[4], ps[5], ps[6]]
                    den_started = [False, False]
                    items = []
                    for g in range(3):
                        m = 2 * g + pr
                        units = []
                        if g == 0:
                            for jb in range(4):
                                J = 4 * tt + jb
                                qsel = (lambda ap, jb=jb: ap[:, 128 * jb:128 * jb + 128])
                                sub = []
                                if J >= 1:
                                    sub.append((qsel, (lambda hp, J=J, m=m: KT[hp, m, 128 * (J - 1):128 * J]), (0, J - 1), 128, 128, True))
                                sub.append((qsel, (lambda hp, J=J, m=m: KT[hp, m, 128 * J:128 * J + 128]), (0, J), 128, 128, J < 1))
                                dap = distT[:, 0:256] if J >= 1 else distT[:, 128:256]
                                units.append((sub, dap))
                        elif g == 1:
                            for r in range(4):
                                qsel = (lambda ap, r=r: ap.rearrange("p (i r) -> p r i", r=4)[:, r, :])
                                sub = []
                                if tt >= 1:
                                    sub.append((qsel, (lambda hp, r=r, m=m: KT[hp, m, 512 * (tt - 1):512 * tt].rearrange("p (u r) -> p r u", r=4)[:, r, :]), (1, 4 * (tt - 1) + r), 128, 128, True))
                                sub.append((qsel, (lambda hp, r=r, m=m: KT[hp, m, 512 * tt:512 * tt + 512].rearrange("p (u r) -> p r u", r=4)[:, r, :]), (1, 4 * tt + r), 128, 128, tt < 1))
                                dap = distT[:, 0:256] if tt >= 1 else distT[:, 128:256]
                                units.append((sub, dap))
                        else:
                            nk = 32 * (tt + 1)
                            for r4 in range(4):
                                sub = []
                                for rr in range(4):
                                    r = 4 * r4 + rr
                                    sub.append(((lambda ap, r=r: ap.rearrange("p (i r) -> p r i", r=16)[:, r, :]),
                                                (lambda hp, r=r, m=m, nk=nk: KT[hp, m, 0:2048].rearrange("p (u r) -> p r u", r=16)[:, r, 0:nk]), (2, r), nk, 32, True))
                                dap = distT[0:nk, 128 + 32 * tt:128 + 32 * tt + 32].unsqueeze(1).to_broadcast([nk, 4, 32])
                                units.append((sub, dap))
                        for (sub, dap) in units:
                            for half in range(2):
                                items.append((g, m, sub, dap, half))
                    LOOK = 3

                    def emit_qk_sm(it, ui):
                        g, m, sub, dap, half = it
                        hp = slice(64 * half, 64 * half + 64)
                        h = 4 * g + 2 * pr + half
                        cval = -SLOPES[h] * DILS[g] / SCALE
                        bi = ui % NBUF
                        pS = ps[bi]
                        pk = PK[bi]
                        nkmax = max(c[3] for c in sub)
                        ntot = sum(c[4] for c in sub)
                        off = 0
                        for (qsel, ksel, vix, nk, NQ, st) in sub:
                            S.op('pe', lambda e: e.matmul(pS[0:nk, off:off + NQ], lhsT=ksel(hp), rhs=qsel(qT[hp, m, :]), start=True, stop=True),
                                 reads=[('KT', m), ('uTb', m)], writes=[pk])
                            off += NQ
                        if g == 2:
                            o_ap = scb[bi][0:nkmax, 0:ntot].rearrange("p (a i) -> p a i", a=4)
                            i_ap = pS[0:nkmax, 0:ntot].rearrange("p (a i) -> p a i", a=4)
                        else:
                            o_ap = scb[bi][0:nkmax, 0:ntot]
                            i_ap = pS[0:nkmax, 0:ntot]
                        S.op('dve', lambda e: e.scalar_tensor_tensor(out=o_ap, in0=dap, scalar=cval, in1=i_ap, op0=ALU.mult, op1=ALU.add),
                             reads=[pk, 'distT'], writes=['scb%d' % bi])
                        S.op('act', lambda e: e.activation(out=PTd[bi][0:nkmax, 0:ntot], in_=scb[bi][0:nkmax, 0:ntot], func=AF.Exp, scale=SCALE),
                             reads=['scb%d' % bi], writes=['PTd%d' % bi])

                    def emit_pv(it, ui):
                        g, m, sub, dap, half = it
                        hp = slice(64 * half, 64 * half + 64)
                        bi = ui % NBUF
                        hc = (2 * pr + half) * 64
                        nsub = len(sub)
                        off = 0
                        for si, (qsel, ksel, vix, nk, NQ, st) in enumerate(sub):
                            last = (si == nsub - 1) or sub[si + 1][5]
                            S.op('pe', lambda e: e.matmul(qsel(psOT[g][hp, :]), lhsT=Vg[0:nk, vix[0], vix[1], hc:hc + 64], rhs=PTd[bi][0:nk, off:off + NQ], start=st, stop=last),
                                 reads=[('Vg', g), 'PTd%d' % bi], writes=[PK[4 + g]])
                            S.op('pe', lambda e: e.matmul(qsel(psDEN[hp, :]), lhsT=onesb[0:nk, 0:64], rhs=PTd[bi][0:nk, off:off + NQ], start=(not den_started[half]), stop=True,
                                                          skip_group_check=True),
                                 reads=['onesb', 'PTd%d' % bi], writes=['ps7'])
                            den_started[half] = True
                            off += NQ

                    for i in range(len(items) + LOOK):
                        if i < len(items):
                            emit_qk_sm(items[i], i)
                        if i - LOOK >= 0:
                            emit_pv(items[i - LOOK], i - LOOK)
                    S.op('dve', lambda e: e.reciprocal(out=rtot[:], in_=psDEN[:, :]), reads=['ps7'], writes=['rtot'])
                    for g in range(3):
                        m = 2 * g + pr
                        at = atmp[g % 2]
                        ak = 'atmp%d' % (g % 2)
                        S.op('dve', lambda e, g=g, at=at: e.tensor_tensor(out=at[:], in0=psOT[g][:, :], in1=rtot[:], op=ALU.mult), reads=[PK[4 + g], 'rtot'], writes=[ak])
                        S.op('pool', lambda e, m=m, at=at: e.tensor_tensor(out=catT[:, m, :], in0=at[:], in1=sg[:, m, :], op=ALU.mult), reads=[ak, ('sg', m)], writes=[('catT', m)])

            def sample_dil_attention():
              with ExitStack() as ess:
                qtok = sbt(ess, "qtok", [64, 768], BF16)
                ktile = sbt(ess, "ktile", [128, 4, 2, 256])
                vb = sbt(ess, "vb", [128, 9, 256], BF16)
                prod = [sbt(ess, "prod%d" % i, [128, 256]) for i in range(2)]
                scs = sbt(ess, "scs", [128, 48])
                scs2 = sbt(ess, "scs2", [128, 48])
                Pb = sbt(ess, "Pb", [128, 48], BF16)
                sbias = sbt(ess, "sbias", [128, 48])
                pv_sb = sbt(ess, "pv_sb", [128, NB * 48])
                den_sb = sbt(ess, "den_sb", [128, NB * 48])
                otot = sbt(ess, "otot", [128, 6, 64])
                dtot = sbt(ess, "dtot", [128, 2, 64])
                ndist = sbt(ess, "ndist", [64, 2, 64])
                scn = [sbt(ess, "scn%d" % i, [64, 64]) for i in range(2)]
                PN = [sbt(ess, "PN%d" % i, [64, 64], BF16) for i in range(2)]
                S.dma('sp', sbias[:], c_sdist.rearrange("p g x -> p (g x)"), writes=['sbias'])
                S.dma('sp', ndist[:], c_ndist[:, :, :], writes=['ndist'])
                for (pq, c0, cw) in [(ps[0], 0, 512), (ps[1], 512, 256)]:
                    for kc in range(8):
                        S.op('pe', lambda e, kc=kc, pq=pq, c0=c0, cw=cw: e.matmul(pq[0:NS, 0:cw], lhsT=x1Tt[:, kc, 0:NS], rhs=w_in_sb[:, kc, c0:c0 + cw], start=(kc == 0), stop=(kc == 7)),
                             reads=[('w_in_sb', kc), ('xT', kc)], writes=[PK[ps.index(pq)]])
                S.op('act', lambda e: e.activation(out=qtok[:, 0:512], in_=ps[0][0:NS, :], func=AF.Copy), reads=['ps0'], writes=['qtok'])
                S.op('dve', lambda e: e.tensor_copy(out=qtok[:, 512:768], in_=ps[1][0:NS, 0:256]), reads=['ps1', 'qtok'], writes=['qtok'])
                idx = 0
                for g in range(3):
                    for hh in range(4):
                        h = 4 * g + hh
                        pr, half = hh // 2, hh % 2
                        m = 2 * g + pr
                        hp = slice(64 * half, 64 * half + 64)
                        cval = -SLOPES[h] * DILS[g] / SCALE
                        bi = half
                        S.op('pe', lambda e, hp=hp, m=m, bi=bi: e.matmul(ps[2 + bi][0:NS, 0:NS], lhsT=KT[hp, m, SEQ:NTOK], rhs=qT[hp, m, 0:NS], start=True, stop=True),
                             reads=[('KT', m), ('uTb', m)], writes=[PK[2 + bi]])
                        S.op('dve', lambda e, bi=bi, g=g, cval=cval: e.scalar_tensor_tensor(out=scn[bi][:, :], in0=ndist[:, (0 if g == 0 else 1), :], scalar=cval, in1=ps[2 + bi][0:NS, 0:NS],
                                                                                          op0=ALU.mult, op1=ALU.add), reads=[PK[2 + bi], 'ndist'], writes=['scn%d' % bi])
                        S.op('act', lambda e, bi=bi: e.activation(out=PN[bi][:, :], in_=scn[bi][:, :], func=AF.Exp, scale=SCALE), reads=['scn%d' % bi], writes=['PN%d' % bi])
                        col = (g * 2 + pr) * 64
                        S.op('pe', lambda e, hp=hp, g=g, hh=hh, bi=bi, col=col: e.matmul(ps[6][hp, col:col + 64], lhsT=Vnew[0:NS, g, hh * 64:(hh + 1) * 64], rhs=PN[bi][:, :], start=True, stop=True),
                             reads=['Vnew', 'PN%d' % bi], writes=['ps6'])
                        S.op('pe', lambda e, hp=hp, bi=bi, col=col: e.matmul(ps[7][hp, col:col + 64], lhsT=onesb[0:NS, 0:64], rhs=PN[bi][:, :], start=True, stop=True),
                             reads=['onesb', 'PN%d' % bi], writes=['ps7'])
                        idx += 1
                kcnt = 0
                for b in range(NB):
                  for tp in range(2):
                    for t in (2 * tp, 2 * tp + 1):
                        s_ = 4 * b + t
                        pa, pb_ = (ps[0], ps[1]) if t % 2 == 0 else (ps[2], ps[3])
                        sel = identb[0:NS, s_:s_ + 1].to_broadcast([NS, 128])
                        S.op('pe', lambda e, pa=pa, sel=sel: e.matmul(pa[:, 0:512], lhsT=sel, rhs=qtok[:, 0:512], start=True, stop=True), reads=['qtok', 'identb'], writes=[PK[ps.index(pa)]])
                        S.op('pe', lambda e, pb_=pb_, sel=sel: e.matmul(pb_[:, 0:256], lhsT=sel, rhs=qtok[:, 512:768], start=True, stop=True), reads=['qtok', 'identb'], writes=[PK[ps.index(pb_)]])
                    for g in range(3):
                        kt = ktile[:, kcnt % 4, :, :]
                        kk = 'ktile%d' % (kcnt % 4)
                        kcnt += 1
                        if g == 0:
                            S.dma('sp', kt[:, 0, :], cd[0][b % NBC, :, 0:256], writes=[kk])
                            if tp == 0:
                                S.dma('pool', vb[:, 0, :], cd[0][b % NBC, :, 256:512], writes=[('vb', g)])
                        else:
                            r_ = 4 if g == 1 else 16
                            kb = 1 if g == 1 else 5
                            srcv = cd[g][b % NBC].rearrange("(u r) c -> u r c", r=r_)
                            S.dma('sp', kt[:, :, :], srcv[:, 2 * tp:2 * tp + 2, 0:256], writes=[kk])
                            S.dma('pool', vb[:, kb + 2 * tp:kb + 2 * tp + 2, :], srcv[:, 2 * tp:2 * tp + 2, 256:512], writes=[('vb', g)])
                        for t in (2 * tp, 2 * tp + 1):
                            pa, pb_ = (ps[0], ps[1]) if t % 2 == 0 else (ps[2], ps[3])
                            qb = pa[:, g * 256:(g + 1) * 256] if g < 2 else pb_[:, 0:256]
                            qk = PK[ps.index(pa)] if g < 2 else PK[ps.index(pb_)]
                            ki = 0 if g == 0 else t - 2 * tp
                            pi = (g * 4 + t) % 2
                            S.op('dve', lambda e, ki=ki, qb=qb, pi=pi, kt=kt: e.tensor_tensor(out=prod[pi][:, :], in0=kt[:, ki, :], in1=qb, op=ALU.mult),
                                 reads=[kk, qk], writes=['prod%d' % pi])
                            S.op('dve', lambda e, g=g, t=t, pi=pi: e.tensor_reduce(out=scs[:, (g * 4 + t) * 4:(g * 4 + t) * 4 + 4], in_=prod[pi][:, :].rearrange("p (h e) -> p h e", h=4),
                                                                                 axis=mybir.AxisListType.X, op=ALU.add),
                                 reads=['prod%d' % pi], writes=['scs'])
                  if True:
                    S.op('dve', lambda e: e.scalar_tensor_tensor(out=scs2[:], in0=scs[:], scalar=SCALE, in1=sbias[:], op0=ALU.mult, op1=ALU.add), reads=['scs', 'sbias'], writes=['scs2'])
                    S.op('act', lambda e: e.activation(out=Pb[:], in_=scs2[:], func=AF.Exp), reads=['scs2'], writes=['Pb'])
                    pB = ps[4 + b % 2]
                    pBk = PK[4 + b % 2]
                    for g in range(3):
                        for t in range(4):
                            ki = 0 if g == 0 else (1 + t if g == 1 else 5 + t)
                            for pr in range(2):
                                col = ((g * 4 + t) * 2 + pr) * 2
                                pcol = (g * 4 + t) * 4 + 2 * pr
                                S.op('pe', lambda e, ki=ki, pr=pr, col=col, pcol=pcol, pB=pB: e.matmul(pB[:, col:col + 2], lhsT=vb[:, ki, pr * 128:(pr + 1) * 128], rhs=Pb[:, pcol:pcol + 2],
                                                                                                      start=True, stop=True),
                                     reads=[('vb', g), 'Pb'], writes=[pBk])
                    S.op('pe', lambda e, pB=pB: e.matmul(pB[:, 64:112], lhsT=onesb[:, :], rhs=Pb[:, :], start=True, stop=True), reads=['onesb', 'Pb'], writes=[pBk])
                    S.op('act', lambda e, b=b, pB=pB: e.activation(out=pv_sb[:, b * 48:(b + 1) * 48], in_=pB[:, 0:48], func=AF.Copy), reads=[pBk], writes=['pv_sb'])
                    S.op('dve', lambda e, b=b, pB=pB: e.tensor_copy(out=den_sb[:, b * 48:(b + 1) * 48], in_=pB[:, 64:112]), reads=[pBk], writes=['den_sb'])
                for half in range(2):
                    hp = slice(64 * half, 64 * half + 64)
                    for g in range(3):
                        pvv = pv_sb[hp, :].rearrange("p (b g t r j) -> p g r j b t", g=3, t=4, r=2, j=2)[:, g, :, half, :, :]
                        dnv = den_sb[hp, :].rearrange("p (b g t r j) -> p g r j b t", g=3, t=4, r=2, j=2)[:, g, :, half, :, :]
                        nv = lambda pp, g=g, hp=hp: pp[hp, 2 * g * 64:(2 * g + 2) * 64].rearrange("p (r b t) -> p r b t", r=2, t=4)
                        S.op('dve', lambda e, pvv=pvv, nv=nv, g=g, hp=hp: e.tensor_tensor(out=otot[hp, 2 * g:2 * g + 2, :].rearrange("p r (b t) -> p r b t", t=4), in0=pvv, in1=nv(ps[6]), op=ALU.add),
                             reads=['pv_sb', 'ps6'], writes=['otot'])
                        dv = dtot[hp, :, :].rearrange("p r (b t) -> p r b t", t=4)
                        if g == 0:
                            S.op('dve', lambda e, dnv=dnv, nv=nv, dv=dv: e.tensor_tensor(out=dv, in0=dnv, in1=nv(ps[7]), op=ALU.add), reads=['den_sb', 'ps7'], writes=['dtot'])
                        else:
                            S.op('dve', lambda e, dnv=dnv, dv=dv: e.tensor_tensor(out=dv, in0=dv, in1=dnv, op=ALU.add), reads=['den_sb', 'dtot'], writes=['dtot'])
                            S.op('dve', lambda e, nv=nv, dv=dv: e.tensor_tensor(out=dv, in0=dv, in1=nv(ps[7]), op=ALU.add), reads=['ps7', 'dtot'], writes=['dtot'])
                S.op('dve', lambda e: e.reciprocal(out=dtot[:], in_=dtot[:]), reads=['dtot'], writes=['dtot'])
                for g in range(3):
                    S.op('dve', lambda e, g=g: e.tensor_tensor(out=otot[:, 2 * g:2 * g + 2, :], in0=otot[:, 2 * g:2 * g + 2, :], in1=dtot[:, :, :], op=ALU.mult), reads=['otot', 'dtot'], writes=['otot'])
                    S.op('pool', lambda e, g=g: e.tensor_tensor(out=catT[:, 2 * g:2 * g + 2, 0:NS], in0=otot[:, 2 * g:2 * g + 2, :], in1=sg[:, 2 * g:2 * g + 2, 0:NS], op=ALU.mult),
                         reads=['otot', ('sg', 2 * g), ('sg', 2 * g + 1)], writes=[('catT', 2 * g), ('catT', 2 * g + 1)])
                S.barrier()

            for tt in range(SEQ // NT2):
                t0 = tt * NT2
                for kc in range(8):
                    S.dma('sp', x1Tt[:, kc, :], x1Tscr[kc, :, t0:t0 + NT2], writes=[('xT', kc)])
                in_proj(T2, NT2, w_in_sb, x1Tt, qT, 1)
                dil_attention(tt)
                mem_attention(T2, 1, NT2, mq, smg, catT, KmT[:, 1, :, :], Vm[:, 1, :, :], [ps[2], ps[3]], [ps[4], ps[6]], [ps[5], ps[7]])
                blks = []
                for a in range(NT2 // 128):
                    ta = t0 + a * 128
                    blks.append((a, 128, x1scr[ta:ta + 128, :], ta, y_p[ta:ta + 128, :]))
                ln_blocks(T2, blks, catT, w_out_sb, False)
            S.barrier()
            esq.close()
            if stage >= 5 and KCUT != 53:
                for kc in range(8):
                    S.dma('sp', x1Tt[:, kc, 0:NS], x1Tscr[kc, :, SEQ:NTOK], writes=[('xT', kc)])
                in_proj(T2, NS, w_in_sb, x1Tt, qT, 1)
                if stage >= 6:
                    sample_dil_attention()
                else:
                    for m in range(6):
                        S.op('pool', lambda e, m=m: e.memset(catT[:, m, 0:NS], 0.0), writes=[('catT', m)])
                mem_attention_sample(T2, 1, smg, mq, catT, ps[2], ps[3], [ps[4], ps[6]], [ps[5], ps[7]])
                ln_blocks(T2, [(0, NS, x1scr[SEQ:NTOK, :], SEQ, y_s[0:NS, :])], catT, w_out_sb, False)
            S.barrier()

        if stage >= 3:
            with ExitStack() as es2:
                _phase2(es2)

        issue_bulk(len(bulk_list))
        S.finish()
        print("ops", S.nops, "waits", S.nwait)
    return nc


_CONST_CACHE = {}


def _consts():
    if _CONST_CACHE:
        return _CONST_CACHE
    ident = np.eye(128, dtype=np.float32)
    sw = np.zeros((128, 128), np.float32)
    for m in range(64):
        sw[64 + m, m] = -1.0
        sw[m, 64 + m] = 1.0
    p = np.arange(128)
    meo = np.zeros((128, 4), np.float32)
    for j in range(4):
        meo[:, j] = ((p // 16) % 4 == j)
    jv = np.tile(np.arange(128, dtype=np.float32)[None, :], (128, 1))
    u = np.arange(128)[:, None]
    i = np.arange(128)[None, :]
    dist = np.zeros((128, 2, 128), np.float32)
    dist[:, 0, :] = np.where(u >= i, i + 128 - u, BIG)
    dist[:, 1, :] = np.where(u <= i, i - u, BIG)
    sd = np.zeros((128, 3, 4, 4), np.float32)
    for g in range(3):
        for t in range(4):
            for hh in range(4):
                uu = np.arange(128)
                if g == 0:
                    dd = 128 + t - uu
                    sd[:, g, t, hh] = np.where(uu >= t, -SLOPES[4 * g + hh] * dd, -30000.0)
                else:
                    sd[:, g, t, hh] = -SLOPES[4 * g + hh] * ((128 - uu) * DILS[g])
    nd = np.full((64, 2, 64), BIG, np.float32)
    for sp_ in range(64):
        for s_ in range(64):
            if sp_ // 4 == s_ // 4 and sp_ % 4 <= s_ % 4:
                nd[sp_, 0, s_] = (s_ % 4) - (sp_ % 4)
            if sp_ == s_:
                nd[sp_, 1, s_] = 0.0
    _CONST_CACHE.update(dict(c_ident=ident, c_swap=sw, c_maskeo=meo, c_jvec=jv, c_dist=dist.reshape(128, 256),
                             c_sdist=sd.reshape(128, 3, 16), c_ndist=nd))
    return _CONST_CACHE


_NC_CACHE = {}


def kernel(x_prompt, x_sample, cache_mem_kv, state_ssm_re, state_ssm_im, cache_dil1_kv, cache_dil4_kv,
           cache_dil16_kv, mem_prompt, w_in, w_out, ln_g, ln_b, w_mem_kv, ssm_lambda_re, ssm_lambda_im,
           ssm_log_dt, ssm_b_re, ssm_b_im, ssm_c_re, ssm_c_im, ssm_d, w_glu, b_glu, w_kv_shared, _stage=99):
    f = lambda a: np.ascontiguousarray(np.asarray(a, dtype=np.float32))
    if _stage not in _NC_CACHE:
        _NC_CACHE[_stage] = build(_stage)
    nc = _NC_CACHE[_stage]
    shared = dict(
        w_in=f(w_in), w_out=f(w_out), ln_g=f(ln_g), ln_b=f(ln_b), w_mem=f(w_mem_kv),
        lam_re=f(ssm_lambda_re)[0], lam_im=f(ssm_lambda_im)[0], log_dt=f(ssm_log_dt)[0],
        b_re=f(ssm_b_re)[0], b_im=f(ssm_b_im)[0],
        c_re=f(ssm_c_re)[0].reshape(768, 64), c_im=f(ssm_c_im)[0].reshape(768, 64),
        ssm_d=f(ssm_d)[0].reshape(768), w_glu=f(w_glu)[0], b_glu=f(b_glu)[0], w_kv=f(w_kv_shared))
    shared.update(_consts())
    x_prompt = np.asarray(x_prompt)
    in_maps = []
    for c in range(NCORES):
        bs = slice(NB * c, NB * (c + 1))
        d = dict(shared)
        d.update(
            x_p=f(x_prompt[c]), x_s=f(np.asarray(x_sample)[bs]).reshape(NS, 1024),
            cmk=f(np.asarray(cache_mem_kv)[:, bs]).reshape(2, NB, 256, 512),
            st_re=f(np.asarray(state_ssm_re)[0, bs]).reshape(NB * 48, 64),
            st_im=f(np.asarray(state_ssm_im)[0, bs]).reshape(NB * 48, 64),
            cd1=f(np.asarray(cache_dil1_kv)[bs]).reshape(NB, 128, 512),
            cd4=f(np.asarray(cache_dil4_kv)[bs]).reshape(NB, 512, 512),
            cd16=f(np.asarray(cache_dil16_kv)[bs]).reshape(NB, 2048, 512),
            memp=f(np.asarray(mem_prompt)[c]))
        if _stage < 5:
            for k in ('cmk',):
                d[k] = np.ascontiguousarray(d[k][:, 0:1])
        if _stage < 5 or (50 <= KCUT < 60):
            for k in ('cd1', 'cd4', 'cd16'):
                d[k] = np.ascontiguousarray(d[k][0:1])
        in_maps.append(d)
    res = run_bass_kernel_spmd(nc, in_maps, core_ids=list(range(NCORES)))
    R = res.results
    cat = lambda k: np.stack([np.asarray(R[c][k]) for c in range(NCORES)], axis=0)
    y_prompt = cat("y_p")
    y_sample = cat("y_s").reshape(128, 4, 1024)
    mem_kv_prompt = cat("mkv_p").transpose(1, 0, 2, 3).reshape(2, 8, 256, 2, 4, 64)
    ssm_re_prompt = cat("sre_p")[None]
    ssm_im_prompt = cat("sim_p")[None]
    d1p = cat("d1_p").reshape(8, 128, 2, 4, 64)
    d4p = cat("d4_p").reshape(8, 512, 2, 4, 64)
    d16p = cat("d16_p").reshape(8, 2048, 2, 4, 64)
    ssm_re_sample = cat("sre_s").reshape(1, 128, 48, 64)
    ssm_im_sample = cat("sim_s").reshape(1, 128, 48, 64)
    if _stage < 5 or (50 <= KCUT < 60):
        z = lambda *sh: np.zeros(sh, np.float32)
        return (y_prompt, y_sample, mem_kv_prompt, ssm_re_prompt, ssm_im_prompt, d1p, d4p, d16p, ssm_re_sample, ssm_im_sample,
                z(128, 128, 2, 4, 64), z(128, 512, 2, 4, 64), z(128, 2048, 2, 4, 64))
    d1s = cat("d1_s").reshape(128, 128, 2, 4, 64)
    d4s = cat("d4_s").reshape(128, 512, 2, 4, 64)
    d16s = cat("d16_s").reshape(128, 2048, 2, 4, 64)
    return (y_prompt, y_sample, mem_kv_prompt, ssm_re_prompt, ssm_im_prompt, d1p, d4p, d16p,
            ssm_re_sample, ssm_im_sample, d1s, d4s, d16s)
```
